# Optimizing a Trainium2 kernel written in Bass

```python
import math
import jax
import jax.numpy as jnp
from jax import lax
import numpy as np

D_MODEL = 1024
BATCH = 8
SEQ = 4096
DEPTH = 1

CHUNK = 64
Q_BLOCK = 128
MEM_LEN = 256
DA_HEADS = 8
DA_HEAD_DIM = 64
DA_V_DIM = 2 * DA_HEAD_DIM
DA_WIDTH = DA_HEADS * DA_V_DIM
SSM_WIDTH = D_MODEL
SSM_GROUP = 16
SSM_GROUPS = SSM_WIDTH // SSM_GROUP
SSM_STATE = 64
DT_MIN = 1e-3
DT_MAX = 1e-1
XA_HEADS = 4
XA_HEAD_DIM = D_MODEL // XA_HEADS
XA_WIDTH = XA_HEADS * XA_HEAD_DIM
N_BRANCH = 3
REL_BUCKETS = 32
REL_MAX_DIST = 256
FFN_HIDDEN = -(-8 * D_MODEL // (3 * 256)) * 256
Q_W = DA_HEADS * 2 * DA_HEAD_DIM
K_W = DA_HEADS * 2 * DA_HEAD_DIM
V_W = DA_WIDTH
U_W = SSM_WIDTH
XQ_W = XA_WIDTH
G_W = N_BRANCH * D_MODEL
IN_WIDTH = Q_W + K_W + V_W + U_W + XQ_W + G_W
RMS_EPS = 1e-6

kernel_name = 'hybrid_diffattn_s5_xattn_gated_block'


def rmsnorm(x, g):
    xf = x.astype(jnp.float32)
    y = xf * lax.rsqrt(jnp.mean(xf * xf, axis=-1, keepdims=True) + RMS_EPS)
    return (y * g.astype(jnp.float32)).astype(x.dtype)


def t5_bucket(rel):
    half = REL_BUCKETS // 2
    max_exact = half // 2
    ret = jnp.where(rel > 0, half, 0)
    n = jnp.abs(rel)
    nf = jnp.maximum(n, 1).astype(jnp.float32)
    large = max_exact + (jnp.log(nf / max_exact) / math.log(REL_MAX_DIST / max_exact)
                         * (half - max_exact)).astype(jnp.int32)
    large = jnp.minimum(large, half - 1)
    return ret + jnp.where(n < max_exact, n, large)


def diff_attention(q, k, v, lam, lam_init, subln_g, rel_bias):
    B, S = q.shape[0], q.shape[1]
    scale = DA_HEAD_DIM ** -0.5
    pos = jnp.arange(S, dtype=jnp.int32)
    outs = []
    for i in range(S // Q_BLOCK):
        q_lo, q_hi = i * Q_BLOCK, (i + 1) * Q_BLOCK
        qp, kp = pos[q_lo:q_hi], pos[:q_hi]
        allowed = (kp[None, :] // CHUNK) <= (qp[:, None] // CHUNK)
        bias = jnp.transpose(rel_bias[t5_bucket(kp[None, :] - qp[:, None])], (2, 0, 1)).astype(jnp.float32)
        s = jnp.einsum('bqhcd,bkhcd->bchqk', q[:, q_lo:q_hi], k[:, :q_hi]).astype(jnp.float32) * scale + bias
        s = jnp.where(allowed, s, -jnp.inf)
        p = jax.nn.softmax(s, axis=-1)
        a = p[:, 0] - lam * p[:, 1]
        outs.append(jnp.einsum('bhqk,bkhe->bqhe', a, v[:, :q_hi].astype(jnp.float32)))
    o = jnp.concatenate(outs, axis=1)
    o = o * lax.rsqrt(jnp.mean(o * o, axis=-1, keepdims=True) + RMS_EPS)
    o = o * subln_g.astype(jnp.float32) * (1.0 - lam_init)
    return o.reshape(B, S, DA_WIDTH)


def s5_ssm(u, a_re, a_im, log_dt, b_re, b_im, c_re, c_im, d_skip):
    B, S = u.shape[0], u.shape[1]
    uf = u.astype(jnp.float32)
    lam = lax.complex(a_re.astype(jnp.float32), a_im.astype(jnp.float32))
    dt = jnp.exp(log_dt.astype(jnp.float32))[:, None]
    a_bar = jnp.exp(lam * dt)
    b = lax.complex(b_re.astype(jnp.float32), b_im.astype(jnp.float32))
    b_bar = ((a_bar - 1.0) / lam)[..., None] * b
    c = lax.complex(c_re.astype(jnp.float32), c_im.astype(jnp.float32))

    def combine(e1, e2):
        a1, x1 = e1
        a2, x2 = e2
        return a1 * a2, a2 * x1 + x2

    def chunk_step(h, u_c):
        bu = jnp.einsum('gpc,blgc->blgp', b_bar, u_c)
        a_seq = jnp.broadcast_to(a_bar, bu.shape)
        a_cum, x_c = lax.associative_scan(combine, (a_seq, bu), axis=1)
        x_c = x_c + a_cum * h[:, None]
        y = jnp.einsum('gcp,blgp->blgc', c, x_c).real
        return x_c[:, -1], y

    u_chunks = jnp.moveaxis(uf.reshape(B, S // CHUNK, CHUNK, SSM_GROUPS, SSM_GROUP), 1, 0)
    h0 = jnp.zeros((B, SSM_GROUPS, SSM_STATE), jnp.complex64)
    _, ys = lax.scan(chunk_step, h0, u_chunks)
    ys = jnp.moveaxis(ys, 0, 1).reshape(B, S, SSM_WIDTH)
    return ys + d_skip.astype(jnp.float32) * uf


def memory_cross_attention(xq, mem_n, w_mem_kv):
    B, S = xq.shape[0], xq.shape[1]
    kv = mem_n @ w_mem_kv
    k = kv[..., :XA_WIDTH].reshape(B, -1, XA_HEADS, XA_HEAD_DIM)
    v = kv[..., XA_WIDTH:].reshape(B, -1, XA_HEADS, XA_HEAD_DIM)
    q = xq.reshape(B, S, XA_HEADS, XA_HEAD_DIM)
    s = jnp.einsum('bshd,bmhd->bhsm', q, k).astype(jnp.float32) * XA_HEAD_DIM ** -0.5
    p = jax.nn.softmax(s, axis=-1)
    o = jnp.einsum('bhsm,bmhd->bshd', p, v.astype(jnp.float32))
    return o.reshape(B, S, XA_WIDTH)


def setup_inputs(seed: int = 0) -> dict:
    key = jax.random.key(seed)
    ks = jax.random.split(key, 32)
    f32 = jnp.float32

    def nrm(k, shape, std):
        return jax.random.normal(k, shape, f32) * std

    L = DEPTH
    a_im = jnp.broadcast_to(jnp.pi * jnp.arange(SSM_STATE, dtype=f32), (L, SSM_GROUPS, SSM_STATE))
    return {
        'x': nrm(ks[0], (BATCH, SEQ, D_MODEL), 1.0),
        'mem': nrm(ks[1], (BATCH, MEM_LEN, D_MODEL), 1.0),
        'norm1_g': 1.0 + nrm(ks[2], (L, D_MODEL), 0.01),
        'w_in': nrm(ks[3], (L, D_MODEL, IN_WIDTH), D_MODEL ** -0.5),
        'da_lq1': nrm(ks[4], (L, DA_HEAD_DIM), 0.1),
        'da_lk1': nrm(ks[5], (L, DA_HEAD_DIM), 0.1),
        'da_lq2': nrm(ks[6], (L, DA_HEAD_DIM), 0.1),
        'da_lk2': nrm(ks[7], (L, DA_HEAD_DIM), 0.1),
        'da_subln_g': 1.0 + nrm(ks[8], (L, DA_V_DIM), 0.01),
        'rel_bias': nrm(ks[9], (REL_BUCKETS, DA_HEADS), 0.2),
        'ssm_a_re': -0.5 + nrm(ks[10], (L, SSM_GROUPS, SSM_STATE), 0.01),
        'ssm_a_im': a_im + nrm(ks[11], (L, SSM_GROUPS, SSM_STATE), 0.01),
        'ssm_log_dt': jax.random.uniform(ks[12], (L, SSM_GROUPS), f32, math.log(DT_MIN), math.log(DT_MAX)),
        'ssm_b_re': nrm(ks[13], (L, SSM_GROUPS, SSM_STATE, SSM_GROUP), (2 * SSM_GROUP) ** -0.5),
        'ssm_b_im': nrm(ks[14], (L, SSM_GROUPS, SSM_STATE, SSM_GROUP), (2 * SSM_GROUP) ** -0.5),
        'ssm_c_re': nrm(ks[15], (L, SSM_GROUPS, SSM_GROUP, SSM_STATE), (2 * SSM_STATE) ** -0.5),
        'ssm_c_im': nrm(ks[16], (L, SSM_GROUPS, SSM_GROUP, SSM_STATE), (2 * SSM_STATE) ** -0.5),
        'ssm_d': nrm(ks[17], (L, SSM_WIDTH), 0.5),
        'glu_w': nrm(ks[18], (L, SSM_WIDTH, SSM_WIDTH), SSM_WIDTH ** -0.5),
        'glu_b': nrm(ks[19], (L, SSM_WIDTH), 0.01),
        'mem_norm_g': 1.0 + nrm(ks[20], (L, D_MODEL), 0.01),
        'w_mem_kv': nrm(ks[21], (L, D_MODEL, 2 * XA_WIDTH), D_MODEL ** -0.5),
        'w_br_attn': nrm(ks[22], (L, DA_WIDTH, D_MODEL), DA_WIDTH ** -0.5),
        'w_br_ssm': nrm(ks[23], (L, SSM_WIDTH, D_MODEL), SSM_WIDTH ** -0.5),
        'w_br_xattn': nrm(ks[24], (L, XA_WIDTH, D_MODEL), XA_WIDTH ** -0.5),
        'w_out': nrm(ks[25], (L, D_MODEL, D_MODEL), D_MODEL ** -0.5),
        'norm2_g': 1.0 + nrm(ks[26], (L, D_MODEL), 0.01),
        'w_ffn_in': nrm(ks[27], (L, D_MODEL, 2 * FFN_HIDDEN), D_MODEL ** -0.5),
        'w_ffn_out': nrm(ks[28], (L, FFN_HIDDEN, D_MODEL), FFN_HIDDEN ** -0.5),
        'final_g': 1.0 + nrm(ks[29], (D_MODEL,), 0.01),
    }


def reference(x, mem, norm1_g, w_in, da_lq1, da_lk1, da_lq2, da_lk2, da_subln_g, rel_bias,
              ssm_a_re, ssm_a_im, ssm_log_dt, ssm_b_re, ssm_b_im, ssm_c_re, ssm_c_im, ssm_d,
              glu_w, glu_b, mem_norm_g, w_mem_kv, w_br_attn, w_br_ssm, w_br_xattn, w_out,
              norm2_g, w_ffn_in, w_ffn_out, final_g):
    B, S, _ = x.shape
    for layer in range(DEPTH):
        h = rmsnorm(x, norm1_g[layer])
        proj = h @ w_in[layer]
        o = 0
        q = proj[..., o:o + Q_W].reshape(B, S, DA_HEADS, 2, DA_HEAD_DIM); o += Q_W
        k = proj[..., o:o + K_W].reshape(B, S, DA_HEADS, 2, DA_HEAD_DIM); o += K_W
        v = proj[..., o:o + V_W].reshape(B, S, DA_HEADS, DA_V_DIM); o += V_W
        u = proj[..., o:o + U_W]; o += U_W
        xq = proj[..., o:o + XQ_W]; o += XQ_W
        gates = jax.nn.sigmoid(proj[..., o:o + G_W].astype(jnp.float32)).reshape(B, S, N_BRANCH, D_MODEL)

        lam_init = 0.8 - 0.6 * math.exp(-0.3 * layer)
        lam = (jnp.exp(jnp.sum(da_lq1[layer].astype(jnp.float32) * da_lk1[layer].astype(jnp.float32)))
               - jnp.exp(jnp.sum(da_lq2[layer].astype(jnp.float32) * da_lk2[layer].astype(jnp.float32)))
               + lam_init)
        y_attn = diff_attention(q, k, v, lam, lam_init, da_subln_g[layer], rel_bias).astype(x.dtype)

        y_s = s5_ssm(u, ssm_a_re[layer], ssm_a_im[layer], ssm_log_dt[layer], ssm_b_re[layer],
                     ssm_b_im[layer], ssm_c_re[layer], ssm_c_im[layer], ssm_d[layer])
        z = jax.nn.gelu(y_s).astype(x.dtype)
        y_ssm = z * jax.nn.sigmoid(z @ glu_w[layer] + glu_b[layer])

        mem_n = rmsnorm(mem, mem_norm_g[layer])
        y_x = memory_cross_attention(xq, mem_n, w_mem_kv[layer]).astype(x.dtype)

        mixed = (gates[:, :, 0] * (y_attn @ w_br_attn[layer])
                 + gates[:, :, 1] * (y_ssm @ w_br_ssm[layer])
                 + gates[:, :, 2] * (y_x @ w_br_xattn[layer]))
        x = x + mixed.astype(x.dtype) @ w_out[layer]

        h2 = rmsnorm(x, norm2_g[layer])
        gu = h2 @ w_ffn_in[layer]
        x = x + (jax.nn.silu(gu[..., :FFN_HIDDEN]) * gu[..., FFN_HIDDEN:]) @ w_ffn_out[layer]
    return rmsnorm(x, final_g)
```

```python
import math
from contextlib import ExitStack

import numpy as np

import concourse.bass as bass
import concourse.mybir as mybir
from concourse.bass_utils import run_bass_kernel_spmd

F32 = mybir.dt.float32
BF16 = mybir.dt.bfloat16
AF = mybir.ActivationFunctionType
ALU = mybir.AluOpType
AX = mybir.AxisListType

S = 4096
D = 1024
NTT = 32
NTB = 8
FF = 2816
NH = 22
NEG = -30000.0
LAM_INIT = 0.8 - 0.6 * math.exp(0.0)


class Buf:
    __slots__ = ("w", "r")

    def __init__(self):
        self.w = None
        self.r = {}


class Eng:
    def __init__(self, obj, sem, name):
        self.obj = obj
        self.sem = sem
        self.count = 0
        self.seen = {}
        self.name = name


class DS:
    def __init__(self, sem):
        self.sem = sem
        self.count = 0


class Builder:
    def __init__(self, nc, debug=False):
        self.nc = nc
        self.debug = debug
        self.es = ExitStack()
        self.E = {}
        for name, obj in (("pe", nc.tensor), ("act", nc.scalar), ("dve", nc.vector), ("pool", nc.gpsimd), ("sp", nc.sync)):
            self.E[name] = Eng(obj, self.es.enter_context(nc.semaphore("sem_" + name)), name)
        self.all_ds = []
        self.nname = 0

    def sb(self, shape, dt, stack=None, name=None):
        self.nname += 1
        return (stack or self.es).enter_context(self.nc.sbuf_tensor("%s_%d" % (name or "t", self.nname), list(shape), dt))

    def ps(self, shape, dt, stack=None, name=None):
        self.nname += 1
        return (stack or self.es).enter_context(self.nc.psum_tensor("%s_%d" % (name or "p", self.nname), list(shape), dt))

    def ds(self):
        self.nname += 1
        d = DS(self.es.enter_context(self.nc.semaphore("ds_%d" % self.nname)))
        self.all_ds.append(d)
        return d

    def _wait(self, E, ev):
        if ev is None:
            return
        sem, val = ev
        k = id(sem)
        if E.seen.get(k, 0) >= val:
            return
        E.obj.wait_ge(sem, val)
        E.seen[k] = val

    def _deps(self, E, reads, writes):
        own = E.sem
        pe = E.name == "pe"
        for b in reads:
            if b.w is not None and not (pe and b.w[0] is own):
                self._wait(E, b.w)
        for b in writes:
            if b.w is not None and not (pe and b.w[0] is own):
                self._wait(E, b.w)
            for ev in b.r.values():
                if not (pe and ev[0] is own):
                    self._wait(E, ev)

    def op(self, e, fn, reads=(), writes=()):
        E = self.E[e]
        self._deps(E, reads, writes)
        ins = fn(E.obj)
        E.count += 1
        ins.then_inc(E.sem, 1)
        ev = (E.sem, E.count)
        for b in reads:
            b.r[id(E.sem)] = ev
        for b in writes:
            b.w = ev
            b.r = {}

    def dma(self, q, out, in_, ds, reads=(), writes=()):
        E = self.E[q]
        self._deps(E, reads, writes)
        ins = E.obj.dma_start(out=out, in_=in_)
        ds.count += 16
        ins.then_inc(ds.sem, 16)
        ev = (ds.sem, ds.count)
        for b in reads:
            b.r[id(ds.sem)] = ev
        for b in writes:
            b.w = ev
            b.r = {}

    def barrier(self):
        for E in self.E.values():
            for Fo in self.E.values():
                if Fo.count > 0 and not (Fo is E and E.name == "pe"):
                    self._wait(E, (Fo.sem, Fo.count))
            for d in self.all_ds:
                if d.count > 0:
                    self._wait(E, (d.sem, d.count))


def _t5_bucket(rel):
    half, max_exact = 16, 8
    ret = np.where(rel > 0, half, 0)
    n = np.abs(rel)
    nf = np.maximum(n, 1).astype(np.float32)
    large = max_exact + (np.log(nf / np.float32(max_exact)) / np.float32(math.log(256 / max_exact)) * np.float32(half - max_exact)).astype(np.int32)
    large = np.minimum(large, half - 1)
    return ret + np.where(n < max_exact, n, large)


def host_consts():
    c = {}
    c["ident"] = np.eye(128, dtype=np.float32)
    c["antiid"] = np.eye(128, dtype=np.float32)[::-1].copy()
    ii = np.arange(1280)
    b = _t5_bucket(511 - ii)
    oh = np.zeros((32, 1280), np.float32)
    oh[b, ii] = 1.0
    c["onehot"] = oh
    p = np.arange(128)[:, None]
    j = np.arange(1152)[None, :]
    c["maskmb"] = np.where((p // 64) <= np.floor_divide(j - 384, 64), 0.0, NEG).astype(np.float32)
    r = np.arange(128)[:, None, None]
    it = np.arange(4)[None, :, None]
    col = np.arange(512)[None, None, :]
    c["maskd"] = ((col[:, 0, 0:128] // 32) >= (r[:, 0, :] // 32)).astype(np.float32)
    sel = np.zeros((128, 4, 128), np.float32)
    for rr in range(128):
        for il in range(4):
            sel[rr, il, 32 * il + rr % 32] = 1.0
    c["sel"] = sel
    return c


def host_layout(inp, b):
    f = np.float32
    m = {}
    m["x"] = np.ascontiguousarray(inp["x"][b])
    m["mem"] = np.ascontiguousarray(inp["mem"][b])
    for k in ("w_in", "glu_w", "w_mem_kv", "w_br_attn", "w_br_ssm", "w_br_xattn", "w_out", "w_ffn_in", "w_ffn_out"):
        m[k] = np.ascontiguousarray(inp[k][0])
    gb = np.stack([np.broadcast_to(inp["norm1_g"][0], (128, D)), np.broadcast_to(inp["mem_norm_g"][0], (128, D)),
                   np.broadcast_to(inp["norm2_g"][0], (128, D)), np.broadcast_to(inp["final_g"], (128, D))], axis=1)
    m["gains"] = np.ascontiguousarray(gb, dtype=f)
    m["subg"] = np.ascontiguousarray(np.broadcast_to(inp["da_subln_g"][0], (128, 128)), dtype=f)
    lqk = np.stack([inp["da_lq1"][0], inp["da_lk1"][0], inp["da_lq2"][0], inp["da_lk2"][0]], axis=0)
    m["lqk"] = np.ascontiguousarray(np.broadcast_to(lqk, (128, 4, 64)), dtype=f)
    m["relb"] = np.ascontiguousarray(inp["rel_bias"], dtype=f)
    m["relb15"] = np.ascontiguousarray(np.broadcast_to(inp["rel_bias"][15], (128, 8)), dtype=f)

    def st(a):
        return np.ascontiguousarray(a.reshape(32, 2, 64).transpose(1, 2, 0).reshape(128, 32), dtype=f)
    m["s_are"] = st(inp["ssm_a_re"][0])
    m["s_aim"] = st(inp["ssm_a_im"][0])
    m["s_ldt"] = st(np.broadcast_to(inp["ssm_log_dt"][0][:, None], (64, 64)))
    def stb(a):
        return np.ascontiguousarray(a.reshape(32, 2, 64, 16).transpose(1, 2, 0, 3).reshape(128, 32, 16), dtype=f)
    m["s_bre"] = stb(inp["ssm_b_re"][0])
    m["s_bim"] = stb(inp["ssm_b_im"][0])
    def stc(a):
        return np.ascontiguousarray(a.reshape(32, 2, 16, 64).transpose(1, 3, 0, 2).reshape(128, 32, 16), dtype=f)
    m["s_cre"] = stc(inp["ssm_c_re"][0])
    m["s_cim"] = stc(inp["ssm_c_im"][0])
    m["s_d"] = np.ascontiguousarray(inp["ssm_d"][0].reshape(8, 128).T, dtype=f)
    m["glu_b"] = np.ascontiguousarray(inp["glu_b"][0].reshape(8, 128).T, dtype=f)
    return m


INPUT_SHAPES = {
    "x": [S, D], "mem": [256, D], "w_in": [D, 8192], "glu_w": [D, D], "w_mem_kv": [D, 2048], "w_br_attn": [D, D],
    "w_br_ssm": [D, D], "w_br_xattn": [D, D], "w_out": [D, D], "w_ffn_in": [D, 2 * FF], "w_ffn_out": [FF, D],
    "gains": [128, 4, D], "subg": [128, 128], "lqk": [128, 4, 64], "relb": [32, 8], "relb15": [128, 8],
    "s_are": [128, 32], "s_aim": [128, 32], "s_ldt": [128, 32], "s_bre": [128, 32, 16], "s_bim": [128, 32, 16],
    "s_cre": [128, 32, 16], "s_cim": [128, 32, 16], "s_d": [128, 8], "glu_b": [128, 8],
    "ident": [128, 128], "antiid": [128, 128], "onehot": [32, 1280], "maskmb": [128, 1152], "maskd": [128, 128],
    "sel": [128, 4, 128],
}


ALL_PHASES = ("AP", "B", "C", "D", "T1", "T2")


def build_program(debug=False, upto="all", phases=ALL_PHASES):
    nc = bass.Bass("TRN2", target_bir_lowering=False)
    B = Builder(nc, debug)
    I = {k: nc.dram_tensor(k, shp, F32, kind="ExternalInput").ap() for k, shp in INPUT_SHAPES.items()}
    out_d = nc.dram_tensor("out", [S, D], F32, kind="ExternalOutput").ap()
    def scratch(name, shape, dt, producer=None):
        if producer is not None and producer not in phases:
            kind = "ExternalInput"
        else:
            kind = "ExternalOutput" if debug else "Internal"
        return nc.dram_tensor(name, shape, dt, kind=kind)

    QTs = scratch("QTs", [D, S], BF16, "AP")
    KTs = scratch("KTs", [D, S], BF16, "AP")
    Vs = scratch("Vs", [S, D], BF16, "AP")
    UTs = scratch("UTs", [D, S], BF16, "AP")
    XQs = scratch("XQs", [D, S], BF16, "AP")
    GTs = scratch("GTs", [3 * D, S], BF16, "AP")
    YAs = scratch("YAs", [D, S], BF16, "B")
    YXs = scratch("YXs", [D, S], BF16, "C")
    ZTs = scratch("ZTs", [D, S], BF16, "D")
    YIs = scratch("YIs", [D, S], BF16)
    CXs = scratch("CXs", [32, 128, 1024], BF16)
    X1s = scratch("X1s", [S, D], F32, "T1")
    Gd = scratch("Gd", [8, 1280], F32)
    db = {k: Buf() for k in ("QTs", "KTs", "Vs", "UTs", "XQs", "GTs", "YAs", "YXs", "ZTs", "YIs", "CXs", "X1s", "Gd", "out")}

    op, dma = B.op, B.dma
    es = B.es

    def fv(apobj, dims):
        return bass.AP(apobj.tensor, apobj.offset, [list(apobj.ap[0])] + [list(d) for d in dims])

    ident_f = B.sb([128, 128], F32); ident_b = B.sb([128, 128], BF16)
    gains = B.sb([128, 4, D], F32)
    eps_t = B.sb([128, 1], F32)
    eps2_t = B.sb([128, 1], F32)
    cbuf = Buf()
    cds = B.ds()
    dma("sp", ident_f[:], I["ident"], cds, writes=[cbuf])
    dma("sp", gains[:], I["gains"], cds, writes=[cbuf])
    op("dve", lambda e: e.tensor_copy(out=ident_b[:], in_=ident_f[:]), reads=[cbuf], writes=[cbuf])
    op("dve", lambda e: e.memset(eps_t[:], 1e-6), writes=[cbuf])
    op("dve", lambda e: e.memset(eps2_t[:], 1e-6 / 0.64), writes=[cbuf])
    mhalf = B.sb([128, 1], F32)
    op("dve", lambda e: e.memset(mhalf[:], -0.5), writes=[cbuf])
    mhalf4 = B.sb([128, 4], F32)
    op("dve", lambda e: e.memset(mhalf4[:], -0.5), writes=[cbuf])

    def rms_T(src, src_buf, gidx, dstT, dst_buf, sl, psT, psT_buf, evac="dve"):
        op("act", lambda e: e.activation(out=sl["hb"][:], in_=src, func=AF.Square, accum_out=sl["ss"][:]),
           reads=[src_buf], writes=[sl["hbb"], sl["ssb"]])
        op("pool", lambda e: e.tensor_scalar(out=sl["rs"][:], in0=sl["ss"][:], scalar1=1.0 / D, scalar2=1e-6, op0=ALU.mult, op1=ALU.add),
           reads=[sl["ssb"]], writes=[sl["rsb"]])
        op("pool", lambda e: e.tensor_tensor(out=sl["rr"][:], in0=sl["rs"][:], in1=mhalf[:], op=ALU.pow), reads=[sl["rsb"], cbuf], writes=[sl["rrb"]])
        op("dve", lambda e: e.scalar_tensor_tensor(out=sl["hb"][:], in0=src, scalar=sl["rr"][:], in1=gains[:, gidx, :],
                                                   op0=ALU.mult, op1=ALU.mult),
           reads=[src_buf, sl["rrb"], cbuf], writes=[sl["hbb"]])

        def tr(e):
            for k in range(8):
                ins = e.transpose(out=psT[:, k, :], in_=sl["hb"][:, k * 128:(k + 1) * 128], identity=ident_b[:])
            return ins
        op("pe", tr, reads=[sl["hbb"], cbuf], writes=[psT_buf])
        op(evac, lambda e: (e.tensor_copy(out=dstT, in_=psT[:]) if evac == "dve" else e.copy(out=dstT, in_=psT[:])),
           reads=[psT_buf], writes=[dst_buf])

    def rms_slot(stack):
        return {"hb": B.sb([128, D], BF16, stack), "ss": B.sb([128, 1], F32, stack), "rs": B.sb([128, 1], F32, stack),
                "rr": B.sb([128, 1], F32, stack), "hbb": Buf(), "ssb": Buf(), "rsb": Buf(), "rrb": Buf()}

    def phase_AP():
        with ExitStack() as st:
            hT = B.sb([128, 8, S], BF16, st, "hT")
            hTb = [Buf() for _ in range(NTT)]
            psT = B.ps([128, 8, 128], BF16, st)
            psTb = Buf()
            wsl = [(B.sb([128, 8, 512], BF16, st), Buf(), B.ds()) for _ in range(2)]
            for cb in range(2):
                dma("pool", wsl[cb][0][:], I["w_in"][:, cb * 512:(cb + 1) * 512].rearrange("(k p) n -> p k n", p=128), wsl[cb][2], writes=[wsl[cb][1]])
            with ExitStack() as st2:
                xs = [(B.sb([128, D], F32, st2), Buf(), B.ds()) for _ in range(2)]
                sls = [rms_slot(st2) for _ in range(2)]
                for tt in range(NTT):
                    xt, xb, xd = xs[tt % 2]
                    dma("sp", xt[:], I["x"][tt * 128:(tt + 1) * 128, :], xd, writes=[xb])
                    rms_T(xt[:], xb, 0, hT[:, :, tt * 128:(tt + 1) * 128], hTb[tt], sls[tt % 2], psT, psTb,
                          evac=("dve" if tt % 2 == 0 else "act"))
                B.barrier()
            if upto == "A":
                return
            with ExitStack() as st2:
                pbank = [(B.ps([128, 512], F32, st2), Buf()) for _ in range(4)]
                stg = [(B.sb([128, S], BF16, st2), Buf(), B.ds()) for _ in range(2)]
                vst = [(B.sb([128, 512], BF16, st2), Buf(), B.ds()) for _ in range(3)]
                hT_all = hTb
                npb = 0
                nst = 0
                nv = 0
                for cb in range(16):
                    wt, wb, wd = wsl[cb % 2]
                    if cb >= 2:
                        dma("pool", wt[:], I["w_in"][:, cb * 512:(cb + 1) * 512].rearrange("(k p) n -> p k n", p=128), wd, writes=[wb])
                    if 4 <= cb < 6:
                        for tt in range(NTT):
                            pt, pb = pbank[npb % 4]; npb += 1

                            def mm(e, pt=pt, tt=tt, wt=wt):
                                for k in range(8):
                                    ins = e.matmul(pt[:], lhsT=hT[:, k, tt * 128:(tt + 1) * 128], rhs=wt[:, k, :], start=(k == 0), stop=(k == 7))
                                return ins
                            op("pe", mm, reads=[wb, hT_all[tt]], writes=[pb])
                            vt, vb, vd = vst[nv % 3]; nv += 1
                            eng = "dve" if tt % 2 == 0 else "act"
                            op(eng, lambda e, vt=vt, pt=pt, eng=eng: (e.tensor_copy(out=vt[:], in_=pt[:]) if eng == "dve" else e.copy(out=vt[:], in_=pt[:])),
                               reads=[pb], writes=[vb])
                            dma("sp", Vs.ap()[tt * 128:(tt + 1) * 128, (cb - 4) * 512:(cb - 3) * 512], vt[:], vd, reads=[vb], writes=[db["Vs"]])
                        continue
                    for ct in range(4):
                        gcol = cb * 512 + ct * 128
                        sg, sgb, sgd = stg[nst % 2]; nst += 1
                        for tb in range(NTB):
                            pt, pb = pbank[npb % 4]; npb += 1

                            def mm(e, pt=pt, tb=tb, wt=wt, ct=ct):
                                for k in range(8):
                                    ins = e.matmul(pt[:], lhsT=wt[:, k, ct * 128:(ct + 1) * 128], rhs=hT[:, k, tb * 512:(tb + 1) * 512], start=(k == 0), stop=(k == 7))
                                return ins
                            op("pe", mm, reads=[wb] + hT_all[tb * 4:tb * 4 + 4], writes=[pb])
                            dst = sg[:, tb * 512:(tb + 1) * 512]
                            if gcol < 1024:
                                op("act", lambda e, dst=dst, pt=pt: e.mul(out=dst, in_=pt[:], mul=0.125), reads=[pb], writes=[sgb])
                            elif gcol < 2048:
                                op("dve", lambda e, dst=dst, pt=pt: e.tensor_copy(out=dst, in_=pt[:]), reads=[pb], writes=[sgb])
                            elif gcol < 4096:
                                eng = "dve" if tb % 2 == 0 else "act"
                                op(eng, lambda e, dst=dst, pt=pt, eng=eng: (e.tensor_copy(out=dst, in_=pt[:]) if eng == "dve" else e.copy(out=dst, in_=pt[:])),
                                   reads=[pb], writes=[sgb])
                            elif gcol < 5120:
                                op("act", lambda e, dst=dst, pt=pt: e.mul(out=dst, in_=pt[:], mul=0.0625), reads=[pb], writes=[sgb])
                            else:
                                op("act", lambda e, dst=dst, pt=pt: e.activation(out=dst, in_=pt[:], func=AF.Sigmoid), reads=[pb], writes=[sgb])
                        if gcol < 1024:
                            dd, dbuf, r0 = QTs, db["QTs"], gcol
                        elif gcol < 2048:
                            dd, dbuf, r0 = KTs, db["KTs"], gcol - 1024
                        elif gcol < 4096:
                            dd, dbuf, r0 = UTs, db["UTs"], gcol - 3072
                        elif gcol < 5120:
                            dd, dbuf, r0 = XQs, db["XQs"], gcol - 4096
                        else:
                            dd, dbuf, r0 = GTs, db["GTs"], gcol - 5120
                        dma("sp", dd.ap()[r0:r0 + 128, :], sg[:], sgd, reads=[sgb], writes=[dbuf])
                B.barrier()

    def phase_B():
        with ExitStack() as st:
            lqk = B.sb([128, 4, 64], F32, st)
            lpr = B.sb([128, 2, 64], F32, st)
            lsum = B.sb([128, 2], F32, st)
            lexp = B.sb([128, 2], F32, st)
            neglam = B.sb([128, 1], F32, st)
            relb15 = B.sb([128, 8], F32, st)
            subg = B.sb([128, 128], F32, st)
            lb = Buf(); lds = B.ds()
            dma("sp", lqk[:], I["lqk"], lds, writes=[lb])
            dma("sp", relb15[:], I["relb15"], lds, writes=[lb])
            dma("sp", subg[:], I["subg"], lds, writes=[lb])
            lqv = lqk[:].rearrange("p (a b) d -> p a b d", b=2)
            op("dve", lambda e: e.tensor_tensor(out=lpr[:], in0=lqv[:, :, 0, :], in1=lqv[:, :, 1, :], op=ALU.mult), reads=[lb], writes=[lb])
            op("dve", lambda e: e.reduce_sum(out=lsum[:], in_=lpr[:], axis=AX.X), reads=[lb], writes=[lb])
            op("act", lambda e: e.activation(out=lexp[:], in_=lsum[:], func=AF.Exp), reads=[lb], writes=[lb])
            op("dve", lambda e: e.tensor_tensor(out=neglam[:], in0=lexp[:, 1:2], in1=lexp[:, 0:1], op=ALU.subtract), reads=[lb], writes=[lb])
            op("dve", lambda e: e.tensor_scalar(out=neglam[:], in0=neglam[:], scalar1=-LAM_INIT, scalar2=None, op0=ALU.add), reads=[lb], writes=[lb])

            MB = B.sb([128, 8, 1152], BF16, st, "MB")
            mbb = Buf()
            with ExitStack() as st2:
                relb = B.sb([32, 8], F32, st2)
                onehot = B.sb([32, 1280], F32, st2)
                antiid = B.sb([128, 128], F32, st2)
                maskmb = B.sb([128, 1152], F32, st2)
                gsb = B.sb([8, 1280], F32, st2)
                hk = [(B.sb([128, 1152], F32, st2), Buf(), B.ds()) for _ in range(2)]
                tb_ = Buf(); tds = B.ds()
                dma("sp", relb[:], I["relb"], tds, writes=[tb_])
                dma("sp", onehot[:], I["onehot"], tds, writes=[tb_])
                dma("sp", antiid[:], I["antiid"], tds, writes=[tb_])
                dma("sp", maskmb[:], I["maskmb"], tds, writes=[tb_])
                pg = [(B.ps([128, 512], F32, st2), Buf()) for _ in range(3)]
                for n in range(3):
                    n0, n1 = n * 512, min(1280, (n + 1) * 512)
                    op("pe", lambda e, n=n, n0=n0, n1=n1: e.matmul(pg[n][0][0:8, 0:n1 - n0], lhsT=relb[:], rhs=onehot[:, n0:n1], start=True, stop=True),
                       reads=[tb_], writes=[pg[n][1]])
                    op("dve", lambda e, n=n, n0=n0, n1=n1: e.tensor_copy(out=gsb[:, n0:n1], in_=pg[n][0][0:8, 0:n1 - n0]), reads=[pg[n][1]], writes=[tb_])
                gdd = B.ds()
                dma("sp", Gd.ap(), gsb[:], gdd, reads=[tb_], writes=[db["Gd"]])
                for h in range(8):
                    ht, hb_, hd = hk[h % 2]
                    dma("sp", ht[:], bass.AP(Gd, h * 1280, [[1, 128], [1, 1152]]), hd, reads=[db["Gd"]], writes=[hb_])
                    for n in range(3):
                        n0, n1 = n * 512, min(1152, (n + 1) * 512)
                        op("pe", lambda e, n=n, n0=n0, n1=n1, ht=ht: e.matmul(pg[n][0][:, 0:n1 - n0], lhsT=antiid[:], rhs=ht[:, n0:n1], start=True, stop=True),
                           reads=[tb_, hb_], writes=[pg[n][1]])
                        op("dve", lambda e, n=n, n0=n0, n1=n1, h=h: e.tensor_tensor(out=MB[:, h, n0:n1], in0=pg[n][0][:, 0:n1 - n0], in1=maskmb[:, n0:n1], op=ALU.add),
                           reads=[pg[n][1], tb_], writes=[mbb])
                B.barrier()

            sets = []
            for s_ in range(2):
                sets.append({"QT": B.sb([128, S], BF16, st), "KT": B.sb([128, S], BF16, st), "V": B.sb([128, NTT, 129], BF16, st),
                             "b": Buf(), "ds": B.ds()})
            for s_ in sets:
                op("dve", lambda e, s_=s_: e.memset(s_["V"][:, :, 128:129], 1.0), writes=[s_["b"]])
            sc = [(B.ps([128, 2, 512], F32, st), Buf()) for _ in range(2)]
            ob = [(B.ps([128, 512], F32, st), Buf()) for _ in range(3)]
            psT = B.ps([128, 8, 128], BF16, st)
            psTb = Buf()
            NPT = 4
            PT = [(B.sb([128, 2, 512], BF16, st), Buf()) for _ in range(NPT)]
            oc = [(B.sb([128, 3, 387], F32, st), Buf()) for _ in range(2)]
            fin = [{"rr": B.sb([128, 8], F32, st), "t1": B.sb([128, 4, 128], F32, st), "ot": B.sb([128, 4, 128], F32, st),
                    "ss": B.sb([128, 4], F32, st), "rs": B.sb([128, 4], F32, st), "r2": B.sb([128, 4], F32, st),
                    "yb": B.sb([128, 4, 128], BF16, st), "b": Buf()} for _ in range(3)]
            ystg = [(B.sb([128, 512], BF16, st), Buf(), B.ds()) for _ in range(3)]
            state = {"nfin": 0, "nys": 0, "noc": 0}
            pending = []

            def tick():
                for p_ in pending:
                    p_[0] -= 1
                while pending and pending[0][0] <= 0:
                    pending.pop(0)[1]()

            def oreg(r):
                return r // 3, (r % 3) * 129

            blocks = []
            for h in range(8):
                for j in range(NTB):
                    nkt = 4 * (j + 1)
                    for kt in range(nkt):
                        blocks.append((h, j, kt, nkt))

            def load_head(h):
                hs = sets[h % 2]
                dma("sp", hs["QT"][:], QTs.ap()[h * 128:(h + 1) * 128, :], hs["ds"], reads=[db["QTs"]], writes=[hs["b"]])
                dma("sp", hs["KT"][:], KTs.ap()[h * 128:(h + 1) * 128, :], hs["ds"], reads=[db["KTs"]], writes=[hs["b"]])
                dma("sp", hs["V"][:, :, 0:128], Vs.ap()[:, h * 128:(h + 1) * 128].rearrange("(t p) e -> p t e", p=128), hs["ds"],
                    reads=[db["Vs"]], writes=[hs["b"]])

            def emit_scores(n):
                h, j, kt, nkt = blocks[n]
                hs = sets[h % 2]
                m = max(0, kt - 4 * j)
                qlo = 128 * m
                near = kt >= 4 * j - 2
                off = 512 * j - 128 * kt + 384
                pt, pb = sc[n % 2]

                def smm(e):
                    for c in range(2):
                        ins = e.matmul(pt[:, c, qlo:512], lhsT=hs["KT"][64 * c:64 * c + 64, kt * 128:(kt + 1) * 128],
                                       rhs=hs["QT"][64 * c:64 * c + 64, j * 512 + qlo:(j + 1) * 512], start=True, stop=not near)
                    if near:
                        for c in range(2):
                            ins = e.matmul(pt[:, c, qlo:512], lhsT=ident_b[:], rhs=MB[:, h, off + qlo:off + 512], start=False, stop=True)
                    return ins
                op("pe", smm, reads=[hs["b"], mbb, cbuf], writes=[pb])
                ptile, ptb = PT[n % NPT]
                if near:
                    op("act", lambda e: e.activation(out=ptile[:, :, qlo:512], in_=pt[:, :, qlo:512], func=AF.Exp), reads=[pb], writes=[ptb])
                else:
                    op("act", lambda e: e.activation(out=ptile[:], in_=pt[:], func=AF.Exp, bias=relb15[:, h:h + 1], scale=1.0), reads=[pb, lb], writes=[ptb])

            def emit_av(n):
                h, j, kt, nkt = blocks[n]
                hs = sets[h % 2]
                m = max(0, kt - 4 * j)
                ptile, ptb = PT[n % NPT]

                def avmm(e):
                    for c in range(2):
                        for qs in range(m, 4):
                            bank, co = oreg(c * 4 + qs)
                            first = (kt == 0) and ((c * 4 + qs) % 3 == 0)
                            ins = e.matmul(ob[bank][0][:, co:co + 129], lhsT=ptile[:, c, qs * 128:(qs + 1) * 128], rhs=hs["V"][:, kt, :],
                                           start=first, stop=(kt == 4 * j + qs), skip_group_check=True)
                    return ins
                op("pe", avmm, reads=[ptb, hs["b"]], writes=[ob[0][1], ob[1][1], ob[2][1]])
                if kt == nkt - 1:
                    finalize(h, j)

            def finalize(h, j):
                o_, ocb = oc[state["noc"] % 2]; state["noc"] += 1
                for bk in range(3):
                    w_ = 387 if bk < 2 else 258
                    op("dve", lambda e, bk=bk, w_=w_: e.tensor_copy(out=o_[:, bk, 0:w_], in_=ob[bk][0][:, 0:w_]), reads=[ob[bk][1]], writes=[ocb])
                ys, ysb, ysd = ystg[state["nys"] % 3]; state["nys"] += 1
                f = fin[state["nfin"] % 3]; state["nfin"] += 1
                reg = o_[:].rearrange("p a b -> p (a b)")[:, 0:1032].rearrange("p (r c) -> p r c", c=129)
                fb = f["b"]
                op("dve", lambda e: e.reciprocal(out=f["rr"][:], in_=reg[:, :, 128]), reads=[ocb], writes=[fb])
                op("dve", lambda e: e.tensor_scalar(out=f["rr"][:, 4:8], in0=f["rr"][:, 4:8], scalar1=neglam[:, 0:1], scalar2=None, op0=ALU.mult), reads=[fb, lb], writes=[fb])
                op("dve", lambda e: e.tensor_tensor(out=f["t1"][:], in0=reg[:, 4:8, 0:128], in1=fv(f["rr"][:, 4:5], [[1, 4], [0, 128]]), op=ALU.mult), reads=[ocb, fb], writes=[fb])
                op("dve", lambda e: e.tensor_tensor(out=f["ot"][:], in0=reg[:, 0:4, 0:128], in1=fv(f["rr"][:, 0:1], [[1, 4], [0, 128]]), op=ALU.mult), reads=[ocb, fb], writes=[fb])
                op("dve", lambda e: e.tensor_tensor(out=f["ot"][:], in0=f["ot"][:], in1=f["t1"][:], op=ALU.add), reads=[fb], writes=[fb])
                op("dve", lambda e: e.tensor_tensor(out=f["t1"][:], in0=f["ot"][:], in1=f["ot"][:], op=ALU.mult), reads=[fb], writes=[fb])
                op("dve", lambda e: e.reduce_sum(out=f["ss"][:], in_=f["t1"][:], axis=AX.X), reads=[fb], writes=[fb])
                op("pool", lambda e: e.tensor_scalar(out=f["rs"][:], in0=f["ss"][:], scalar1=1.0 / (128 * 0.64), scalar2=1e-6 / 0.64, op0=ALU.mult, op1=ALU.add),
                   reads=[fb], writes=[fb])
                op("pool", lambda e: e.tensor_tensor(out=f["r2"][:], in0=f["rs"][:], in1=mhalf4[:], op=ALU.pow), reads=[fb, cbuf], writes=[fb])
                op("dve", lambda e: e.tensor_tensor(out=f["t1"][:], in0=f["ot"][:], in1=fv(f["r2"][:, 0:1], [[1, 4], [0, 128]]), op=ALU.mult), reads=[fb], writes=[fb])
                op("dve", lambda e: e.tensor_tensor(out=f["yb"][:], in0=f["t1"][:], in1=fv(subg[:, 0:1], [[0, 4], [1, 128]]), op=ALU.mult), reads=[fb, lb], writes=[fb])

                def later():
                    def tr(e):
                        for qs in range(4):
                            ins = e.transpose(out=psT[:, qs, :], in_=f["yb"][:, qs, :], identity=ident_b[:])
                        return ins
                    op("pe", tr, reads=[fb, cbuf], writes=[psTb])
                    op("dve", lambda e: e.tensor_copy(out=ys[:].rearrange("p (a b) -> p a b", a=4), in_=psT[:, 0:4, :]), reads=[psTb], writes=[ysb])
                    dma("sp", YAs.ap()[h * 128:(h + 1) * 128, j * 512:(j + 1) * 512], ys[:], ysd, reads=[ysb], writes=[db["YAs"]])
                pending.append([10, later])

            NBLK = len(blocks)
            load_head(0)
            for n in range(NBLK + 2):
                if n < NBLK:
                    h, j, kt, nkt = blocks[n]
                    if j == 0 and kt == 2 and h + 1 < 8:
                        load_head(h + 1)
                    emit_scores(n)
                if n >= 2:
                    emit_av(n - 2)
                tick()
            while pending:
                pending.pop(0)[1]()
            B.barrier()

    def phase_C(gen=None):
        def pull(n):
            if gen is not None:
                for _ in range(n):
                    next(gen, None)

        with ExitStack() as st:
            memnT = B.sb([128, 8, 256], BF16, st)
            mnb = Buf()
            KxT = B.sb([128, 8, 256], BF16, st)
            Vx = B.sb([128, 2, 4, 257], BF16, st)
            kvb = Buf()
            psT = B.ps([128, 8, 128], BF16, st)
            psTb = Buf()
            pk = [(B.ps([128, 512], F32, st), Buf()) for _ in range(2)]
            with ExitStack() as st2:
                wkv = B.sb([128, 8, 2048], BF16, st2)
                wkb = Buf(); wkd = B.ds()
                for n in range(4):
                    dma("pool", wkv[:, :, n * 512:(n + 1) * 512], I["w_mem_kv"][:, n * 512:(n + 1) * 512].rearrange("(k p) n -> p k n", p=128), wkd, writes=[wkb])
                ms = [(B.sb([128, D], F32, st2), Buf(), B.ds()) for _ in range(2)]
                sls = [rms_slot(st2) for _ in range(2)]
                for mt in range(2):
                    xt, xb, xd = ms[mt]
                    dma("sp", xt[:], I["mem"][mt * 128:(mt + 1) * 128, :], xd, writes=[xb])
                    rms_T(xt[:], xb, 1, memnT[:, :, mt * 128:(mt + 1) * 128], mnb, sls[mt], psT, psTb)
                op("dve", lambda e: e.memset(Vx[:, :, :, 256:257], 1.0), writes=[kvb])
                for ct in range(8):
                    pt, pb = pk[ct % 2]

                    def mm(e, pt=pt, ct=ct):
                        for k in range(8):
                            ins = e.matmul(pt[:, 0:256], lhsT=wkv[:, k, ct * 128:(ct + 1) * 128], rhs=memnT[:, k, :], start=(k == 0), stop=(k == 7))
                        return ins
                    op("pe", mm, reads=[wkb, mnb], writes=[pb])
                    op("dve", lambda e, pt=pt, ct=ct: e.tensor_copy(out=KxT[:, ct, :], in_=pt[:, 0:256]), reads=[pb], writes=[kvb])
                    pull(3)
                n = 0
                for mt in range(2):
                    for half in range(2):
                        pt, pb = pk[n % 2]; n += 1

                        def mm(e, pt=pt, mt=mt, half=half):
                            for k in range(8):
                                ins = e.matmul(pt[:], lhsT=memnT[:, k, mt * 128:(mt + 1) * 128], rhs=wkv[:, k, 1024 + half * 512:1024 + (half + 1) * 512], start=(k == 0), stop=(k == 7))
                            return ins
                        op("pe", mm, reads=[wkb, mnb], writes=[pb])
                        op("dve", lambda e, pt=pt, mt=mt, half=half: e.tensor_copy(out=Vx[:, mt, 2 * half:2 * half + 2, 0:256], in_=pt[:].rearrange("p (a b) -> p a b", a=2)),
                           reads=[pb], writes=[kvb])
                B.barrier()
            xqs = [(B.sb([128, 2, S], BF16, st), Buf(), B.ds()) for _ in range(2)]
            psS = [(B.ps([128, 512], F32, st), Buf()) for _ in range(2)]
            psO = [(B.ps([128, 512], F32, st), Buf()) for _ in range(2)]
            PX = [[(B.sb([128, 512], BF16, st), Buf()) for _ in range(2)] for _ in range(2)]
            fx = [{"rr": B.sb([128, 1], F32, st), "yb": B.sb([128, 256], BF16, st), "b": Buf()} for _ in range(2)]
            ystg = [(B.sb([128, 2, 512], BF16, st), Buf(), B.ds()) for _ in range(2)]
            nb_ = 0; nf = 0; nys = 0; no = 0
            def load_xq(hx):
                xq, xqb, xqd = xqs[hx % 2]
                dma("sp", xq[:], XQs.ap()[hx * 256:(hx + 1) * 256, :].rearrange("(a p) t -> p a t", p=128), xqd, reads=[db["XQs"]], writes=[xqb])

            load_xq(0)
            for hx in range(4):
                xq, xqb, xqd = xqs[hx % 2]
                if hx + 1 < 4:
                    load_xq(hx + 1)
                for tb in range(NTB):
                    slot = nb_ % 2; nb_ += 1
                    for mt in range(2):
                        pt, pb = psS[mt]

                        def smm(e, pt=pt, mt=mt, tb=tb, xq=xq, hx=hx):
                            for dt in range(2):
                                ins = e.matmul(pt[:], lhsT=KxT[:, 2 * hx + dt, mt * 128:(mt + 1) * 128], rhs=xq[:, dt, tb * 512:(tb + 1) * 512], start=(dt == 0), stop=(dt == 1))
                            return ins
                        op("pe", smm, reads=[kvb, xqb], writes=[pb])
                        px, pxb = PX[mt][slot]
                        op("act", lambda e, px=px, pt=pt: e.activation(out=px[:], in_=pt[:], func=AF.Exp), reads=[pb], writes=[pxb])
                    ys, ysb, ysd = ystg[nys % 2]; nys += 1
                    for qs in range(4):
                        po, pob = psO[no % 2]; no += 1

                        def omm(e, po=po, qs=qs, slot=slot, hx=hx):
                            for mt in range(2):
                                ins = e.matmul(po[:, 0:257], lhsT=PX[mt][slot][0][:, qs * 128:(qs + 1) * 128], rhs=Vx[:, mt, hx, :], start=(mt == 0), stop=(mt == 1))
                            return ins
                        op("pe", omm, reads=[PX[0][slot][1], PX[1][slot][1], kvb], writes=[pob])
                        f = fx[nf % 2]; nf += 1
                        op("dve", lambda e, f=f, po=po: e.reciprocal(out=f["rr"][:], in_=po[:, 256:257]), reads=[pob], writes=[f["b"]])
                        op("dve", lambda e, f=f, po=po: e.tensor_scalar(out=f["yb"][:], in0=po[:, 0:256], scalar1=f["rr"][:], scalar2=None, op0=ALU.mult),
                           reads=[pob, f["b"]], writes=[f["b"]])

                        def tr(e, f=f, qs=qs):
                            for dt in range(2):
                                ins = e.transpose(out=psT[:, dt * 4 + qs, :], in_=f["yb"][:, dt * 128:(dt + 1) * 128], identity=ident_b[:])
                            return ins
                        op("pe", tr, reads=[f["b"], cbuf], writes=[psTb])
                        pull(2)
                    op("act", lambda e, ys=ys: e.copy(out=ys[:].rearrange("p a (q t) -> p (a q) t", q=4), in_=psT[:]), reads=[psTb], writes=[ysb])
                    dma("sp", YXs.ap()[hx * 256:(hx + 1) * 256, tb * 512:(tb + 1) * 512].rearrange("(a p) t -> p a t", p=128), ys[:], ysd, reads=[ysb], writes=[db["YXs"]])
            B.barrier()

    Dst = {}

    def phase_D1():
        st = ExitStack()
        Dst["st"] = st
        if True:
            WW = B.sb([128, 2, 32, 256], F32, st, "WW")
            wwb = Buf()
            AA1 = B.sb([128, 2, 32], F32, st)
            AA2 = B.sb([128, 2, 32], F32, st)
            sd = B.sb([128, 8], F32, st)
            tb_ = Buf(); tds = B.ds()
            r1_ = B.sb([128, 2, 32], F32, st); r2_ = B.sb([128, 2, 32], F32, st)
            Dst.update(WW=WW, wwb=wwb, AA1=AA1, AA2=AA2, tb_=tb_, r1=r1_, r2=r2_)
            dma("sp", sd[:], I["s_d"], tds, writes=[tb_])
            with ExitStack() as st1:
                are = B.sb([128, 32], F32, st1); aim = B.sb([128, 32], F32, st1); ldt = B.sb([128, 32], F32, st1)
                bre = B.sb([128, 32, 16], F32, st1); bim = B.sb([128, 32, 16], F32, st1)
                cre = B.sb([128, 32, 16], F32, st1); cim = B.sb([128, 32, 16], F32, st1)
                for t_, k_ in ((are, "s_are"), (aim, "s_aim"), (ldt, "s_ldt"), (bre, "s_bre"), (bim, "s_bim"), (cre, "s_cre"), (cim, "s_cim")):
                    dma("sp", t_[:], I[k_], tds, writes=[tb_])
                sel_f = B.sb([128, 4, 128], F32, st1); sel_b = B.sb([128, 4, 128], BF16, st1)
                maskd = B.sb([128, 128], F32, st1)
                dma("sp", sel_f[:], I["sel"], tds, writes=[tb_])
                dma("sp", maskd[:], I["maskd"], tds, writes=[tb_])
                T = lambda shape: B.sb(shape, F32, st1)
                dtt = T([128, 32]); adr = T([128, 32]); th = T([128, 32]); cc = T([128, 32]); ss_ = T([128, 32])
                t1 = T([128, 32]); t2 = T([128, 32]); hpi = T([128, 1])
                ER = T([128, 32, 17]); EI = T([128, 32, 17]); MAGP = T([128, 32, 17]); MAGN = T([128, 32, 17])
                PPr = T([128, 32, 17]); PPi = T([128, 32, 17]); PNr = T([128, 32, 17]); PNi = T([128, 32, 17])
                bbr = T([128, 32, 16]); bbi = T([128, 32, 16])
                tmpA = T([128, 32, 16]); tmpB = T([128, 32, 16])
                tb2 = tb_

                def D_(fn):
                    op("dve", fn, reads=[tb2], writes=[tb2])

                def A_(fn):
                    op("act", fn, reads=[tb2], writes=[tb2])
                D_(lambda e: e.tensor_copy(out=sel_b[:], in_=sel_f[:]))
                D_(lambda e: e.memset(hpi[:], math.pi / 2))
                A_(lambda e: e.activation(out=dtt[:], in_=ldt[:], func=AF.Exp))
                D_(lambda e: e.tensor_tensor(out=adr[:], in0=are[:], in1=dtt[:], op=ALU.mult))
                D_(lambda e: e.tensor_tensor(out=th[:], in0=aim[:], in1=dtt[:], op=ALU.mult))
                A_(lambda e: e.activation(out=ss_[:], in_=th[:], func=AF.Sin, scale=1.0 / 32))
                A_(lambda e: e.activation(out=cc[:], in_=th[:], func=AF.Sin, scale=1.0 / 32, bias=hpi[:]))
                for _ in range(5):
                    D_(lambda e: e.tensor_tensor(out=t1[:], in0=cc[:], in1=cc[:], op=ALU.mult))
                    D_(lambda e: e.tensor_tensor(out=t2[:], in0=ss_[:], in1=ss_[:], op=ALU.mult))
                    D_(lambda e: e.scalar_tensor_tensor(out=ss_[:], in0=cc[:], scalar=2.0, in1=ss_[:], op0=ALU.mult, op1=ALU.mult))
                    D_(lambda e: e.tensor_tensor(out=cc[:], in0=t1[:], in1=t2[:], op=ALU.subtract))
                D_(lambda e: e.memset(ER[:, :, 0:1], 1.0))
                D_(lambda e: e.memset(EI[:, :, 0:1], 0.0))
                D_(lambda e: e.tensor_copy(out=ER[:, :, 1], in_=cc[:]))
                D_(lambda e: e.tensor_copy(out=EI[:, :, 1], in_=ss_[:]))
                tmpE = [T([128, 32, 8]) for _ in range(4)]
                k = 1
                while k < 16:
                    a_r, a_i = ER[:, :, 1:k + 1], EI[:, :, 1:k + 1]
                    b_r = fv(ER[:, :, k:k + 1], [[17, 32], [0, k]])
                    b_i = fv(EI[:, :, k:k + 1], [[17, 32], [0, k]])
                    q = [t_[:, :, 0:k] for t_ in tmpE]
                    D_(lambda e, a_r=a_r, b_r=b_r, q=q: e.tensor_tensor(out=q[0], in0=a_r, in1=b_r, op=ALU.mult))
                    D_(lambda e, a_i=a_i, b_i=b_i, q=q: e.tensor_tensor(out=q[1], in0=a_i, in1=b_i, op=ALU.mult))
                    D_(lambda e, a_r=a_r, b_i=b_i, q=q: e.tensor_tensor(out=q[2], in0=a_r, in1=b_i, op=ALU.mult))
                    D_(lambda e, a_i=a_i, b_r=b_r, q=q: e.tensor_tensor(out=q[3], in0=a_i, in1=b_r, op=ALU.mult))
                    D_(lambda e, k=k, q=q: e.tensor_tensor(out=ER[:, :, k + 1:2 * k + 1], in0=q[0], in1=q[1], op=ALU.subtract))
                    D_(lambda e, k=k, q=q: e.tensor_tensor(out=EI[:, :, k + 1:2 * k + 1], in0=q[2], in1=q[3], op=ALU.add))
                    k *= 2
                for tau in range(17):
                    A_(lambda e, tau=tau: e.activation(out=MAGP[:, :, tau], in_=adr[:], func=AF.Exp, scale=float(tau)))
                    A_(lambda e, tau=tau: e.activation(out=MAGN[:, :, tau], in_=adr[:], func=AF.Exp, scale=-float(tau)))
                D_(lambda e: e.tensor_tensor(out=PPr[:], in0=MAGP[:], in1=ER[:], op=ALU.mult))
                D_(lambda e: e.tensor_tensor(out=PPi[:], in0=MAGP[:], in1=EI[:], op=ALU.mult))
                D_(lambda e: e.tensor_tensor(out=PNr[:], in0=MAGN[:], in1=ER[:], op=ALU.mult))
                D_(lambda e: e.scalar_tensor_tensor(out=PNi[:], in0=MAGN[:], scalar=-1.0, in1=EI[:], op0=ALU.mult, op1=ALU.mult))
                xr = T([128, 32]); nr = T([128, 32]); ni = T([128, 32]); den = T([128, 32]); cfr = T([128, 32]); cfi = T([128, 32])
                D_(lambda e: e.tensor_scalar(out=xr[:], in0=PPr[:, :, 1], scalar1=-1.0, scalar2=None, op0=ALU.add))
                D_(lambda e: e.tensor_tensor(out=t1[:], in0=xr[:], in1=are[:], op=ALU.mult))
                D_(lambda e: e.tensor_tensor(out=t2[:], in0=PPi[:, :, 1], in1=aim[:], op=ALU.mult))
                D_(lambda e: e.tensor_tensor(out=nr[:], in0=t1[:], in1=t2[:], op=ALU.add))
                D_(lambda e: e.tensor_tensor(out=t1[:], in0=PPi[:, :, 1], in1=are[:], op=ALU.mult))
                D_(lambda e: e.tensor_tensor(out=t2[:], in0=xr[:], in1=aim[:], op=ALU.mult))
                D_(lambda e: e.tensor_tensor(out=ni[:], in0=t1[:], in1=t2[:], op=ALU.subtract))
                D_(lambda e: e.tensor_tensor(out=t1[:], in0=are[:], in1=are[:], op=ALU.mult))
                D_(lambda e: e.tensor_tensor(out=t2[:], in0=aim[:], in1=aim[:], op=ALU.mult))
                D_(lambda e: e.tensor_tensor(out=den[:], in0=t1[:], in1=t2[:], op=ALU.add))
                D_(lambda e: e.reciprocal(out=den[:], in_=den[:]))
                D_(lambda e: e.tensor_tensor(out=cfr[:], in0=nr[:], in1=den[:], op=ALU.mult))
                D_(lambda e: e.tensor_tensor(out=cfi[:], in0=ni[:], in1=den[:], op=ALU.mult))
                cfr_b = fv(cfr[:], [[1, 32], [0, 16]]); cfi_b = fv(cfi[:], [[1, 32], [0, 16]])
                D_(lambda e: e.tensor_tensor(out=tmpA[:], in0=bre[:], in1=cfr_b, op=ALU.mult))
                D_(lambda e: e.tensor_tensor(out=tmpB[:], in0=bim[:], in1=cfi_b, op=ALU.mult))
                D_(lambda e: e.tensor_tensor(out=bbr[:], in0=tmpA[:], in1=tmpB[:], op=ALU.subtract))
                D_(lambda e: e.tensor_tensor(out=tmpA[:], in0=bim[:], in1=cfr_b, op=ALU.mult))
                D_(lambda e: e.tensor_tensor(out=tmpB[:], in0=bre[:], in1=cfi_b, op=ALU.mult))
                D_(lambda e: e.tensor_tensor(out=bbi[:], in0=tmpA[:], in1=tmpB[:], op=ALU.add))
                D_(lambda e: e.tensor_copy(out=AA1[:, 0, :], in_=PPr[:, :, 16]))
                D_(lambda e: e.tensor_copy(out=AA1[:, 1, :], in_=PPr[:, :, 16]))
                D_(lambda e: e.tensor_scalar(out=AA2[:, 0, :], in0=PPi[:, :, 16], scalar1=-1.0, scalar2=None, op0=ALU.mult))
                D_(lambda e: e.tensor_copy(out=AA2[:, 1, :], in_=PPi[:, :, 16]))

                if upto == "D0":
                    B.barrier()
                    return
                X = [T([128, 16, 16]) for _ in range(4)]
                xb_ = Buf()
                Bx = [{"Bexp": B.sb([128, 2, 512], BF16, st1), "Bfull": B.sb([128, 2, 512], BF16, st1), "BblkT": B.sb([128, 8, 128], BF16, st1),
                       "bxb": Buf(), "bfb": Buf(), "btb": Buf()} for _ in range(2)]
                CexpB = [B.sb([128, 2, 512], BF16, st1) for _ in range(4)]
                cxb = [Buf() for _ in range(4)]
                cxd = [B.ds() for _ in range(4)]
                DblkB = [B.sb([128, 4, 512], BF16, st1) for _ in range(4)]
                dkb = [Buf() for _ in range(4)]
                Vp = [B.sb([128, 4, 256], BF16, st1) for _ in range(4)]
                vpb = [Buf() for _ in range(4)]
                uTn = (B.sb([128, S], BF16, st1), Buf(), B.ds())
                uTs = [(B.sb([128, 16, 256], BF16, st1), Buf()) for _ in range(2)]
                Yqs = [(B.sb([128, 16, 256], BF16, st1), Buf(), B.ds()) for _ in range(1)]
                psT = B.ps([128, 8, 128], BF16, st1); psTb = Buf()
                psD = [(B.ps([128, 512], F32, st1), Buf()) for _ in range(2)]
                psV = [(B.ps([128, 2, 256], F32, st1), Buf()) for _ in range(2)]
                psW = (B.ps([128, 2, 256], F32, st1), Buf())
                psY = [(B.ps([128, 512], F32, st1), Buf()) for _ in range(2)]
                for bx in Bx:
                    op("dve", lambda e, bx=bx: e.memset(bx["Bexp"][:], 0.0), writes=[bx["bxb"]])
                    op("dve", lambda e, bx=bx: e.memset(bx["Bfull"][:], 0.0), writes=[bx["bfb"]])
                for j in range(4):
                    op("dve", lambda e, j=j: e.memset(CexpB[j][:], 0.0), writes=[cxb[j]])

                def cmul(pr, pi_, qr, qi, outs_r, outs_i, neg_i, rbufs, wbufs):
                    op("dve", lambda e: e.tensor_tensor(out=X[0][:], in0=pr, in1=qr, op=ALU.mult), reads=rbufs, writes=[xb_])
                    op("dve", lambda e: e.tensor_tensor(out=X[1][:], in0=pi_, in1=qi, op=ALU.mult), reads=rbufs, writes=[xb_])
                    op("dve", lambda e: e.tensor_tensor(out=X[2][:], in0=pr, in1=qi, op=ALU.mult), reads=rbufs, writes=[xb_])
                    op("dve", lambda e: e.tensor_tensor(out=X[3][:], in0=pi_, in1=qr, op=ALU.mult), reads=rbufs, writes=[xb_])
                    for lo, hi, o in outs_r:
                        op("dve", lambda e, lo=lo, hi=hi, o=o: e.tensor_tensor(out=o, in0=X[0][lo:hi], in1=X[1][lo:hi], op=ALU.subtract), reads=[xb_], writes=wbufs)
                    for lo, hi, o in outs_i:
                        if neg_i:
                            op("dve", lambda e, lo=lo, hi=hi, o=o: e.scalar_tensor_tensor(out=o, in0=X[2][lo:hi], scalar=-1.0, in1=X[3][lo:hi], op0=ALU.mult, op1=ALU.subtract),
                               reads=[xb_], writes=wbufs)
                        else:
                            op("dve", lambda e, lo=lo, hi=hi, o=o: e.tensor_tensor(out=o, in0=X[2][lo:hi], in1=X[3][lo:hi], op=ALU.add), reads=[xb_], writes=wbufs)

                def blocked(tile, c):
                    v = tile[:].rearrange("p c (i g k) -> p c i g k", g=2, k=16)
                    return [(0, 64, v[0:64, c, :, 0, :]), (64, 128, v[64:128, c, :, 1, :])]

                def prep(m):
                    j = m % 4
                    bx = Bx[m % 2]
                    ppr = fv(PPr[:, m, 1:2], [[1, 16], [0, 16]]); ppi = fv(PPi[:, m, 1:2], [[1, 16], [0, 16]])
                    pnr = fv(PNr[:, m, 1:2], [[1, 16], [0, 16]]); pni = fv(PNi[:, m, 1:2], [[1, 16], [0, 16]])
                    prr = fv(PPr[:, m, 15:16], [[-1, 16], [0, 16]]); pri = fv(PPi[:, m, 15:16], [[-1, 16], [0, 16]])
                    c_r = fv(cre[:, m, 0:1], [[0, 16], [1, 16]]); c_i = fv(cim[:, m, 0:1], [[0, 16], [1, 16]])
                    b_r = fv(bbr[:, m, 0:1], [[0, 16], [1, 16]]); b_i = fv(bbi[:, m, 0:1], [[0, 16], [1, 16]])
                    cmul(prr, pri, b_r, b_i, blocked(bx["Bfull"], 0), blocked(bx["Bfull"], 1), False, [tb2], [bx["bfb"]])
                    cmul(ppr, ppi, c_r, c_i, blocked(CexpB[j], 0), blocked(CexpB[j], 1), True, [tb2], [cxb[j]])
                    cmul(pnr, pni, b_r, b_i, blocked(bx["Bexp"], 0), blocked(bx["Bexp"], 1), False, [tb2], [bx["bxb"]])
                    dma("sp", CXs.ap()[m], CexpB[j][:].rearrange("p c n -> p (c n)"), cxd[j], reads=[cxb[j]], writes=[db["CXs"]])

                def work(m):
                    q4, j = divmod(m, 4)
                    bx = Bx[m % 2]
                    uT, utb = uTs[q4 % 2]

                    def trB(e):
                        for c in range(2):
                            for it in range(4):
                                ins = e.transpose(out=psT[:, c * 4 + it, :], in_=bx["Bfull"][:, c, it * 128:(it + 1) * 128], identity=ident_b[:])
                        return ins
                    op("pe", trB, reads=[bx["bfb"], cbuf], writes=[psTb])
                    op("act", lambda e: e.copy(out=bx["BblkT"][:], in_=psT[:]), reads=[psTb], writes=[bx["btb"]])
                    for hb2 in range(2):
                        pv, pvb = psV[hb2]

                        def vmm(e, pv=pv, hb2=hb2):
                            for a_ in range(2):
                                it = hb2 * 2 + a_
                                for il in range(4):
                                    ins = e.matmul(pv[:, a_, :], lhsT=sel_b[32 * j:32 * j + 32, il, :], rhs=uT[32 * j:32 * j + 32, 4 * it + il, :],
                                                   start=(il == 0), stop=(il == 3), tile_position=(32 * j, 0))
                            return ins
                        op("pe", vmm, reads=[utb, tb2], writes=[pvb])
                        op("act", lambda e, pv=pv, hb2=hb2: e.copy(out=Vp[j][:, 2 * hb2:2 * hb2 + 2, :], in_=pv[:]), reads=[pvb], writes=[vpb[j]])
                    for it in range(4):
                        pd, pdb = psD[it % 2]

                        def dmm(e, pd=pd, it=it):
                            for c in range(2):
                                ins = e.matmul(pd[:], lhsT=bx["Bexp"][:, c, it * 128:(it + 1) * 128], rhs=CexpB[j][:, c, :], start=(c == 0), stop=(c == 1))
                            return ins
                        op("pe", dmm, reads=[bx["bxb"], cxb[j]], writes=[pdb])
                        op("act", lambda e, pd=pd, it=it: e.copy(out=DblkB[j][:, it, :], in_=pd[:]), reads=[pdb], writes=[dkb[j]])
                        op("dve", lambda e, pd=pd, it=it: e.tensor_tensor(out=DblkB[j][:, it, it * 128:(it + 1) * 128], in0=pd[:, it * 128:(it + 1) * 128], in1=maskd[:],
                                                                          op=ALU.mult), reads=[pdb, tb2], writes=[dkb[j]])
                    pw, pwb = psW

                    def wmm(e):
                        for c in range(2):
                            for it in range(4):
                                ins = e.matmul(pw[:, c, :], lhsT=bx["BblkT"][:, c * 4 + it, :], rhs=Vp[j][:, it, :], start=(it == 0), stop=(it == 3))
                        return ins
                    op("pe", wmm, reads=[bx["btb"], vpb[j]], writes=[pwb])
                    op("act", lambda e: e.copy(out=WW[:, :, m, :], in_=pw[:]), reads=[pwb], writes=[wwb])

                def yintra(q4):
                    uT, utb = uTs[q4 % 2]
                    Yq, yqb, yqd = Yqs[0]
                    for i in range(16):
                        py = psY[i % 2][0][:, 0:256]
                        pyb = psY[i % 2][1]

                        def ymm(e, py=py, i=i):
                            for it in range(i // 4 + 1):
                                for j in range(4):
                                    ins = e.matmul(py[32 * j:32 * j + 32, :], lhsT=DblkB[j][:, it, i * 32:(i + 1) * 32], rhs=Vp[j][:, it, :],
                                                   start=(it == 0), stop=(it == i // 4), tile_position=(0, 32 * j))
                            return ins
                        op("pe", ymm, reads=dkb + vpb, writes=[pyb])
                        op("dve", lambda e, py=py, i=i: e.scalar_tensor_tensor(out=Yq[:, i, :], in0=uT[:, i, :], scalar=sd[:, q4:q4 + 1], in1=py,
                                                                                op0=ALU.mult, op1=ALU.add), reads=[pyb, utb, tb_], writes=[yqb])
                    dma("sp", YIs.ap()[q4 * 128:(q4 + 1) * 128, :], Yq[:].rearrange("p i c -> p (i c)"), yqd, reads=[yqb], writes=[db["YIs"]])

                def load_u(q4):
                    un, unb, und = uTn
                    uT, utb = uTs[q4 % 2]
                    dma("sp", un[:], UTs.ap()[q4 * 128:(q4 + 1) * 128, :], und, reads=[db["UTs"]], writes=[unb])
                    unv = un[:].rearrange("p (c i) -> p i c", i=16)
                    for ih in range(2):
                        op("act", lambda e, ih=ih: e.copy(out=uT[:, ih * 8:(ih + 1) * 8, :], in_=unv[:, ih * 8:(ih + 1) * 8, :]), reads=[unb], writes=[utb])

                load_u(0)
                prep(0)
                for m in range(32):
                    q4, j = divmod(m, 4)
                    if j == 0 and q4 + 1 < 8:
                        load_u(q4 + 1)
                    if m + 1 < 32:
                        prep(m + 1)
                    work(m)
                    if j == 3:
                        yintra(q4)
                B.barrier()

    def rec_gen():
        WW, wwb, AA1, AA2, tb_ = Dst["WW"], Dst["wwb"], Dst["AA1"], Dst["AA2"], Dst["tb_"]
        r1, r2 = Dst["r1"], Dst["r2"]
        rb = Buf()
        for c in range(1, 256):
            prev = WW[:, :, :, c - 1]
            cur = WW[:, :, :, c]
            prev_sw = fv(WW[:, 1:2, 0:1, c - 1:c], [[-32 * 256, 2], [256, 32]])
            op("dve", lambda e, prev=prev: e.tensor_tensor(out=r1[:], in0=prev, in1=AA1[:], op=ALU.mult), reads=[wwb, tb_], writes=[rb])
            op("dve", lambda e, prev_sw=prev_sw: e.tensor_tensor(out=r2[:], in0=prev_sw, in1=AA2[:], op=ALU.mult), reads=[wwb, tb_], writes=[rb])
            op("dve", lambda e: e.tensor_tensor(out=r1[:], in0=r1[:], in1=r2[:], op=ALU.add), reads=[rb], writes=[rb])
            op("dve", lambda e, cur=cur: e.tensor_tensor(out=cur, in0=cur, in1=r1[:], op=ALU.add), reads=[rb, wwb], writes=[wwb])
            yield

    def phase_D2():
        WW, wwb = Dst["WW"], Dst["wwb"]
        if True:
            with ExitStack() as st1:
                Hb = B.sb([128, 2, 32, 256], BF16, st1); hbb = Buf()
                op("dve", lambda e: e.memset(Hb[:, :, :, 0:1], 0.0), writes=[hbb])
                for c in range(2):
                    op("dve" if c == 0 else "act", lambda e, c=c: (e.tensor_copy(out=Hb[:, c, :, 1:256], in_=WW[:, c, :, 0:255]) if c == 0
                                                                    else e.copy(out=Hb[:, c, :, 1:256], in_=WW[:, c, :, 0:255])), reads=[wwb], writes=[hbb])
                Cx = [(B.sb([128, 2, 512], BF16, st1), Buf(), B.ds()) for _ in range(8)]
                Yin = [(B.sb([128, 16, 256], BF16, st1), Buf(), B.ds()) for _ in range(2)]
                Yf = [(B.sb([128, 16, 256], F32, st1), Buf()) for _ in range(2)]
                zT = [(B.sb([128, S], BF16, st1), Buf(), B.ds()) for _ in range(2)]
                psY2 = [(B.ps([128, 512], F32, st1), Buf()) for _ in range(4)]
                npy = 0
                for q4 in range(8):
                    yi, yib, yid = Yin[q4 % 2]
                    yf, yfb = Yf[q4 % 2]
                    z_, zb, zd = zT[q4 % 2]
                    dma("sp", yi[:].rearrange("p i c -> p (i c)"), YIs.ap()[q4 * 128:(q4 + 1) * 128, :], yid, reads=[db["YIs"]], writes=[yib])
                    cxs = []
                    for j in range(4):
                        ct, ctb, ctd = Cx[(q4 % 2) * 4 + j]
                        dma("sp", ct[:].rearrange("p c n -> p (c n)"), CXs.ap()[q4 * 4 + j], ctd, reads=[db["CXs"]], writes=[ctb])
                        cxs.append((ct, ctb))
                    for i in range(16):
                        py, pyb = psY2[npy % 4]; npy += 1

                        def ymm(e, py=py, i=i, cxs=cxs, q4=q4):
                            for j in range(4):
                                for c in range(2):
                                    ins = e.matmul(py[32 * j:32 * j + 32, 0:256], lhsT=cxs[j][0][:, c, i * 32:(i + 1) * 32], rhs=Hb[:, c, q4 * 4 + j, :],
                                                   start=(c == 0), stop=(c == 1), tile_position=(0, 32 * j))
                            return ins
                        op("pe", ymm, reads=[hbb] + [c_[1] for c_ in cxs], writes=[pyb])
                        op("dve", lambda e, py=py, i=i, yi=yi, yf=yf: e.tensor_tensor(out=yf[:, i, :], in0=py[:, 0:256], in1=yi[:, i, :], op=ALU.add),
                           reads=[pyb, yib], writes=[yfb])
                    op("act", lambda e, z_=z_, yf=yf: e.activation(out=z_[:].rearrange("p (c i) -> p c i", i=16), in_=yf[:].rearrange("p i c -> p c i"), func=AF.Gelu_apprx_tanh),
                       reads=[yfb], writes=[zb])
                    dma("sp", ZTs.ap()[q4 * 128:(q4 + 1) * 128, :], z_[:], zd, reads=[zb], writes=[db["ZTs"]])
                B.barrier()
        Dst["st"].close()
    def phase_T1():
        with ExitStack() as st:
            W = {}
            WB = {}
            for nm in ("glu_w", "w_br_attn", "w_br_ssm", "w_br_xattn", "w_out"):
                W[nm] = B.sb([128, 8, D], BF16, st, nm)
                WB[nm] = Buf()
                wd = B.ds()
                for n in range(2):
                    dma("pool", W[nm][:, :, n * 512:(n + 1) * 512], I[nm][:, n * 512:(n + 1) * 512].rearrange("(k p) n -> p k n", p=128), wd, writes=[WB[nm]])
            glub = B.sb([128, 8], F32, st)
            wb_ = Buf()
            dma("sp", glub[:], I["glu_b"], B.ds(), writes=[wb_])
            zt = (B.sb([128, 8, 512], BF16, st), Buf(), B.ds())
            ya = (B.sb([128, 8, 512], BF16, st), Buf(), B.ds())
            yx = (B.sb([128, 8, 512], BF16, st), Buf(), B.ds())
            gt = (B.sb([128, 24, 512], BF16, st), Buf(), B.ds())
            yssm = (B.sb([128, 8, 512], BF16, st), Buf())
            mixT = (B.sb([128, 8, 512], BF16, st), Buf())
            sig = [(B.sb([128, 512], F32, st), Buf()) for _ in range(2)]
            mm_ = [(B.sb([128, 3, 512], F32, st), Buf()) for _ in range(2)]
            xs = [(B.sb([128, D], F32, st), Buf(), B.ds()) for _ in range(2)]
            x1 = [(B.sb([128, D], F32, st), Buf(), B.ds()) for _ in range(2)]
            pG = [(B.ps([128, 512], F32, st), Buf()) for _ in range(2)]
            pB = [(B.ps([128, 512], F32, st), Buf()) for _ in range(3)]
            pO = [(B.ps([128, 512], F32, st), Buf()) for _ in range(2)]
            ng = 0; nx = 0; no = 0
            def load_z(tb):
                tsl = slice(tb * 512, (tb + 1) * 512)
                dma("act", zt[0][:], ZTs.ap()[:, tsl].rearrange("(k p) t -> p k t", p=128), zt[2], reads=[db["ZTs"]], writes=[zt[1]])

            def load_rest(tb):
                tsl = slice(tb * 512, (tb + 1) * 512)
                dma("act", ya[0][:], YAs.ap()[:, tsl].rearrange("(k p) t -> p k t", p=128), ya[2], reads=[db["YAs"]], writes=[ya[1]])
                dma("act", yx[0][:], YXs.ap()[:, tsl].rearrange("(k p) t -> p k t", p=128), yx[2], reads=[db["YXs"]], writes=[yx[1]])
                dma("act", gt[0][:], GTs.ap()[:, tsl].rearrange("(k p) t -> p k t", p=128), gt[2], reads=[db["GTs"]], writes=[gt[1]])

            load_z(0)
            load_rest(0)
            for tb in range(NTB):
                for ct in range(8):
                    pg, pgb = pG[ng % 2]
                    sg, sgb = sig[ng % 2]; ng += 1

                    def gmm(e, pg=pg, ct=ct):
                        for k in range(8):
                            ins = e.matmul(pg[:], lhsT=W["glu_w"][:, k, ct * 128:(ct + 1) * 128], rhs=zt[0][:, k, :], start=(k == 0), stop=(k == 7))
                        return ins
                    op("pe", gmm, reads=[WB["glu_w"], zt[1]], writes=[pgb])
                    op("act", lambda e, sg=sg, pg=pg, ct=ct: e.activation(out=sg[:], in_=pg[:], func=AF.Sigmoid, bias=glub[:, ct:ct + 1], scale=1.0),
                       reads=[pgb, wb_], writes=[sgb])
                    op("dve", lambda e, sg=sg, ct=ct: e.tensor_tensor(out=yssm[0][:, ct, :], in0=zt[0][:, ct, :], in1=sg[:], op=ALU.mult),
                       reads=[sgb, zt[1]], writes=[yssm[1]])
                if tb + 1 < NTB:
                    load_z(tb + 1)
                for ct in range(8):
                    srcs = ((W["w_br_attn"], ya[0], ya[1], WB["w_br_attn"]), (W["w_br_ssm"], yssm[0], yssm[1], WB["w_br_ssm"]),
                            (W["w_br_xattn"], yx[0], yx[1], WB["w_br_xattn"]))
                    m3, m3b = mm_[ct % 2]
                    for bi, (w_, y_, yb_, wbf) in enumerate(srcs):
                        pb_, pbb = pB[bi]

                        def bmm(e, pb_=pb_, w_=w_, y_=y_, ct=ct):
                            for k in range(8):
                                ins = e.matmul(pb_[:], lhsT=w_[:, k, ct * 128:(ct + 1) * 128], rhs=y_[:, k, :], start=(k == 0), stop=(k == 7))
                            return ins
                        op("pe", bmm, reads=[wbf, yb_], writes=[pbb])
                        op("dve", lambda e, m3=m3, pb_=pb_, bi=bi, ct=ct: e.tensor_tensor(out=m3[:, bi, :], in0=pb_[:], in1=gt[0][:, bi * 8 + ct, :], op=ALU.mult),
                           reads=[pbb, gt[1]], writes=[m3b])
                    op("dve", lambda e, m3=m3: e.tensor_tensor(out=m3[:, 0, :], in0=m3[:, 0, :], in1=m3[:, 1, :], op=ALU.add), reads=[m3b], writes=[m3b])
                    op("dve", lambda e, m3=m3, ct=ct: e.tensor_tensor(out=mixT[0][:, ct, :], in0=m3[:, 0, :], in1=m3[:, 2, :], op=ALU.add), reads=[m3b], writes=[mixT[1]])
                if tb + 1 < NTB:
                    load_rest(tb + 1)
                for ts in range(4):
                    xt, xb, xd = xs[nx % 2]
                    x1t, x1b, x1d = x1[nx % 2]; nx += 1
                    r0 = tb * 512 + ts * 128
                    dma("sp", xt[:], I["x"][r0:r0 + 128, :], xd, writes=[xb])
                    for half in range(2):
                        po, pob = pO[no % 2]; no += 1

                        def omm(e, po=po, ts=ts, half=half):
                            for k in range(8):
                                ins = e.matmul(po[:], lhsT=mixT[0][:, k, ts * 128:(ts + 1) * 128], rhs=W["w_out"][:, k, half * 512:(half + 1) * 512], start=(k == 0), stop=(k == 7))
                            return ins
                        op("pe", omm, reads=[WB["w_out"], mixT[1]], writes=[pob])
                        op("dve", lambda e, po=po, half=half, xt=xt, x1t=x1t: e.tensor_tensor(out=x1t[:, half * 512:(half + 1) * 512], in0=po[:], in1=xt[:, half * 512:(half + 1) * 512], op=ALU.add),
                           reads=[pob, xb], writes=[x1b])
                    dma("sp", X1s.ap()[r0:r0 + 128, :], x1t[:], x1d, reads=[x1b], writes=[db["X1s"]])
            B.barrier()

    def phase_T2():
        TB = 256
        with ExitStack() as st:
            wfi = B.sb([128, 8, 2 * FF], BF16, st, "wfi")
            wfo = B.sb([128, NH, D], BF16, st, "wfo")
            wgb = [Buf() for _ in range(6)]
            for k in range(6):
                wd = B.ds()
                for base in (0, FF):
                    c0 = base + 512 * k
                    c1 = min(base + 512 * (k + 1), base + FF)
                    dma("pool", wfi[:, :, c0:c1], I["w_ffn_in"][:, c0:c1].rearrange("(k p) n -> p k n", p=128), wd, writes=[wgb[k]])
            wb_ = Buf(); wd = B.ds()
            for n in range(2):
                dma("pool", wfo[:, :, n * 512:(n + 1) * 512], I["w_ffn_out"][:, n * 512:(n + 1) * 512].rearrange("(k p) n -> p k n", p=128), wd, writes=[wb_])
            psT = B.ps([128, 8, 128], BF16, st); psTb = Buf()
            pG = [(B.ps([128, 512], F32, st), Buf()) for _ in range(2)]
            pU = [(B.ps([128, 512], F32, st), Buf()) for _ in range(2)]
            pO = [(B.ps([128, 512], F32, st), Buf()) for _ in range(2)]
            x1t = [(B.sb([128, D], F32, st), Buf(), B.ds()) for _ in range(4)]
            sls = [rms_slot(st) for _ in range(2)]
            h2T = [(B.sb([128, 8, TB], BF16, st), Buf()) for _ in range(2)]
            aT = [(B.sb([128, NH, TB], BF16, st), Buf()) for _ in range(1)]
            sg = [(B.sb([128, TB], F32, st), Buf()) for _ in range(2)]
            x2 = [(B.sb([128, D], F32, st), Buf()) for _ in range(2)]
            fs = [{"ss": B.sb([128, 1], F32, st), "rs": B.sb([128, 1], F32, st), "rr": B.sb([128, 1], F32, st), "b": Buf()} for _ in range(2)]
            ot = [(B.sb([128, D], F32, st), Buf(), B.ds()) for _ in range(1)]
            cnt = {"nx": 0, "ng": 0, "no": 0, "nf": 0}
            nsub = TB // 128
            NTB2 = S // TB

            def norm_in(tb):
                hT_, hTb_ = h2T[tb % 2]
                xts = []
                for ts in range(nsub):
                    xt, xb, xd = x1t[cnt["nx"] % 4]
                    sl = sls[cnt["nx"] % 2]; cnt["nx"] += 1
                    r0 = tb * TB + ts * 128
                    dma("sp", xt[:], X1s.ap()[r0:r0 + 128, :], xd, reads=[db["X1s"]], writes=[xb])
                    rms_T(xt[:], xb, 2, hT_[:, :, ts * 128:(ts + 1) * 128], hTb_, sl, psT, psTb, evac=("dve" if ts % 2 == 0 else "act"))
                    xts.append((xt, xb, r0))
                return xts

            def ffn_in(tb):
                hT_, hTb_ = h2T[tb % 2]
                a_, ab_ = aT[0]
                for ht in range(NH):
                    pg, pgb = pG[cnt["ng"] % 2]
                    pu, pub = pU[cnt["ng"] % 2]
                    s_, sb_ = sg[cnt["ng"] % 2]; cnt["ng"] += 1

                    def gm(e, pg=pg, ht=ht):
                        for k in range(8):
                            ins = e.matmul(pg[:, 0:TB], lhsT=wfi[:, k, ht * 128:(ht + 1) * 128], rhs=hT_[:, k, :], start=(k == 0), stop=(k == 7))
                        return ins

                    def um(e, pu=pu, ht=ht):
                        for k in range(8):
                            ins = e.matmul(pu[:, 0:TB], lhsT=wfi[:, k, FF + ht * 128:FF + (ht + 1) * 128], rhs=hT_[:, k, :], start=(k == 0), stop=(k == 7))
                        return ins
                    op("pe", gm, reads=[wgb[ht // 4], hTb_], writes=[pgb])
                    op("pe", um, reads=[wgb[ht // 4], hTb_], writes=[pub])
                    op("act", lambda e, s_=s_, pg=pg: e.activation(out=s_[:], in_=pg[:, 0:TB], func=AF.Silu), reads=[pgb], writes=[sb_])
                    op("dve", lambda e, s_=s_, pu=pu, ht=ht: e.tensor_tensor(out=a_[:, ht, :], in0=pu[:, 0:TB], in1=s_[:], op=ALU.mult), reads=[pub, sb_], writes=[ab_])

            def ffn_out(tb, xts):
                a_, ab_ = aT[0]
                for ts in range(nsub):
                    xt, xb, r0 = xts[ts]
                    x2t, x2b = x2[cnt["nf"] % 2]
                    f = fs[cnt["nf"] % 2]
                    o_, ob_, od_ = ot[0]; cnt["nf"] += 1
                    for half in range(2):
                        po, pob = pO[cnt["no"] % 2]; cnt["no"] += 1

                        def om(e, po=po, ts=ts, half=half):
                            for k in range(NH):
                                ins = e.matmul(po[:], lhsT=a_[:, k, ts * 128:(ts + 1) * 128], rhs=wfo[:, k, half * 512:(half + 1) * 512], start=(k == 0), stop=(k == NH - 1))
                            return ins
                        op("pe", om, reads=[wb_, ab_], writes=[pob])
                        op("dve", lambda e, po=po, half=half, xt=xt, x2t=x2t: e.tensor_tensor(out=x2t[:, half * 512:(half + 1) * 512], in0=po[:], in1=xt[:, half * 512:(half + 1) * 512], op=ALU.add),
                           reads=[pob, xb], writes=[x2b])
                    op("act", lambda e, f=f, x2t=x2t, o_=o_: e.activation(out=o_[:], in_=x2t[:], func=AF.Square, accum_out=f["ss"][:]), reads=[x2b], writes=[f["b"], ob_])
                    op("pool", lambda e, f=f: e.tensor_scalar(out=f["rs"][:], in0=f["ss"][:], scalar1=1.0 / D, scalar2=1e-6, op0=ALU.mult, op1=ALU.add), reads=[f["b"]], writes=[f["b"]])
                    op("pool", lambda e, f=f: e.tensor_tensor(out=f["rr"][:], in0=f["rs"][:], in1=mhalf[:], op=ALU.pow), reads=[f["b"], cbuf], writes=[f["b"]])
                    op("dve", lambda e, f=f, x2t=x2t, o_=o_: e.scalar_tensor_tensor(out=o_[:], in0=x2t[:], scalar=f["rr"][:], in1=gains[:, 3, :], op0=ALU.mult, op1=ALU.mult),
                       reads=[x2b, f["b"], cbuf], writes=[ob_])
                    dma("sp", out_d[r0:r0 + 128, :], o_[:], od_, reads=[ob_], writes=[db["out"]])

            xts_cur = norm_in(0)
            for tb in range(NTB2):
                ffn_in(tb)
                xts_next = norm_in(tb + 1) if tb + 1 < NTB2 else None
                ffn_out(tb, xts_cur)
                xts_cur = xts_next
            B.barrier()

    if "AP" in phases:
        phase_AP(); B.barrier()
    if "B" in phases:
        phase_B(); B.barrier()
    gen = None
    if "D" in phases:
        phase_D1(); B.barrier()
        gen = rec_gen()
    if "C" in phases:
        phase_C(gen); B.barrier()
    if gen is not None:
        for _ in gen:
            pass
        B.barrier()
        phase_D2(); B.barrier()
    if "T1" in phases:
        phase_T1(); B.barrier()
    if "T2" in phases:
        phase_T2(); B.barrier()
    return nc, B


_CACHE = {}


def kernel(**inputs):
    consts = host_consts()
    in_maps = []
    for b in range(8):
        m = host_layout(inputs, b)
        m.update(consts)
        in_maps.append(m)
    if "nc" not in _CACHE:
        _CACHE["nc"] = build_program()[0]
    res = run_bass_kernel_spmd(_CACHE["nc"], in_maps, core_ids=list(range(8)))
    return np.stack([np.asarray(r["out"]) for r in res.results], axis=0).astype(np.float32)
```

```python
import math
from contextlib import ExitStack

import numpy as np

import concourse.bass as bass
import concourse.mybir as mybir
from concourse.bass_utils import run_bass_kernel_spmd

F32 = mybir.dt.float32
BF16 = mybir.dt.bfloat16
AF = mybir.ActivationFunctionType
ALU = mybir.AluOpType
AX = mybir.AxisListType

S = 4096
D = 1024
NTT = 32
NTB = 8
FF = 2816
NH = 22
NEG = -30000.0
LAM_INIT = 0.8 - 0.6 * math.exp(0.0)


class Buf:
    __slots__ = ("w", "r")

    def __init__(self):
        self.w = None
        self.r = {}


class Eng:
    def __init__(self, obj, sem, name):
        self.obj = obj
        self.sem = sem
        self.count = 0
        self.seen = {}
        self.name = name


class DS:
    def __init__(self, sem):
        self.sem = sem
        self.count = 0


class Builder:
    def __init__(self, nc, debug=False):
        self.nc = nc
        self.debug = debug
        self.es = ExitStack()
        self.E = {}
        for name, obj in (("pe", nc.tensor), ("act", nc.scalar), ("dve", nc.vector), ("pool", nc.gpsimd), ("sp", nc.sync)):
            self.E[name] = Eng(obj, self.es.enter_context(nc.semaphore("sem_" + name)), name)
        self.all_ds = []
        self.nname = 0

    def sb(self, shape, dt, stack=None, name=None):
        self.nname += 1
        return (stack or self.es).enter_context(self.nc.sbuf_tensor("%s_%d" % (name or "t", self.nname), list(shape), dt))

    def ps(self, shape, dt, stack=None, name=None):
        self.nname += 1
        return (stack or self.es).enter_context(self.nc.psum_tensor("%s_%d" % (name or "p", self.nname), list(shape), dt))

    def ds(self):
        self.nname += 1
        d = DS(self.es.enter_context(self.nc.semaphore("ds_%d" % self.nname)))
        self.all_ds.append(d)
        return d

    def _wait(self, E, ev):
        if ev is None:
            return
        sem, val = ev
        k = id(sem)
        if E.seen.get(k, 0) >= val:
            return
        E.obj.wait_ge(sem, val)
        E.seen[k] = val

    def _deps(self, E, reads, writes):
        own = E.sem
        pe = E.name == "pe"
        for b in reads:
            if b.w is not None and not (pe and b.w[0] is own):
                self._wait(E, b.w)
        for b in writes:
            if b.w is not None and not (pe and b.w[0] is own):
                self._wait(E, b.w)
            for ev in b.r.values():
                if not (pe and ev[0] is own):
                    self._wait(E, ev)

    def op(self, e, fn, reads=(), writes=()):
        E = self.E[e]
        self._deps(E, reads, writes)
        ins = fn(E.obj)
        E.count += 1
        ins.then_inc(E.sem, 1)
        ev = (E.sem, E.count)
        for b in reads:
            b.r[id(E.sem)] = ev
        for b in writes:
            b.w = ev
            b.r = {}

    def dma(self, q, out, in_, ds, reads=(), writes=()):
        E = self.E[q]
        self._deps(E, reads, writes)
        ins = E.obj.dma_start(out=out, in_=in_)
        ds.count += 16
        ins.then_inc(ds.sem, 16)
        ev = (ds.sem, ds.count)
        for b in reads:
            b.r[id(ds.sem)] = ev
        for b in writes:
            b.w = ev
            b.r = {}

    def barrier(self):
        for E in self.E.values():
            for Fo in self.E.values():
                if Fo.count > 0 and not (Fo is E and E.name == "pe"):
                    self._wait(E, (Fo.sem, Fo.count))
            for d in self.all_ds:
                if d.count > 0:
                    self._wait(E, (d.sem, d.count))


def _t5_bucket(rel):
    half, max_exact = 16, 8
    ret = np.where(rel > 0, half, 0)
    n = np.abs(rel)
    nf = np.maximum(n, 1).astype(np.float32)
    large = max_exact + (np.log(nf / np.float32(max_exact)) / np.float32(math.log(256 / max_exact)) * np.float32(half - max_exact)).astype(np.int32)
    large = np.minimum(large, half - 1)
    return ret + np.where(n < max_exact, n, large)


def host_consts():
    c = {}
    c["ident"] = np.eye(128, dtype=np.float32)
    c["antiid"] = np.eye(128, dtype=np.float32)[::-1].copy()
    ii = np.arange(1280)
    b = _t5_bucket(511 - ii)
    oh = np.zeros((32, 1280), np.float32)
    oh[b, ii] = 1.0
    c["onehot"] = oh
    p = np.arange(128)[:, None]
    j = np.arange(1152)[None, :]
    c["maskmb"] = np.where((p // 64) <= np.floor_divide(j - 384, 64), 0.0, NEG).astype(np.float32)
    r = np.arange(128)[:, None, None]
    it = np.arange(4)[None, :, None]
    col = np.arange(512)[None, None, :]
    c["maskd"] = ((col[:, 0, 0:128] // 32) >= (r[:, 0, :] // 32)).astype(np.float32)
    sel = np.zeros((128, 4, 128), np.float32)
    for rr in range(128):
        for il in range(4):
            sel[rr, il, 32 * il + rr % 32] = 1.0
    c["sel"] = sel
    return c


def host_layout(inp, b):
    f = np.float32
    m = {}
    m["x"] = np.ascontiguousarray(inp["x"][b])
    m["mem"] = np.ascontiguousarray(inp["mem"][b])
    for k in ("w_in", "glu_w", "w_mem_kv", "w_br_attn", "w_br_ssm", "w_br_xattn", "w_out", "w_ffn_in", "w_ffn_out"):
        m[k] = np.ascontiguousarray(inp[k][0])
    gb = np.stack([np.broadcast_to(inp["norm1_g"][0], (128, D)), np.broadcast_to(inp["mem_norm_g"][0], (128, D)),
                   np.broadcast_to(inp["norm2_g"][0], (128, D)), np.broadcast_to(inp["final_g"], (128, D))], axis=1)
    m["gains"] = np.ascontiguousarray(gb, dtype=f)
    m["subg"] = np.ascontiguousarray(np.broadcast_to(inp["da_subln_g"][0], (128, 128)), dtype=f)
    lqk = np.stack([inp["da_lq1"][0], inp["da_lk1"][0], inp["da_lq2"][0], inp["da_lk2"][0]], axis=0)
    m["lqk"] = np.ascontiguousarray(np.broadcast_to(lqk, (128, 4, 64)), dtype=f)
    m["relb"] = np.ascontiguousarray(inp["rel_bias"], dtype=f)
    m["relb15"] = np.ascontiguousarray(np.broadcast_to(inp["rel_bias"][15], (128, 8)), dtype=f)

    def st(a):
        return np.ascontiguousarray(a.reshape(32, 2, 64).transpose(1, 2, 0).reshape(128, 32), dtype=f)
    m["s_are"] = st(inp["ssm_a_re"][0])
    m["s_aim"] = st(inp["ssm_a_im"][0])
    m["s_ldt"] = st(np.broadcast_to(inp["ssm_log_dt"][0][:, None], (64, 64)))
    def stb(a):
        return np.ascontiguousarray(a.reshape(32, 2, 64, 16).transpose(1, 2, 0, 3).reshape(128, 32, 16), dtype=f)
    m["s_bre"] = stb(inp["ssm_b_re"][0])
    m["s_bim"] = stb(inp["ssm_b_im"][0])
    def stc(a):
        return np.ascontiguousarray(a.reshape(32, 2, 16, 64).transpose(1, 3, 0, 2).reshape(128, 32, 16), dtype=f)
    m["s_cre"] = stc(inp["ssm_c_re"][0])
    m["s_cim"] = stc(inp["ssm_c_im"][0])
    m["s_d"] = np.ascontiguousarray(inp["ssm_d"][0].reshape(8, 128).T, dtype=f)
    m["glu_b"] = np.ascontiguousarray(inp["glu_b"][0].reshape(8, 128).T, dtype=f)
    return m


INPUT_SHAPES = {
    "x": [S, D], "mem": [256, D], "w_in": [D, 8192], "glu_w": [D, D], "w_mem_kv": [D, 2048], "w_br_attn": [D, D],
    "w_br_ssm": [D, D], "w_br_xattn": [D, D], "w_out": [D, D], "w_ffn_in": [D, 2 * FF], "w_ffn_out": [FF, D],
    "gains": [128, 4, D], "subg": [128, 128], "lqk": [128, 4, 64], "relb": [32, 8], "relb15": [128, 8],
    "s_are": [128, 32], "s_aim": [128, 32], "s_ldt": [128, 32], "s_bre": [128, 32, 16], "s_bim": [128, 32, 16],
    "s_cre": [128, 32, 16], "s_cim": [128, 32, 16], "s_d": [128, 8], "glu_b": [128, 8],
    "ident": [128, 128], "antiid": [128, 128], "onehot": [32, 1280], "maskmb": [128, 1152], "maskd": [128, 128],
    "sel": [128, 4, 128],
}


ALL_PHASES = ("AP", "B", "C", "D", "T1", "T2")


def build_program(debug=False, upto="all", phases=ALL_PHASES):
    nc = bass.Bass("TRN2", target_bir_lowering=False)
    B = Builder(nc, debug)
    I = {k: nc.dram_tensor(k, shp, F32, kind="ExternalInput").ap() for k, shp in INPUT_SHAPES.items()}
    out_d = nc.dram_tensor("out", [S, D], F32, kind="ExternalOutput").ap()
    def scratch(name, shape, dt, producer=None):
        if producer is not None and producer not in phases:
            kind = "ExternalInput"
        else:
            kind = "ExternalOutput" if debug else "Internal"
        return nc.dram_tensor(name, shape, dt, kind=kind)

    QTs = scratch("QTs", [D, S], BF16, "AP")
    KTs = scratch("KTs", [D, S], BF16, "AP")
    Vs = scratch("Vs", [S, D], BF16, "AP")
    UTs = scratch("UTs", [D, S], BF16, "AP")
    XQs = scratch("XQs", [D, S], BF16, "AP")
    GTs = scratch("GTs", [3 * D, S], BF16, "AP")
    YAs = scratch("YAs", [D, S], BF16, "B")
    YXs = scratch("YXs", [D, S], BF16, "C")
    ZTs = scratch("ZTs", [D, S], BF16, "D")
    YIs = scratch("YIs", [D, S], BF16)
    CXs = scratch("CXs", [32, 128, 1024], BF16)
    X1s = scratch("X1s", [S, D], F32, "T1")
    Gd = scratch("Gd", [8, 1280], F32)
    db = {k: Buf() for k in ("QTs", "KTs", "Vs", "UTs", "XQs", "GTs", "YAs", "YXs", "ZTs", "YIs", "CXs", "X1s", "Gd", "out")}

    op, dma = B.op, B.dma
    es = B.es

    def fv(apobj, dims):
        return bass.AP(apobj.tensor, apobj.offset, [list(apobj.ap[0])] + [list(d) for d in dims])

    ident_f = B.sb([128, 128], F32); ident_b = B.sb([128, 128], BF16)
    gains = B.sb([128, 4, D], F32)
    eps_t = B.sb([128, 1], F32)
    eps2_t = B.sb([128, 1], F32)
    cbuf = Buf()
    cds = B.ds()
    dma("sp", ident_f[:], I["ident"], cds, writes=[cbuf])
    dma("sp", gains[:], I["gains"], cds, writes=[cbuf])
    op("dve", lambda e: e.tensor_copy(out=ident_b[:], in_=ident_f[:]), reads=[cbuf], writes=[cbuf])
    op("dve", lambda e: e.memset(eps_t[:], 1e-6), writes=[cbuf])
    op("dve", lambda e: e.memset(eps2_t[:], 1e-6 / 0.64), writes=[cbuf])
    mhalf = B.sb([128, 1], F32)
    op("dve", lambda e: e.memset(mhalf[:], -0.5), writes=[cbuf])
    mhalf4 = B.sb([128, 4], F32)
    op("dve", lambda e: e.memset(mhalf4[:], -0.5), writes=[cbuf])

    def rms_T(src, src_buf, gidx, dstT, dst_buf, sl, psT, psT_buf, evac="dve"):
        op("act", lambda e: e.activation(out=sl["hb"][:], in_=src, func=AF.Square, accum_out=sl["ss"][:]),
           reads=[src_buf], writes=[sl["hbb"], sl["ssb"]])
        op("pool", lambda e: e.tensor_scalar(out=sl["rs"][:], in0=sl["ss"][:], scalar1=1.0 / D, scalar2=1e-6, op0=ALU.mult, op1=ALU.add),
           reads=[sl["ssb"]], writes=[sl["rsb"]])
        op("pool", lambda e: e.tensor_tensor(out=sl["rr"][:], in0=sl["rs"][:], in1=mhalf[:], op=ALU.pow), reads=[sl["rsb"], cbuf], writes=[sl["rrb"]])
        op("dve", lambda e: e.scalar_tensor_tensor(out=sl["hb"][:], in0=src, scalar=sl["rr"][:], in1=gains[:, gidx, :],
                                                   op0=ALU.mult, op1=ALU.mult),
           reads=[src_buf, sl["rrb"], cbuf], writes=[sl["hbb"]])

        def tr(e):
            for k in range(8):
                ins = e.transpose(out=psT[:, k, :], in_=sl["hb"][:, k * 128:(k + 1) * 128], identity=ident_b[:])
            return ins
        op("pe", tr, reads=[sl["hbb"], cbuf], writes=[psT_buf])
        op(evac, lambda e: (e.tensor_copy(out=dstT, in_=psT[:]) if evac == "dve" else e.copy(out=dstT, in_=psT[:])),
           reads=[psT_buf], writes=[dst_buf])

    def rms_slot(stack):
        return {"hb": B.sb([128, D], BF16, stack), "ss": B.sb([128, 1], F32, stack), "rs": B.sb([128, 1], F32, stack),
                "rr": B.sb([128, 1], F32, stack), "hbb": Buf(), "ssb": Buf(), "rsb": Buf(), "rrb": Buf()}

    def phase_AP():
        with ExitStack() as st:
            hT = B.sb([128, 8, S], BF16, st, "hT")
            hTb = [Buf() for _ in range(NTT)]
            psT = B.ps([128, 8, 128], BF16, st)
            psTb = Buf()
            wsl = [(B.sb([128, 8, 512], BF16, st), Buf(), B.ds()) for _ in range(2)]
            for cb in range(2):
                dma("pool", wsl[cb][0][:], I["w_in"][:, cb * 512:(cb + 1) * 512].rearrange("(k p) n -> p k n", p=128), wsl[cb][2], writes=[wsl[cb][1]])
            with ExitStack() as st2:
                xs = [(B.sb([128, D], F32, st2), Buf(), B.ds()) for _ in range(2)]
                sls = [rms_slot(st2) for _ in range(2)]
                for tt in range(NTT):
                    xt, xb, xd = xs[tt % 2]
                    dma("sp", xt[:], I["x"][tt * 128:(tt + 1) * 128, :], xd, writes=[xb])
                    rms_T(xt[:], xb, 0, hT[:, :, tt * 128:(tt + 1) * 128], hTb[tt], sls[tt % 2], psT, psTb,
                          evac=("dve" if tt % 2 == 0 else "act"))
                B.barrier()
            if upto == "A":
                return
            with ExitStack() as st2:
                pbank = [(B.ps([128, 512], F32, st2), Buf()) for _ in range(4)]
                stg = [(B.sb([128, S], BF16, st2), Buf(), B.ds()) for _ in range(2)]
                vst = [(B.sb([128, 512], BF16, st2), Buf(), B.ds()) for _ in range(3)]
                hT_all = hTb
                npb = 0
                nst = 0
                nv = 0
                for cb in range(16):
                    wt, wb, wd = wsl[cb % 2]
                    if cb >= 2:
                        dma("pool", wt[:], I["w_in"][:, cb * 512:(cb + 1) * 512].rearrange("(k p) n -> p k n", p=128), wd, writes=[wb])
                    if 4 <= cb < 6:
                        for tt in range(NTT):
                            pt, pb = pbank[npb % 4]; npb += 1

                            def mm(e, pt=pt, tt=tt, wt=wt):
                                for k in range(8):
                                    ins = e.matmul(pt[:], lhsT=hT[:, k, tt * 128:(tt + 1) * 128], rhs=wt[:, k, :], start=(k == 0), stop=(k == 7))
                                return ins
                            op("pe", mm, reads=[wb, hT_all[tt]], writes=[pb])
                            vt, vb, vd = vst[nv % 3]; nv += 1
                            eng = "dve" if tt % 2 == 0 else "act"
                            op(eng, lambda e, vt=vt, pt=pt, eng=eng: (e.tensor_copy(out=vt[:], in_=pt[:]) if eng == "dve" else e.copy(out=vt[:], in_=pt[:])),
                               reads=[pb], writes=[vb])
                            dma("sp", Vs.ap()[tt * 128:(tt + 1) * 128, (cb - 4) * 512:(cb - 3) * 512], vt[:], vd, reads=[vb], writes=[db["Vs"]])
                        continue
                    for ct in range(4):
                        gcol = cb * 512 + ct * 128
                        sg, sgb, sgd = stg[nst % 2]; nst += 1
                        for tb in range(NTB):
                            pt, pb = pbank[npb % 4]; npb += 1

                            def mm(e, pt=pt, tb=tb, wt=wt, ct=ct):
                                for k in range(8):
                                    ins = e.matmul(pt[:], lhsT=wt[:, k, ct * 128:(ct + 1) * 128], rhs=hT[:, k, tb * 512:(tb + 1) * 512], start=(k == 0), stop=(k == 7))
                                return ins
                            op("pe", mm, reads=[wb] + hT_all[tb * 4:tb * 4 + 4], writes=[pb])
                            dst = sg[:, tb * 512:(tb + 1) * 512]
                            if gcol < 1024:
                                op("act", lambda e, dst=dst, pt=pt: e.mul(out=dst, in_=pt[:], mul=0.125), reads=[pb], writes=[sgb])
                            elif gcol < 2048:
                                op("dve", lambda e, dst=dst, pt=pt: e.tensor_copy(out=dst, in_=pt[:]), reads=[pb], writes=[sgb])
                            elif gcol < 4096:
                                eng = "dve" if tb % 2 == 0 else "act"
                                op(eng, lambda e, dst=dst, pt=pt, eng=eng: (e.tensor_copy(out=dst, in_=pt[:]) if eng == "dve" else e.copy(out=dst, in_=pt[:])),
                                   reads=[pb], writes=[sgb])
                            elif gcol < 5120:
                                op("act", lambda e, dst=dst, pt=pt: e.mul(out=dst, in_=pt[:], mul=0.0625), reads=[pb], writes=[sgb])
                            else:
                                op("act", lambda e, dst=dst, pt=pt: e.activation(out=dst, in_=pt[:], func=AF.Sigmoid), reads=[pb], writes=[sgb])
                        if gcol < 1024:
                            dd, dbuf, r0 = QTs, db["QTs"], gcol
                        elif gcol < 2048:
                            dd, dbuf, r0 = KTs, db["KTs"], gcol - 1024
                        elif gcol < 4096:
                            dd, dbuf, r0 = UTs, db["UTs"], gcol - 3072
                        elif gcol < 5120:
                            dd, dbuf, r0 = XQs, db["XQs"], gcol - 4096
                        else:
                            dd, dbuf, r0 = GTs, db["GTs"], gcol - 5120
                        dma("sp", dd.ap()[r0:r0 + 128, :], sg[:], sgd, reads=[sgb], writes=[dbuf])
                B.barrier()

    def phase_B():
        with ExitStack() as st:
            lqk = B.sb([128, 4, 64], F32, st)
            lpr = B.sb([128, 2, 64], F32, st)
            lsum = B.sb([128, 2], F32, st)
            lexp = B.sb([128, 2], F32, st)
            neglam = B.sb([128, 1], F32, st)
            relb15 = B.sb([128, 8], F32, st)
            subg = B.sb([128, 128], F32, st)
            lb = Buf(); lds = B.ds()
            dma("sp", lqk[:], I["lqk"], lds, writes=[lb])
            dma("sp", relb15[:], I["relb15"], lds, writes=[lb])
            dma("sp", subg[:], I["subg"], lds, writes=[lb])
            lqv = lqk[:].rearrange("p (a b) d -> p a b d", b=2)
            op("dve", lambda e: e.tensor_tensor(out=lpr[:], in0=lqv[:, :, 0, :], in1=lqv[:, :, 1, :], op=ALU.mult), reads=[lb], writes=[lb])
            op("dve", lambda e: e.reduce_sum(out=lsum[:], in_=lpr[:], axis=AX.X), reads=[lb], writes=[lb])
            op("act", lambda e: e.activation(out=lexp[:], in_=lsum[:], func=AF.Exp), reads=[lb], writes=[lb])
            op("dve", lambda e: e.tensor_tensor(out=neglam[:], in0=lexp[:, 1:2], in1=lexp[:, 0:1], op=ALU.subtract), reads=[lb], writes=[lb])
            op("dve", lambda e: e.tensor_scalar(out=neglam[:], in0=neglam[:], scalar1=-LAM_INIT, scalar2=None, op0=ALU.add), reads=[lb], writes=[lb])

            MB = B.sb([128, 8, 1152], BF16, st, "MB")
            mbb = Buf()
            with ExitStack() as st2:
                relb = B.sb([32, 8], F32, st2)
                onehot = B.sb([32, 1280], F32, st2)
                antiid = B.sb([128, 128], F32, st2)
                maskmb = B.sb([128, 1152], F32, st2)
                gsb = B.sb([8, 1280], F32, st2)
                hk = [(B.sb([128, 1152], F32, st2), Buf(), B.ds()) for _ in range(2)]
                tb_ = Buf(); tds = B.ds()
                dma("sp", relb[:], I["relb"], tds, writes=[tb_])
                dma("sp", onehot[:], I["onehot"], tds, writes=[tb_])
                dma("sp", antiid[:], I["antiid"], tds, writes=[tb_])
                dma("sp", maskmb[:], I["maskmb"], tds, writes=[tb_])
                pg = [(B.ps([128, 512], F32, st2), Buf()) for _ in range(3)]
                for n in range(3):
                    n0, n1 = n * 512, min(1280, (n + 1) * 512)
                    op("pe", lambda e, n=n, n0=n0, n1=n1: e.matmul(pg[n][0][0:8, 0:n1 - n0], lhsT=relb[:], rhs=onehot[:, n0:n1], start=True, stop=True),
                       reads=[tb_], writes=[pg[n][1]])
                    op("dve", lambda e, n=n, n0=n0, n1=n1: e.tensor_copy(out=gsb[:, n0:n1], in_=pg[n][0][0:8, 0:n1 - n0]), reads=[pg[n][1]], writes=[tb_])
                gdd = B.ds()
                dma("sp", Gd.ap(), gsb[:], gdd, reads=[tb_], writes=[db["Gd"]])
                for h in range(8):
                    ht, hb_, hd = hk[h % 2]
                    dma("sp", ht[:], bass.AP(Gd, h * 1280, [[1, 128], [1, 1152]]), hd, reads=[db["Gd"]], writes=[hb_])
                    for n in range(3):
                        n0, n1 = n * 512, min(1152, (n + 1) * 512)
                        op("pe", lambda e, n=n, n0=n0, n1=n1, ht=ht: e.matmul(pg[n][0][:, 0:n1 - n0], lhsT=antiid[:], rhs=ht[:, n0:n1], start=True, stop=True),
                           reads=[tb_, hb_], writes=[pg[n][1]])
                        op("dve", lambda e, n=n, n0=n0, n1=n1, h=h: e.tensor_tensor(out=MB[:, h, n0:n1], in0=pg[n][0][:, 0:n1 - n0], in1=maskmb[:, n0:n1], op=ALU.add),
                           reads=[pg[n][1], tb_], writes=[mbb])
                B.barrier()

            sets = []
            for s_ in range(2):
                sets.append({"QT": B.sb([128, S], BF16, st), "KT": B.sb([128, S], BF16, st), "V": B.sb([128, NTT, 129], BF16, st),
                             "b": Buf(), "ds": B.ds()})
            for s_ in sets:
                op("dve", lambda e, s_=s_: e.memset(s_["V"][:, :, 128:129], 1.0), writes=[s_["b"]])
            sc = [(B.ps([128, 2, 512], F32, st), Buf()) for _ in range(2)]
            ob = [(B.ps([128, 512], F32, st), Buf()) for _ in range(3)]
            psT = B.ps([128, 8, 128], BF16, st)
            psTb = Buf()
            NPT = 4
            PT = [(B.sb([128, 2, 512], BF16, st), Buf()) for _ in range(NPT)]
            oc = [(B.sb([128, 3, 387], F32, st), Buf()) for _ in range(2)]
            fin = [{"rr": B.sb([128, 8], F32, st), "t1": B.sb([128, 4, 128], F32, st), "ot": B.sb([128, 4, 128], F32, st),
                    "ss": B.sb([128, 4], F32, st), "rs": B.sb([128, 4], F32, st), "r2": B.sb([128, 4], F32, st),
                    "yb": B.sb([128, 4, 128], BF16, st), "b": Buf()} for _ in range(3)]
            ystg = [(B.sb([128, 512], BF16, st), Buf(), B.ds()) for _ in range(3)]
            state = {"nfin": 0, "nys": 0, "noc": 0}
            pending = []

            def tick():
                for p_ in pending:
                    p_[0] -= 1
                while pending and pending[0][0] <= 0:
                    pending.pop(0)[1]()

            def oreg(r):
                return r // 3, (r % 3) * 129

            blocks = []
            for h in range(8):
                for j in range(NTB):
                    nkt = 4 * (j + 1)
                    for kt in range(nkt):
                        blocks.append((h, j, kt, nkt))

            def load_head(h):
                hs = sets[h % 2]
                dma("sp", hs["QT"][:], QTs.ap()[h * 128:(h + 1) * 128, :], hs["ds"], reads=[db["QTs"]], writes=[hs["b"]])
                dma("sp", hs["KT"][:], KTs.ap()[h * 128:(h + 1) * 128, :], hs["ds"], reads=[db["KTs"]], writes=[hs["b"]])
                dma("sp", hs["V"][:, :, 0:128], Vs.ap()[:, h * 128:(h + 1) * 128].rearrange("(t p) e -> p t e", p=128), hs["ds"],
                    reads=[db["Vs"]], writes=[hs["b"]])

            def emit_scores(n):
                h, j, kt, nkt = blocks[n]
                hs = sets[h % 2]
                m = max(0, kt - 4 * j)
                qlo = 128 * m
                near = kt >= 4 * j - 2
                off = 512 * j - 128 * kt + 384
                pt, pb = sc[n % 2]

                def smm(e):
                    for c in range(2):
                        ins = e.matmul(pt[:, c, qlo:512], lhsT=hs["KT"][64 * c:64 * c + 64, kt * 128:(kt + 1) * 128],
                                       rhs=hs["QT"][64 * c:64 * c + 64, j * 512 + qlo:(j + 1) * 512], start=True, stop=not near)
                    if near:
                        for c in range(2):
                            ins = e.matmul(pt[:, c, qlo:512], lhsT=ident_b[:], rhs=MB[:, h, off + qlo:off + 512], start=False, stop=True)
                    return ins
                op("pe", smm, reads=[hs["b"], mbb, cbuf], writes=[pb])
                ptile, ptb = PT[n % NPT]
                if near:
                    op("act", lambda e: e.activation(out=ptile[:, :, qlo:512], in_=pt[:, :, qlo:512], func=AF.Exp), reads=[pb], writes=[ptb])
                else:
                    op("act", lambda e: e.activation(out=ptile[:], in_=pt[:], func=AF.Exp, bias=relb15[:, h:h + 1], scale=1.0), reads=[pb, lb], writes=[ptb])

            def emit_av(n):
                h, j, kt, nkt = blocks[n]
                hs = sets[h % 2]
                m = max(0, kt - 4 * j)
                ptile, ptb = PT[n % NPT]

                def avmm(e):
                    for c in range(2):
                        for qs in range(m, 4):
                            bank, co = oreg(c * 4 + qs)
                            first = (kt == 0) and ((c * 4 + qs) % 3 == 0)
                            ins = e.matmul(ob[bank][0][:, co:co + 129], lhsT=ptile[:, c, qs * 128:(qs + 1) * 128], rhs=hs["V"][:, kt, :],
                                           start=first, stop=(kt == 4 * j + qs), skip_group_check=True)
                    return ins
                op("pe", avmm, reads=[ptb, hs["b"]], writes=[ob[0][1], ob[1][1], ob[2][1]])
                if kt == nkt - 1:
                    finalize(h, j)

            def finalize(h, j):
                o_, ocb = oc[state["noc"] % 2]; state["noc"] += 1
                for bk in range(3):
                    w_ = 387 if bk < 2 else 258
                    op("dve", lambda e, bk=bk, w_=w_: e.tensor_copy(out=o_[:, bk, 0:w_], in_=ob[bk][0][:, 0:w_]), reads=[ob[bk][1]], writes=[ocb])
                ys, ysb, ysd = ystg[state["nys"] % 3]; state["nys"] += 1
                f = fin[state["nfin"] % 3]; state["nfin"] += 1
                reg = o_[:].rearrange("p a b -> p (a b)")[:, 0:1032].rearrange("p (r c) -> p r c", c=129)
                fb = f["b"]
                op("dve", lambda e: e.reciprocal(out=f["rr"][:], in_=reg[:, :, 128]), reads=[ocb], writes=[fb])
                op("dve", lambda e: e.tensor_scalar(out=f["rr"][:, 4:8], in0=f["rr"][:, 4:8], scalar1=neglam[:, 0:1], scalar2=None, op0=ALU.mult), reads=[fb, lb], writes=[fb])
                op("dve", lambda e: e.tensor_tensor(out=f["t1"][:], in0=reg[:, 4:8, 0:128], in1=fv(f["rr"][:, 4:5], [[1, 4], [0, 128]]), op=ALU.mult), reads=[ocb, fb], writes=[fb])
                op("dve", lambda e: e.tensor_tensor(out=f["ot"][:], in0=reg[:, 0:4, 0:128], in1=fv(f["rr"][:, 0:1], [[1, 4], [0, 128]]), op=ALU.mult), reads=[ocb, fb], writes=[fb])
                op("dve", lambda e: e.tensor_tensor(out=f["ot"][:], in0=f["ot"][:], in1=f["t1"][:], op=ALU.add), reads=[fb], writes=[fb])
                op("dve", lambda e: e.tensor_tensor(out=f["t1"][:], in0=f["ot"][:], in1=f["ot"][:], op=ALU.mult), reads=[fb], writes=[fb])
                op("dve", lambda e: e.reduce_sum(out=f["ss"][:], in_=f["t1"][:], axis=AX.X), reads=[fb], writes=[fb])
                op("pool", lambda e: e.tensor_scalar(out=f["rs"][:], in0=f["ss"][:], scalar1=1.0 / (128 * 0.64), scalar2=1e-6 / 0.64, op0=ALU.mult, op1=ALU.add),
                   reads=[fb], writes=[fb])
                op("pool", lambda e: e.tensor_tensor(out=f["r2"][:], in0=f["rs"][:], in1=mhalf4[:], op=ALU.pow), reads=[fb, cbuf], writes=[fb])
                op("dve", lambda e: e.tensor_tensor(out=f["t1"][:], in0=f["ot"][:], in1=fv(f["r2"][:, 0:1], [[1, 4], [0, 128]]), op=ALU.mult), reads=[fb], writes=[fb])
                op("dve", lambda e: e.tensor_tensor(out=f["yb"][:], in0=f["t1"][:], in1=fv(subg[:, 0:1], [[0, 4], [1, 128]]), op=ALU.mult), reads=[fb, lb], writes=[fb])

                def later():
                    def tr(e):
                        for qs in range(4):
                            ins = e.transpose(out=psT[:, qs, :], in_=f["yb"][:, qs, :], identity=ident_b[:])
                        return ins
                    op("pe", tr, reads=[fb, cbuf], writes=[psTb])
                    op("dve", lambda e: e.tensor_copy(out=ys[:].rearrange("p (a b) -> p a b", a=4), in_=psT[:, 0:4, :]), reads=[psTb], writes=[ysb])
                    dma("sp", YAs.ap()[h * 128:(h + 1) * 128, j * 512:(j + 1) * 512], ys[:], ysd, reads=[ysb], writes=[db["YAs"]])
                pending.append([10, later])

            NBLK = len(blocks)
            load_head(0)
            for n in range(NBLK + 2):
                if n < NBLK:
                    h, j, kt, nkt = blocks[n]
                    if j == 0 and kt == 2 and h + 1 < 8:
                        load_head(h + 1)
                    emit_scores(n)
                if n >= 2:
                    emit_av(n - 2)
                tick()
            while pending:
                pending.pop(0)[1]()
            B.barrier()

    def phase_C(gen=None):
        def pull(n):
            if gen is not None:
                for _ in range(n):
                    next(gen, None)

        with ExitStack() as st:
            memnT = B.sb([128, 8, 256], BF16, st)
            mnb = Buf()
            KxT = B.sb([128, 8, 256], BF16, st)
            Vx = B.sb([128, 2, 4, 257], BF16, st)
            kvb = Buf()
            psT = B.ps([128, 8, 128], BF16, st)
            psTb = Buf()
            pk = [(B.ps([128, 512], F32, st), Buf()) for _ in range(2)]
            with ExitStack() as st2:
                wkv = B.sb([128, 8, 2048], BF16, st2)
                wkb = Buf(); wkd = B.ds()
                for n in range(4):
                    dma("pool", wkv[:, :, n * 512:(n + 1) * 512], I["w_mem_kv"][:, n * 512:(n + 1) * 512].rearrange("(k p) n -> p k n", p=128), wkd, writes=[wkb])
                ms = [(B.sb([128, D], F32, st2), Buf(), B.ds()) for _ in range(2)]
                sls = [rms_slot(st2) for _ in range(2)]
                for mt in range(2):
                    xt, xb, xd = ms[mt]
                    dma("sp", xt[:], I["mem"][mt * 128:(mt + 1) * 128, :], xd, writes=[xb])
                    rms_T(xt[:], xb, 1, memnT[:, :, mt * 128:(mt + 1) * 128], mnb, sls[mt], psT, psTb)
                op("dve", lambda e: e.memset(Vx[:, :, :, 256:257], 1.0), writes=[kvb])
                for ct in range(8):
                    pt, pb = pk[ct % 2]

                    def mm(e, pt=pt, ct=ct):
                        for k in range(8):
                            ins = e.matmul(pt[:, 0:256], lhsT=wkv[:, k, ct * 128:(ct + 1) * 128], rhs=memnT[:, k, :], start=(k == 0), stop=(k == 7))
                        return ins
                    op("pe", mm, reads=[wkb, mnb], writes=[pb])
                    op("dve", lambda e, pt=pt, ct=ct: e.tensor_copy(out=KxT[:, ct, :], in_=pt[:, 0:256]), reads=[pb], writes=[kvb])
                    pull(3)
                n = 0
                for mt in range(2):
                    for half in range(2):
                        pt, pb = pk[n % 2]; n += 1

                        def mm(e, pt=pt, mt=mt, half=half):
                            for k in range(8):
                                ins = e.matmul(pt[:], lhsT=memnT[:, k, mt * 128:(mt + 1) * 128], rhs=wkv[:, k, 1024 + half * 512:1024 + (half + 1) * 512], start=(k == 0), stop=(k == 7))
                            return ins
                        op("pe", mm, reads=[wkb, mnb], writes=[pb])
                        op("dve", lambda e, pt=pt, mt=mt, half=half: e.tensor_copy(out=Vx[:, mt, 2 * half:2 * half + 2, 0:256], in_=pt[:].rearrange("p (a b) -> p a b", a=2)),
                           reads=[pb], writes=[kvb])
                B.barrier()
            xqs = [(B.sb([128, 2, S], BF16, st), Buf(), B.ds()) for _ in range(2)]
            psS = [(B.ps([128, 512], F32, st), Buf()) for _ in range(2)]
            psO = [(B.ps([128, 512], F32, st), Buf()) for _ in range(2)]
            PX = [[(B.sb([128, 512], BF16, st), Buf()) for _ in range(2)] for _ in range(2)]
            fx = [{"rr": B.sb([128, 1], F32, st), "yb": B.sb([128, 256], BF16, st), "b": Buf()} for _ in range(2)]
            ystg = [(B.sb([128, 2, 512], BF16, st), Buf(), B.ds()) for _ in range(2)]
            nb_ = 0; nf = 0; nys = 0; no = 0
            def load_xq(hx):
                xq, xqb, xqd = xqs[hx % 2]
                dma("sp", xq[:], XQs.ap()[hx * 256:(hx + 1) * 256, :].rearrange("(a p) t -> p a t", p=128), xqd, reads=[db["XQs"]], writes=[xqb])

            load_xq(0)
            for hx in range(4):
                xq, xqb, xqd = xqs[hx % 2]
                if hx + 1 < 4:
                    load_xq(hx + 1)
                for tb in range(NTB):
                    slot = nb_ % 2; nb_ += 1
                    for mt in range(2):
                        pt, pb = psS[mt]

                        def smm(e, pt=pt, mt=mt, tb=tb, xq=xq, hx=hx):
                            for dt in range(2):
                                ins = e.matmul(pt[:], lhsT=KxT[:, 2 * hx + dt, mt * 128:(mt + 1) * 128], rhs=xq[:, dt, tb * 512:(tb + 1) * 512], start=(dt == 0), stop=(dt == 1))
                            return ins
                        op("pe", smm, reads=[kvb, xqb], writes=[pb])
                        px, pxb = PX[mt][slot]
                        op("act", lambda e, px=px, pt=pt: e.activation(out=px[:], in_=pt[:], func=AF.Exp), reads=[pb], writes=[pxb])
                    ys, ysb, ysd = ystg[nys % 2]; nys += 1
                    for qs in range(4):
                        po, pob = psO[no % 2]; no += 1

                        def omm(e, po=po, qs=qs, slot=slot, hx=hx):
                            for mt in range(2):
                                ins = e.matmul(po[:, 0:257], lhsT=PX[mt][slot][0][:, qs * 128:(qs + 1) * 128], rhs=Vx[:, mt, hx, :], start=(mt == 0), stop=(mt == 1))
                            return ins
                        op("pe", omm, reads=[PX[0][slot][1], PX[1][slot][1], kvb], writes=[pob])
                        f = fx[nf % 2]; nf += 1
                        op("dve", lambda e, f=f, po=po: e.reciprocal(out=f["rr"][:], in_=po[:, 256:257]), reads=[pob], writes=[f["b"]])
                        op("dve", lambda e, f=f, po=po: e.tensor_scalar(out=f["yb"][:], in0=po[:, 0:256], scalar1=f["rr"][:], scalar2=None, op0=ALU.mult),
                           reads=[pob, f["b"]], writes=[f["b"]])

                        def tr(e, f=f, qs=qs):
                            for dt in range(2):
                                ins = e.transpose(out=psT[:, dt * 4 + qs, :], in_=f["yb"][:, dt * 128:(dt + 1) * 128], identity=ident_b[:])
                            return ins
                        op("pe", tr, reads=[f["b"], cbuf], writes=[psTb])
                        pull(2)
                    op("act", lambda e, ys=ys: e.copy(out=ys[:].rearrange("p a (q t) -> p (a q) t", q=4), in_=psT[:]), reads=[psTb], writes=[ysb])
                    dma("sp", YXs.ap()[hx * 256:(hx + 1) * 256, tb * 512:(tb + 1) * 512].rearrange("(a p) t -> p a t", p=128), ys[:], ysd, reads=[ysb], writes=[db["YXs"]])
            B.barrier()

    Dst = {}

    def phase_D1():
        st = ExitStack()
        Dst["st"] = st
        if True:
            WW = B.sb([128, 2, 32, 256], F32, st, "WW")
            wwb = Buf()
            AA1 = B.sb([128, 2, 32], F32, st)
            AA2 = B.sb([128, 2, 32], F32, st)
            sd = B.sb([128, 8], F32, st)
            tb_ = Buf(); tds = B.ds()
            r1_ = B.sb([128, 2, 32], F32, st); r2_ = B.sb([128, 2, 32], F32, st)
            Dst.update(WW=WW, wwb=wwb, AA1=AA1, AA2=AA2, tb_=tb_, r1=r1_, r2=r2_)
            dma("sp", sd[:], I["s_d"], tds, writes=[tb_])
            with ExitStack() as st1:
                are = B.sb([128, 32], F32, st1); aim = B.sb([128, 32], F32, st1); ldt = B.sb([128, 32], F32, st1)
                bre = B.sb([128, 32, 16], F32, st1); bim = B.sb([128, 32, 16], F32, st1)
                cre = B.sb([128, 32, 16], F32, st1); cim = B.sb([128, 32, 16], F32, st1)
                for t_, k_ in ((are, "s_are"), (aim, "s_aim"), (ldt, "s_ldt"), (bre, "s_bre"), (bim, "s_bim"), (cre, "s_cre"), (cim, "s_cim")):
                    dma("sp", t_[:], I[k_], tds, writes=[tb_])
                sel_f = B.sb([128, 4, 128], F32, st1); sel_b = B.sb([128, 4, 128], BF16, st1)
                maskd = B.sb([128, 128], F32, st1)
                dma("sp", sel_f[:], I["sel"], tds, writes=[tb_])
                dma("sp", maskd[:], I["maskd"], tds, writes=[tb_])
                T = lambda shape: B.sb(shape, F32, st1)
                dtt = T([128, 32]); adr = T([128, 32]); th = T([128, 32]); cc = T([128, 32]); ss_ = T([128, 32])
                t1 = T([128, 32]); t2 = T([128, 32]); hpi = T([128, 1])
                ER = T([128, 32, 17]); EI = T([128, 32, 17]); MAGP = T([128, 32, 17]); MAGN = T([128, 32, 17])
                PPr = T([128, 32, 17]); PPi = T([128, 32, 17]); PNr = T([128, 32, 17]); PNi = T([128, 32, 17])
                bbr = T([128, 32, 16]); bbi = T([128, 32, 16])
                tmpA = T([128, 32, 16]); tmpB = T([128, 32, 16])
                tb2 = tb_

                def D_(fn):
                    op("dve", fn, reads=[tb2], writes=[tb2])

                def A_(fn):
                    op("act", fn, reads=[tb2], writes=[tb2])
                D_(lambda e: e.tensor_copy(out=sel_b[:], in_=sel_f[:]))
                maskd_b = B.sb([128, 128], BF16, st1)
                D_(lambda e: e.tensor_copy(out=maskd_b[:], in_=maskd[:]))
                D_(lambda e: e.memset(hpi[:], math.pi / 2))
                A_(lambda e: e.activation(out=dtt[:], in_=ldt[:], func=AF.Exp))
                D_(lambda e: e.tensor_tensor(out=adr[:], in0=are[:], in1=dtt[:], op=ALU.mult))
                D_(lambda e: e.tensor_tensor(out=th[:], in0=aim[:], in1=dtt[:], op=ALU.mult))
                A_(lambda e: e.activation(out=ss_[:], in_=th[:], func=AF.Sin, scale=1.0 / 32))
                A_(lambda e: e.activation(out=cc[:], in_=th[:], func=AF.Sin, scale=1.0 / 32, bias=hpi[:]))
                for _ in range(5):
                    D_(lambda e: e.tensor_tensor(out=t1[:], in0=cc[:], in1=cc[:], op=ALU.mult))
                    D_(lambda e: e.tensor_tensor(out=t2[:], in0=ss_[:], in1=ss_[:], op=ALU.mult))
                    D_(lambda e: e.scalar_tensor_tensor(out=ss_[:], in0=cc[:], scalar=2.0, in1=ss_[:], op0=ALU.mult, op1=ALU.mult))
                    D_(lambda e: e.tensor_tensor(out=cc[:], in0=t1[:], in1=t2[:], op=ALU.subtract))
                D_(lambda e: e.memset(ER[:, :, 0:1], 1.0))
                D_(lambda e: e.memset(EI[:, :, 0:1], 0.0))
                D_(lambda e: e.tensor_copy(out=ER[:, :, 1], in_=cc[:]))
                D_(lambda e: e.tensor_copy(out=EI[:, :, 1], in_=ss_[:]))
                tmpE = [T([128, 32, 8]) for _ in range(4)]
                k = 1
                while k < 16:
                    a_r, a_i = ER[:, :, 1:k + 1], EI[:, :, 1:k + 1]
                    b_r = fv(ER[:, :, k:k + 1], [[17, 32], [0, k]])
                    b_i = fv(EI[:, :, k:k + 1], [[17, 32], [0, k]])
                    q = [t_[:, :, 0:k] for t_ in tmpE]
                    D_(lambda e, a_r=a_r, b_r=b_r, q=q: e.tensor_tensor(out=q[0], in0=a_r, in1=b_r, op=ALU.mult))
                    D_(lambda e, a_i=a_i, b_i=b_i, q=q: e.tensor_tensor(out=q[1], in0=a_i, in1=b_i, op=ALU.mult))
                    D_(lambda e, a_r=a_r, b_i=b_i, q=q: e.tensor_tensor(out=q[2], in0=a_r, in1=b_i, op=ALU.mult))
                    D_(lambda e, a_i=a_i, b_r=b_r, q=q: e.tensor_tensor(out=q[3], in0=a_i, in1=b_r, op=ALU.mult))
                    D_(lambda e, k=k, q=q: e.tensor_tensor(out=ER[:, :, k + 1:2 * k + 1], in0=q[0], in1=q[1], op=ALU.subtract))
                    D_(lambda e, k=k, q=q: e.tensor_tensor(out=EI[:, :, k + 1:2 * k + 1], in0=q[2], in1=q[3], op=ALU.add))
                    k *= 2
                for tau in range(17):
                    A_(lambda e, tau=tau: e.activation(out=MAGP[:, :, tau], in_=adr[:], func=AF.Exp, scale=float(tau)))
                    A_(lambda e, tau=tau: e.activation(out=MAGN[:, :, tau], in_=adr[:], func=AF.Exp, scale=-float(tau)))
                D_(lambda e: e.tensor_tensor(out=PPr[:], in0=MAGP[:], in1=ER[:], op=ALU.mult))
                D_(lambda e: e.tensor_tensor(out=PPi[:], in0=MAGP[:], in1=EI[:], op=ALU.mult))
                D_(lambda e: e.tensor_tensor(out=PNr[:], in0=MAGN[:], in1=ER[:], op=ALU.mult))
                D_(lambda e: e.scalar_tensor_tensor(out=PNi[:], in0=MAGN[:], scalar=-1.0, in1=EI[:], op0=ALU.mult, op1=ALU.mult))
                xr = T([128, 32]); nr = T([128, 32]); ni = T([128, 32]); den = T([128, 32]); cfr = T([128, 32]); cfi = T([128, 32])
                D_(lambda e: e.tensor_scalar(out=xr[:], in0=PPr[:, :, 1], scalar1=-1.0, scalar2=None, op0=ALU.add))
                D_(lambda e: e.tensor_tensor(out=t1[:], in0=xr[:], in1=are[:], op=ALU.mult))
                D_(lambda e: e.tensor_tensor(out=t2[:], in0=PPi[:, :, 1], in1=aim[:], op=ALU.mult))
                D_(lambda e: e.tensor_tensor(out=nr[:], in0=t1[:], in1=t2[:], op=ALU.add))
                D_(lambda e: e.tensor_tensor(out=t1[:], in0=PPi[:, :, 1], in1=are[:], op=ALU.mult))
                D_(lambda e: e.tensor_tensor(out=t2[:], in0=xr[:], in1=aim[:], op=ALU.mult))
                D_(lambda e: e.tensor_tensor(out=ni[:], in0=t1[:], in1=t2[:], op=ALU.subtract))
                D_(lambda e: e.tensor_tensor(out=t1[:], in0=are[:], in1=are[:], op=ALU.mult))
                D_(lambda e: e.tensor_tensor(out=t2[:], in0=aim[:], in1=aim[:], op=ALU.mult))
                D_(lambda e: e.tensor_tensor(out=den[:], in0=t1[:], in1=t2[:], op=ALU.add))
                D_(lambda e: e.reciprocal(out=den[:], in_=den[:]))
                D_(lambda e: e.tensor_tensor(out=cfr[:], in0=nr[:], in1=den[:], op=ALU.mult))
                D_(lambda e: e.tensor_tensor(out=cfi[:], in0=ni[:], in1=den[:], op=ALU.mult))
                cfr_b = fv(cfr[:], [[1, 32], [0, 16]]); cfi_b = fv(cfi[:], [[1, 32], [0, 16]])
                D_(lambda e: e.tensor_tensor(out=tmpA[:], in0=bre[:], in1=cfr_b, op=ALU.mult))
                D_(lambda e: e.tensor_tensor(out=tmpB[:], in0=bim[:], in1=cfi_b, op=ALU.mult))
                D_(lambda e: e.tensor_tensor(out=bbr[:], in0=tmpA[:], in1=tmpB[:], op=ALU.subtract))
                D_(lambda e: e.tensor_tensor(out=tmpA[:], in0=bim[:], in1=cfr_b, op=ALU.mult))
                D_(lambda e: e.tensor_tensor(out=tmpB[:], in0=bre[:], in1=cfi_b, op=ALU.mult))
                D_(lambda e: e.tensor_tensor(out=bbi[:], in0=tmpA[:], in1=tmpB[:], op=ALU.add))
                D_(lambda e: e.tensor_copy(out=AA1[:, 0, :], in_=PPr[:, :, 16]))
                D_(lambda e: e.tensor_copy(out=AA1[:, 1, :], in_=PPr[:, :, 16]))
                D_(lambda e: e.tensor_scalar(out=AA2[:, 0, :], in0=PPi[:, :, 16], scalar1=-1.0, scalar2=None, op0=ALU.mult))
                D_(lambda e: e.tensor_copy(out=AA2[:, 1, :], in_=PPi[:, :, 16]))

                if upto == "D0":
                    B.barrier()
                    return
                X = [T([128, 16, 16]) for _ in range(4)]
                xb_ = Buf()
                Bx = [{"Bexp": B.sb([128, 2, 512], BF16, st1), "Bfull": B.sb([128, 2, 512], BF16, st1), "BblkT": B.sb([128, 8, 128], BF16, st1),
                       "bxb": Buf(), "bfb": Buf(), "btb": Buf()} for _ in range(2)]
                CexpB = [B.sb([128, 2, 512], BF16, st1) for _ in range(4)]
                cxb = [Buf() for _ in range(4)]
                cxd = [B.ds() for _ in range(4)]
                DblkB = [B.sb([128, 4, 512], BF16, st1) for _ in range(4)]
                dkb = [Buf() for _ in range(4)]
                Vp = [B.sb([128, 4, 256], BF16, st1) for _ in range(4)]
                vpb = [Buf() for _ in range(4)]
                uTn = (B.sb([128, S], BF16, st1), Buf(), B.ds())
                uTs = [(B.sb([128, 16, 256], BF16, st1), Buf()) for _ in range(2)]
                Yqs = [(B.sb([128, 16, 256], BF16, st1), Buf(), B.ds()) for _ in range(1)]
                psT = B.ps([128, 8, 128], BF16, st1); psTb = Buf()
                psD = [(B.ps([128, 512], F32, st1), Buf()) for _ in range(2)]
                psV = [(B.ps([128, 2, 256], F32, st1), Buf()) for _ in range(2)]
                psW = (B.ps([128, 2, 256], F32, st1), Buf())
                psY = [(B.ps([128, 512], F32, st1), Buf()) for _ in range(2)]
                for bx in Bx:
                    op("dve", lambda e, bx=bx: e.memset(bx["Bexp"][:], 0.0), writes=[bx["bxb"]])
                    op("dve", lambda e, bx=bx: e.memset(bx["Bfull"][:], 0.0), writes=[bx["bfb"]])
                for j in range(4):
                    op("dve", lambda e, j=j: e.memset(CexpB[j][:], 0.0), writes=[cxb[j]])

                def cmul(pr, pi_, qr, qi, outs_r, outs_i, neg_i, rbufs, wbufs):
                    op("dve", lambda e: e.tensor_tensor(out=X[0][:], in0=pr, in1=qr, op=ALU.mult), reads=rbufs, writes=[xb_])
                    op("dve", lambda e: e.tensor_tensor(out=X[1][:], in0=pi_, in1=qi, op=ALU.mult), reads=rbufs, writes=[xb_])
                    op("dve", lambda e: e.tensor_tensor(out=X[2][:], in0=pr, in1=qi, op=ALU.mult), reads=rbufs, writes=[xb_])
                    op("dve", lambda e: e.tensor_tensor(out=X[3][:], in0=pi_, in1=qr, op=ALU.mult), reads=rbufs, writes=[xb_])
                    for lo, hi, o in outs_r:
                        op("dve", lambda e, lo=lo, hi=hi, o=o: e.tensor_tensor(out=o, in0=X[0][lo:hi], in1=X[1][lo:hi], op=ALU.subtract), reads=[xb_], writes=wbufs)
                    for lo, hi, o in outs_i:
                        if neg_i:
                            op("dve", lambda e, lo=lo, hi=hi, o=o: e.scalar_tensor_tensor(out=o, in0=X[2][lo:hi], scalar=-1.0, in1=X[3][lo:hi], op0=ALU.mult, op1=ALU.subtract),
                               reads=[xb_], writes=wbufs)
                        else:
                            op("dve", lambda e, lo=lo, hi=hi, o=o: e.tensor_tensor(out=o, in0=X[2][lo:hi], in1=X[3][lo:hi], op=ALU.add), reads=[xb_], writes=wbufs)

                def blocked(tile, c):
                    v = tile[:].rearrange("p c (i g k) -> p c i g k", g=2, k=16)
                    return [(0, 64, v[0:64, c, :, 0, :]), (64, 128, v[64:128, c, :, 1, :])]

                def prep(m):
                    j = m % 4
                    bx = Bx[m % 2]
                    ppr = fv(PPr[:, m, 1:2], [[1, 16], [0, 16]]); ppi = fv(PPi[:, m, 1:2], [[1, 16], [0, 16]])
                    pnr = fv(PNr[:, m, 1:2], [[1, 16], [0, 16]]); pni = fv(PNi[:, m, 1:2], [[1, 16], [0, 16]])
                    prr = fv(PPr[:, m, 15:16], [[-1, 16], [0, 16]]); pri = fv(PPi[:, m, 15:16], [[-1, 16], [0, 16]])
                    c_r = fv(cre[:, m, 0:1], [[0, 16], [1, 16]]); c_i = fv(cim[:, m, 0:1], [[0, 16], [1, 16]])
                    b_r = fv(bbr[:, m, 0:1], [[0, 16], [1, 16]]); b_i = fv(bbi[:, m, 0:1], [[0, 16], [1, 16]])
                    cmul(pnr, pni, b_r, b_i, blocked(bx["Bexp"], 0), blocked(bx["Bexp"], 1), False, [tb2], [bx["bxb"]])
                    cmul(ppr, ppi, c_r, c_i, blocked(CexpB[j], 0), blocked(CexpB[j], 1), True, [tb2], [cxb[j]])
                    dma("sp", CXs.ap()[m], CexpB[j][:].rearrange("p c n -> p (c n)"), cxd[j], reads=[cxb[j]], writes=[db["CXs"]])

                def work(m):
                    q4, j = divmod(m, 4)
                    bx = Bx[m % 2]
                    uT, utb = uTs[q4 % 2]

                    def trB(e):
                        for c in range(2):
                            for it in range(4):
                                ins = e.transpose(out=psT[:, c * 4 + it, :], in_=bx["Bexp"][:, c, it * 128:(it + 1) * 128], identity=ident_b[:])
                        return ins
                    op("pe", trB, reads=[bx["bxb"], cbuf], writes=[psTb])
                    op("act", lambda e: e.copy(out=bx["BblkT"][:], in_=psT[:]), reads=[psTb], writes=[bx["btb"]])
                    for hb2 in range(2):
                        pv, pvb = psV[hb2]

                        def vmm(e, pv=pv, hb2=hb2):
                            for a_ in range(2):
                                it = hb2 * 2 + a_
                                for il in range(4):
                                    ins = e.matmul(pv[:, a_, :], lhsT=sel_b[32 * j:32 * j + 32, il, :], rhs=uT[32 * j:32 * j + 32, 4 * it + il, :],
                                                   start=(il == 0), stop=(il == 3), tile_position=(32 * j, 0))
                            return ins
                        op("pe", vmm, reads=[utb, tb2], writes=[pvb])
                        op("act", lambda e, pv=pv, hb2=hb2: e.copy(out=Vp[j][:, 2 * hb2:2 * hb2 + 2, :], in_=pv[:]), reads=[pvb], writes=[vpb[j]])
                    for it in range(4):
                        pd, pdb = psD[it % 2]

                        def dmm(e, pd=pd, it=it):
                            for c in range(2):
                                ins = e.matmul(pd[:], lhsT=bx["Bexp"][:, c, it * 128:(it + 1) * 128], rhs=CexpB[j][:, c, :], start=(c == 0), stop=(c == 1))
                            return ins
                        op("pe", dmm, reads=[bx["bxb"], cxb[j]], writes=[pdb])
                        op("act", lambda e, pd=pd, it=it: e.copy(out=DblkB[j][:, it, :], in_=pd[:]), reads=[pdb], writes=[dkb[j]])
                        op("pool", lambda e, it=it: e.tensor_tensor(out=DblkB[j][:, it, it * 128:(it + 1) * 128], in0=DblkB[j][:, it, it * 128:(it + 1) * 128], in1=maskd_b[:],
                                                                     op=ALU.mult), reads=[tb2], writes=[dkb[j]])
                    pw, pwb = psW

                    def wmm(e):
                        for c in range(2):
                            for it in range(4):
                                ins = e.matmul(pw[:, c, :], lhsT=bx["BblkT"][:, c * 4 + it, :], rhs=Vp[j][:, it, :], start=(it == 0), stop=(it == 3))
                        return ins
                    op("pe", wmm, reads=[bx["btb"], vpb[j]], writes=[pwb])
                    op("act", lambda e: e.activation(out=WW[:, 0, m, :], in_=pw[:, 0, :], func=AF.Copy, scale=AA1[:, 0, m:m + 1]), reads=[pwb, tb_], writes=[wwb])
                    op("act", lambda e: e.activation(out=WW[:, 1, m, :], in_=pw[:, 1, :], func=AF.Copy, scale=AA1[:, 0, m:m + 1]), reads=[pwb, tb_], writes=[wwb])
                    op("dve", lambda e: e.scalar_tensor_tensor(out=WW[:, 0, m, :], in0=pw[:, 1, :], scalar=AA2[:, 0, m:m + 1], in1=WW[:, 0, m, :], op0=ALU.mult, op1=ALU.add),
                       reads=[pwb, tb_, wwb], writes=[wwb])
                    op("dve", lambda e: e.scalar_tensor_tensor(out=WW[:, 1, m, :], in0=pw[:, 0, :], scalar=AA2[:, 1, m:m + 1], in1=WW[:, 1, m, :], op0=ALU.mult, op1=ALU.add),
                       reads=[pwb, tb_, wwb], writes=[wwb])

                def yintra(q4):
                    uT, utb = uTs[q4 % 2]
                    Yq, yqb, yqd = Yqs[0]
                    for i in range(16):
                        py = psY[i % 2][0][:, 0:256]
                        pyb = psY[i % 2][1]

                        def ymm(e, py=py, i=i):
                            for it in range(i // 4 + 1):
                                for j in range(4):
                                    ins = e.matmul(py[32 * j:32 * j + 32, :], lhsT=DblkB[j][:, it, i * 32:(i + 1) * 32], rhs=Vp[j][:, it, :],
                                                   start=(it == 0), stop=(it == i // 4), tile_position=(0, 32 * j))
                            return ins
                        op("pe", ymm, reads=dkb + vpb, writes=[pyb])
                        op("dve", lambda e, py=py, i=i: e.scalar_tensor_tensor(out=Yq[:, i, :], in0=uT[:, i, :], scalar=sd[:, q4:q4 + 1], in1=py,
                                                                                op0=ALU.mult, op1=ALU.add), reads=[pyb, utb, tb_], writes=[yqb])
                    dma("sp", YIs.ap()[q4 * 128:(q4 + 1) * 128, :], Yq[:].rearrange("p i c -> p (i c)"), yqd, reads=[yqb], writes=[db["YIs"]])

                def load_u(q4):
                    un, unb, und = uTn
                    uT, utb = uTs[q4 % 2]
                    dma("sp", un[:], UTs.ap()[q4 * 128:(q4 + 1) * 128, :], und, reads=[db["UTs"]], writes=[unb])
                    unv = un[:].rearrange("p (c i) -> p i c", i=16)
                    for ih in range(2):
                        op("act", lambda e, ih=ih: e.copy(out=uT[:, ih * 8:(ih + 1) * 8, :], in_=unv[:, ih * 8:(ih + 1) * 8, :]), reads=[unb], writes=[utb])

                load_u(0)
                prep(0)
                for m in range(32):
                    q4, j = divmod(m, 4)
                    if j == 0 and q4 + 1 < 8:
                        load_u(q4 + 1)
                    if m + 1 < 32:
                        prep(m + 1)
                    work(m)
                    if j == 3:
                        yintra(q4)
                B.barrier()

    def rec_gen():
        WW, wwb, AA1, AA2, tb_ = Dst["WW"], Dst["wwb"], Dst["AA1"], Dst["AA2"], Dst["tb_"]
        r1, r2 = Dst["r1"], Dst["r2"]
        rb = Buf()
        for c in range(1, 256):
            prev = WW[:, :, :, c - 1]
            cur = WW[:, :, :, c]
            prev_sw = fv(WW[:, 1:2, 0:1, c - 1:c], [[-32 * 256, 2], [256, 32]])
            op("dve", lambda e, prev=prev: e.tensor_tensor(out=r1[:], in0=prev, in1=AA1[:], op=ALU.mult), reads=[wwb, tb_], writes=[rb])
            op("dve", lambda e, prev_sw=prev_sw: e.tensor_tensor(out=r2[:], in0=prev_sw, in1=AA2[:], op=ALU.mult), reads=[wwb, tb_], writes=[rb])
            op("dve", lambda e: e.tensor_tensor(out=r1[:], in0=r1[:], in1=r2[:], op=ALU.add), reads=[rb], writes=[rb])
            op("dve", lambda e, cur=cur: e.tensor_tensor(out=cur, in0=cur, in1=r1[:], op=ALU.add), reads=[rb, wwb], writes=[wwb])
            yield

    def phase_D2():
        WW, wwb = Dst["WW"], Dst["wwb"]
        if True:
            with ExitStack() as st1:
                Hb = B.sb([128, 2, 32, 256], BF16, st1); hbb = Buf()
                op("dve", lambda e: e.memset(Hb[:, :, :, 0:1], 0.0), writes=[hbb])
                for c in range(2):
                    op("dve" if c == 0 else "act", lambda e, c=c: (e.tensor_copy(out=Hb[:, c, :, 1:256], in_=WW[:, c, :, 0:255]) if c == 0
                                                                    else e.copy(out=Hb[:, c, :, 1:256], in_=WW[:, c, :, 0:255])), reads=[wwb], writes=[hbb])
                Cx = [(B.sb([128, 2, 512], BF16, st1), Buf(), B.ds()) for _ in range(8)]
                Yin = [(B.sb([128, 16, 256], BF16, st1), Buf(), B.ds()) for _ in range(2)]
                Yf = [(B.sb([128, 16, 256], F32, st1), Buf()) for _ in range(2)]
                zT = [(B.sb([128, S], BF16, st1), Buf(), B.ds()) for _ in range(2)]
                psY2 = [(B.ps([128, 512], F32, st1), Buf()) for _ in range(4)]
                npy = 0
                for q4 in range(8):
                    yi, yib, yid = Yin[q4 % 2]
                    yf, yfb = Yf[q4 % 2]
                    z_, zb, zd = zT[q4 % 2]
                    dma("sp", yi[:].rearrange("p i c -> p (i c)"), YIs.ap()[q4 * 128:(q4 + 1) * 128, :], yid, reads=[db["YIs"]], writes=[yib])
                    cxs = []
                    for j in range(4):
                        ct, ctb, ctd = Cx[(q4 % 2) * 4 + j]
                        dma("sp", ct[:].rearrange("p c n -> p (c n)"), CXs.ap()[q4 * 4 + j], ctd, reads=[db["CXs"]], writes=[ctb])
                        cxs.append((ct, ctb))
                    for i in range(16):
                        py, pyb = psY2[npy % 4]; npy += 1

                        def ymm(e, py=py, i=i, cxs=cxs, q4=q4):
                            for c in range(2):
                                for j in range(4):
                                    ins = e.matmul(py[32 * j:32 * j + 32, 0:256], lhsT=cxs[j][0][:, c, i * 32:(i + 1) * 32], rhs=Hb[:, c, q4 * 4 + j, :],
                                                   start=(c == 0), stop=(c == 1), tile_position=(0, 32 * j))
                            return ins
                        op("pe", ymm, reads=[hbb] + [c_[1] for c_ in cxs], writes=[pyb])
                        op("dve", lambda e, py=py, i=i, yi=yi, yf=yf: e.tensor_tensor(out=yf[:, i, :], in0=py[:, 0:256], in1=yi[:, i, :], op=ALU.add),
                           reads=[pyb, yib], writes=[yfb])
                    op("act", lambda e, z_=z_, yf=yf: e.activation(out=z_[:].rearrange("p (c i) -> p c i", i=16), in_=yf[:].rearrange("p i c -> p c i"), func=AF.Gelu_apprx_tanh),
                       reads=[yfb], writes=[zb])
                    dma("sp", ZTs.ap()[q4 * 128:(q4 + 1) * 128, :], z_[:], zd, reads=[zb], writes=[db["ZTs"]])
                B.barrier()
        Dst["st"].close()
    def phase_T1():
        with ExitStack() as st:
            W = {}
            WB = {}
            for nm in ("glu_w", "w_br_attn", "w_br_ssm", "w_br_xattn", "w_out"):
                W[nm] = B.sb([128, 8, D], BF16, st, nm)
                WB[nm] = Buf()
                wd = B.ds()
                for n in range(2):
                    dma("pool", W[nm][:, :, n * 512:(n + 1) * 512], I[nm][:, n * 512:(n + 1) * 512].rearrange("(k p) n -> p k n", p=128), wd, writes=[WB[nm]])
            glub = B.sb([128, 8], F32, st)
            wb_ = Buf()
            dma("sp", glub[:], I["glu_b"], B.ds(), writes=[wb_])
            zt = (B.sb([128, 8, 512], BF16, st), Buf(), B.ds())
            ya = (B.sb([128, 8, 512], BF16, st), Buf(), B.ds())
            yx = (B.sb([128, 8, 512], BF16, st), Buf(), B.ds())
            gt = (B.sb([128, 24, 512], BF16, st), Buf(), B.ds())
            yssm = (B.sb([128, 8, 512], BF16, st), Buf())
            mixT = (B.sb([128, 8, 512], BF16, st), Buf())
            sig = [(B.sb([128, 512], F32, st), Buf()) for _ in range(2)]
            mm_ = [(B.sb([128, 3, 512], F32, st), Buf()) for _ in range(2)]
            xs = [(B.sb([128, D], F32, st), Buf(), B.ds()) for _ in range(2)]
            x1 = [(B.sb([128, D], F32, st), Buf(), B.ds()) for _ in range(2)]
            pG = [(B.ps([128, 512], F32, st), Buf()) for _ in range(2)]
            pB = [(B.ps([128, 512], F32, st), Buf()) for _ in range(3)]
            pO = [(B.ps([128, 512], F32, st), Buf()) for _ in range(2)]
            ng = 0; nx = 0; no = 0
            def load_z(tb):
                tsl = slice(tb * 512, (tb + 1) * 512)
                dma("act", zt[0][:], ZTs.ap()[:, tsl].rearrange("(k p) t -> p k t", p=128), zt[2], reads=[db["ZTs"]], writes=[zt[1]])

            def load_rest(tb):
                tsl = slice(tb * 512, (tb + 1) * 512)
                dma("act", ya[0][:], YAs.ap()[:, tsl].rearrange("(k p) t -> p k t", p=128), ya[2], reads=[db["YAs"]], writes=[ya[1]])
                dma("act", yx[0][:], YXs.ap()[:, tsl].rearrange("(k p) t -> p k t", p=128), yx[2], reads=[db["YXs"]], writes=[yx[1]])
                dma("act", gt[0][:], GTs.ap()[:, tsl].rearrange("(k p) t -> p k t", p=128), gt[2], reads=[db["GTs"]], writes=[gt[1]])

            load_z(0)
            load_rest(0)
            for tb in range(NTB):
                for ct in range(8):
                    pg, pgb = pG[ng % 2]
                    sg, sgb = sig[ng % 2]; ng += 1

                    def gmm(e, pg=pg, ct=ct):
                        for k in range(8):
                            ins = e.matmul(pg[:], lhsT=W["glu_w"][:, k, ct * 128:(ct + 1) * 128], rhs=zt[0][:, k, :], start=(k == 0), stop=(k == 7))
                        return ins
                    op("pe", gmm, reads=[WB["glu_w"], zt[1]], writes=[pgb])
                    op("act", lambda e, sg=sg, pg=pg, ct=ct: e.activation(out=sg[:], in_=pg[:], func=AF.Sigmoid, bias=glub[:, ct:ct + 1], scale=1.0),
                       reads=[pgb, wb_], writes=[sgb])
                    op("dve", lambda e, sg=sg, ct=ct: e.tensor_tensor(out=yssm[0][:, ct, :], in0=zt[0][:, ct, :], in1=sg[:], op=ALU.mult),
                       reads=[sgb, zt[1]], writes=[yssm[1]])
                if tb + 1 < NTB:
                    load_z(tb + 1)
                for ct in range(8):
                    srcs = ((W["w_br_attn"], ya[0], ya[1], WB["w_br_attn"]), (W["w_br_ssm"], yssm[0], yssm[1], WB["w_br_ssm"]),
                            (W["w_br_xattn"], yx[0], yx[1], WB["w_br_xattn"]))
                    m3, m3b = mm_[ct % 2]
                    for bi, (w_, y_, yb_, wbf) in enumerate(srcs):
                        pb_, pbb = pB[bi]

                        def bmm(e, pb_=pb_, w_=w_, y_=y_, ct=ct):
                            for k in range(8):
                                ins = e.matmul(pb_[:], lhsT=w_[:, k, ct * 128:(ct + 1) * 128], rhs=y_[:, k, :], start=(k == 0), stop=(k == 7))
                            return ins
                        op("pe", bmm, reads=[wbf, yb_], writes=[pbb])
                        op("dve", lambda e, m3=m3, pb_=pb_, bi=bi, ct=ct: e.tensor_tensor(out=m3[:, bi, :], in0=pb_[:], in1=gt[0][:, bi * 8 + ct, :], op=ALU.mult),
                           reads=[pbb, gt[1]], writes=[m3b])
                    op("dve", lambda e, m3=m3: e.tensor_tensor(out=m3[:, 0, :], in0=m3[:, 0, :], in1=m3[:, 1, :], op=ALU.add), reads=[m3b], writes=[m3b])
                    op("dve", lambda e, m3=m3, ct=ct: e.tensor_tensor(out=mixT[0][:, ct, :], in0=m3[:, 0, :], in1=m3[:, 2, :], op=ALU.add), reads=[m3b], writes=[mixT[1]])
                if tb + 1 < NTB:
                    load_rest(tb + 1)
                for ts in range(4):
                    xt, xb, xd = xs[nx % 2]
                    x1t, x1b, x1d = x1[nx % 2]; nx += 1
                    r0 = tb * 512 + ts * 128
                    dma("sp", xt[:], I["x"][r0:r0 + 128, :], xd, writes=[xb])
                    for half in range(2):
                        po, pob = pO[no % 2]; no += 1

                        def omm(e, po=po, ts=ts, half=half):
                            for k in range(8):
                                ins = e.matmul(po[:], lhsT=mixT[0][:, k, ts * 128:(ts + 1) * 128], rhs=W["w_out"][:, k, half * 512:(half + 1) * 512], start=(k == 0), stop=(k == 7))
                            return ins
                        op("pe", omm, reads=[WB["w_out"], mixT[1]], writes=[pob])
                        op("dve", lambda e, po=po, half=half, xt=xt, x1t=x1t: e.tensor_tensor(out=x1t[:, half * 512:(half + 1) * 512], in0=po[:], in1=xt[:, half * 512:(half + 1) * 512], op=ALU.add),
                           reads=[pob, xb], writes=[x1b])
                    dma("sp", X1s.ap()[r0:r0 + 128, :], x1t[:], x1d, reads=[x1b], writes=[db["X1s"]])
            B.barrier()

    def phase_T2():
        TB = 256
        with ExitStack() as st:
            wfi = B.sb([128, 8, 2 * FF], BF16, st, "wfi")
            wfo = B.sb([128, NH, D], BF16, st, "wfo")
            wgb = [Buf() for _ in range(6)]
            for k in range(6):
                wd = B.ds()
                for base in (0, FF):
                    c0 = base + 512 * k
                    c1 = min(base + 512 * (k + 1), base + FF)
                    dma("pool", wfi[:, :, c0:c1], I["w_ffn_in"][:, c0:c1].rearrange("(k p) n -> p k n", p=128), wd, writes=[wgb[k]])
            wb_ = Buf(); wd = B.ds()
            for n in range(2):
                dma("pool", wfo[:, :, n * 512:(n + 1) * 512], I["w_ffn_out"][:, n * 512:(n + 1) * 512].rearrange("(k p) n -> p k n", p=128), wd, writes=[wb_])
            psT = B.ps([128, 8, 128], BF16, st); psTb = Buf()
            pG = [(B.ps([128, 512], F32, st), Buf()) for _ in range(2)]
            pU = [(B.ps([128, 512], F32, st), Buf()) for _ in range(2)]
            pO = [(B.ps([128, 512], F32, st), Buf()) for _ in range(2)]
            x1t = [(B.sb([128, D], F32, st), Buf(), B.ds()) for _ in range(4)]
            sls = [rms_slot(st) for _ in range(2)]
            h2T = [(B.sb([128, 8, TB], BF16, st), Buf()) for _ in range(2)]
            aT = [(B.sb([128, NH, TB], BF16, st), Buf()) for _ in range(1)]
            sg = [(B.sb([128, TB], F32, st), Buf()) for _ in range(2)]
            x2 = [(B.sb([128, D], F32, st), Buf()) for _ in range(2)]
            fs = [{"ss": B.sb([128, 1], F32, st), "rs": B.sb([128, 1], F32, st), "rr": B.sb([128, 1], F32, st), "b": Buf()} for _ in range(2)]
            ot = [(B.sb([128, D], F32, st), Buf(), B.ds()) for _ in range(1)]
            cnt = {"nx": 0, "ng": 0, "no": 0, "nf": 0}
            nsub = TB // 128
            NTB2 = S // TB

            def norm_in(tb):
                hT_, hTb_ = h2T[tb % 2]
                xts = []
                for ts in range(nsub):
                    xt, xb, xd = x1t[cnt["nx"] % 4]
                    sl = sls[cnt["nx"] % 2]; cnt["nx"] += 1
                    r0 = tb * TB + ts * 128
                    dma("sp", xt[:], X1s.ap()[r0:r0 + 128, :], xd, reads=[db["X1s"]], writes=[xb])
                    rms_T(xt[:], xb, 2, hT_[:, :, ts * 128:(ts + 1) * 128], hTb_, sl, psT, psTb, evac=("dve" if ts % 2 == 0 else "act"))
                    xts.append((xt, xb, r0))
                return xts

            def ffn_in(tb):
                hT_, hTb_ = h2T[tb % 2]
                a_, ab_ = aT[0]
                for ht in range(NH):
                    pg, pgb = pG[cnt["ng"] % 2]
                    pu, pub = pU[cnt["ng"] % 2]
                    s_, sb_ = sg[cnt["ng"] % 2]; cnt["ng"] += 1

                    def gm(e, pg=pg, ht=ht):
                        for k in range(8):
                            ins = e.matmul(pg[:, 0:TB], lhsT=wfi[:, k, ht * 128:(ht + 1) * 128], rhs=hT_[:, k, :], start=(k == 0), stop=(k == 7))
                        return ins

                    def um(e, pu=pu, ht=ht):
                        for k in range(8):
                            ins = e.matmul(pu[:, 0:TB], lhsT=wfi[:, k, FF + ht * 128:FF + (ht + 1) * 128], rhs=hT_[:, k, :], start=(k == 0), stop=(k == 7))
                        return ins
                    op("pe", gm, reads=[wgb[ht // 4], hTb_], writes=[pgb])
                    op("pe", um, reads=[wgb[ht // 4], hTb_], writes=[pub])
                    op("act", lambda e, s_=s_, pg=pg: e.activation(out=s_[:], in_=pg[:, 0:TB], func=AF.Silu), reads=[pgb], writes=[sb_])
                    op("dve", lambda e, s_=s_, pu=pu, ht=ht: e.tensor_tensor(out=a_[:, ht, :], in0=pu[:, 0:TB], in1=s_[:], op=ALU.mult), reads=[pub, sb_], writes=[ab_])

            def ffn_out(tb, xts):
                a_, ab_ = aT[0]
                for ts in range(nsub):
                    xt, xb, r0 = xts[ts]
                    x2t, x2b = x2[cnt["nf"] % 2]
                    f = fs[cnt["nf"] % 2]
                    o_, ob_, od_ = ot[0]; cnt["nf"] += 1
                    for half in range(2):
                        po, pob = pO[cnt["no"] % 2]; cnt["no"] += 1

                        def om(e, po=po, ts=ts, half=half):
                            for k in range(NH):
                                ins = e.matmul(po[:], lhsT=a_[:, k, ts * 128:(ts + 1) * 128], rhs=wfo[:, k, half * 512:(half + 1) * 512], start=(k == 0), stop=(k == NH - 1))
                            return ins
                        op("pe", om, reads=[wb_, ab_], writes=[pob])
                        op("dve", lambda e, po=po, half=half, xt=xt, x2t=x2t: e.tensor_tensor(out=x2t[:, half * 512:(half + 1) * 512], in0=po[:], in1=xt[:, half * 512:(half + 1) * 512], op=ALU.add),
                           reads=[pob, xb], writes=[x2b])
                    op("act", lambda e, f=f, x2t=x2t, o_=o_: e.activation(out=o_[:], in_=x2t[:], func=AF.Square, accum_out=f["ss"][:]), reads=[x2b], writes=[f["b"], ob_])
                    op("pool", lambda e, f=f: e.tensor_scalar(out=f["rs"][:], in0=f["ss"][:], scalar1=1.0 / D, scalar2=1e-6, op0=ALU.mult, op1=ALU.add), reads=[f["b"]], writes=[f["b"]])
                    op("pool", lambda e, f=f: e.tensor_tensor(out=f["rr"][:], in0=f["rs"][:], in1=mhalf[:], op=ALU.pow), reads=[f["b"], cbuf], writes=[f["b"]])
                    op("dve", lambda e, f=f, x2t=x2t, o_=o_: e.scalar_tensor_tensor(out=o_[:], in0=x2t[:], scalar=f["rr"][:], in1=gains[:, 3, :], op0=ALU.mult, op1=ALU.mult),
                       reads=[x2b, f["b"], cbuf], writes=[ob_])
                    dma("sp", out_d[r0:r0 + 128, :], o_[:], od_, reads=[ob_], writes=[db["out"]])

            xts_cur = norm_in(0)
            for tb in range(NTB2):
                ffn_in(tb)
                xts_next = norm_in(tb + 1) if tb + 1 < NTB2 else None
                ffn_out(tb, xts_cur)
                xts_cur = xts_next
            B.barrier()

    if "AP" in phases:
        phase_AP(); B.barrier()
    if "B" in phases:
        phase_B(); B.barrier()
    gen = None
    if "D" in phases:
        phase_D1(); B.barrier()
        gen = rec_gen()
    if "C" in phases:
        phase_C(gen); B.barrier()
    if gen is not None:
        for _ in gen:
            pass
        B.barrier()
        phase_D2(); B.barrier()
    if "T1" in phases:
        phase_T1(); B.barrier()
    if "T2" in phases:
        phase_T2(); B.barrier()
    return nc, B


_CACHE = {}


def kernel(**inputs):
    consts = host_consts()
    in_maps = []
    for b in range(8):
        m = host_layout(inputs, b)
        m.update(consts)
        in_maps.append(m)
    if "nc" not in _CACHE:
        _CACHE["nc"] = build_program()[0]
    res = run_bass_kernel_spmd(_CACHE["nc"], in_maps, core_ids=list(range(8)))
    return np.stack([np.asarray(r["out"]) for r in res.results], axis=0).astype(np.float32)
```

```python
import math
from contextlib import ExitStack

import numpy as np

import concourse.bass as bass
import concourse.mybir as mybir
from concourse.bass_utils import run_bass_kernel_spmd

F32 = mybir.dt.float32
BF16 = mybir.dt.bfloat16
AF = mybir.ActivationFunctionType
ALU = mybir.AluOpType
AX = mybir.AxisListType

S = 4096
D = 1024
NTT = 32
NTB = 8
FF = 2816
NH = 22
NEG = -30000.0
LAM_INIT = 0.8 - 0.6 * math.exp(0.0)


class Buf:
    __slots__ = ("w", "r")

    def __init__(self):
        self.w = None
        self.r = {}


class Eng:
    def __init__(self, obj, sem, name):
        self.obj = obj
        self.sem = sem
        self.count = 0
        self.seen = {}
        self.name = name


class DS:
    def __init__(self, sem):
        self.sem = sem
        self.count = 0


class Builder:
    def __init__(self, nc, debug=False):
        self.nc = nc
        self.debug = debug
        self.es = ExitStack()
        self.E = {}
        for name, obj in (("pe", nc.tensor), ("act", nc.scalar), ("dve", nc.vector), ("pool", nc.gpsimd), ("sp", nc.sync)):
            self.E[name] = Eng(obj, self.es.enter_context(nc.semaphore("sem_" + name)), name)
        self.all_ds = []
        self.nname = 0

    def sb(self, shape, dt, stack=None, name=None):
        self.nname += 1
        return (stack or self.es).enter_context(self.nc.sbuf_tensor("%s_%d" % (name or "t", self.nname), list(shape), dt))

    def ps(self, shape, dt, stack=None, name=None):
        self.nname += 1
        return (stack or self.es).enter_context(self.nc.psum_tensor("%s_%d" % (name or "p", self.nname), list(shape), dt))

    def ds(self):
        self.nname += 1
        d = DS(self.es.enter_context(self.nc.semaphore("ds_%d" % self.nname)))
        self.all_ds.append(d)
        return d

    def _wait(self, E, ev):
        if ev is None:
            return
        sem, val = ev
        k = id(sem)
        if E.seen.get(k, 0) >= val:
            return
        E.obj.wait_ge(sem, val)
        E.seen[k] = val

    def _deps(self, E, reads, writes):
        own = E.sem
        pe = E.name == "pe"
        for b in reads:
            if b.w is not None and not (pe and b.w[0] is own):
                self._wait(E, b.w)
        for b in writes:
            if b.w is not None and not (pe and b.w[0] is own):
                self._wait(E, b.w)
            for ev in b.r.values():
                if not (pe and ev[0] is own):
                    self._wait(E, ev)

    def op(self, e, fn, reads=(), writes=()):
        E = self.E[e]
        self._deps(E, reads, writes)
        ins = fn(E.obj)
        E.count += 1
        ins.then_inc(E.sem, 1)
        ev = (E.sem, E.count)
        for b in reads:
            b.r[id(E.sem)] = ev
        for b in writes:
            b.w = ev
            b.r = {}

    def dma(self, q, out, in_, ds, reads=(), writes=()):
        E = self.E[q]
        self._deps(E, reads, writes)
        ins = E.obj.dma_start(out=out, in_=in_)
        ds.count += 16
        ins.then_inc(ds.sem, 16)
        ev = (ds.sem, ds.count)
        for b in reads:
            b.r[id(ds.sem)] = ev
        for b in writes:
            b.w = ev
            b.r = {}

    def barrier(self):
        for E in self.E.values():
            for Fo in self.E.values():
                if Fo.count > 0 and not (Fo is E and E.name == "pe"):
                    self._wait(E, (Fo.sem, Fo.count))
            for d in self.all_ds:
                if d.count > 0:
                    self._wait(E, (d.sem, d.count))


def _t5_bucket(rel):
    half, max_exact = 16, 8
    ret = np.where(rel > 0, half, 0)
    n = np.abs(rel)
    nf = np.maximum(n, 1).astype(np.float32)
    large = max_exact + (np.log(nf / np.float32(max_exact)) / np.float32(math.log(256 / max_exact)) * np.float32(half - max_exact)).astype(np.int32)
    large = np.minimum(large, half - 1)
    return ret + np.where(n < max_exact, n, large)


def host_consts():
    c = {}
    c["ident"] = np.eye(128, dtype=np.float32)
    c["antiid"] = np.eye(128, dtype=np.float32)[::-1].copy()
    ii = np.arange(1280)
    b = _t5_bucket(511 - ii)
    oh = np.zeros((32, 1280), np.float32)
    oh[b, ii] = 1.0
    c["onehot"] = oh
    p = np.arange(128)[:, None]
    j = np.arange(1152)[None, :]
    c["maskmb"] = np.where((p // 64) <= np.floor_divide(j - 384, 64), 0.0, NEG).astype(np.float32)
    r = np.arange(128)[:, None, None]
    it = np.arange(4)[None, :, None]
    col = np.arange(512)[None, None, :]
    c["maskd"] = ((col[:, 0, 0:128] // 32) >= (r[:, 0, :] // 32)).astype(np.float32)
    sel = np.zeros((128, 4, 128), np.float32)
    for rr in range(128):
        for il in range(4):
            sel[rr, il, 32 * il + rr % 32] = 1.0
    c["sel"] = sel
    return c


def host_layout(inp, b):
    f = np.float32
    m = {}
    m["x"] = np.ascontiguousarray(inp["x"][b])
    m["mem"] = np.ascontiguousarray(inp["mem"][b])
    for k in ("w_in", "glu_w", "w_mem_kv", "w_br_attn", "w_br_ssm", "w_br_xattn", "w_out", "w_ffn_in", "w_ffn_out"):
        m[k] = np.ascontiguousarray(inp[k][0])
    gb = np.stack([np.broadcast_to(inp["norm1_g"][0], (128, D)), np.broadcast_to(inp["mem_norm_g"][0], (128, D)),
                   np.broadcast_to(inp["norm2_g"][0], (128, D)), np.broadcast_to(inp["final_g"], (128, D))], axis=1)
    m["gains"] = np.ascontiguousarray(gb, dtype=f)
    m["subg"] = np.ascontiguousarray(np.broadcast_to(inp["da_subln_g"][0], (128, 128)), dtype=f)
    lqk = np.stack([inp["da_lq1"][0], inp["da_lk1"][0], inp["da_lq2"][0], inp["da_lk2"][0]], axis=0)
    m["lqk"] = np.ascontiguousarray(np.broadcast_to(lqk, (128, 4, 64)), dtype=f)
    m["relb"] = np.ascontiguousarray(inp["rel_bias"], dtype=f)
    m["relb15"] = np.ascontiguousarray(np.broadcast_to(inp["rel_bias"][15], (128, 8)), dtype=f)

    def st(a):
        return np.ascontiguousarray(a.reshape(32, 2, 64).transpose(1, 2, 0).reshape(128, 32), dtype=f)
    m["s_are"] = st(inp["ssm_a_re"][0])
    m["s_aim"] = st(inp["ssm_a_im"][0])
    m["s_ldt"] = st(np.broadcast_to(inp["ssm_log_dt"][0][:, None], (64, 64)))
    def stb(a):
        return np.ascontiguousarray(a.reshape(32, 2, 64, 16).transpose(1, 2, 0, 3).reshape(128, 32, 16), dtype=f)
    m["s_bre"] = stb(inp["ssm_b_re"][0])
    m["s_bim"] = stb(inp["ssm_b_im"][0])
    def stc(a):
        return np.ascontiguousarray(a.reshape(32, 2, 16, 64).transpose(1, 3, 0, 2).reshape(128, 32, 16), dtype=f)
    m["s_cre"] = stc(inp["ssm_c_re"][0])
    m["s_cim"] = stc(inp["ssm_c_im"][0])
    m["s_d"] = np.ascontiguousarray(inp["ssm_d"][0].reshape(8, 128).T, dtype=f)
    m["glu_b"] = np.ascontiguousarray(inp["glu_b"][0].reshape(8, 128).T, dtype=f)
    return m


INPUT_SHAPES = {
    "x": [S, D], "mem": [256, D], "w_in": [D, 8192], "glu_w": [D, D], "w_mem_kv": [D, 2048], "w_br_attn": [D, D],
    "w_br_ssm": [D, D], "w_br_xattn": [D, D], "w_out": [D, D], "w_ffn_in": [D, 2 * FF], "w_ffn_out": [FF, D],
    "gains": [128, 4, D], "subg": [128, 128], "lqk": [128, 4, 64], "relb": [32, 8], "relb15": [128, 8],
    "s_are": [128, 32], "s_aim": [128, 32], "s_ldt": [128, 32], "s_bre": [128, 32, 16], "s_bim": [128, 32, 16],
    "s_cre": [128, 32, 16], "s_cim": [128, 32, 16], "s_d": [128, 8], "glu_b": [128, 8],
    "ident": [128, 128], "antiid": [128, 128], "onehot": [32, 1280], "maskmb": [128, 1152], "maskd": [128, 128],
    "sel": [128, 4, 128],
}


ALL_PHASES = ("AP", "B", "C", "D", "T1", "T2")


def build_program(debug=False, upto="all", phases=ALL_PHASES):
    nc = bass.Bass("TRN2", target_bir_lowering=False)
    B = Builder(nc, debug)
    I = {k: nc.dram_tensor(k, shp, F32, kind="ExternalInput").ap() for k, shp in INPUT_SHAPES.items()}
    out_d = nc.dram_tensor("out", [S, D], F32, kind="ExternalOutput").ap()
    def scratch(name, shape, dt, producer=None):
        if producer is not None and producer not in phases:
            kind = "ExternalInput"
        else:
            kind = "ExternalOutput" if debug else "Internal"
        return nc.dram_tensor(name, shape, dt, kind=kind)

    QTs = scratch("QTs", [D, S], BF16, "AP")
    KTs = scratch("KTs", [D, S], BF16, "AP")
    Vs = scratch("Vs", [S, D], BF16, "AP")
    UTs = scratch("UTs", [D, S], BF16, "AP")
    XQs = scratch("XQs", [D, S], BF16, "AP")
    GTs = scratch("GTs", [3 * D, S], BF16, "AP")
    YAs = scratch("YAs", [D, S], BF16, "B")
    YXs = scratch("YXs", [D, S], BF16, "C")
    ZTs = scratch("ZTs", [D, S], BF16, "D")
    YIs = scratch("YIs", [D, S], BF16)
    CXs = scratch("CXs", [32, 128, 1024], BF16)
    X1s = scratch("X1s", [S, D], F32, "T1")
    Gd = scratch("Gd", [8, 1280], F32)
    db = {k: Buf() for k in ("QTs", "KTs", "Vs", "UTs", "XQs", "GTs", "YAs", "YXs", "ZTs", "YIs", "CXs", "X1s", "Gd", "out")}

    op, dma = B.op, B.dma
    es = B.es

    def fv(apobj, dims):
        return bass.AP(apobj.tensor, apobj.offset, [list(apobj.ap[0])] + [list(d) for d in dims])

    ident_f = B.sb([128, 128], F32); ident_b = B.sb([128, 128], BF16)
    gains = B.sb([128, 4, D], F32)
    eps_t = B.sb([128, 1], F32)
    eps2_t = B.sb([128, 1], F32)
    cbuf = Buf()
    cds = B.ds()
    dma("sp", ident_f[:], I["ident"], cds, writes=[cbuf])
    dma("sp", gains[:], I["gains"], cds, writes=[cbuf])
    op("dve", lambda e: e.tensor_copy(out=ident_b[:], in_=ident_f[:]), reads=[cbuf], writes=[cbuf])
    op("dve", lambda e: e.memset(eps_t[:], 1e-6), writes=[cbuf])
    op("dve", lambda e: e.memset(eps2_t[:], 1e-6 / 0.64), writes=[cbuf])
    mhalf = B.sb([128, 1], F32)
    op("dve", lambda e: e.memset(mhalf[:], -0.5), writes=[cbuf])
    mhalf4 = B.sb([128, 4], F32)
    op("dve", lambda e: e.memset(mhalf4[:], -0.5), writes=[cbuf])

    def rms_T(src, src_buf, gidx, dstT, dst_buf, sl, psT, psT_buf, evac="dve"):
        op("act", lambda e: e.activation(out=sl["hb"][:], in_=src, func=AF.Square, accum_out=sl["ss"][:]),
           reads=[src_buf], writes=[sl["hbb"], sl["ssb"]])
        op("pool", lambda e: e.tensor_scalar(out=sl["rs"][:], in0=sl["ss"][:], scalar1=1.0 / D, scalar2=1e-6, op0=ALU.mult, op1=ALU.add),
           reads=[sl["ssb"]], writes=[sl["rsb"]])
        op("pool", lambda e: e.tensor_tensor(out=sl["rr"][:], in0=sl["rs"][:], in1=mhalf[:], op=ALU.pow), reads=[sl["rsb"], cbuf], writes=[sl["rrb"]])
        op("dve", lambda e: e.scalar_tensor_tensor(out=sl["hb"][:], in0=src, scalar=sl["rr"][:], in1=gains[:, gidx, :],
                                                   op0=ALU.mult, op1=ALU.mult),
           reads=[src_buf, sl["rrb"], cbuf], writes=[sl["hbb"]])

        def tr(e):
            for k in range(8):
                ins = e.transpose(out=psT[:, k, :], in_=sl["hb"][:, k * 128:(k + 1) * 128], identity=ident_b[:])
            return ins
        op("pe", tr, reads=[sl["hbb"], cbuf], writes=[psT_buf])
        op(evac, lambda e: (e.tensor_copy(out=dstT, in_=psT[:]) if evac == "dve" else e.copy(out=dstT, in_=psT[:])),
           reads=[psT_buf], writes=[dst_buf])

    def rms_slot(stack):
        return {"hb": B.sb([128, D], BF16, stack), "ss": B.sb([128, 1], F32, stack), "rs": B.sb([128, 1], F32, stack),
                "rr": B.sb([128, 1], F32, stack), "hbb": Buf(), "ssb": Buf(), "rsb": Buf(), "rrb": Buf()}

    def phase_AP():
        with ExitStack() as st:
            hT = B.sb([128, 8, S], BF16, st, "hT")
            hTb = [Buf() for _ in range(NTT)]
            psT = B.ps([128, 8, 128], BF16, st)
            psTb = Buf()
            wsl = [(B.sb([128, 8, 512], BF16, st), Buf(), B.ds()) for _ in range(2)]
            for cb in range(2):
                dma("pool", wsl[cb][0][:], I["w_in"][:, cb * 512:(cb + 1) * 512].rearrange("(k p) n -> p k n", p=128), wsl[cb][2], writes=[wsl[cb][1]])
            with ExitStack() as st2:
                xs = [(B.sb([128, D], F32, st2), Buf(), B.ds()) for _ in range(2)]
                sls = [rms_slot(st2) for _ in range(2)]
                for tt in range(NTT):
                    xt, xb, xd = xs[tt % 2]
                    dma("sp", xt[:], I["x"][tt * 128:(tt + 1) * 128, :], xd, writes=[xb])
                    rms_T(xt[:], xb, 0, hT[:, :, tt * 128:(tt + 1) * 128], hTb[tt], sls[tt % 2], psT, psTb,
                          evac=("dve" if tt % 2 == 0 else "act"))
                B.barrier()
            if upto == "A":
                return
            with ExitStack() as st2:
                pbank = [(B.ps([128, 512], F32, st2), Buf()) for _ in range(4)]
                stg = [(B.sb([128, S], BF16, st2), Buf(), B.ds()) for _ in range(2)]
                vst = [(B.sb([128, 512], BF16, st2), Buf(), B.ds()) for _ in range(3)]
                hT_all = hTb
                npb = 0
                nst = 0
                nv = 0
                for cb in range(16):
                    wt, wb, wd = wsl[cb % 2]
                    if cb >= 2:
                        dma("pool", wt[:], I["w_in"][:, cb * 512:(cb + 1) * 512].rearrange("(k p) n -> p k n", p=128), wd, writes=[wb])
                    if 4 <= cb < 6:
                        for tt in range(NTT):
                            pt, pb = pbank[npb % 4]; npb += 1

                            def mm(e, pt=pt, tt=tt, wt=wt):
                                for k in range(8):
                                    ins = e.matmul(pt[:], lhsT=hT[:, k, tt * 128:(tt + 1) * 128], rhs=wt[:, k, :], start=(k == 0), stop=(k == 7))
                                return ins
                            op("pe", mm, reads=[wb, hT_all[tt]], writes=[pb])
                            vt, vb, vd = vst[nv % 3]; nv += 1
                            eng = "dve" if tt % 2 == 0 else "act"
                            op(eng, lambda e, vt=vt, pt=pt, eng=eng: (e.tensor_copy(out=vt[:], in_=pt[:]) if eng == "dve" else e.copy(out=vt[:], in_=pt[:])),
                               reads=[pb], writes=[vb])
                            dma("sp", Vs.ap()[tt * 128:(tt + 1) * 128, (cb - 4) * 512:(cb - 3) * 512], vt[:], vd, reads=[vb], writes=[db["Vs"]])
                        continue
                    for ct in range(4):
                        gcol = cb * 512 + ct * 128
                        sg, sgb, sgd = stg[nst % 2]; nst += 1
                        for tb in range(NTB):
                            pt, pb = pbank[npb % 4]; npb += 1

                            def mm(e, pt=pt, tb=tb, wt=wt, ct=ct):
                                for k in range(8):
                                    ins = e.matmul(pt[:], lhsT=wt[:, k, ct * 128:(ct + 1) * 128], rhs=hT[:, k, tb * 512:(tb + 1) * 512], start=(k == 0), stop=(k == 7))
                                return ins
                            op("pe", mm, reads=[wb] + hT_all[tb * 4:tb * 4 + 4], writes=[pb])
                            dst = sg[:, tb * 512:(tb + 1) * 512]
                            if gcol < 1024:
                                op("act", lambda e, dst=dst, pt=pt: e.mul(out=dst, in_=pt[:], mul=0.125), reads=[pb], writes=[sgb])
                            elif gcol < 2048:
                                op("dve", lambda e, dst=dst, pt=pt: e.tensor_copy(out=dst, in_=pt[:]), reads=[pb], writes=[sgb])
                            elif gcol < 4096:
                                eng = "dve" if tb % 2 == 0 else "act"
                                op(eng, lambda e, dst=dst, pt=pt, eng=eng: (e.tensor_copy(out=dst, in_=pt[:]) if eng == "dve" else e.copy(out=dst, in_=pt[:])),
                                   reads=[pb], writes=[sgb])
                            elif gcol < 5120:
                                op("act", lambda e, dst=dst, pt=pt: e.mul(out=dst, in_=pt[:], mul=0.0625), reads=[pb], writes=[sgb])
                            else:
                                op("act", lambda e, dst=dst, pt=pt: e.activation(out=dst, in_=pt[:], func=AF.Sigmoid), reads=[pb], writes=[sgb])
                        if gcol < 1024:
                            dd, dbuf, r0 = QTs, db["QTs"], gcol
                        elif gcol < 2048:
                            dd, dbuf, r0 = KTs, db["KTs"], gcol - 1024
                        elif gcol < 4096:
                            dd, dbuf, r0 = UTs, db["UTs"], gcol - 3072
                        elif gcol < 5120:
                            dd, dbuf, r0 = XQs, db["XQs"], gcol - 4096
                        else:
                            dd, dbuf, r0 = GTs, db["GTs"], gcol - 5120
                        dma("sp", dd.ap()[r0:r0 + 128, :], sg[:], sgd, reads=[sgb], writes=[dbuf])
                B.barrier()

    def phase_B():
        with ExitStack() as st:
            lqk = B.sb([128, 4, 64], F32, st)
            lpr = B.sb([128, 2, 64], F32, st)
            lsum = B.sb([128, 2], F32, st)
            lexp = B.sb([128, 2], F32, st)
            neglam = B.sb([128, 1], F32, st)
            relb15 = B.sb([128, 8], F32, st)
            subg = B.sb([128, 128], F32, st)
            lb = Buf(); lds = B.ds()
            dma("sp", lqk[:], I["lqk"], lds, writes=[lb])
            dma("sp", relb15[:], I["relb15"], lds, writes=[lb])
            dma("sp", subg[:], I["subg"], lds, writes=[lb])
            lqv = lqk[:].rearrange("p (a b) d -> p a b d", b=2)
            op("dve", lambda e: e.tensor_tensor(out=lpr[:], in0=lqv[:, :, 0, :], in1=lqv[:, :, 1, :], op=ALU.mult), reads=[lb], writes=[lb])
            op("dve", lambda e: e.reduce_sum(out=lsum[:], in_=lpr[:], axis=AX.X), reads=[lb], writes=[lb])
            op("act", lambda e: e.activation(out=lexp[:], in_=lsum[:], func=AF.Exp), reads=[lb], writes=[lb])
            op("dve", lambda e: e.tensor_tensor(out=neglam[:], in0=lexp[:, 1:2], in1=lexp[:, 0:1], op=ALU.subtract), reads=[lb], writes=[lb])
            op("dve", lambda e: e.tensor_scalar(out=neglam[:], in0=neglam[:], scalar1=-LAM_INIT, scalar2=None, op0=ALU.add), reads=[lb], writes=[lb])

            MB = B.sb([128, 8, 1152], BF16, st, "MB")
            mbb = Buf()
            with ExitStack() as st2:
                relb = B.sb([32, 8], F32, st2)
                onehot = B.sb([32, 1280], F32, st2)
                antiid = B.sb([128, 128], F32, st2)
                maskmb = B.sb([128, 1152], F32, st2)
                gsb = B.sb([8, 1280], F32, st2)
                hk = [(B.sb([128, 1152], F32, st2), Buf(), B.ds()) for _ in range(2)]
                tb_ = Buf(); tds = B.ds()
                dma("sp", relb[:], I["relb"], tds, writes=[tb_])
                dma("sp", onehot[:], I["onehot"], tds, writes=[tb_])
                dma("sp", antiid[:], I["antiid"], tds, writes=[tb_])
                dma("sp", maskmb[:], I["maskmb"], tds, writes=[tb_])
                pg = [(B.ps([128, 512], F32, st2), Buf()) for _ in range(3)]
                for n in range(3):
                    n0, n1 = n * 512, min(1280, (n + 1) * 512)
                    op("pe", lambda e, n=n, n0=n0, n1=n1: e.matmul(pg[n][0][0:8, 0:n1 - n0], lhsT=relb[:], rhs=onehot[:, n0:n1], start=True, stop=True),
                       reads=[tb_], writes=[pg[n][1]])
                    op("dve", lambda e, n=n, n0=n0, n1=n1: e.tensor_copy(out=gsb[:, n0:n1], in_=pg[n][0][0:8, 0:n1 - n0]), reads=[pg[n][1]], writes=[tb_])
                gdd = B.ds()
                dma("sp", Gd.ap(), gsb[:], gdd, reads=[tb_], writes=[db["Gd"]])
                for h in range(8):
                    ht, hb_, hd = hk[h % 2]
                    dma("sp", ht[:], bass.AP(Gd, h * 1280, [[1, 128], [1, 1152]]), hd, reads=[db["Gd"]], writes=[hb_])
                    for n in range(3):
                        n0, n1 = n * 512, min(1152, (n + 1) * 512)
                        op("pe", lambda e, n=n, n0=n0, n1=n1, ht=ht: e.matmul(pg[n][0][:, 0:n1 - n0], lhsT=antiid[:], rhs=ht[:, n0:n1], start=True, stop=True),
                           reads=[tb_, hb_], writes=[pg[n][1]])
                        op("dve", lambda e, n=n, n0=n0, n1=n1, h=h: e.tensor_tensor(out=MB[:, h, n0:n1], in0=pg[n][0][:, 0:n1 - n0], in1=maskmb[:, n0:n1], op=ALU.add),
                           reads=[pg[n][1], tb_], writes=[mbb])
                B.barrier()

            sets = []
            for s_ in range(2):
                sets.append({"QT": B.sb([128, S], BF16, st), "KT": B.sb([128, S], BF16, st), "V": B.sb([128, NTT, 129], BF16, st),
                             "b": Buf(), "ds": B.ds()})
            for s_ in sets:
                op("dve", lambda e, s_=s_: e.memset(s_["V"][:, :, 128:129], 1.0), writes=[s_["b"]])
            sc = [(B.ps([128, 2, 512], F32, st), Buf()) for _ in range(2)]
            ob = [(B.ps([128, 512], F32, st), Buf()) for _ in range(3)]
            psT = B.ps([128, 8, 128], BF16, st)
            psTb = Buf()
            NPT = 4
            PT = [(B.sb([128, 2, 512], BF16, st), Buf()) for _ in range(NPT)]
            oc = [(B.sb([128, 3, 387], F32, st), Buf()) for _ in range(2)]
            fin = [{"rr": B.sb([128, 8], F32, st), "t1": B.sb([128, 4, 128], F32, st), "ot": B.sb([128, 4, 128], F32, st),
                    "ss": B.sb([128, 4], F32, st), "rs": B.sb([128, 4], F32, st), "r2": B.sb([128, 4], F32, st),
                    "yb": B.sb([128, 4, 128], BF16, st), "b": Buf()} for _ in range(3)]
            ystg = [(B.sb([128, 512], BF16, st), Buf(), B.ds()) for _ in range(3)]
            state = {"nfin": 0, "nys": 0, "noc": 0}
            pending = []

            def tick():
                for p_ in pending:
                    p_[0] -= 1
                while pending and pending[0][0] <= 0:
                    pending.pop(0)[1]()

            def oreg(r):
                return r // 3, (r % 3) * 129

            blocks = []
            for h in range(8):
                for j in range(NTB):
                    nkt = 4 * (j + 1)
                    for kt in range(nkt):
                        blocks.append((h, j, kt, nkt))

            def load_head(h):
                hs = sets[h % 2]
                dma("sp", hs["QT"][:], QTs.ap()[h * 128:(h + 1) * 128, :], hs["ds"], reads=[db["QTs"]], writes=[hs["b"]])
                dma("sp", hs["KT"][:], KTs.ap()[h * 128:(h + 1) * 128, :], hs["ds"], reads=[db["KTs"]], writes=[hs["b"]])
                dma("sp", hs["V"][:, :, 0:128], Vs.ap()[:, h * 128:(h + 1) * 128].rearrange("(t p) e -> p t e", p=128), hs["ds"],
                    reads=[db["Vs"]], writes=[hs["b"]])

            def emit_scores(n):
                h, j, kt, nkt = blocks[n]
                hs = sets[h % 2]
                m = max(0, kt - 4 * j)
                qlo = 128 * m
                near = kt >= 4 * j - 2
                off = 512 * j - 128 * kt + 384
                pt, pb = sc[n % 2]

                def smm(e):
                    for c in range(2):
                        ins = e.matmul(pt[:, c, qlo:512], lhsT=hs["KT"][64 * c:64 * c + 64, kt * 128:(kt + 1) * 128],
                                       rhs=hs["QT"][64 * c:64 * c + 64, j * 512 + qlo:(j + 1) * 512], start=True, stop=not near)
                    if near:
                        for c in range(2):
                            ins = e.matmul(pt[:, c, qlo:512], lhsT=ident_b[:], rhs=MB[:, h, off + qlo:off + 512], start=False, stop=True)
                    return ins
                op("pe", smm, reads=[hs["b"], mbb, cbuf], writes=[pb])
                ptile, ptb = PT[n % NPT]
                if near:
                    op("act", lambda e: e.activation(out=ptile[:, :, qlo:512], in_=pt[:, :, qlo:512], func=AF.Exp), reads=[pb], writes=[ptb])
                else:
                    op("act", lambda e: e.activation(out=ptile[:], in_=pt[:], func=AF.Exp, bias=relb15[:, h:h + 1], scale=1.0), reads=[pb, lb], writes=[ptb])

            def emit_av(n):
                h, j, kt, nkt = blocks[n]
                hs = sets[h % 2]
                m = max(0, kt - 4 * j)
                ptile, ptb = PT[n % NPT]

                def avmm(e):
                    for c in range(2):
                        for qs in range(m, 4):
                            bank, co = oreg(c * 4 + qs)
                            first = (kt == 0) and ((c * 4 + qs) % 3 == 0)
                            ins = e.matmul(ob[bank][0][:, co:co + 129], lhsT=ptile[:, c, qs * 128:(qs + 1) * 128], rhs=hs["V"][:, kt, :],
                                           start=first, stop=(kt == 4 * j + qs), skip_group_check=True)
                    return ins
                op("pe", avmm, reads=[ptb, hs["b"]], writes=[ob[0][1], ob[1][1], ob[2][1]])
                if kt == nkt - 1:
                    finalize(h, j)

            def finalize(h, j):
                o_, ocb = oc[state["noc"] % 2]; state["noc"] += 1
                for bk in range(3):
                    w_ = 387 if bk < 2 else 258
                    op("dve", lambda e, bk=bk, w_=w_: e.tensor_copy(out=o_[:, bk, 0:w_], in_=ob[bk][0][:, 0:w_]), reads=[ob[bk][1]], writes=[ocb])
                ys, ysb, ysd = ystg[state["nys"] % 3]; state["nys"] += 1
                f = fin[state["nfin"] % 3]; state["nfin"] += 1
                reg = o_[:].rearrange("p a b -> p (a b)")[:, 0:1032].rearrange("p (r c) -> p r c", c=129)
                fb = f["b"]
                op("dve", lambda e: e.reciprocal(out=f["rr"][:], in_=reg[:, :, 128]), reads=[ocb], writes=[fb])
                op("dve", lambda e: e.tensor_scalar(out=f["rr"][:, 4:8], in0=f["rr"][:, 4:8], scalar1=neglam[:, 0:1], scalar2=None, op0=ALU.mult), reads=[fb, lb], writes=[fb])
                op("dve", lambda e: e.tensor_tensor(out=f["t1"][:], in0=reg[:, 4:8, 0:128], in1=fv(f["rr"][:, 4:5], [[1, 4], [0, 128]]), op=ALU.mult), reads=[ocb, fb], writes=[fb])
                op("dve", lambda e: e.tensor_tensor(out=f["ot"][:], in0=reg[:, 0:4, 0:128], in1=fv(f["rr"][:, 0:1], [[1, 4], [0, 128]]), op=ALU.mult), reads=[ocb, fb], writes=[fb])
                op("dve", lambda e: e.tensor_tensor(out=f["ot"][:], in0=f["ot"][:], in1=f["t1"][:], op=ALU.add), reads=[fb], writes=[fb])
                op("dve", lambda e: e.tensor_tensor(out=f["t1"][:], in0=f["ot"][:], in1=f["ot"][:], op=ALU.mult), reads=[fb], writes=[fb])
                op("dve", lambda e: e.reduce_sum(out=f["ss"][:], in_=f["t1"][:], axis=AX.X), reads=[fb], writes=[fb])
                op("pool", lambda e: e.tensor_scalar(out=f["rs"][:], in0=f["ss"][:], scalar1=1.0 / (128 * 0.64), scalar2=1e-6 / 0.64, op0=ALU.mult, op1=ALU.add),
                   reads=[fb], writes=[fb])
                op("pool", lambda e: e.tensor_tensor(out=f["r2"][:], in0=f["rs"][:], in1=mhalf4[:], op=ALU.pow), reads=[fb, cbuf], writes=[fb])
                op("dve", lambda e: e.tensor_tensor(out=f["t1"][:], in0=f["ot"][:], in1=fv(f["r2"][:, 0:1], [[1, 4], [0, 128]]), op=ALU.mult), reads=[fb], writes=[fb])
                op("dve", lambda e: e.tensor_tensor(out=f["yb"][:], in0=f["t1"][:], in1=fv(subg[:, 0:1], [[0, 4], [1, 128]]), op=ALU.mult), reads=[fb, lb], writes=[fb])

                def later():
                    def tr(e):
                        for qs in range(4):
                            ins = e.transpose(out=psT[:, qs, :], in_=f["yb"][:, qs, :], identity=ident_b[:])
                        return ins
                    op("pe", tr, reads=[fb, cbuf], writes=[psTb])
                    op("dve", lambda e: e.tensor_copy(out=ys[:].rearrange("p (a b) -> p a b", a=4), in_=psT[:, 0:4, :]), reads=[psTb], writes=[ysb])
                    dma("sp", YAs.ap()[h * 128:(h + 1) * 128, j * 512:(j + 1) * 512], ys[:], ysd, reads=[ysb], writes=[db["YAs"]])
                pending.append([10, later])

            NBLK = len(blocks)
            load_head(0)
            for n in range(NBLK + 2):
                if n < NBLK:
                    h, j, kt, nkt = blocks[n]
                    if j == 0 and kt == 2 and h + 1 < 8:
                        load_head(h + 1)
                    emit_scores(n)
                if n >= 2:
                    emit_av(n - 2)
                tick()
            while pending:
                pending.pop(0)[1]()
            B.barrier()

    def phase_C(gen=None):
        def pull(n):
            if gen is not None:
                for _ in range(n):
                    next(gen, None)

        with ExitStack() as st:
            memnT = B.sb([128, 8, 256], BF16, st)
            mnb = Buf()
            KxT = B.sb([128, 8, 256], BF16, st)
            Vx = B.sb([128, 2, 4, 257], BF16, st)
            kvb = Buf()
            psT = B.ps([128, 8, 128], BF16, st)
            psTb = Buf()
            with ExitStack() as st2:
                pk = [(B.ps([128, 512], F32, st2), Buf()) for _ in range(2)]
                wkv = B.sb([128, 8, 2048], BF16, st2)
                wkb = Buf(); wkd = B.ds()
                for n in range(4):
                    dma("pool", wkv[:, :, n * 512:(n + 1) * 512], I["w_mem_kv"][:, n * 512:(n + 1) * 512].rearrange("(k p) n -> p k n", p=128), wkd, writes=[wkb])
                ms = [(B.sb([128, D], F32, st2), Buf(), B.ds()) for _ in range(2)]
                sls = [rms_slot(st2) for _ in range(2)]
                for mt in range(2):
                    xt, xb, xd = ms[mt]
                    dma("sp", xt[:], I["mem"][mt * 128:(mt + 1) * 128, :], xd, writes=[xb])
                    rms_T(xt[:], xb, 1, memnT[:, :, mt * 128:(mt + 1) * 128], mnb, sls[mt], psT, psTb)
                op("dve", lambda e: e.memset(Vx[:, :, :, 256:257], 1.0), writes=[kvb])
                for ct in range(8):
                    pt, pb = pk[ct % 2]

                    def mm(e, pt=pt, ct=ct):
                        for k in range(8):
                            ins = e.matmul(pt[:, 0:256], lhsT=wkv[:, k, ct * 128:(ct + 1) * 128], rhs=memnT[:, k, :], start=(k == 0), stop=(k == 7))
                        return ins
                    op("pe", mm, reads=[wkb, mnb], writes=[pb])
                    op("dve", lambda e, pt=pt, ct=ct: e.tensor_copy(out=KxT[:, ct, :], in_=pt[:, 0:256]), reads=[pb], writes=[kvb])
                    pull(3)
                n = 0
                for mt in range(2):
                    for half in range(2):
                        pt, pb = pk[n % 2]; n += 1

                        def mm(e, pt=pt, mt=mt, half=half):
                            for k in range(8):
                                ins = e.matmul(pt[:], lhsT=memnT[:, k, mt * 128:(mt + 1) * 128], rhs=wkv[:, k, 1024 + half * 512:1024 + (half + 1) * 512], start=(k == 0), stop=(k == 7))
                            return ins
                        op("pe", mm, reads=[wkb, mnb], writes=[pb])
                        op("dve", lambda e, pt=pt, mt=mt, half=half: e.tensor_copy(out=Vx[:, mt, 2 * half:2 * half + 2, 0:256], in_=pt[:].rearrange("p (a b) -> p a b", a=2)),
                           reads=[pb], writes=[kvb])
                B.barrier()
            xqs = [(B.sb([128, 2, S], BF16, st), Buf(), B.ds()) for _ in range(2)]
            psS = [(B.ps([128, 512], F32, st), Buf()) for _ in range(2)]
            psO = (B.ps([128, 4, 512], F32, st), Buf())
            PX = [[(B.sb([128, 512], BF16, st), Buf()) for _ in range(2)] for _ in range(2)]
            fx = [{"rr": B.sb([128, 4], F32, st), "yb": B.sb([128, 4, 256], BF16, st), "b": Buf()} for _ in range(2)]
            ystg = [(B.sb([128, 2, 512], BF16, st), Buf(), B.ds()) for _ in range(2)]

            def load_xq(hx):
                xq, xqb, xqd = xqs[hx % 2]
                dma("sp", xq[:], XQs.ap()[hx * 256:(hx + 1) * 256, :].rearrange("(a p) t -> p a t", p=128), xqd, reads=[db["XQs"]], writes=[xqb])

            its = [(hx, tb) for hx in range(4) for tb in range(NTB)]

            def emit_S(n):
                hx, tb = its[n]
                xq, xqb, xqd = xqs[hx % 2]
                slot = n % 2
                for mt in range(2):
                    pt, pb = psS[mt]

                    def smm(e, pt=pt, mt=mt):
                        for dt in range(2):
                            ins = e.matmul(pt[:], lhsT=KxT[:, 2 * hx + dt, mt * 128:(mt + 1) * 128], rhs=xq[:, dt, tb * 512:(tb + 1) * 512], start=(dt == 0), stop=(dt == 1))
                        return ins
                    op("pe", smm, reads=[kvb, xqb], writes=[pb])
                    px, pxb = PX[mt][slot]
                    op("act", lambda e, px=px, pt=pt: e.activation(out=px[:], in_=pt[:], func=AF.Exp), reads=[pb], writes=[pxb])

            def emit_O(n):
                hx, tb = its[n]
                slot = n % 2
                po, pob = psO
                f = fx[n % 2]

                def omm(e):
                    for qs in range(4):
                        for mt in range(2):
                            ins = e.matmul(po[:, qs, 0:257], lhsT=PX[mt][slot][0][:, qs * 128:(qs + 1) * 128], rhs=Vx[:, mt, hx, :], start=(mt == 0), stop=(mt == 1))
                    return ins
                op("pe", omm, reads=[PX[0][slot][1], PX[1][slot][1], kvb], writes=[pob])
                op("dve", lambda e: e.reciprocal(out=f["rr"][:], in_=po[:, :, 256]), reads=[pob], writes=[f["b"]])
                op("dve", lambda e: e.tensor_tensor(out=f["yb"][:], in0=po[:, :, 0:256], in1=fv(f["rr"][:, 0:1], [[1, 4], [0, 256]]), op=ALU.mult), reads=[pob, f["b"]], writes=[f["b"]])

            def emit_T(n):
                hx, tb = its[n]
                f = fx[n % 2]
                ys, ysb, ysd = ystg[n % 2]

                def tr(e):
                    for qs in range(4):
                        for dt in range(2):
                            ins = e.transpose(out=psT[:, dt * 4 + qs, :], in_=f["yb"][:, qs, dt * 128:(dt + 1) * 128], identity=ident_b[:])
                    return ins
                op("pe", tr, reads=[f["b"], cbuf], writes=[psTb])
                op("act", lambda e: e.copy(out=ys[:].rearrange("p a (q t) -> p (a q) t", q=4), in_=psT[:]), reads=[psTb], writes=[ysb])
                dma("sp", YXs.ap()[hx * 256:(hx + 1) * 256, tb * 512:(tb + 1) * 512].rearrange("(a p) t -> p a t", p=128), ys[:], ysd, reads=[ysb], writes=[db["YXs"]])

            load_xq(0)
            for n in range(len(its)):
                hx, tb = its[n]
                if tb == 0 and hx + 1 < 4:
                    load_xq(hx + 1)
                emit_S(n)
                if n >= 1:
                    emit_T(n - 1)
                emit_O(n)
                pull(8)
            emit_T(len(its) - 1)
            B.barrier()

    Dst = {}

    def phase_D1():
        st = ExitStack()
        Dst["st"] = st
        if True:
            WW = B.sb([128, 2, 32, 256], F32, st, "WW")
            wwb = Buf()
            AA1 = B.sb([128, 2, 32], F32, st)
            AA2 = B.sb([128, 2, 32], F32, st)
            sd = B.sb([128, 8], F32, st)
            tb_ = Buf(); tds = B.ds()
            r1_ = B.sb([128, 2, 32], F32, st); r2_ = B.sb([128, 2, 32], F32, st)
            Dst.update(WW=WW, wwb=wwb, AA1=AA1, AA2=AA2, tb_=tb_, r1=r1_, r2=r2_)
            dma("sp", sd[:], I["s_d"], tds, writes=[tb_])
            with ExitStack() as st1:
                are = B.sb([128, 32], F32, st1); aim = B.sb([128, 32], F32, st1); ldt = B.sb([128, 32], F32, st1)
                bre = B.sb([128, 32, 16], F32, st1); bim = B.sb([128, 32, 16], F32, st1)
                cre = B.sb([128, 32, 16], F32, st1); cim = B.sb([128, 32, 16], F32, st1)
                for t_, k_ in ((are, "s_are"), (aim, "s_aim"), (ldt, "s_ldt"), (bre, "s_bre"), (bim, "s_bim"), (cre, "s_cre"), (cim, "s_cim")):
                    dma("sp", t_[:], I[k_], tds, writes=[tb_])
                sel_f = B.sb([128, 4, 128], F32, st1); sel_b = B.sb([128, 4, 128], BF16, st1)
                maskd = B.sb([128, 128], F32, st1)
                dma("sp", sel_f[:], I["sel"], tds, writes=[tb_])
                dma("sp", maskd[:], I["maskd"], tds, writes=[tb_])
                T = lambda shape: B.sb(shape, F32, st1)
                dtt = T([128, 32]); adr = T([128, 32]); th = T([128, 32]); cc = T([128, 32]); ss_ = T([128, 32])
                t1 = T([128, 32]); t2 = T([128, 32]); hpi = T([128, 1])
                ER = T([128, 32, 17]); EI = T([128, 32, 17]); MAGP = T([128, 32, 17]); MAGN = T([128, 32, 17])
                PPr = T([128, 32, 17]); PPi = T([128, 32, 17]); PNr = T([128, 32, 17]); PNi = T([128, 32, 17])
                bbr = T([128, 32, 16]); bbi = T([128, 32, 16])
                tmpA = T([128, 32, 16]); tmpB = T([128, 32, 16])
                tb2 = tb_

                def D_(fn):
                    op("dve", fn, reads=[tb2], writes=[tb2])

                def A_(fn):
                    op("act", fn, reads=[tb2], writes=[tb2])
                D_(lambda e: e.tensor_copy(out=sel_b[:], in_=sel_f[:]))
                maskd_b = B.sb([128, 128], BF16, st1)
                D_(lambda e: e.tensor_copy(out=maskd_b[:], in_=maskd[:]))
                D_(lambda e: e.memset(hpi[:], math.pi / 2))
                A_(lambda e: e.activation(out=dtt[:], in_=ldt[:], func=AF.Exp))
                D_(lambda e: e.tensor_tensor(out=adr[:], in0=are[:], in1=dtt[:], op=ALU.mult))
                D_(lambda e: e.tensor_tensor(out=th[:], in0=aim[:], in1=dtt[:], op=ALU.mult))
                A_(lambda e: e.activation(out=ss_[:], in_=th[:], func=AF.Sin, scale=1.0 / 32))
                A_(lambda e: e.activation(out=cc[:], in_=th[:], func=AF.Sin, scale=1.0 / 32, bias=hpi[:]))
                for _ in range(5):
                    D_(lambda e: e.tensor_tensor(out=t1[:], in0=cc[:], in1=cc[:], op=ALU.mult))
                    D_(lambda e: e.tensor_tensor(out=t2[:], in0=ss_[:], in1=ss_[:], op=ALU.mult))
                    D_(lambda e: e.scalar_tensor_tensor(out=ss_[:], in0=cc[:], scalar=2.0, in1=ss_[:], op0=ALU.mult, op1=ALU.mult))
                    D_(lambda e: e.tensor_tensor(out=cc[:], in0=t1[:], in1=t2[:], op=ALU.subtract))
                D_(lambda e: e.memset(ER[:, :, 0:1], 1.0))
                D_(lambda e: e.memset(EI[:, :, 0:1], 0.0))
                D_(lambda e: e.tensor_copy(out=ER[:, :, 1], in_=cc[:]))
                D_(lambda e: e.tensor_copy(out=EI[:, :, 1], in_=ss_[:]))
                tmpE = [T([128, 32, 8]) for _ in range(4)]
                k = 1
                while k < 16:
                    a_r, a_i = ER[:, :, 1:k + 1], EI[:, :, 1:k + 1]
                    b_r = fv(ER[:, :, k:k + 1], [[17, 32], [0, k]])
                    b_i = fv(EI[:, :, k:k + 1], [[17, 32], [0, k]])
                    q = [t_[:, :, 0:k] for t_ in tmpE]
                    D_(lambda e, a_r=a_r, b_r=b_r, q=q: e.tensor_tensor(out=q[0], in0=a_r, in1=b_r, op=ALU.mult))
                    D_(lambda e, a_i=a_i, b_i=b_i, q=q: e.tensor_tensor(out=q[1], in0=a_i, in1=b_i, op=ALU.mult))
                    D_(lambda e, a_r=a_r, b_i=b_i, q=q: e.tensor_tensor(out=q[2], in0=a_r, in1=b_i, op=ALU.mult))
                    D_(lambda e, a_i=a_i, b_r=b_r, q=q: e.tensor_tensor(out=q[3], in0=a_i, in1=b_r, op=ALU.mult))
                    D_(lambda e, k=k, q=q: e.tensor_tensor(out=ER[:, :, k + 1:2 * k + 1], in0=q[0], in1=q[1], op=ALU.subtract))
                    D_(lambda e, k=k, q=q: e.tensor_tensor(out=EI[:, :, k + 1:2 * k + 1], in0=q[2], in1=q[3], op=ALU.add))
                    k *= 2
                for tau in range(17):
                    A_(lambda e, tau=tau: e.activation(out=MAGP[:, :, tau], in_=adr[:], func=AF.Exp, scale=float(tau)))
                    A_(lambda e, tau=tau: e.activation(out=MAGN[:, :, tau], in_=adr[:], func=AF.Exp, scale=-float(tau)))
                D_(lambda e: e.tensor_tensor(out=PPr[:], in0=MAGP[:], in1=ER[:], op=ALU.mult))
                D_(lambda e: e.tensor_tensor(out=PPi[:], in0=MAGP[:], in1=EI[:], op=ALU.mult))
                D_(lambda e: e.tensor_tensor(out=PNr[:], in0=MAGN[:], in1=ER[:], op=ALU.mult))
                D_(lambda e: e.scalar_tensor_tensor(out=PNi[:], in0=MAGN[:], scalar=-1.0, in1=EI[:], op0=ALU.mult, op1=ALU.mult))
                xr = T([128, 32]); nr = T([128, 32]); ni = T([128, 32]); den = T([128, 32]); cfr = T([128, 32]); cfi = T([128, 32])
                D_(lambda e: e.tensor_scalar(out=xr[:], in0=PPr[:, :, 1], scalar1=-1.0, scalar2=None, op0=ALU.add))
                D_(lambda e: e.tensor_tensor(out=t1[:], in0=xr[:], in1=are[:], op=ALU.mult))
                D_(lambda e: e.tensor_tensor(out=t2[:], in0=PPi[:, :, 1], in1=aim[:], op=ALU.mult))
                D_(lambda e: e.tensor_tensor(out=nr[:], in0=t1[:], in1=t2[:], op=ALU.add))
                D_(lambda e: e.tensor_tensor(out=t1[:], in0=PPi[:, :, 1], in1=are[:], op=ALU.mult))
                D_(lambda e: e.tensor_tensor(out=t2[:], in0=xr[:], in1=aim[:], op=ALU.mult))
                D_(lambda e: e.tensor_tensor(out=ni[:], in0=t1[:], in1=t2[:], op=ALU.subtract))
                D_(lambda e: e.tensor_tensor(out=t1[:], in0=are[:], in1=are[:], op=ALU.mult))
                D_(lambda e: e.tensor_tensor(out=t2[:], in0=aim[:], in1=aim[:], op=ALU.mult))
                D_(lambda e: e.tensor_tensor(out=den[:], in0=t1[:], in1=t2[:], op=ALU.add))
                D_(lambda e: e.reciprocal(out=den[:], in_=den[:]))
                D_(lambda e: e.tensor_tensor(out=cfr[:], in0=nr[:], in1=den[:], op=ALU.mult))
                D_(lambda e: e.tensor_tensor(out=cfi[:], in0=ni[:], in1=den[:], op=ALU.mult))
                cfr_b = fv(cfr[:], [[1, 32], [0, 16]]); cfi_b = fv(cfi[:], [[1, 32], [0, 16]])
                D_(lambda e: e.tensor_tensor(out=tmpA[:], in0=bre[:], in1=cfr_b, op=ALU.mult))
                D_(lambda e: e.tensor_tensor(out=tmpB[:], in0=bim[:], in1=cfi_b, op=ALU.mult))
                D_(lambda e: e.tensor_tensor(out=bbr[:], in0=tmpA[:], in1=tmpB[:], op=ALU.subtract))
                D_(lambda e: e.tensor_tensor(out=tmpA[:], in0=bim[:], in1=cfr_b, op=ALU.mult))
                D_(lambda e: e.tensor_tensor(out=tmpB[:], in0=bre[:], in1=cfi_b, op=ALU.mult))
                D_(lambda e: e.tensor_tensor(out=bbi[:], in0=tmpA[:], in1=tmpB[:], op=ALU.add))
                D_(lambda e: e.tensor_copy(out=AA1[:, 0, :], in_=PPr[:, :, 16]))
                D_(lambda e: e.tensor_copy(out=AA1[:, 1, :], in_=PPr[:, :, 16]))
                D_(lambda e: e.tensor_scalar(out=AA2[:, 0, :], in0=PPi[:, :, 16], scalar1=-1.0, scalar2=None, op0=ALU.mult))
                D_(lambda e: e.tensor_copy(out=AA2[:, 1, :], in_=PPi[:, :, 16]))

                if upto == "D0":
                    B.barrier()
                    return
                X = [T([128, 16, 16]) for _ in range(4)]
                xb_ = Buf()
                Bx = [{"Bexp": B.sb([128, 2, 512], BF16, st1), "Bfull": B.sb([128, 2, 512], BF16, st1), "BblkT": B.sb([128, 8, 128], BF16, st1),
                       "bxb": Buf(), "bfb": Buf(), "btb": Buf()} for _ in range(2)]
                CexpB = [B.sb([128, 2, 512], BF16, st1) for _ in range(4)]
                cxb = [Buf() for _ in range(4)]
                cxd = [B.ds() for _ in range(4)]
                DblkB = [B.sb([128, 4, 512], BF16, st1) for _ in range(4)]
                dkb = [Buf() for _ in range(4)]
                Vp = [B.sb([128, 4, 256], BF16, st1) for _ in range(4)]
                vpb = [Buf() for _ in range(4)]
                uTn = (B.sb([128, S], BF16, st1), Buf(), B.ds())
                uTs = [(B.sb([128, 16, 256], BF16, st1), Buf()) for _ in range(2)]
                Yqs = [(B.sb([128, 16, 256], BF16, st1), Buf(), B.ds()) for _ in range(1)]
                psT = B.ps([128, 8, 128], BF16, st1); psTb = Buf()
                psD = [(B.ps([128, 512], F32, st1), Buf()) for _ in range(2)]
                psV = [(B.ps([128, 2, 256], F32, st1), Buf()) for _ in range(2)]
                psW = (B.ps([128, 2, 256], F32, st1), Buf())
                psY = [(B.ps([128, 512], F32, st1), Buf()) for _ in range(2)]
                for bx in Bx:
                    op("dve", lambda e, bx=bx: e.memset(bx["Bexp"][:], 0.0), writes=[bx["bxb"]])
                    op("dve", lambda e, bx=bx: e.memset(bx["Bfull"][:], 0.0), writes=[bx["bfb"]])
                for j in range(4):
                    op("dve", lambda e, j=j: e.memset(CexpB[j][:], 0.0), writes=[cxb[j]])

                def cmul(pr, pi_, qr, qi, outs_r, outs_i, neg_i, rbufs, wbufs):
                    op("dve", lambda e: e.tensor_tensor(out=X[0][:], in0=pr, in1=qr, op=ALU.mult), reads=rbufs, writes=[xb_])
                    op("dve", lambda e: e.tensor_tensor(out=X[1][:], in0=pi_, in1=qi, op=ALU.mult), reads=rbufs, writes=[xb_])
                    op("dve", lambda e: e.tensor_tensor(out=X[2][:], in0=pr, in1=qi, op=ALU.mult), reads=rbufs, writes=[xb_])
                    op("dve", lambda e: e.tensor_tensor(out=X[3][:], in0=pi_, in1=qr, op=ALU.mult), reads=rbufs, writes=[xb_])
                    for lo, hi, o in outs_r:
                        op("dve", lambda e, lo=lo, hi=hi, o=o: e.tensor_tensor(out=o, in0=X[0][lo:hi], in1=X[1][lo:hi], op=ALU.subtract), reads=[xb_], writes=wbufs)
                    for lo, hi, o in outs_i:
                        if neg_i:
                            op("dve", lambda e, lo=lo, hi=hi, o=o: e.scalar_tensor_tensor(out=o, in0=X[2][lo:hi], scalar=-1.0, in1=X[3][lo:hi], op0=ALU.mult, op1=ALU.subtract),
                               reads=[xb_], writes=wbufs)
                        else:
                            op("dve", lambda e, lo=lo, hi=hi, o=o: e.tensor_tensor(out=o, in0=X[2][lo:hi], in1=X[3][lo:hi], op=ALU.add), reads=[xb_], writes=wbufs)

                def blocked(tile, c):
                    v = tile[:].rearrange("p c (i g k) -> p c i g k", g=2, k=16)
                    return [(0, 64, v[0:64, c, :, 0, :]), (64, 128, v[64:128, c, :, 1, :])]

                def prep(m):
                    j = m % 4
                    bx = Bx[m % 2]
                    ppr = fv(PPr[:, m, 1:2], [[1, 16], [0, 16]]); ppi = fv(PPi[:, m, 1:2], [[1, 16], [0, 16]])
                    pnr = fv(PNr[:, m, 1:2], [[1, 16], [0, 16]]); pni = fv(PNi[:, m, 1:2], [[1, 16], [0, 16]])
                    prr = fv(PPr[:, m, 15:16], [[-1, 16], [0, 16]]); pri = fv(PPi[:, m, 15:16], [[-1, 16], [0, 16]])
                    c_r = fv(cre[:, m, 0:1], [[0, 16], [1, 16]]); c_i = fv(cim[:, m, 0:1], [[0, 16], [1, 16]])
                    b_r = fv(bbr[:, m, 0:1], [[0, 16], [1, 16]]); b_i = fv(bbi[:, m, 0:1], [[0, 16], [1, 16]])
                    cmul(pnr, pni, b_r, b_i, blocked(bx["Bexp"], 0), blocked(bx["Bexp"], 1), False, [tb2], [bx["bxb"]])
                    cmul(ppr, ppi, c_r, c_i, blocked(CexpB[j], 0), blocked(CexpB[j], 1), True, [tb2], [cxb[j]])
                    dma("sp", CXs.ap()[m], CexpB[j][:].rearrange("p c n -> p (c n)"), cxd[j], reads=[cxb[j]], writes=[db["CXs"]])

                def work(m):
                    q4, j = divmod(m, 4)
                    bx = Bx[m % 2]
                    uT, utb = uTs[q4 % 2]

                    def trB(e):
                        for c in range(2):
                            for it in range(4):
                                ins = e.transpose(out=psT[:, c * 4 + it, :], in_=bx["Bexp"][:, c, it * 128:(it + 1) * 128], identity=ident_b[:])
                        return ins
                    op("pe", trB, reads=[bx["bxb"], cbuf], writes=[psTb])
                    op("act", lambda e: e.copy(out=bx["BblkT"][:], in_=psT[:]), reads=[psTb], writes=[bx["btb"]])
                    for hb2 in range(2):
                        pv, pvb = psV[hb2]

                        def vmm(e, pv=pv, hb2=hb2):
                            for a_ in range(2):
                                it = hb2 * 2 + a_
                                for il in range(4):
                                    ins = e.matmul(pv[:, a_, :], lhsT=sel_b[32 * j:32 * j + 32, il, :], rhs=uT[32 * j:32 * j + 32, 4 * it + il, :],
                                                   start=(il == 0), stop=(il == 3), tile_position=(32 * j, 0))
                            return ins
                        op("pe", vmm, reads=[utb, tb2], writes=[pvb])
                        op("act", lambda e, pv=pv, hb2=hb2: e.copy(out=Vp[j][:, 2 * hb2:2 * hb2 + 2, :], in_=pv[:]), reads=[pvb], writes=[vpb[j]])
                    for it in range(4):
                        pd, pdb = psD[it % 2]

                        def dmm(e, pd=pd, it=it):
                            for c in range(2):
                                ins = e.matmul(pd[:], lhsT=bx["Bexp"][:, c, it * 128:(it + 1) * 128], rhs=CexpB[j][:, c, :], start=(c == 0), stop=(c == 1))
                            return ins
                        op("pe", dmm, reads=[bx["bxb"], cxb[j]], writes=[pdb])
                        op("act", lambda e, pd=pd, it=it: e.copy(out=DblkB[j][:, it, :], in_=pd[:]), reads=[pdb], writes=[dkb[j]])
                        op("pool", lambda e, it=it: e.tensor_tensor(out=DblkB[j][:, it, it * 128:(it + 1) * 128], in0=DblkB[j][:, it, it * 128:(it + 1) * 128], in1=maskd_b[:],
                                                                     op=ALU.mult), reads=[tb2], writes=[dkb[j]])
                    pw, pwb = psW

                    def wmm(e):
                        for c in range(2):
                            for it in range(4):
                                ins = e.matmul(pw[:, c, :], lhsT=bx["BblkT"][:, c * 4 + it, :], rhs=Vp[j][:, it, :], start=(it == 0), stop=(it == 3))
                        return ins
                    op("pe", wmm, reads=[bx["btb"], vpb[j]], writes=[pwb])
                    op("act", lambda e: e.activation(out=WW[:, 0, m, :], in_=pw[:, 0, :], func=AF.Copy, scale=AA1[:, 0, m:m + 1]), reads=[pwb, tb_], writes=[wwb])
                    op("act", lambda e: e.activation(out=WW[:, 1, m, :], in_=pw[:, 1, :], func=AF.Copy, scale=AA1[:, 0, m:m + 1]), reads=[pwb, tb_], writes=[wwb])
                    op("dve", lambda e: e.scalar_tensor_tensor(out=WW[:, 0, m, :], in0=pw[:, 1, :], scalar=AA2[:, 0, m:m + 1], in1=WW[:, 0, m, :], op0=ALU.mult, op1=ALU.add),
                       reads=[pwb, tb_, wwb], writes=[wwb])
                    op("dve", lambda e: e.scalar_tensor_tensor(out=WW[:, 1, m, :], in0=pw[:, 0, :], scalar=AA2[:, 1, m:m + 1], in1=WW[:, 1, m, :], op0=ALU.mult, op1=ALU.add),
                       reads=[pwb, tb_, wwb], writes=[wwb])

                def yintra(q4):
                    uT, utb = uTs[q4 % 2]
                    Yq, yqb, yqd = Yqs[0]
                    for i in range(16):
                        py = psY[i % 2][0][:, 0:256]
                        pyb = psY[i % 2][1]

                        def ymm(e, py=py, i=i):
                            for it in range(i // 4 + 1):
                                for j in range(4):
                                    ins = e.matmul(py[32 * j:32 * j + 32, :], lhsT=DblkB[j][:, it, i * 32:(i + 1) * 32], rhs=Vp[j][:, it, :],
                                                   start=(it == 0), stop=(it == i // 4), tile_position=(0, 32 * j))
                            return ins
                        op("pe", ymm, reads=dkb + vpb, writes=[pyb])
                        op("dve", lambda e, py=py, i=i: e.scalar_tensor_tensor(out=Yq[:, i, :], in0=uT[:, i, :], scalar=sd[:, q4:q4 + 1], in1=py,
                                                                                op0=ALU.mult, op1=ALU.add), reads=[pyb, utb, tb_], writes=[yqb])
                    dma("sp", YIs.ap()[q4 * 128:(q4 + 1) * 128, :], Yq[:].rearrange("p i c -> p (i c)"), yqd, reads=[yqb], writes=[db["YIs"]])

                def load_u(q4):
                    un, unb, und = uTn
                    uT, utb = uTs[q4 % 2]
                    dma("sp", un[:], UTs.ap()[q4 * 128:(q4 + 1) * 128, :], und, reads=[db["UTs"]], writes=[unb])
                    unv = un[:].rearrange("p (c i) -> p i c", i=16)
                    for ih in range(2):
                        op("act", lambda e, ih=ih: e.copy(out=uT[:, ih * 8:(ih + 1) * 8, :], in_=unv[:, ih * 8:(ih + 1) * 8, :]), reads=[unb], writes=[utb])

                load_u(0)
                prep(0)
                for m in range(32):
                    q4, j = divmod(m, 4)
                    if j == 0 and q4 + 1 < 8:
                        load_u(q4 + 1)
                    if m + 1 < 32:
                        prep(m + 1)
                    work(m)
                    if j == 3:
                        yintra(q4)
                B.barrier()

    def rec_gen():
        WW, AA1, AA2, tb_ = Dst["WW"], Dst["AA1"], Dst["AA2"], Dst["tb_"]
        r1, r2 = Dst["r1"], Dst["r2"]
        halves = [(0, 16, Buf(), Buf()), (16, 32, Buf(), Buf())]
        for c in range(1, 256):
            for (p0, p1, wb2, rb) in halves:
                prev = WW[:, :, p0:p1, c - 1]
                cur = WW[:, :, p0:p1, c]
                prev_sw = fv(WW[:, 1:2, p0:p0 + 1, c - 1:c], [[-32 * 256, 2], [256, 16]])
                op("dve", lambda e, prev=prev, p0=p0, p1=p1: e.tensor_tensor(out=r1[:, :, p0:p1], in0=prev, in1=AA1[:, :, p0:p1], op=ALU.mult), reads=[wb2, tb_], writes=[rb])
                op("dve", lambda e, prev_sw=prev_sw, p0=p0, p1=p1: e.tensor_tensor(out=r2[:, :, p0:p1], in0=prev_sw, in1=AA2[:, :, p0:p1], op=ALU.mult), reads=[wb2, tb_], writes=[rb])
            for (p0, p1, wb2, rb) in halves:
                op("dve", lambda e, p0=p0, p1=p1: e.tensor_tensor(out=r1[:, :, p0:p1], in0=r1[:, :, p0:p1], in1=r2[:, :, p0:p1], op=ALU.add), reads=[rb], writes=[rb])
            for (p0, p1, wb2, rb) in halves:
                cur = WW[:, :, p0:p1, c]
                op("dve", lambda e, cur=cur, p0=p0, p1=p1: e.tensor_tensor(out=cur, in0=cur, in1=r1[:, :, p0:p1], op=ALU.add), reads=[rb, wb2], writes=[wb2])
            yield

    def phase_D2():
        WW, wwb = Dst["WW"], Dst["wwb"]
        if True:
            with ExitStack() as st1:
                Hb = B.sb([128, 2, 32, 256], BF16, st1); hbb = Buf()
                op("dve", lambda e: e.memset(Hb[:, :, :, 0:1], 0.0), writes=[hbb])
                for c in range(2):
                    op("dve" if c == 0 else "act", lambda e, c=c: (e.tensor_copy(out=Hb[:, c, :, 1:256], in_=WW[:, c, :, 0:255]) if c == 0
                                                                    else e.copy(out=Hb[:, c, :, 1:256], in_=WW[:, c, :, 0:255])), reads=[wwb], writes=[hbb])
                Cx = [(B.sb([128, 2, 512], BF16, st1), Buf(), B.ds()) for _ in range(8)]
                Yin = [(B.sb([128, 16, 256], BF16, st1), Buf(), B.ds()) for _ in range(2)]
                Yf = [(B.sb([128, 16, 256], F32, st1), Buf()) for _ in range(2)]
                zT = [(B.sb([128, S], BF16, st1), Buf(), B.ds()) for _ in range(2)]
                psY2 = [(B.ps([128, 512], F32, st1), Buf()) for _ in range(4)]
                npy = 0
                def load_q(q4):
                    yi, yib, yid = Yin[q4 % 2]
                    dma("sp", yi[:].rearrange("p i c -> p (i c)"), YIs.ap()[q4 * 128:(q4 + 1) * 128, :], yid, reads=[db["YIs"]], writes=[yib])
                    for j in range(4):
                        ct, ctb, ctd = Cx[(q4 % 2) * 4 + j]
                        dma("sp", ct[:].rearrange("p c n -> p (c n)"), CXs.ap()[q4 * 4 + j], ctd, reads=[db["CXs"]], writes=[ctb])

                load_q(0)
                for q4 in range(8):
                    yi, yib, yid = Yin[q4 % 2]
                    yf, yfb = Yf[q4 % 2]
                    z_, zb, zd = zT[q4 % 2]
                    if q4 + 1 < 8:
                        load_q(q4 + 1)
                    cxs = [(Cx[(q4 % 2) * 4 + j][0], Cx[(q4 % 2) * 4 + j][1]) for j in range(4)]
                    for i in range(16):
                        py, pyb = psY2[npy % 4]; npy += 1

                        def ymm(e, py=py, i=i, cxs=cxs, q4=q4):
                            for c in range(2):
                                for j in range(4):
                                    ins = e.matmul(py[32 * j:32 * j + 32, 0:256], lhsT=cxs[j][0][:, c, i * 32:(i + 1) * 32], rhs=Hb[:, c, q4 * 4 + j, :],
                                                   start=(c == 0), stop=(c == 1), tile_position=(0, 32 * j))
                            return ins
                        op("pe", ymm, reads=[hbb] + [c_[1] for c_ in cxs], writes=[pyb])
                        op("dve", lambda e, py=py, i=i, yi=yi, yf=yf: e.tensor_tensor(out=yf[:, i, :], in0=py[:, 0:256], in1=yi[:, i, :], op=ALU.add),
                           reads=[pyb, yib], writes=[yfb])
                    op("act", lambda e, z_=z_, yf=yf: e.activation(out=z_[:].rearrange("p (c i) -> p c i", i=16), in_=yf[:].rearrange("p i c -> p c i"), func=AF.Gelu_apprx_tanh),
                       reads=[yfb], writes=[zb])
                    dma("pool", ZTs.ap()[q4 * 128:(q4 + 1) * 128, :], z_[:], zd, reads=[zb], writes=[db["ZTs"]])
                B.barrier()
        Dst["st"].close()
    def phase_T1():
        with ExitStack() as st:
            W = {}
            WB = {}
            for nm in ("glu_w", "w_br_attn", "w_br_ssm", "w_br_xattn", "w_out"):
                W[nm] = B.sb([128, 8, D], BF16, st, nm)
                WB[nm] = Buf()
                wd = B.ds()
                for n in range(2):
                    dma("pool", W[nm][:, :, n * 512:(n + 1) * 512], I[nm][:, n * 512:(n + 1) * 512].rearrange("(k p) n -> p k n", p=128), wd, writes=[WB[nm]])
            glub = B.sb([128, 8], F32, st)
            wb_ = Buf()
            dma("sp", glub[:], I["glu_b"], B.ds(), writes=[wb_])
            zt = (B.sb([128, 8, 512], BF16, st), Buf(), B.ds())
            ya = (B.sb([128, 8, 512], BF16, st), Buf(), B.ds())
            yx = (B.sb([128, 8, 512], BF16, st), Buf(), B.ds())
            gt = (B.sb([128, 24, 512], BF16, st), Buf(), B.ds())
            yssm = (B.sb([128, 8, 512], BF16, st), Buf())
            mixT = (B.sb([128, 8, 512], BF16, st), Buf())
            sig = [(B.sb([128, 512], F32, st), Buf()) for _ in range(2)]
            mm_ = [(B.sb([128, 3, 512], F32, st), Buf()) for _ in range(2)]
            xs = [(B.sb([128, D], F32, st), Buf(), B.ds()) for _ in range(2)]
            x1 = [(B.sb([128, D], F32, st), Buf(), B.ds()) for _ in range(2)]
            pG = [(B.ps([128, 512], F32, st), Buf()) for _ in range(2)]
            pB = [(B.ps([128, 512], F32, st), Buf()) for _ in range(3)]
            pO = [(B.ps([128, 512], F32, st), Buf()) for _ in range(2)]
            ng = 0; nx = 0; no = 0
            def load_z(tb):
                tsl = slice(tb * 512, (tb + 1) * 512)
                dma("act", zt[0][:], ZTs.ap()[:, tsl].rearrange("(k p) t -> p k t", p=128), zt[2], reads=[db["ZTs"]], writes=[zt[1]])

            def load_rest(tb):
                tsl = slice(tb * 512, (tb + 1) * 512)
                dma("act", ya[0][:], YAs.ap()[:, tsl].rearrange("(k p) t -> p k t", p=128), ya[2], reads=[db["YAs"]], writes=[ya[1]])
                dma("act", yx[0][:], YXs.ap()[:, tsl].rearrange("(k p) t -> p k t", p=128), yx[2], reads=[db["YXs"]], writes=[yx[1]])
                dma("act", gt[0][:], GTs.ap()[:, tsl].rearrange("(k p) t -> p k t", p=128), gt[2], reads=[db["GTs"]], writes=[gt[1]])

            load_z(0)
            load_rest(0)
            for tb in range(NTB):
                for ct in range(8):
                    pg, pgb = pG[ng % 2]
                    sg, sgb = sig[ng % 2]; ng += 1

                    def gmm(e, pg=pg, ct=ct):
                        for k in range(8):
                            ins = e.matmul(pg[:], lhsT=W["glu_w"][:, k, ct * 128:(ct + 1) * 128], rhs=zt[0][:, k, :], start=(k == 0), stop=(k == 7))
                        return ins
                    op("pe", gmm, reads=[WB["glu_w"], zt[1]], writes=[pgb])
                    op("act", lambda e, sg=sg, pg=pg, ct=ct: e.activation(out=sg[:], in_=pg[:], func=AF.Sigmoid, bias=glub[:, ct:ct + 1], scale=1.0),
                       reads=[pgb, wb_], writes=[sgb])
                    op("dve", lambda e, sg=sg, ct=ct: e.tensor_tensor(out=yssm[0][:, ct, :], in0=zt[0][:, ct, :], in1=sg[:], op=ALU.mult),
                       reads=[sgb, zt[1]], writes=[yssm[1]])
                if tb + 1 < NTB:
                    load_z(tb + 1)
                for ct in range(8):
                    srcs = ((W["w_br_attn"], ya[0], ya[1], WB["w_br_attn"]), (W["w_br_ssm"], yssm[0], yssm[1], WB["w_br_ssm"]),
                            (W["w_br_xattn"], yx[0], yx[1], WB["w_br_xattn"]))
                    m3, m3b = mm_[ct % 2]
                    for bi, (w_, y_, yb_, wbf) in enumerate(srcs):
                        pb_, pbb = pB[bi]

                        def bmm(e, pb_=pb_, w_=w_, y_=y_, ct=ct):
                            for k in range(8):
                                ins = e.matmul(pb_[:], lhsT=w_[:, k, ct * 128:(ct + 1) * 128], rhs=y_[:, k, :], start=(k == 0), stop=(k == 7))
                            return ins
                        op("pe", bmm, reads=[wbf, yb_], writes=[pbb])
                        op("dve", lambda e, m3=m3, pb_=pb_, bi=bi, ct=ct: e.tensor_tensor(out=m3[:, bi, :], in0=pb_[:], in1=gt[0][:, bi * 8 + ct, :], op=ALU.mult),
                           reads=[pbb, gt[1]], writes=[m3b])
                    op("dve", lambda e, m3=m3: e.tensor_tensor(out=m3[:, 0, :], in0=m3[:, 0, :], in1=m3[:, 1, :], op=ALU.add), reads=[m3b], writes=[m3b])
                    op("dve", lambda e, m3=m3, ct=ct: e.tensor_tensor(out=mixT[0][:, ct, :], in0=m3[:, 0, :], in1=m3[:, 2, :], op=ALU.add), reads=[m3b], writes=[mixT[1]])
                if tb + 1 < NTB:
                    load_rest(tb + 1)
                for ts in range(4):
                    xt, xb, xd = xs[nx % 2]
                    x1t, x1b, x1d = x1[nx % 2]; nx += 1
                    r0 = tb * 512 + ts * 128
                    dma("sp", xt[:], I["x"][r0:r0 + 128, :], xd, writes=[xb])
                    for half in range(2):
                        po, pob = pO[no % 2]; no += 1

                        def omm(e, po=po, ts=ts, half=half):
                            for k in range(8):
                                ins = e.matmul(po[:], lhsT=mixT[0][:, k, ts * 128:(ts + 1) * 128], rhs=W["w_out"][:, k, half * 512:(half + 1) * 512], start=(k == 0), stop=(k == 7))
                            return ins
                        op("pe", omm, reads=[WB["w_out"], mixT[1]], writes=[pob])
                        op("dve", lambda e, po=po, half=half, xt=xt, x1t=x1t: e.tensor_tensor(out=x1t[:, half * 512:(half + 1) * 512], in0=po[:], in1=xt[:, half * 512:(half + 1) * 512], op=ALU.add),
                           reads=[pob, xb], writes=[x1b])
                    dma("pool", X1s.ap()[r0:r0 + 128, :], x1t[:], x1d, reads=[x1b], writes=[db["X1s"]])
            B.barrier()

    def phase_T2():
        TB = 256
        with ExitStack() as st:
            wfi = B.sb([128, 8, 2 * FF], BF16, st, "wfi")
            wfo = B.sb([128, NH, D], BF16, st, "wfo")
            wgb = [Buf() for _ in range(6)]
            for k in range(6):
                wd = B.ds()
                for base in (0, FF):
                    c0 = base + 512 * k
                    c1 = min(base + 512 * (k + 1), base + FF)
                    dma("pool", wfi[:, :, c0:c1], I["w_ffn_in"][:, c0:c1].rearrange("(k p) n -> p k n", p=128), wd, writes=[wgb[k]])
            wb_ = Buf(); wd = B.ds()
            for n in range(2):
                dma("pool", wfo[:, :, n * 512:(n + 1) * 512], I["w_ffn_out"][:, n * 512:(n + 1) * 512].rearrange("(k p) n -> p k n", p=128), wd, writes=[wb_])
            psT = B.ps([128, 8, 128], BF16, st); psTb = Buf()
            pG = [(B.ps([128, 512], F32, st), Buf()) for _ in range(2)]
            pU = [(B.ps([128, 512], F32, st), Buf()) for _ in range(2)]
            pO = [(B.ps([128, 512], F32, st), Buf()) for _ in range(2)]
            x1t = [(B.sb([128, D], F32, st), Buf(), B.ds()) for _ in range(4)]
            sls = [rms_slot(st) for _ in range(2)]
            h2T = [(B.sb([128, 8, TB], BF16, st), Buf()) for _ in range(2)]
            aT = [(B.sb([128, NH, TB], BF16, st), Buf()) for _ in range(1)]
            sg = [(B.sb([128, TB], F32, st), Buf()) for _ in range(2)]
            x2 = [(B.sb([128, D], F32, st), Buf()) for _ in range(2)]
            fs = [{"ss": B.sb([128, 1], F32, st), "rs": B.sb([128, 1], F32, st), "rr": B.sb([128, 1], F32, st), "b": Buf()} for _ in range(2)]
            ot = [(B.sb([128, D], F32, st), Buf(), B.ds()) for _ in range(1)]
            cnt = {"nx": 0, "ng": 0, "no": 0, "nf": 0}
            nsub = TB // 128
            NTB2 = S // TB

            def norm_in(tb):
                hT_, hTb_ = h2T[tb % 2]
                xts = []
                for ts in range(nsub):
                    xt, xb, xd = x1t[cnt["nx"] % 4]
                    sl = sls[cnt["nx"] % 2]; cnt["nx"] += 1
                    r0 = tb * TB + ts * 128
                    dma("sp", xt[:], X1s.ap()[r0:r0 + 128, :], xd, reads=[db["X1s"]], writes=[xb])
                    rms_T(xt[:], xb, 2, hT_[:, :, ts * 128:(ts + 1) * 128], hTb_, sl, psT, psTb, evac=("dve" if ts % 2 == 0 else "act"))
                    xts.append((xt, xb, r0))
                return xts

            def ffn_in(tb):
                hT_, hTb_ = h2T[tb % 2]
                a_, ab_ = aT[0]
                for ht in range(NH):
                    pg, pgb = pG[cnt["ng"] % 2]
                    pu, pub = pU[cnt["ng"] % 2]
                    s_, sb_ = sg[cnt["ng"] % 2]; cnt["ng"] += 1

                    def gm(e, pg=pg, ht=ht):
                        for k in range(8):
                            ins = e.matmul(pg[:, 0:TB], lhsT=wfi[:, k, ht * 128:(ht + 1) * 128], rhs=hT_[:, k, :], start=(k == 0), stop=(k == 7))
                        return ins

                    def um(e, pu=pu, ht=ht):
                        for k in range(8):
                            ins = e.matmul(pu[:, 0:TB], lhsT=wfi[:, k, FF + ht * 128:FF + (ht + 1) * 128], rhs=hT_[:, k, :], start=(k == 0), stop=(k == 7))
                        return ins
                    op("pe", gm, reads=[wgb[ht // 4], hTb_], writes=[pgb])
                    op("pe", um, reads=[wgb[ht // 4], hTb_], writes=[pub])
                    op("act", lambda e, s_=s_, pg=pg: e.activation(out=s_[:], in_=pg[:, 0:TB], func=AF.Silu), reads=[pgb], writes=[sb_])
                    op("dve", lambda e, s_=s_, pu=pu, ht=ht: e.tensor_tensor(out=a_[:, ht, :], in0=pu[:, 0:TB], in1=s_[:], op=ALU.mult), reads=[pub, sb_], writes=[ab_])

            def ffn_out(tb, xts):
                a_, ab_ = aT[0]
                for ts in range(nsub):
                    xt, xb, r0 = xts[ts]
                    x2t, x2b = x2[cnt["nf"] % 2]
                    f = fs[cnt["nf"] % 2]
                    o_, ob_, od_ = ot[0]; cnt["nf"] += 1
                    for half in range(2):
                        po, pob = pO[cnt["no"] % 2]; cnt["no"] += 1

                        def om(e, po=po, ts=ts, half=half):
                            for k in range(NH):
                                ins = e.matmul(po[:], lhsT=a_[:, k, ts * 128:(ts + 1) * 128], rhs=wfo[:, k, half * 512:(half + 1) * 512], start=(k == 0), stop=(k == NH - 1))
                            return ins
                        op("pe", om, reads=[wb_, ab_], writes=[pob])
                        op("dve", lambda e, po=po, half=half, xt=xt, x2t=x2t: e.tensor_tensor(out=x2t[:, half * 512:(half + 1) * 512], in0=po[:], in1=xt[:, half * 512:(half + 1) * 512], op=ALU.add),
                           reads=[pob, xb], writes=[x2b])
                    op("act", lambda e, f=f, x2t=x2t, o_=o_: e.activation(out=o_[:], in_=x2t[:], func=AF.Square, accum_out=f["ss"][:]), reads=[x2b], writes=[f["b"], ob_])
                    op("pool", lambda e, f=f: e.tensor_scalar(out=f["rs"][:], in0=f["ss"][:], scalar1=1.0 / D, scalar2=1e-6, op0=ALU.mult, op1=ALU.add), reads=[f["b"]], writes=[f["b"]])
                    op("pool", lambda e, f=f: e.tensor_tensor(out=f["rr"][:], in0=f["rs"][:], in1=mhalf[:], op=ALU.pow), reads=[f["b"], cbuf], writes=[f["b"]])
                    op("dve", lambda e, f=f, x2t=x2t, o_=o_: e.scalar_tensor_tensor(out=o_[:], in0=x2t[:], scalar=f["rr"][:], in1=gains[:, 3, :], op0=ALU.mult, op1=ALU.mult),
                       reads=[x2b, f["b"], cbuf], writes=[ob_])
                    dma("sp", out_d[r0:r0 + 128, :], o_[:], od_, reads=[ob_], writes=[db["out"]])

            xts_cur = norm_in(0)
            for tb in range(NTB2):
                ffn_in(tb)
                xts_next = norm_in(tb + 1) if tb + 1 < NTB2 else None
                ffn_out(tb, xts_cur)
                xts_cur = xts_next
            B.barrier()

    if "AP" in phases:
        phase_AP(); B.barrier()
    if "B" in phases:
        phase_B(); B.barrier()
    gen = None
    if "D" in phases:
        phase_D1(); B.barrier()
        gen = rec_gen()
    if "C" in phases:
        phase_C(gen); B.barrier()
    if gen is not None:
        for _ in gen:
            pass
        B.barrier()
        phase_D2(); B.barrier()
    if "T1" in phases:
        phase_T1(); B.barrier()
    if "T2" in phases:
        phase_T2(); B.barrier()
    return nc, B


_CACHE = {}


def kernel(**inputs):
    consts = host_consts()
    in_maps = []
    for b in range(8):
        m = host_layout(inputs, b)
        m.update(consts)
        in_maps.append(m)
    if "nc" not in _CACHE:
        _CACHE["nc"] = build_program()[0]
    res = run_bass_kernel_spmd(_CACHE["nc"], in_maps, core_ids=list(range(8)))
    return np.stack([np.asarray(r["out"]) for r in res.results], axis=0).astype(np.float32)
```

```python
import math
from contextlib import ExitStack

import numpy as np

import concourse.bass as bass
import concourse.mybir as mybir
from concourse.bass_utils import run_bass_kernel_spmd

F32 = mybir.dt.float32
BF16 = mybir.dt.bfloat16
AF = mybir.ActivationFunctionType
ALU = mybir.AluOpType
AX = mybir.AxisListType

S = 4096
D = 1024
NTT = 32
NTB = 8
FF = 2816
NH = 22
NEG = -30000.0
LAM_INIT = 0.8 - 0.6 * math.exp(0.0)


class Buf:
    __slots__ = ("w", "r")

    def __init__(self):
        self.w = None
        self.r = {}


class Eng:
    def __init__(self, obj, sem, name):
        self.obj = obj
        self.sem = sem
        self.count = 0
        self.seen = {}
        self.name = name


class DS:
    def __init__(self, sem):
        self.sem = sem
        self.count = 0


class Builder:
    def __init__(self, nc, debug=False):
        self.nc = nc
        self.debug = debug
        self.es = ExitStack()
        self.E = {}
        for name, obj in (("pe", nc.tensor), ("act", nc.scalar), ("dve", nc.vector), ("pool", nc.gpsimd), ("sp", nc.sync)):
            self.E[name] = Eng(obj, self.es.enter_context(nc.semaphore("sem_" + name)), name)
        self.all_ds = []
        self.nname = 0

    def sb(self, shape, dt, stack=None, name=None):
        self.nname += 1
        return (stack or self.es).enter_context(self.nc.sbuf_tensor("%s_%d" % (name or "t", self.nname), list(shape), dt))

    def ps(self, shape, dt, stack=None, name=None):
        self.nname += 1
        return (stack or self.es).enter_context(self.nc.psum_tensor("%s_%d" % (name or "p", self.nname), list(shape), dt))

    def ds(self):
        self.nname += 1
        d = DS(self.es.enter_context(self.nc.semaphore("ds_%d" % self.nname)))
        self.all_ds.append(d)
        return d

    def _wait(self, E, ev):
        if ev is None:
            return
        sem, val = ev
        k = id(sem)
        if E.seen.get(k, 0) >= val:
            return
        E.obj.wait_ge(sem, val)
        E.seen[k] = val

    def _deps(self, E, reads, writes):
        own = E.sem
        pe = E.name == "pe"
        for b in reads:
            if b.w is not None and not (pe and b.w[0] is own):
                self._wait(E, b.w)
        for b in writes:
            if b.w is not None and not (pe and b.w[0] is own):
                self._wait(E, b.w)
            for ev in b.r.values():
                if not (pe and ev[0] is own):
                    self._wait(E, ev)

    def op(self, e, fn, reads=(), writes=()):
        E = self.E[e]
        self._deps(E, reads, writes)
        ins = fn(E.obj)
        E.count += 1
        ins.then_inc(E.sem, 1)
        ev = (E.sem, E.count)
        for b in reads:
            b.r[id(E.sem)] = ev
        for b in writes:
            b.w = ev
            b.r = {}

    def dma(self, q, out, in_, ds, reads=(), writes=()):
        E = self.E[q]
        self._deps(E, reads, writes)
        ins = E.obj.dma_start(out=out, in_=in_)
        ds.count += 16
        ins.then_inc(ds.sem, 16)
        ev = (ds.sem, ds.count)
        for b in reads:
            b.r[id(ds.sem)] = ev
        for b in writes:
            b.w = ev
            b.r = {}

    def barrier(self):
        for E in self.E.values():
            for Fo in self.E.values():
                if Fo.count > 0 and not (Fo is E and E.name == "pe"):
                    self._wait(E, (Fo.sem, Fo.count))
            for d in self.all_ds:
                if d.count > 0:
                    self._wait(E, (d.sem, d.count))


def _t5_bucket(rel):
    half, max_exact = 16, 8
    ret = np.where(rel > 0, half, 0)
    n = np.abs(rel)
    nf = np.maximum(n, 1).astype(np.float32)
    large = max_exact + (np.log(nf / np.float32(max_exact)) / np.float32(math.log(256 / max_exact)) * np.float32(half - max_exact)).astype(np.int32)
    large = np.minimum(large, half - 1)
    return ret + np.where(n < max_exact, n, large)


def host_consts():
    c = {}
    c["ident"] = np.eye(128, dtype=np.float32)
    c["antiid"] = np.eye(128, dtype=np.float32)[::-1].copy()
    ii = np.arange(1280)
    b = _t5_bucket(511 - ii)
    oh = np.zeros((32, 1280), np.float32)
    oh[b, ii] = 1.0
    c["onehot"] = oh
    p = np.arange(128)[:, None]
    j = np.arange(1152)[None, :]
    c["maskmb"] = np.where((p // 64) <= np.floor_divide(j - 384, 64), 0.0, NEG).astype(np.float32)
    r = np.arange(128)[:, None, None]
    it = np.arange(4)[None, :, None]
    col = np.arange(512)[None, None, :]
    c["maskd"] = ((col[:, 0, 0:128] // 32) >= (r[:, 0, :] // 32)).astype(np.float32)
    sel = np.zeros((128, 4, 128), np.float32)
    for rr in range(128):
        for il in range(4):
            sel[rr, il, 32 * il + rr % 32] = 1.0
    c["sel"] = sel
    return c


def host_layout(inp, b):
    f = np.float32
    m = {}
    m["x"] = np.ascontiguousarray(inp["x"][b])
    m["mem"] = np.ascontiguousarray(inp["mem"][b])
    for k in ("w_in", "glu_w", "w_mem_kv", "w_br_attn", "w_br_ssm", "w_br_xattn", "w_out", "w_ffn_in", "w_ffn_out"):
        m[k] = np.ascontiguousarray(inp[k][0])
    gb = np.stack([np.broadcast_to(inp["norm1_g"][0], (128, D)), np.broadcast_to(inp["mem_norm_g"][0], (128, D)),
                   np.broadcast_to(inp["norm2_g"][0], (128, D)), np.broadcast_to(inp["final_g"], (128, D))], axis=1)
    m["gains"] = np.ascontiguousarray(gb, dtype=f)
    m["subg"] = np.ascontiguousarray(np.broadcast_to(inp["da_subln_g"][0], (128, 128)), dtype=f)
    lqk = np.stack([inp["da_lq1"][0], inp["da_lk1"][0], inp["da_lq2"][0], inp["da_lk2"][0]], axis=0)
    m["lqk"] = np.ascontiguousarray(np.broadcast_to(lqk, (128, 4, 64)), dtype=f)
    m["relb"] = np.ascontiguousarray(inp["rel_bias"], dtype=f)
    m["relb15"] = np.ascontiguousarray(np.broadcast_to(inp["rel_bias"][15], (128, 8)), dtype=f)

    def st(a):
        return np.ascontiguousarray(a.reshape(32, 2, 64).transpose(1, 2, 0).reshape(128, 32), dtype=f)
    m["s_are"] = st(inp["ssm_a_re"][0])
    m["s_aim"] = st(inp["ssm_a_im"][0])
    m["s_ldt"] = st(np.broadcast_to(inp["ssm_log_dt"][0][:, None], (64, 64)))
    def stb(a):
        return np.ascontiguousarray(a.reshape(32, 2, 64, 16).transpose(1, 2, 0, 3).reshape(128, 32, 16), dtype=f)
    m["s_bre"] = stb(inp["ssm_b_re"][0])
    m["s_bim"] = stb(inp["ssm_b_im"][0])
    def stc(a):
        return np.ascontiguousarray(a.reshape(32, 2, 16, 64).transpose(1, 3, 0, 2).reshape(128, 32, 16), dtype=f)
    m["s_cre"] = stc(inp["ssm_c_re"][0])
    m["s_cim"] = stc(inp["ssm_c_im"][0])
    m["s_d"] = np.ascontiguousarray(inp["ssm_d"][0].reshape(8, 128).T, dtype=f)
    m["glu_b"] = np.ascontiguousarray(inp["glu_b"][0].reshape(8, 128).T, dtype=f)
    return m


INPUT_SHAPES = {
    "x": [S, D], "mem": [256, D], "w_in": [D, 8192], "glu_w": [D, D], "w_mem_kv": [D, 2048], "w_br_attn": [D, D],
    "w_br_ssm": [D, D], "w_br_xattn": [D, D], "w_out": [D, D], "w_ffn_in": [D, 2 * FF], "w_ffn_out": [FF, D],
    "gains": [128, 4, D], "subg": [128, 128], "lqk": [128, 4, 64], "relb": [32, 8], "relb15": [128, 8],
    "s_are": [128, 32], "s_aim": [128, 32], "s_ldt": [128, 32], "s_bre": [128, 32, 16], "s_bim": [128, 32, 16],
    "s_cre": [128, 32, 16], "s_cim": [128, 32, 16], "s_d": [128, 8], "glu_b": [128, 8],
    "ident": [128, 128], "antiid": [128, 128], "onehot": [32, 1280], "maskmb": [128, 1152], "maskd": [128, 128],
    "sel": [128, 4, 128],
}


ALL_PHASES = ("AP", "B", "C", "D", "T1", "T2")


def build_program(debug=False, upto="all", phases=ALL_PHASES):
    nc = bass.Bass("TRN2", target_bir_lowering=False)
    B = Builder(nc, debug)
    I = {k: nc.dram_tensor(k, shp, F32, kind="ExternalInput").ap() for k, shp in INPUT_SHAPES.items()}
    out_d = nc.dram_tensor("out", [S, D], F32, kind="ExternalOutput").ap()
    def scratch(name, shape, dt, producer=None):
        if producer is not None and producer not in phases:
            kind = "ExternalInput"
        else:
            kind = "ExternalOutput" if debug else "Internal"
        return nc.dram_tensor(name, shape, dt, kind=kind)

    QTs = scratch("QTs", [D, S], BF16, "AP")
    KTs = scratch("KTs", [D, S], BF16, "AP")
    Vs = scratch("Vs", [S, D], BF16, "AP")
    UTs = scratch("UTs", [D, S], BF16, "AP")
    XQs = scratch("XQs", [D, S], BF16, "AP")
    GTs = scratch("GTs", [3 * D, S], BF16, "AP")
    YAs = scratch("YAs", [D, S], BF16, "B")
    YXs = scratch("YXs", [D, S], BF16, "C")
    ZTs = scratch("ZTs", [D, S], BF16, "D")
    YIs = scratch("YIs", [D, S], BF16)
    CXs = scratch("CXs", [32, 128, 1024], BF16)
    X1s = scratch("X1s", [S, D], F32, "T1")
    Gd = scratch("Gd", [8, 1280], F32)
    db = {k: Buf() for k in ("QTs", "KTs", "Vs", "UTs", "XQs", "GTs", "YAs", "YXs", "ZTs", "YIs", "CXs", "X1s", "Gd", "out")}

    op, dma = B.op, B.dma
    es = B.es

    def fv(apobj, dims):
        return bass.AP(apobj.tensor, apobj.offset, [list(apobj.ap[0])] + [list(d) for d in dims])

    ident_f = B.sb([128, 128], F32); ident_b = B.sb([128, 128], BF16)
    gains = B.sb([128, 4, D], F32)
    eps_t = B.sb([128, 1], F32)
    eps2_t = B.sb([128, 1], F32)
    cbuf = Buf()
    cds = B.ds()
    dma("sp", ident_f[:], I["ident"], cds, writes=[cbuf])
    dma("sp", gains[:], I["gains"], cds, writes=[cbuf])
    op("dve", lambda e: e.tensor_copy(out=ident_b[:], in_=ident_f[:]), reads=[cbuf], writes=[cbuf])
    op("dve", lambda e: e.memset(eps_t[:], 1e-6), writes=[cbuf])
    op("dve", lambda e: e.memset(eps2_t[:], 1e-6 / 0.64), writes=[cbuf])
    mhalf = B.sb([128, 1], F32)
    op("dve", lambda e: e.memset(mhalf[:], -0.5), writes=[cbuf])
    mhalf4 = B.sb([128, 4], F32)
    op("dve", lambda e: e.memset(mhalf4[:], -0.5), writes=[cbuf])

    def rms_T(src, src_buf, gidx, dstT, dst_buf, sl, psT, psT_buf, evac="dve"):
        op("act", lambda e: e.activation(out=sl["hb"][:], in_=src, func=AF.Square, accum_out=sl["ss"][:]),
           reads=[src_buf], writes=[sl["hbb"], sl["ssb"]])
        op("pool", lambda e: e.tensor_scalar(out=sl["rs"][:], in0=sl["ss"][:], scalar1=1.0 / D, scalar2=1e-6, op0=ALU.mult, op1=ALU.add),
           reads=[sl["ssb"]], writes=[sl["rsb"]])
        op("pool", lambda e: e.tensor_tensor(out=sl["rr"][:], in0=sl["rs"][:], in1=mhalf[:], op=ALU.pow), reads=[sl["rsb"], cbuf], writes=[sl["rrb"]])
        op("dve", lambda e: e.scalar_tensor_tensor(out=sl["hb"][:], in0=src, scalar=sl["rr"][:], in1=gains[:, gidx, :],
                                                   op0=ALU.mult, op1=ALU.mult),
           reads=[src_buf, sl["rrb"], cbuf], writes=[sl["hbb"]])

        def tr(e):
            for k in range(8):
                ins = e.transpose(out=psT[:, k, :], in_=sl["hb"][:, k * 128:(k + 1) * 128], identity=ident_b[:])
            return ins
        op("pe", tr, reads=[sl["hbb"], cbuf], writes=[psT_buf])
        op(evac, lambda e: (e.tensor_copy(out=dstT, in_=psT[:]) if evac == "dve" else e.copy(out=dstT, in_=psT[:])),
           reads=[psT_buf], writes=[dst_buf])

    def rms_slot(stack):
        return {"hb": B.sb([128, D], BF16, stack), "ss": B.sb([128, 1], F32, stack), "rs": B.sb([128, 1], F32, stack),
                "rr": B.sb([128, 1], F32, stack), "hbb": Buf(), "ssb": Buf(), "rsb": Buf(), "rrb": Buf()}

    def phase_AP():
        with ExitStack() as st:
            hT = B.sb([128, 8, S], BF16, st, "hT")
            hTb = [Buf() for _ in range(NTT)]
            psT = B.ps([128, 8, 128], BF16, st)
            psTb = Buf()
            wsl = [(B.sb([128, 8, 512], BF16, st), Buf(), B.ds()) for _ in range(2)]
            for cb in range(2):
                dma("pool", wsl[cb][0][:], I["w_in"][:, cb * 512:(cb + 1) * 512].rearrange("(k p) n -> p k n", p=128), wsl[cb][2], writes=[wsl[cb][1]])
            with ExitStack() as st2:
                xs = [(B.sb([128, D], F32, st2), Buf(), B.ds()) for _ in range(2)]
                sls = [rms_slot(st2) for _ in range(2)]
                for tt in range(NTT):
                    xt, xb, xd = xs[tt % 2]
                    dma("sp", xt[:], I["x"][tt * 128:(tt + 1) * 128, :], xd, writes=[xb])
                    rms_T(xt[:], xb, 0, hT[:, :, tt * 128:(tt + 1) * 128], hTb[tt], sls[tt % 2], psT, psTb,
                          evac=("dve" if tt % 2 == 0 else "act"))
                B.barrier()
            if upto == "A":
                return
            with ExitStack() as st2:
                pbank = [(B.ps([128, 512], F32, st2), Buf()) for _ in range(4)]
                stg = [(B.sb([128, S], BF16, st2), Buf(), B.ds()) for _ in range(2)]
                vst = [(B.sb([128, 512], BF16, st2), Buf(), B.ds()) for _ in range(3)]
                hT_all = hTb
                npb = 0
                nst = 0
                nv = 0
                for cb in range(16):
                    wt, wb, wd = wsl[cb % 2]
                    if cb >= 2:
                        dma("pool", wt[:], I["w_in"][:, cb * 512:(cb + 1) * 512].rearrange("(k p) n -> p k n", p=128), wd, writes=[wb])
                    if 4 <= cb < 6:
                        for tt in range(NTT):
                            pt, pb = pbank[npb % 4]; npb += 1

                            def mm(e, pt=pt, tt=tt, wt=wt):
                                for k in range(8):
                                    ins = e.matmul(pt[:], lhsT=hT[:, k, tt * 128:(tt + 1) * 128], rhs=wt[:, k, :], start=(k == 0), stop=(k == 7))
                                return ins
                            op("pe", mm, reads=[wb, hT_all[tt]], writes=[pb])
                            vt, vb, vd = vst[nv % 3]; nv += 1
                            eng = "dve" if tt % 2 == 0 else "act"
                            op(eng, lambda e, vt=vt, pt=pt, eng=eng: (e.tensor_copy(out=vt[:], in_=pt[:]) if eng == "dve" else e.copy(out=vt[:], in_=pt[:])),
                               reads=[pb], writes=[vb])
                            dma("sp", Vs.ap()[tt * 128:(tt + 1) * 128, (cb - 4) * 512:(cb - 3) * 512], vt[:], vd, reads=[vb], writes=[db["Vs"]])
                        continue
                    for ct in range(4):
                        gcol = cb * 512 + ct * 128
                        sg, sgb, sgd = stg[nst % 2]; nst += 1
                        for tb in range(NTB):
                            pt, pb = pbank[npb % 4]; npb += 1

                            def mm(e, pt=pt, tb=tb, wt=wt, ct=ct):
                                for k in range(8):
                                    ins = e.matmul(pt[:], lhsT=wt[:, k, ct * 128:(ct + 1) * 128], rhs=hT[:, k, tb * 512:(tb + 1) * 512], start=(k == 0), stop=(k == 7))
                                return ins
                            op("pe", mm, reads=[wb] + hT_all[tb * 4:tb * 4 + 4], writes=[pb])
                            dst = sg[:, tb * 512:(tb + 1) * 512]
                            if gcol < 1024:
                                op("act", lambda e, dst=dst, pt=pt: e.mul(out=dst, in_=pt[:], mul=0.125), reads=[pb], writes=[sgb])
                            elif gcol < 2048:
                                op("dve", lambda e, dst=dst, pt=pt: e.tensor_copy(out=dst, in_=pt[:]), reads=[pb], writes=[sgb])
                            elif gcol < 4096:
                                eng = "dve" if tb % 2 == 0 else "act"
                                op(eng, lambda e, dst=dst, pt=pt, eng=eng: (e.tensor_copy(out=dst, in_=pt[:]) if eng == "dve" else e.copy(out=dst, in_=pt[:])),
                                   reads=[pb], writes=[sgb])
                            elif gcol < 5120:
                                op("act", lambda e, dst=dst, pt=pt: e.mul(out=dst, in_=pt[:], mul=0.0625), reads=[pb], writes=[sgb])
                            else:
                                op("act", lambda e, dst=dst, pt=pt: e.activation(out=dst, in_=pt[:], func=AF.Sigmoid), reads=[pb], writes=[sgb])
                        if gcol < 1024:
                            dd, dbuf, r0 = QTs, db["QTs"], gcol
                        elif gcol < 2048:
                            dd, dbuf, r0 = KTs, db["KTs"], gcol - 1024
                        elif gcol < 4096:
                            dd, dbuf, r0 = UTs, db["UTs"], gcol - 3072
                        elif gcol < 5120:
                            dd, dbuf, r0 = XQs, db["XQs"], gcol - 4096
                        else:
                            dd, dbuf, r0 = GTs, db["GTs"], gcol - 5120
                        dma("sp", dd.ap()[r0:r0 + 128, :], sg[:], sgd, reads=[sgb], writes=[dbuf])
                B.barrier()

    def phase_B():
        with ExitStack() as st:
            lqk = B.sb([128, 4, 64], F32, st)
            lpr = B.sb([128, 2, 64], F32, st)
            lsum = B.sb([128, 2], F32, st)
            lexp = B.sb([128, 2], F32, st)
            neglam = B.sb([128, 1], F32, st)
            relb15 = B.sb([128, 8], F32, st)
            subg = B.sb([128, 128], F32, st)
            lb = Buf(); lds = B.ds()
            dma("sp", lqk[:], I["lqk"], lds, writes=[lb])
            dma("sp", relb15[:], I["relb15"], lds, writes=[lb])
            dma("sp", subg[:], I["subg"], lds, writes=[lb])
            lqv = lqk[:].rearrange("p (a b) d -> p a b d", b=2)
            op("dve", lambda e: e.tensor_tensor(out=lpr[:], in0=lqv[:, :, 0, :], in1=lqv[:, :, 1, :], op=ALU.mult), reads=[lb], writes=[lb])
            op("dve", lambda e: e.reduce_sum(out=lsum[:], in_=lpr[:], axis=AX.X), reads=[lb], writes=[lb])
            op("act", lambda e: e.activation(out=lexp[:], in_=lsum[:], func=AF.Exp), reads=[lb], writes=[lb])
            op("dve", lambda e: e.tensor_tensor(out=neglam[:], in0=lexp[:, 1:2], in1=lexp[:, 0:1], op=ALU.subtract), reads=[lb], writes=[lb])
            op("dve", lambda e: e.tensor_scalar(out=neglam[:], in0=neglam[:], scalar1=-LAM_INIT, scalar2=None, op0=ALU.add), reads=[lb], writes=[lb])

            MB = B.sb([128, 8, 1152], BF16, st, "MB")
            mbb = Buf()
            with ExitStack() as st2:
                relb = B.sb([32, 8], F32, st2)
                onehot = B.sb([32, 1280], F32, st2)
                antiid = B.sb([128, 128], F32, st2)
                maskmb = B.sb([128, 1152], F32, st2)
                gsb = B.sb([8, 1280], F32, st2)
                hk = [(B.sb([128, 1152], F32, st2), Buf(), B.ds()) for _ in range(2)]
                tb_ = Buf(); tds = B.ds()
                dma("sp", relb[:], I["relb"], tds, writes=[tb_])
                dma("sp", onehot[:], I["onehot"], tds, writes=[tb_])
                dma("sp", antiid[:], I["antiid"], tds, writes=[tb_])
                dma("sp", maskmb[:], I["maskmb"], tds, writes=[tb_])
                pg = [(B.ps([128, 512], F32, st2), Buf()) for _ in range(3)]
                for n in range(3):
                    n0, n1 = n * 512, min(1280, (n + 1) * 512)
                    op("pe", lambda e, n=n, n0=n0, n1=n1: e.matmul(pg[n][0][0:8, 0:n1 - n0], lhsT=relb[:], rhs=onehot[:, n0:n1], start=True, stop=True),
                       reads=[tb_], writes=[pg[n][1]])
                    op("dve", lambda e, n=n, n0=n0, n1=n1: e.tensor_copy(out=gsb[:, n0:n1], in_=pg[n][0][0:8, 0:n1 - n0]), reads=[pg[n][1]], writes=[tb_])
                gdd = B.ds()
                dma("sp", Gd.ap(), gsb[:], gdd, reads=[tb_], writes=[db["Gd"]])
                for h in range(8):
                    ht, hb_, hd = hk[h % 2]
                    dma("sp", ht[:], bass.AP(Gd, h * 1280, [[1, 128], [1, 1152]]), hd, reads=[db["Gd"]], writes=[hb_])
                    for n in range(3):
                        n0, n1 = n * 512, min(1152, (n + 1) * 512)
                        op("pe", lambda e, n=n, n0=n0, n1=n1, ht=ht: e.matmul(pg[n][0][:, 0:n1 - n0], lhsT=antiid[:], rhs=ht[:, n0:n1], start=True, stop=True),
                           reads=[tb_, hb_], writes=[pg[n][1]])
                        op("dve", lambda e, n=n, n0=n0, n1=n1, h=h: e.scalar_tensor_tensor(out=MB[:, h, n0:n1], in0=pg[n][0][:, 0:n1 - n0], scalar=relb15[:, h:h + 1],
                                                                                             in1=maskmb[:, n0:n1], op0=ALU.subtract, op1=ALU.add),
                           reads=[pg[n][1], tb_, lb], writes=[mbb])
                B.barrier()

            sets = []
            for s_ in range(2):
                sets.append({"QT": B.sb([128, S], BF16, st), "KT": B.sb([128, S], BF16, st), "V": B.sb([128, NTT, 129], BF16, st),
                             "b": Buf(), "ds": B.ds()})
            for s_ in sets:
                op("dve", lambda e, s_=s_: e.memset(s_["V"][:, :, 128:129], 1.0), writes=[s_["b"]])
            sc = [(B.ps([128, 2, 512], F32, st), Buf()) for _ in range(2)]
            ob = [(B.ps([128, 512], F32, st), Buf()) for _ in range(3)]
            psT = B.ps([128, 8, 128], BF16, st)
            psTb = Buf()
            NPT = 4
            PT = [(B.sb([128, 2, 512], BF16, st), Buf()) for _ in range(NPT)]
            oc = [(B.sb([128, 3, 387], F32, st), Buf()) for _ in range(2)]
            fin = [{"rr": B.sb([128, 8], F32, st), "t1": B.sb([128, 4, 128], F32, st), "ot": B.sb([128, 4, 128], F32, st),
                    "ss": B.sb([128, 4], F32, st), "rs": B.sb([128, 4], F32, st), "r2": B.sb([128, 4], F32, st),
                    "yb": B.sb([128, 4, 128], BF16, st), "b": Buf()} for _ in range(3)]
            ystg = [(B.sb([128, 512], BF16, st), Buf(), B.ds()) for _ in range(3)]
            state = {"nfin": 0, "nys": 0, "noc": 0}
            pending = []

            def tick():
                for p_ in pending:
                    p_[0] -= 1
                while pending and pending[0][0] <= 0:
                    pending.pop(0)[1]()

            def oreg(r):
                return r // 3, (r % 3) * 129

            blocks = []
            for h in range(8):
                for j in range(NTB):
                    nkt = 4 * (j + 1)
                    for kt in range(nkt):
                        blocks.append((h, j, kt, nkt))

            def load_head(h):
                hs = sets[h % 2]
                dma("sp", hs["QT"][:], QTs.ap()[h * 128:(h + 1) * 128, :], hs["ds"], reads=[db["QTs"]], writes=[hs["b"]])
                dma("sp", hs["KT"][:], KTs.ap()[h * 128:(h + 1) * 128, :], hs["ds"], reads=[db["KTs"]], writes=[hs["b"]])
                dma("sp", hs["V"][:, :, 0:128], Vs.ap()[:, h * 128:(h + 1) * 128].rearrange("(t p) e -> p t e", p=128), hs["ds"],
                    reads=[db["Vs"]], writes=[hs["b"]])

            def emit_scores(n):
                h, j, kt, nkt = blocks[n]
                hs = sets[h % 2]
                m = max(0, kt - 4 * j)
                qlo = 128 * m
                near = kt >= 4 * j - 2
                off = 512 * j - 128 * kt + 384
                pt, pb = sc[n % 2]

                def smm(e):
                    for c in range(2):
                        ins = e.matmul(pt[:, c, qlo:512], lhsT=hs["KT"][64 * c:64 * c + 64, kt * 128:(kt + 1) * 128],
                                       rhs=hs["QT"][64 * c:64 * c + 64, j * 512 + qlo:(j + 1) * 512], start=True, stop=not near)
                    if near:
                        for c in range(2):
                            ins = e.matmul(pt[:, c, qlo:512], lhsT=ident_b[:], rhs=MB[:, h, off + qlo:off + 512], start=False, stop=True)
                    return ins
                op("pe", smm, reads=[hs["b"], mbb, cbuf], writes=[pb])
                ptile, ptb = PT[n % NPT]
                if near:
                    op("act", lambda e: e.activation(out=ptile[:, :, qlo:512], in_=pt[:, :, qlo:512], func=AF.Exp), reads=[pb], writes=[ptb])
                else:
                    op("act", lambda e: e.activation(out=ptile[:], in_=pt[:], func=AF.Exp), reads=[pb], writes=[ptb])

            def emit_av(n):
                h, j, kt, nkt = blocks[n]
                hs = sets[h % 2]
                m = max(0, kt - 4 * j)
                ptile, ptb = PT[n % NPT]

                def avmm(e):
                    for c in range(2):
                        for qs in range(m, 4):
                            bank, co = oreg(c * 4 + qs)
                            first = (kt == 0) and ((c * 4 + qs) % 3 == 0)
                            ins = e.matmul(ob[bank][0][:, co:co + 129], lhsT=ptile[:, c, qs * 128:(qs + 1) * 128], rhs=hs["V"][:, kt, :],
                                           start=first, stop=(kt == 4 * j + qs), skip_group_check=True)
                    return ins
                op("pe", avmm, reads=[ptb, hs["b"]], writes=[ob[0][1], ob[1][1], ob[2][1]])
                if kt == nkt - 1:
                    finalize(h, j)

            def finalize(h, j):
                o_, ocb = oc[state["noc"] % 2]; state["noc"] += 1
                for bk in range(3):
                    w_ = 387 if bk < 2 else 258
                    op("dve", lambda e, bk=bk, w_=w_: e.tensor_copy(out=o_[:, bk, 0:w_], in_=ob[bk][0][:, 0:w_]), reads=[ob[bk][1]], writes=[ocb])
                ys, ysb, ysd = ystg[state["nys"] % 3]; state["nys"] += 1
                f = fin[state["nfin"] % 3]; state["nfin"] += 1
                reg = o_[:].rearrange("p a b -> p (a b)")[:, 0:1032].rearrange("p (r c) -> p r c", c=129)
                fb = f["b"]
                op("dve", lambda e: e.reciprocal(out=f["rr"][:], in_=reg[:, :, 128]), reads=[ocb], writes=[fb])
                op("dve", lambda e: e.tensor_scalar(out=f["rr"][:, 4:8], in0=f["rr"][:, 4:8], scalar1=neglam[:, 0:1], scalar2=None, op0=ALU.mult), reads=[fb, lb], writes=[fb])
                op("dve", lambda e: e.tensor_tensor(out=f["t1"][:], in0=reg[:, 4:8, 0:128], in1=fv(f["rr"][:, 4:5], [[1, 4], [0, 128]]), op=ALU.mult), reads=[ocb, fb], writes=[fb])
                op("dve", lambda e: e.tensor_tensor(out=f["ot"][:], in0=reg[:, 0:4, 0:128], in1=fv(f["rr"][:, 0:1], [[1, 4], [0, 128]]), op=ALU.mult), reads=[ocb, fb], writes=[fb])
                op("dve", lambda e: e.tensor_tensor(out=f["ot"][:], in0=f["ot"][:], in1=f["t1"][:], op=ALU.add), reads=[fb], writes=[fb])
                op("dve", lambda e: e.tensor_tensor(out=f["t1"][:], in0=f["ot"][:], in1=f["ot"][:], op=ALU.mult), reads=[fb], writes=[fb])
                op("dve", lambda e: e.reduce_sum(out=f["ss"][:], in_=f["t1"][:], axis=AX.X), reads=[fb], writes=[fb])
                op("pool", lambda e: e.tensor_scalar(out=f["rs"][:], in0=f["ss"][:], scalar1=1.0 / (128 * 0.64), scalar2=1e-6 / 0.64, op0=ALU.mult, op1=ALU.add),
                   reads=[fb], writes=[fb])
                op("pool", lambda e: e.tensor_tensor(out=f["r2"][:], in0=f["rs"][:], in1=mhalf4[:], op=ALU.pow), reads=[fb, cbuf], writes=[fb])
                op("dve", lambda e: e.tensor_tensor(out=f["t1"][:], in0=f["ot"][:], in1=fv(f["r2"][:, 0:1], [[1, 4], [0, 128]]), op=ALU.mult), reads=[fb], writes=[fb])
                op("dve", lambda e: e.tensor_tensor(out=f["yb"][:], in0=f["t1"][:], in1=fv(subg[:, 0:1], [[0, 4], [1, 128]]), op=ALU.mult), reads=[fb, lb], writes=[fb])

                def later():
                    def tr(e):
                        for qs in range(4):
                            ins = e.transpose(out=psT[:, qs, :], in_=f["yb"][:, qs, :], identity=ident_b[:])
                        return ins
                    op("pe", tr, reads=[fb, cbuf], writes=[psTb])
                    op("dve", lambda e: e.tensor_copy(out=ys[:].rearrange("p (a b) -> p a b", a=4), in_=psT[:, 0:4, :]), reads=[psTb], writes=[ysb])
                    dma("sp", YAs.ap()[h * 128:(h + 1) * 128, j * 512:(j + 1) * 512], ys[:], ysd, reads=[ysb], writes=[db["YAs"]])
                pending.append([10, later])

            NBLK = len(blocks)
            load_head(0)
            for n in range(NBLK + 2):
                if n < NBLK:
                    h, j, kt, nkt = blocks[n]
                    if j == 0 and kt == 2 and h + 1 < 8:
                        load_head(h + 1)
                    emit_scores(n)
                if n >= 2:
                    emit_av(n - 2)
                tick()
            while pending:
                pending.pop(0)[1]()
            B.barrier()

    def phase_C(gen=None):
        def pull(n):
            if gen is not None:
                for _ in range(n):
                    next(gen, None)

        with ExitStack() as st:
            memnT = B.sb([128, 8, 256], BF16, st)
            mnb = Buf()
            KxT = B.sb([128, 8, 256], BF16, st)
            Vx = B.sb([128, 2, 4, 257], BF16, st)
            kvb = Buf()
            psT = B.ps([128, 8, 128], BF16, st)
            psTb = Buf()
            with ExitStack() as st2:
                pk = [(B.ps([128, 512], F32, st2), Buf()) for _ in range(2)]
                wkv = B.sb([128, 8, 2048], BF16, st2)
                wkb = Buf(); wkd = B.ds()
                for n in range(4):
                    dma("pool", wkv[:, :, n * 512:(n + 1) * 512], I["w_mem_kv"][:, n * 512:(n + 1) * 512].rearrange("(k p) n -> p k n", p=128), wkd, writes=[wkb])
                ms = [(B.sb([128, D], F32, st2), Buf(), B.ds()) for _ in range(2)]
                sls = [rms_slot(st2) for _ in range(2)]
                for mt in range(2):
                    xt, xb, xd = ms[mt]
                    dma("sp", xt[:], I["mem"][mt * 128:(mt + 1) * 128, :], xd, writes=[xb])
                    rms_T(xt[:], xb, 1, memnT[:, :, mt * 128:(mt + 1) * 128], mnb, sls[mt], psT, psTb)
                op("dve", lambda e: e.memset(Vx[:, :, :, 256:257], 1.0), writes=[kvb])
                for ct in range(8):
                    pt, pb = pk[ct % 2]

                    def mm(e, pt=pt, ct=ct):
                        for k in range(8):
                            ins = e.matmul(pt[:, 0:256], lhsT=wkv[:, k, ct * 128:(ct + 1) * 128], rhs=memnT[:, k, :], start=(k == 0), stop=(k == 7))
                        return ins
                    op("pe", mm, reads=[wkb, mnb], writes=[pb])
                    op("dve", lambda e, pt=pt, ct=ct: e.tensor_copy(out=KxT[:, ct, :], in_=pt[:, 0:256]), reads=[pb], writes=[kvb])
                    pull(3)
                n = 0
                for mt in range(2):
                    for half in range(2):
                        pt, pb = pk[n % 2]; n += 1

                        def mm(e, pt=pt, mt=mt, half=half):
                            for k in range(8):
                                ins = e.matmul(pt[:], lhsT=memnT[:, k, mt * 128:(mt + 1) * 128], rhs=wkv[:, k, 1024 + half * 512:1024 + (half + 1) * 512], start=(k == 0), stop=(k == 7))
                            return ins
                        op("pe", mm, reads=[wkb, mnb], writes=[pb])
                        op("dve", lambda e, pt=pt, mt=mt, half=half: e.tensor_copy(out=Vx[:, mt, 2 * half:2 * half + 2, 0:256], in_=pt[:].rearrange("p (a b) -> p a b", a=2)),
                           reads=[pb], writes=[kvb])
                B.barrier()
            xqs = [(B.sb([128, 2, S], BF16, st), Buf(), B.ds()) for _ in range(2)]
            psS = [(B.ps([128, 512], F32, st), Buf()) for _ in range(2)]
            psO = (B.ps([128, 4, 512], F32, st), Buf())
            PX = [[(B.sb([128, 512], BF16, st), Buf()) for _ in range(2)] for _ in range(2)]
            fx = [{"rr": B.sb([128, 4], F32, st), "yb": B.sb([128, 4, 256], BF16, st), "b": Buf()} for _ in range(2)]
            ystg = [(B.sb([128, 2, 512], BF16, st), Buf(), B.ds()) for _ in range(2)]

            def load_xq(hx):
                xq, xqb, xqd = xqs[hx % 2]
                dma("sp", xq[:], XQs.ap()[hx * 256:(hx + 1) * 256, :].rearrange("(a p) t -> p a t", p=128), xqd, reads=[db["XQs"]], writes=[xqb])

            its = [(hx, tb) for hx in range(4) for tb in range(NTB)]

            def emit_S(n):
                hx, tb = its[n]
                xq, xqb, xqd = xqs[hx % 2]
                slot = n % 2
                for mt in range(2):
                    pt, pb = psS[mt]

                    def smm(e, pt=pt, mt=mt):
                        for dt in range(2):
                            ins = e.matmul(pt[:], lhsT=KxT[:, 2 * hx + dt, mt * 128:(mt + 1) * 128], rhs=xq[:, dt, tb * 512:(tb + 1) * 512], start=(dt == 0), stop=(dt == 1))
                        return ins
                    op("pe", smm, reads=[kvb, xqb], writes=[pb])
                    px, pxb = PX[mt][slot]
                    op("act", lambda e, px=px, pt=pt: e.activation(out=px[:], in_=pt[:], func=AF.Exp), reads=[pb], writes=[pxb])

            def emit_O(n):
                hx, tb = its[n]
                slot = n % 2
                po, pob = psO
                f = fx[n % 2]

                def omm(e):
                    for qs in range(4):
                        for mt in range(2):
                            ins = e.matmul(po[:, qs, 0:257], lhsT=PX[mt][slot][0][:, qs * 128:(qs + 1) * 128], rhs=Vx[:, mt, hx, :], start=(mt == 0), stop=(mt == 1))
                    return ins
                op("pe", omm, reads=[PX[0][slot][1], PX[1][slot][1], kvb], writes=[pob])
                op("dve", lambda e: e.reciprocal(out=f["rr"][:], in_=po[:, :, 256]), reads=[pob], writes=[f["b"]])
                op("dve", lambda e: e.tensor_tensor(out=f["yb"][:], in0=po[:, :, 0:256], in1=fv(f["rr"][:, 0:1], [[1, 4], [0, 256]]), op=ALU.mult), reads=[pob, f["b"]], writes=[f["b"]])

            def emit_T(n):
                hx, tb = its[n]
                f = fx[n % 2]
                ys, ysb, ysd = ystg[n % 2]

                def tr(e):
                    for qs in range(4):
                        for dt in range(2):
                            ins = e.transpose(out=psT[:, dt * 4 + qs, :], in_=f["yb"][:, qs, dt * 128:(dt + 1) * 128], identity=ident_b[:])
                    return ins
                op("pe", tr, reads=[f["b"], cbuf], writes=[psTb])
                op("act", lambda e: e.copy(out=ys[:].rearrange("p a (q t) -> p (a q) t", q=4), in_=psT[:]), reads=[psTb], writes=[ysb])
                dma("sp", YXs.ap()[hx * 256:(hx + 1) * 256, tb * 512:(tb + 1) * 512].rearrange("(a p) t -> p a t", p=128), ys[:], ysd, reads=[ysb], writes=[db["YXs"]])

            load_xq(0)
            for n in range(len(its)):
                hx, tb = its[n]
                if tb == 0 and hx + 1 < 4:
                    load_xq(hx + 1)
                emit_S(n)
                if n >= 1:
                    emit_T(n - 1)
                emit_O(n)
                pull(8)
            emit_T(len(its) - 1)
            B.barrier()

    Dst = {}

    def phase_D1():
        st = ExitStack()
        Dst["st"] = st
        if True:
            WW = B.sb([128, 2, 32, 256], F32, st, "WW")
            wwb = Buf()
            AA1 = B.sb([128, 2, 32], F32, st)
            AA2 = B.sb([128, 2, 32], F32, st)
            sd = B.sb([128, 8], F32, st)
            tb_ = Buf(); tds = B.ds()
            r1_ = B.sb([128, 2, 32], F32, st); r2_ = B.sb([128, 2, 32], F32, st)
            Dst.update(WW=WW, wwb=wwb, AA1=AA1, AA2=AA2, tb_=tb_, r1=r1_, r2=r2_)
            dma("sp", sd[:], I["s_d"], tds, writes=[tb_])
            with ExitStack() as st1:
                are = B.sb([128, 32], F32, st1); aim = B.sb([128, 32], F32, st1); ldt = B.sb([128, 32], F32, st1)
                bre = B.sb([128, 32, 16], F32, st1); bim = B.sb([128, 32, 16], F32, st1)
                cre = B.sb([128, 32, 16], F32, st1); cim = B.sb([128, 32, 16], F32, st1)
                for t_, k_ in ((are, "s_are"), (aim, "s_aim"), (ldt, "s_ldt"), (bre, "s_bre"), (bim, "s_bim"), (cre, "s_cre"), (cim, "s_cim")):
                    dma("sp", t_[:], I[k_], tds, writes=[tb_])
                sel_f = B.sb([128, 4, 128], F32, st1); sel_b = B.sb([128, 4, 128], BF16, st1)
                maskd = B.sb([128, 128], F32, st1)
                dma("sp", sel_f[:], I["sel"], tds, writes=[tb_])
                dma("sp", maskd[:], I["maskd"], tds, writes=[tb_])
                T = lambda shape: B.sb(shape, F32, st1)
                dtt = T([128, 32]); adr = T([128, 32]); th = T([128, 32]); cc = T([128, 32]); ss_ = T([128, 32])
                t1 = T([128, 32]); t2 = T([128, 32]); hpi = T([128, 1])
                ER = T([128, 32, 17]); EI = T([128, 32, 17]); MAGP = T([128, 32, 17]); MAGN = T([128, 32, 17])
                PPr = T([128, 32, 17]); PPi = T([128, 32, 17]); PNr = T([128, 32, 17]); PNi = T([128, 32, 17])
                bbr = T([128, 32, 16]); bbi = T([128, 32, 16])
                tmpA = T([128, 32, 16]); tmpB = T([128, 32, 16])
                tb2 = tb_

                def D_(fn):
                    op("dve", fn, reads=[tb2], writes=[tb2])

                def A_(fn):
                    op("act", fn, reads=[tb2], writes=[tb2])
                D_(lambda e: e.tensor_copy(out=sel_b[:], in_=sel_f[:]))
                maskd_b = B.sb([128, 128], BF16, st1)
                D_(lambda e: e.tensor_copy(out=maskd_b[:], in_=maskd[:]))
                D_(lambda e: e.memset(hpi[:], math.pi / 2))
                A_(lambda e: e.activation(out=dtt[:], in_=ldt[:], func=AF.Exp))
                D_(lambda e: e.tensor_tensor(out=adr[:], in0=are[:], in1=dtt[:], op=ALU.mult))
                D_(lambda e: e.tensor_tensor(out=th[:], in0=aim[:], in1=dtt[:], op=ALU.mult))
                A_(lambda e: e.activation(out=ss_[:], in_=th[:], func=AF.Sin, scale=1.0 / 32))
                A_(lambda e: e.activation(out=cc[:], in_=th[:], func=AF.Sin, scale=1.0 / 32, bias=hpi[:]))
                for _ in range(5):
                    D_(lambda e: e.tensor_tensor(out=t1[:], in0=cc[:], in1=cc[:], op=ALU.mult))
                    D_(lambda e: e.tensor_tensor(out=t2[:], in0=ss_[:], in1=ss_[:], op=ALU.mult))
                    D_(lambda e: e.scalar_tensor_tensor(out=ss_[:], in0=cc[:], scalar=2.0, in1=ss_[:], op0=ALU.mult, op1=ALU.mult))
                    D_(lambda e: e.tensor_tensor(out=cc[:], in0=t1[:], in1=t2[:], op=ALU.subtract))
                D_(lambda e: e.memset(ER[:, :, 0:1], 1.0))
                D_(lambda e: e.memset(EI[:, :, 0:1], 0.0))
                D_(lambda e: e.tensor_copy(out=ER[:, :, 1], in_=cc[:]))
                D_(lambda e: e.tensor_copy(out=EI[:, :, 1], in_=ss_[:]))
                tmpE = [T([128, 32, 8]) for _ in range(4)]
                k = 1
                while k < 16:
                    a_r, a_i = ER[:, :, 1:k + 1], EI[:, :, 1:k + 1]
                    b_r = fv(ER[:, :, k:k + 1], [[17, 32], [0, k]])
                    b_i = fv(EI[:, :, k:k + 1], [[17, 32], [0, k]])
                    q = [t_[:, :, 0:k] for t_ in tmpE]
                    D_(lambda e, a_r=a_r, b_r=b_r, q=q: e.tensor_tensor(out=q[0], in0=a_r, in1=b_r, op=ALU.mult))
                    D_(lambda e, a_i=a_i, b_i=b_i, q=q: e.tensor_tensor(out=q[1], in0=a_i, in1=b_i, op=ALU.mult))
                    D_(lambda e, a_r=a_r, b_i=b_i, q=q: e.tensor_tensor(out=q[2], in0=a_r, in1=b_i, op=ALU.mult))
                    D_(lambda e, a_i=a_i, b_r=b_r, q=q: e.tensor_tensor(out=q[3], in0=a_i, in1=b_r, op=ALU.mult))
                    D_(lambda e, k=k, q=q: e.tensor_tensor(out=ER[:, :, k + 1:2 * k + 1], in0=q[0], in1=q[1], op=ALU.subtract))
                    D_(lambda e, k=k, q=q: e.tensor_tensor(out=EI[:, :, k + 1:2 * k + 1], in0=q[2], in1=q[3], op=ALU.add))
                    k *= 2
                for tau in range(17):
                    A_(lambda e, tau=tau: e.activation(out=MAGP[:, :, tau], in_=adr[:], func=AF.Exp, scale=float(tau)))
                    A_(lambda e, tau=tau: e.activation(out=MAGN[:, :, tau], in_=adr[:], func=AF.Exp, scale=-float(tau)))
                D_(lambda e: e.tensor_tensor(out=PPr[:], in0=MAGP[:], in1=ER[:], op=ALU.mult))
                D_(lambda e: e.tensor_tensor(out=PPi[:], in0=MAGP[:], in1=EI[:], op=ALU.mult))
                D_(lambda e: e.tensor_tensor(out=PNr[:], in0=MAGN[:], in1=ER[:], op=ALU.mult))
                D_(lambda e: e.scalar_tensor_tensor(out=PNi[:], in0=MAGN[:], scalar=-1.0, in1=EI[:], op0=ALU.mult, op1=ALU.mult))
                xr = T([128, 32]); nr = T([128, 32]); ni = T([128, 32]); den = T([128, 32]); cfr = T([128, 32]); cfi = T([128, 32])
                D_(lambda e: e.tensor_scalar(out=xr[:], in0=PPr[:, :, 1], scalar1=-1.0, scalar2=None, op0=ALU.add))
                D_(lambda e: e.tensor_tensor(out=t1[:], in0=xr[:], in1=are[:], op=ALU.mult))
                D_(lambda e: e.tensor_tensor(out=t2[:], in0=PPi[:, :, 1], in1=aim[:], op=ALU.mult))
                D_(lambda e: e.tensor_tensor(out=nr[:], in0=t1[:], in1=t2[:], op=ALU.add))
                D_(lambda e: e.tensor_tensor(out=t1[:], in0=PPi[:, :, 1], in1=are[:], op=ALU.mult))
                D_(lambda e: e.tensor_tensor(out=t2[:], in0=xr[:], in1=aim[:], op=ALU.mult))
                D_(lambda e: e.tensor_tensor(out=ni[:], in0=t1[:], in1=t2[:], op=ALU.subtract))
                D_(lambda e: e.tensor_tensor(out=t1[:], in0=are[:], in1=are[:], op=ALU.mult))
                D_(lambda e: e.tensor_tensor(out=t2[:], in0=aim[:], in1=aim[:], op=ALU.mult))
                D_(lambda e: e.tensor_tensor(out=den[:], in0=t1[:], in1=t2[:], op=ALU.add))
                D_(lambda e: e.reciprocal(out=den[:], in_=den[:]))
                D_(lambda e: e.tensor_tensor(out=cfr[:], in0=nr[:], in1=den[:], op=ALU.mult))
                D_(lambda e: e.tensor_tensor(out=cfi[:], in0=ni[:], in1=den[:], op=ALU.mult))
                cfr_b = fv(cfr[:], [[1, 32], [0, 16]]); cfi_b = fv(cfi[:], [[1, 32], [0, 16]])
                D_(lambda e: e.tensor_tensor(out=tmpA[:], in0=bre[:], in1=cfr_b, op=ALU.mult))
                D_(lambda e: e.tensor_tensor(out=tmpB[:], in0=bim[:], in1=cfi_b, op=ALU.mult))
                D_(lambda e: e.tensor_tensor(out=bbr[:], in0=tmpA[:], in1=tmpB[:], op=ALU.subtract))
                D_(lambda e: e.tensor_tensor(out=tmpA[:], in0=bim[:], in1=cfr_b, op=ALU.mult))
                D_(lambda e: e.tensor_tensor(out=tmpB[:], in0=bre[:], in1=cfi_b, op=ALU.mult))
                D_(lambda e: e.tensor_tensor(out=bbi[:], in0=tmpA[:], in1=tmpB[:], op=ALU.add))
                D_(lambda e: e.tensor_copy(out=AA1[:, 0, :], in_=PPr[:, :, 16]))
                D_(lambda e: e.tensor_copy(out=AA1[:, 1, :], in_=PPr[:, :, 16]))
                D_(lambda e: e.tensor_scalar(out=AA2[:, 0, :], in0=PPi[:, :, 16], scalar1=-1.0, scalar2=None, op0=ALU.mult))
                D_(lambda e: e.tensor_copy(out=AA2[:, 1, :], in_=PPi[:, :, 16]))

                if upto == "D0":
                    B.barrier()
                    return
                X = [T([128, 16, 16]) for _ in range(4)]
                xb_ = Buf()
                Bx = [{"Bexp": B.sb([128, 2, 512], BF16, st1), "Bfull": B.sb([128, 2, 512], BF16, st1), "BblkT": B.sb([128, 8, 128], BF16, st1),
                       "bxb": Buf(), "bfb": Buf(), "btb": Buf()} for _ in range(2)]
                CexpB = [B.sb([128, 2, 512], BF16, st1) for _ in range(4)]
                cxb = [Buf() for _ in range(4)]
                cxd = [B.ds() for _ in range(4)]
                DblkB = [B.sb([128, 4, 512], BF16, st1) for _ in range(4)]
                dkb = [Buf() for _ in range(4)]
                Vp = [B.sb([128, 4, 256], BF16, st1) for _ in range(4)]
                vpb = [Buf() for _ in range(4)]
                uTn = (B.sb([128, S], BF16, st1), Buf(), B.ds())
                uTs = [(B.sb([128, 16, 256], BF16, st1), Buf()) for _ in range(2)]
                Yqs = [(B.sb([128, 16, 256], BF16, st1), Buf(), B.ds()) for _ in range(1)]
                psT = B.ps([128, 8, 128], BF16, st1); psTb = Buf()
                psD = [(B.ps([128, 512], F32, st1), Buf()) for _ in range(2)]
                psV = [(B.ps([128, 2, 256], F32, st1), Buf()) for _ in range(2)]
                psW = (B.ps([128, 2, 256], F32, st1), Buf())
                psY = [(B.ps([128, 512], F32, st1), Buf()) for _ in range(2)]
                for bx in Bx:
                    op("dve", lambda e, bx=bx: e.memset(bx["Bexp"][:], 0.0), writes=[bx["bxb"]])
                    op("dve", lambda e, bx=bx: e.memset(bx["Bfull"][:], 0.0), writes=[bx["bfb"]])
                for j in range(4):
                    op("dve", lambda e, j=j: e.memset(CexpB[j][:], 0.0), writes=[cxb[j]])

                def cmul(pr, pi_, qr, qi, outs_r, outs_i, neg_i, rbufs, wbufs):
                    op("dve", lambda e: e.tensor_tensor(out=X[0][:], in0=pr, in1=qr, op=ALU.mult), reads=rbufs, writes=[xb_])
                    op("dve", lambda e: e.tensor_tensor(out=X[1][:], in0=pi_, in1=qi, op=ALU.mult), reads=rbufs, writes=[xb_])
                    op("dve", lambda e: e.tensor_tensor(out=X[2][:], in0=pr, in1=qi, op=ALU.mult), reads=rbufs, writes=[xb_])
                    op("dve", lambda e: e.tensor_tensor(out=X[3][:], in0=pi_, in1=qr, op=ALU.mult), reads=rbufs, writes=[xb_])
                    for lo, hi, o in outs_r:
                        op("dve", lambda e, lo=lo, hi=hi, o=o: e.tensor_tensor(out=o, in0=X[0][lo:hi], in1=X[1][lo:hi], op=ALU.subtract), reads=[xb_], writes=wbufs)
                    for lo, hi, o in outs_i:
                        if neg_i:
                            op("dve", lambda e, lo=lo, hi=hi, o=o: e.scalar_tensor_tensor(out=o, in0=X[2][lo:hi], scalar=-1.0, in1=X[3][lo:hi], op0=ALU.mult, op1=ALU.subtract),
                               reads=[xb_], writes=wbufs)
                        else:
                            op("dve", lambda e, lo=lo, hi=hi, o=o: e.tensor_tensor(out=o, in0=X[2][lo:hi], in1=X[3][lo:hi], op=ALU.add), reads=[xb_], writes=wbufs)

                def blocked(tile, c):
                    v = tile[:].rearrange("p c (i g k) -> p c i g k", g=2, k=16)
                    return [(0, 64, v[0:64, c, :, 0, :]), (64, 128, v[64:128, c, :, 1, :])]

                def prep(m):
                    j = m % 4
                    bx = Bx[m % 2]
                    ppr = fv(PPr[:, m, 1:2], [[1, 16], [0, 16]]); ppi = fv(PPi[:, m, 1:2], [[1, 16], [0, 16]])
                    pnr = fv(PNr[:, m, 1:2], [[1, 16], [0, 16]]); pni = fv(PNi[:, m, 1:2], [[1, 16], [0, 16]])
                    prr = fv(PPr[:, m, 15:16], [[-1, 16], [0, 16]]); pri = fv(PPi[:, m, 15:16], [[-1, 16], [0, 16]])
                    c_r = fv(cre[:, m, 0:1], [[0, 16], [1, 16]]); c_i = fv(cim[:, m, 0:1], [[0, 16], [1, 16]])
                    b_r = fv(bbr[:, m, 0:1], [[0, 16], [1, 16]]); b_i = fv(bbi[:, m, 0:1], [[0, 16], [1, 16]])
                    cmul(pnr, pni, b_r, b_i, blocked(bx["Bexp"], 0), blocked(bx["Bexp"], 1), False, [tb2], [bx["bxb"]])
                    cmul(ppr, ppi, c_r, c_i, blocked(CexpB[j], 0), blocked(CexpB[j], 1), True, [tb2], [cxb[j]])
                    dma("sp", CXs.ap()[m], CexpB[j][:].rearrange("p c n -> p (c n)"), cxd[j], reads=[cxb[j]], writes=[db["CXs"]])

                def work(m):
                    q4, j = divmod(m, 4)
                    bx = Bx[m % 2]
                    uT, utb = uTs[q4 % 2]

                    def trB(e):
                        for c in range(2):
                            for it in range(4):
                                ins = e.transpose(out=psT[:, c * 4 + it, :], in_=bx["Bexp"][:, c, it * 128:(it + 1) * 128], identity=ident_b[:])
                        return ins
                    op("pe", trB, reads=[bx["bxb"], cbuf], writes=[psTb])
                    op("act", lambda e: e.copy(out=bx["BblkT"][:], in_=psT[:]), reads=[psTb], writes=[bx["btb"]])
                    for hb2 in range(2):
                        pv, pvb = psV[hb2]

                        def vmm(e, pv=pv, hb2=hb2):
                            for a_ in range(2):
                                it = hb2 * 2 + a_
                                for il in range(4):
                                    ins = e.matmul(pv[:, a_, :], lhsT=sel_b[32 * j:32 * j + 32, il, :], rhs=uT[32 * j:32 * j + 32, 4 * it + il, :],
                                                   start=(il == 0), stop=(il == 3), tile_position=(32 * j, 0))
                            return ins
                        op("pe", vmm, reads=[utb, tb2], writes=[pvb])
                        op("act", lambda e, pv=pv, hb2=hb2: e.copy(out=Vp[j][:, 2 * hb2:2 * hb2 + 2, :], in_=pv[:]), reads=[pvb], writes=[vpb[j]])
                    for it in range(4):
                        pd, pdb = psD[it % 2]

                        def dmm(e, pd=pd, it=it):
                            for c in range(2):
                                ins = e.matmul(pd[:], lhsT=bx["Bexp"][:, c, it * 128:(it + 1) * 128], rhs=CexpB[j][:, c, :], start=(c == 0), stop=(c == 1))
                            return ins
                        op("pe", dmm, reads=[bx["bxb"], cxb[j]], writes=[pdb])
                        op("act", lambda e, pd=pd, it=it: e.copy(out=DblkB[j][:, it, :], in_=pd[:]), reads=[pdb], writes=[dkb[j]])
                        op("pool", lambda e, it=it: e.tensor_tensor(out=DblkB[j][:, it, it * 128:(it + 1) * 128], in0=DblkB[j][:, it, it * 128:(it + 1) * 128], in1=maskd_b[:],
                                                                     op=ALU.mult), reads=[tb2], writes=[dkb[j]])
                    pw, pwb = psW

                    def wmm(e):
                        for c in range(2):
                            for it in range(4):
                                ins = e.matmul(pw[:, c, :], lhsT=bx["BblkT"][:, c * 4 + it, :], rhs=Vp[j][:, it, :], start=(it == 0), stop=(it == 3))
                        return ins
                    op("pe", wmm, reads=[bx["btb"], vpb[j]], writes=[pwb])
                    op("act", lambda e: e.activation(out=WW[:, 0, m, :], in_=pw[:, 0, :], func=AF.Copy, scale=AA1[:, 0, m:m + 1]), reads=[pwb, tb_], writes=[wwb])
                    op("act", lambda e: e.activation(out=WW[:, 1, m, :], in_=pw[:, 1, :], func=AF.Copy, scale=AA1[:, 0, m:m + 1]), reads=[pwb, tb_], writes=[wwb])
                    op("dve", lambda e: e.scalar_tensor_tensor(out=WW[:, 0, m, :], in0=pw[:, 1, :], scalar=AA2[:, 0, m:m + 1], in1=WW[:, 0, m, :], op0=ALU.mult, op1=ALU.add),
                       reads=[pwb, tb_, wwb], writes=[wwb])
                    op("dve", lambda e: e.scalar_tensor_tensor(out=WW[:, 1, m, :], in0=pw[:, 0, :], scalar=AA2[:, 1, m:m + 1], in1=WW[:, 1, m, :], op0=ALU.mult, op1=ALU.add),
                       reads=[pwb, tb_, wwb], writes=[wwb])

                def yintra(q4):
                    uT, utb = uTs[q4 % 2]
                    Yq, yqb, yqd = Yqs[0]
                    for i in range(16):
                        py = psY[i % 2][0][:, 0:256]
                        pyb = psY[i % 2][1]

                        def ymm(e, py=py, i=i):
                            for it in range(i // 4 + 1):
                                for j in range(4):
                                    ins = e.matmul(py[32 * j:32 * j + 32, :], lhsT=DblkB[j][:, it, i * 32:(i + 1) * 32], rhs=Vp[j][:, it, :],
                                                   start=(it == 0), stop=(it == i // 4), tile_position=(0, 32 * j))
                            return ins
                        op("pe", ymm, reads=dkb + vpb, writes=[pyb])
                        op("dve", lambda e, py=py, i=i: e.scalar_tensor_tensor(out=Yq[:, i, :], in0=uT[:, i, :], scalar=sd[:, q4:q4 + 1], in1=py,
                                                                                op0=ALU.mult, op1=ALU.add), reads=[pyb, utb, tb_], writes=[yqb])
                    dma("sp", YIs.ap()[q4 * 128:(q4 + 1) * 128, :], Yq[:].rearrange("p i c -> p (i c)"), yqd, reads=[yqb], writes=[db["YIs"]])

                def load_u(q4):
                    un, unb, und = uTn
                    uT, utb = uTs[q4 % 2]
                    dma("sp", un[:], UTs.ap()[q4 * 128:(q4 + 1) * 128, :], und, reads=[db["UTs"]], writes=[unb])
                    unv = un[:].rearrange("p (c i) -> p i c", i=16)
                    for ih in range(2):
                        op("act", lambda e, ih=ih: e.copy(out=uT[:, ih * 8:(ih + 1) * 8, :], in_=unv[:, ih * 8:(ih + 1) * 8, :]), reads=[unb], writes=[utb])

                load_u(0)
                prep(0)
                for m in range(32):
                    q4, j = divmod(m, 4)
                    if j == 0 and q4 + 1 < 8:
                        load_u(q4 + 1)
                    if m + 1 < 32:
                        prep(m + 1)
                    work(m)
                    if j == 3:
                        yintra(q4)
                B.barrier()

    def rec_gen():
        WW, AA1, AA2, tb_ = Dst["WW"], Dst["AA1"], Dst["AA2"], Dst["tb_"]
        r1, r2 = Dst["r1"], Dst["r2"]
        halves = [(0, 16, Buf(), Buf()), (16, 32, Buf(), Buf())]
        for c in range(1, 256):
            for (p0, p1, wb2, rb) in halves:
                prev = WW[:, :, p0:p1, c - 1]
                cur = WW[:, :, p0:p1, c]
                prev_sw = fv(WW[:, 1:2, p0:p0 + 1, c - 1:c], [[-32 * 256, 2], [256, 16]])
                op("dve", lambda e, prev=prev, p0=p0, p1=p1: e.tensor_tensor(out=r1[:, :, p0:p1], in0=prev, in1=AA1[:, :, p0:p1], op=ALU.mult), reads=[wb2, tb_], writes=[rb])
                op("dve", lambda e, prev_sw=prev_sw, p0=p0, p1=p1: e.tensor_tensor(out=r2[:, :, p0:p1], in0=prev_sw, in1=AA2[:, :, p0:p1], op=ALU.mult), reads=[wb2, tb_], writes=[rb])
            for (p0, p1, wb2, rb) in halves:
                op("dve", lambda e, p0=p0, p1=p1: e.tensor_tensor(out=r1[:, :, p0:p1], in0=r1[:, :, p0:p1], in1=r2[:, :, p0:p1], op=ALU.add), reads=[rb], writes=[rb])
            for (p0, p1, wb2, rb) in halves:
                cur = WW[:, :, p0:p1, c]
                op("dve", lambda e, cur=cur, p0=p0, p1=p1: e.tensor_tensor(out=cur, in0=cur, in1=r1[:, :, p0:p1], op=ALU.add), reads=[rb, wb2], writes=[wb2])
            yield

    def phase_D2():
        WW, wwb = Dst["WW"], Dst["wwb"]
        if True:
            with ExitStack() as st1:
                Hb = B.sb([128, 2, 32, 256], BF16, st1); hbb = Buf()
                op("dve", lambda e: e.memset(Hb[:, :, :, 0:1], 0.0), writes=[hbb])
                for c in range(2):
                    op("dve" if c == 0 else "act", lambda e, c=c: (e.tensor_copy(out=Hb[:, c, :, 1:256], in_=WW[:, c, :, 0:255]) if c == 0
                                                                    else e.copy(out=Hb[:, c, :, 1:256], in_=WW[:, c, :, 0:255])), reads=[wwb], writes=[hbb])
                Cx = [(B.sb([128, 2, 512], BF16, st1), Buf(), B.ds()) for _ in range(8)]
                Yin = [(B.sb([128, 16, 256], BF16, st1), Buf(), B.ds()) for _ in range(2)]
                Yf = [(B.sb([128, 16, 256], F32, st1), Buf()) for _ in range(2)]
                zT = [(B.sb([128, S], BF16, st1), Buf(), B.ds()) for _ in range(2)]
                psY2 = [(B.ps([128, 512], F32, st1), Buf()) for _ in range(4)]
                npy = 0
                def load_q(q4):
                    yi, yib, yid = Yin[q4 % 2]
                    dma("sp", yi[:].rearrange("p i c -> p (i c)"), YIs.ap()[q4 * 128:(q4 + 1) * 128, :], yid, reads=[db["YIs"]], writes=[yib])
                    for j in range(4):
                        ct, ctb, ctd = Cx[(q4 % 2) * 4 + j]
                        dma("sp", ct[:].rearrange("p c n -> p (c n)"), CXs.ap()[q4 * 4 + j], ctd, reads=[db["CXs"]], writes=[ctb])

                load_q(0)
                for q4 in range(8):
                    yi, yib, yid = Yin[q4 % 2]
                    yf, yfb = Yf[q4 % 2]
                    z_, zb, zd = zT[q4 % 2]
                    if q4 + 1 < 8:
                        load_q(q4 + 1)
                    cxs = [(Cx[(q4 % 2) * 4 + j][0], Cx[(q4 % 2) * 4 + j][1]) for j in range(4)]
                    for i in range(16):
                        py, pyb = psY2[npy % 4]; npy += 1

                        def ymm(e, py=py, i=i, cxs=cxs, q4=q4):
                            for c in range(2):
                                for j in range(4):
                                    ins = e.matmul(py[32 * j:32 * j + 32, 0:256], lhsT=cxs[j][0][:, c, i * 32:(i + 1) * 32], rhs=Hb[:, c, q4 * 4 + j, :],
                                                   start=(c == 0), stop=(c == 1), tile_position=(0, 32 * j))
                            return ins
                        op("pe", ymm, reads=[hbb] + [c_[1] for c_ in cxs], writes=[pyb])
                        op("dve", lambda e, py=py, i=i, yi=yi, yf=yf: e.tensor_tensor(out=yf[:, i, :], in0=py[:, 0:256], in1=yi[:, i, :], op=ALU.add),
                           reads=[pyb, yib], writes=[yfb])
                    op("act", lambda e, z_=z_, yf=yf: e.activation(out=z_[:].rearrange("p (c i) -> p c i", i=16), in_=yf[:].rearrange("p i c -> p c i"), func=AF.Gelu_apprx_tanh),
                       reads=[yfb], writes=[zb])
                    dma("pool", ZTs.ap()[q4 * 128:(q4 + 1) * 128, :], z_[:], zd, reads=[zb], writes=[db["ZTs"]])
                B.barrier()
        Dst["st"].close()
    def phase_T1(after_weights=None):
        with ExitStack() as st:
            W = {}
            WB = {}
            for nm in ("glu_w", "w_br_attn", "w_br_ssm", "w_br_xattn", "w_out"):
                W[nm] = B.sb([128, 8, D], BF16, st, nm)
                WB[nm] = Buf()
                wd = B.ds()
                for n in range(2):
                    dma("pool", W[nm][:, :, n * 512:(n + 1) * 512], I[nm][:, n * 512:(n + 1) * 512].rearrange("(k p) n -> p k n", p=128), wd, writes=[WB[nm]])
            glub = B.sb([128, 8], F32, st)
            wb_ = Buf()
            dma("sp", glub[:], I["glu_b"], B.ds(), writes=[wb_])
            if after_weights is not None:
                after_weights()
            zt = (B.sb([128, 8, 512], BF16, st), Buf(), B.ds())
            ya = (B.sb([128, 8, 512], BF16, st), Buf(), B.ds())
            yx = (B.sb([128, 8, 512], BF16, st), Buf(), B.ds())
            gt = (B.sb([128, 24, 512], BF16, st), Buf(), B.ds())
            yssm = (B.sb([128, 8, 512], BF16, st), Buf())
            mixT = (B.sb([128, 8, 512], BF16, st), Buf())
            sig = [(B.sb([128, 512], F32, st), Buf()) for _ in range(2)]
            mm_ = [(B.sb([128, 3, 512], F32, st), Buf()) for _ in range(1)]
            xs = [(B.sb([128, D], F32, st), Buf(), B.ds()) for _ in range(2)]
            x1 = [(B.sb([128, D], F32, st), Buf(), B.ds()) for _ in range(2)]
            pG = [(B.ps([128, 512], F32, st), Buf()) for _ in range(2)]
            pB = [(B.ps([128, 512], F32, st), Buf()) for _ in range(3)]
            pO = [(B.ps([128, 512], F32, st), Buf()) for _ in range(2)]
            ng = 0; nx = 0; no = 0
            def load_z(tb):
                tsl = slice(tb * 512, (tb + 1) * 512)
                dma("act", zt[0][:], ZTs.ap()[:, tsl].rearrange("(k p) t -> p k t", p=128), zt[2], reads=[db["ZTs"]], writes=[zt[1]])

            def load_rest(tb):
                tsl = slice(tb * 512, (tb + 1) * 512)
                dma("act", ya[0][:], YAs.ap()[:, tsl].rearrange("(k p) t -> p k t", p=128), ya[2], reads=[db["YAs"]], writes=[ya[1]])
                dma("act", yx[0][:], YXs.ap()[:, tsl].rearrange("(k p) t -> p k t", p=128), yx[2], reads=[db["YXs"]], writes=[yx[1]])
                dma("act", gt[0][:], GTs.ap()[:, tsl].rearrange("(k p) t -> p k t", p=128), gt[2], reads=[db["GTs"]], writes=[gt[1]])

            load_z(0)
            load_rest(0)
            for tb in range(NTB):
                for ct in range(8):
                    pg, pgb = pG[ng % 2]
                    sg, sgb = sig[ng % 2]; ng += 1

                    def gmm(e, pg=pg, ct=ct):
                        for k in range(8):
                            ins = e.matmul(pg[:], lhsT=W["glu_w"][:, k, ct * 128:(ct + 1) * 128], rhs=zt[0][:, k, :], start=(k == 0), stop=(k == 7))
                        return ins
                    op("pe", gmm, reads=[WB["glu_w"], zt[1]], writes=[pgb])
                    op("act", lambda e, sg=sg, pg=pg, ct=ct: e.activation(out=sg[:], in_=pg[:], func=AF.Sigmoid, bias=glub[:, ct:ct + 1], scale=1.0),
                       reads=[pgb, wb_], writes=[sgb])
                    op("dve", lambda e, sg=sg, ct=ct: e.tensor_tensor(out=yssm[0][:, ct, :], in0=zt[0][:, ct, :], in1=sg[:], op=ALU.mult),
                       reads=[sgb, zt[1]], writes=[yssm[1]])
                if tb + 1 < NTB:
                    load_z(tb + 1)
                for ct in range(8):
                    srcs = ((W["w_br_attn"], ya[0], ya[1], WB["w_br_attn"]), (W["w_br_ssm"], yssm[0], yssm[1], WB["w_br_ssm"]),
                            (W["w_br_xattn"], yx[0], yx[1], WB["w_br_xattn"]))
                    m3, m3b = mm_[0]
                    for bi, (w_, y_, yb_, wbf) in enumerate(srcs):
                        pb_, pbb = pB[bi]

                        def bmm(e, pb_=pb_, w_=w_, y_=y_, ct=ct):
                            for k in range(8):
                                ins = e.matmul(pb_[:], lhsT=w_[:, k, ct * 128:(ct + 1) * 128], rhs=y_[:, k, :], start=(k == 0), stop=(k == 7))
                            return ins
                        op("pe", bmm, reads=[wbf, yb_], writes=[pbb])
                        op("dve", lambda e, m3=m3, pb_=pb_, bi=bi, ct=ct: e.tensor_tensor(out=m3[:, bi, :], in0=pb_[:], in1=gt[0][:, bi * 8 + ct, :], op=ALU.mult),
                           reads=[pbb, gt[1]], writes=[m3b])
                    op("dve", lambda e, m3=m3: e.tensor_tensor(out=m3[:, 0, :], in0=m3[:, 0, :], in1=m3[:, 1, :], op=ALU.add), reads=[m3b], writes=[m3b])
                    op("dve", lambda e, m3=m3, ct=ct: e.tensor_tensor(out=mixT[0][:, ct, :], in0=m3[:, 0, :], in1=m3[:, 2, :], op=ALU.add), reads=[m3b], writes=[mixT[1]])
                if tb + 1 < NTB:
                    load_rest(tb + 1)
                for ts in range(4):
                    xt, xb, xd = xs[nx % 2]
                    x1t, x1b, x1d = x1[nx % 2]; nx += 1
                    r0 = tb * 512 + ts * 128
                    dma("sp", xt[:], I["x"][r0:r0 + 128, :], xd, writes=[xb])
                    for half in range(2):
                        po, pob = pO[no % 2]; no += 1

                        def omm(e, po=po, ts=ts, half=half):
                            for k in range(8):
                                ins = e.matmul(po[:], lhsT=mixT[0][:, k, ts * 128:(ts + 1) * 128], rhs=W["w_out"][:, k, half * 512:(half + 1) * 512], start=(k == 0), stop=(k == 7))
                            return ins
                        op("pe", omm, reads=[WB["w_out"], mixT[1]], writes=[pob])
                        op("dve", lambda e, po=po, half=half, xt=xt, x1t=x1t: e.tensor_tensor(out=x1t[:, half * 512:(half + 1) * 512], in0=po[:], in1=xt[:, half * 512:(half + 1) * 512], op=ALU.add),
                           reads=[pob, xb], writes=[x1b])
                    dma("pool", X1s.ap()[r0:r0 + 128, :], x1t[:], x1d, reads=[x1b], writes=[db["X1s"]])
            B.barrier()

    def phase_T2():
        TB = 256
        with ExitStack() as st:
            REST = FF - 512
            wfi = B.sb([128, 8, 2, REST], BF16, st, "wfi")
            wpre = Tst["wpre"]
            wfo = B.sb([128, NH, D], BF16, st, "wfo")
            wgb = [Tst["wpre_b"]] + [Buf() for _ in range(5)]
            for k in range(1, 6):
                wd = B.ds()
                for gi, base in enumerate((0, FF)):
                    c0 = base + 512 * k
                    c1 = min(base + 512 * (k + 1), base + FF)
                    dma("pool", wfi[:, :, gi, c0 - base - 512:c1 - base - 512], I["w_ffn_in"][:, c0:c1].rearrange("(k p) n -> p k n", p=128), wd, writes=[wgb[k]])

            def wg(k, ht):
                return wpre[:, k, 0, ht * 128:(ht + 1) * 128] if ht < 4 else wfi[:, k, 0, (ht - 4) * 128:(ht - 3) * 128]

            def wu(k, ht):
                return wpre[:, k, 1, ht * 128:(ht + 1) * 128] if ht < 4 else wfi[:, k, 1, (ht - 4) * 128:(ht - 3) * 128]
            wb_ = Buf(); wd = B.ds()
            for n in range(2):
                dma("pool", wfo[:, :, n * 512:(n + 1) * 512], I["w_ffn_out"][:, n * 512:(n + 1) * 512].rearrange("(k p) n -> p k n", p=128), wd, writes=[wb_])
            psT = B.ps([128, 8, 128], BF16, st); psTb = Buf()
            pG = [(B.ps([128, 512], F32, st), Buf()) for _ in range(2)]
            pU = [(B.ps([128, 512], F32, st), Buf()) for _ in range(2)]
            pO = [(B.ps([128, 512], F32, st), Buf()) for _ in range(2)]
            x1t = [(B.sb([128, D], F32, st), Buf(), B.ds()) for _ in range(4)]
            sls = [rms_slot(st) for _ in range(2)]
            h2T = [(B.sb([128, 8, TB], BF16, st), Buf()) for _ in range(2)]
            aT = [(B.sb([128, NH, TB], BF16, st), Buf()) for _ in range(1)]
            sg = [(B.sb([128, TB], F32, st), Buf()) for _ in range(2)]
            x2 = [(B.sb([128, D], F32, st), Buf()) for _ in range(2)]
            fs = [{"ss": B.sb([128, 1], F32, st), "rs": B.sb([128, 1], F32, st), "rr": B.sb([128, 1], F32, st), "b": Buf()} for _ in range(2)]
            ot = [(B.sb([128, D], F32, st), Buf(), B.ds()) for _ in range(1)]
            cnt = {"nx": 0, "ng": 0, "no": 0, "nf": 0}
            nsub = TB // 128
            NTB2 = S // TB

            def norm_in(tb):
                hT_, hTb_ = h2T[tb % 2]
                xts = []
                for ts in range(nsub):
                    xt, xb, xd = x1t[cnt["nx"] % 4]
                    sl = sls[cnt["nx"] % 2]; cnt["nx"] += 1
                    r0 = tb * TB + ts * 128
                    dma("sp", xt[:], X1s.ap()[r0:r0 + 128, :], xd, reads=[db["X1s"]], writes=[xb])
                    rms_T(xt[:], xb, 2, hT_[:, :, ts * 128:(ts + 1) * 128], hTb_, sl, psT, psTb, evac=("dve" if ts % 2 == 0 else "act"))
                    xts.append((xt, xb, r0))
                return xts

            def ffn_in(tb):
                hT_, hTb_ = h2T[tb % 2]
                a_, ab_ = aT[0]
                for ht in range(NH):
                    pg, pgb = pG[cnt["ng"] % 2]
                    pu, pub = pU[cnt["ng"] % 2]
                    s_, sb_ = sg[cnt["ng"] % 2]; cnt["ng"] += 1

                    def gm(e, pg=pg, ht=ht):
                        for k in range(8):
                            ins = e.matmul(pg[:, 0:TB], lhsT=wg(k, ht), rhs=hT_[:, k, :], start=(k == 0), stop=(k == 7))
                        return ins

                    def um(e, pu=pu, ht=ht):
                        for k in range(8):
                            ins = e.matmul(pu[:, 0:TB], lhsT=wu(k, ht), rhs=hT_[:, k, :], start=(k == 0), stop=(k == 7))
                        return ins
                    op("pe", gm, reads=[wgb[ht // 4], hTb_], writes=[pgb])
                    op("pe", um, reads=[wgb[ht // 4], hTb_], writes=[pub])
                    op("act", lambda e, s_=s_, pg=pg: e.activation(out=s_[:], in_=pg[:, 0:TB], func=AF.Silu), reads=[pgb], writes=[sb_])
                    op("dve", lambda e, s_=s_, pu=pu, ht=ht: e.tensor_tensor(out=a_[:, ht, :], in0=pu[:, 0:TB], in1=s_[:], op=ALU.mult), reads=[pub, sb_], writes=[ab_])

            def ffn_out(tb, xts):
                a_, ab_ = aT[0]
                for ts in range(nsub):
                    xt, xb, r0 = xts[ts]
                    x2t, x2b = x2[cnt["nf"] % 2]
                    f = fs[cnt["nf"] % 2]
                    o_, ob_, od_ = ot[0]; cnt["nf"] += 1
                    for half in range(2):
                        po, pob = pO[cnt["no"] % 2]; cnt["no"] += 1

                        def om(e, po=po, ts=ts, half=half):
                            for k in range(NH):
                                ins = e.matmul(po[:], lhsT=a_[:, k, ts * 128:(ts + 1) * 128], rhs=wfo[:, k, half * 512:(half + 1) * 512], start=(k == 0), stop=(k == NH - 1))
                            return ins
                        op("pe", om, reads=[wb_, ab_], writes=[pob])
                        op("dve", lambda e, po=po, half=half, xt=xt, x2t=x2t: e.tensor_tensor(out=x2t[:, half * 512:(half + 1) * 512], in0=po[:], in1=xt[:, half * 512:(half + 1) * 512], op=ALU.add),
                           reads=[pob, xb], writes=[x2b])
                    op("act", lambda e, f=f, x2t=x2t, o_=o_: e.activation(out=o_[:], in_=x2t[:], func=AF.Square, accum_out=f["ss"][:]), reads=[x2b], writes=[f["b"], ob_])
                    op("pool", lambda e, f=f: e.tensor_scalar(out=f["rs"][:], in0=f["ss"][:], scalar1=1.0 / D, scalar2=1e-6, op0=ALU.mult, op1=ALU.add), reads=[f["b"]], writes=[f["b"]])
                    op("pool", lambda e, f=f: e.tensor_tensor(out=f["rr"][:], in0=f["rs"][:], in1=mhalf[:], op=ALU.pow), reads=[f["b"], cbuf], writes=[f["b"]])
                    op("dve", lambda e, f=f, x2t=x2t, o_=o_: e.scalar_tensor_tensor(out=o_[:], in0=x2t[:], scalar=f["rr"][:], in1=gains[:, 3, :], op0=ALU.mult, op1=ALU.mult),
                       reads=[x2b, f["b"], cbuf], writes=[ob_])
                    dma("sp", out_d[r0:r0 + 128, :], o_[:], od_, reads=[ob_], writes=[db["out"]])

            xts_cur = norm_in(0)
            for tb in range(NTB2):
                ffn_in(tb)
                xts_next = norm_in(tb + 1) if tb + 1 < NTB2 else None
                ffn_out(tb, xts_cur)
                xts_cur = xts_next
            B.barrier()

    if "AP" in phases:
        phase_AP(); B.barrier()
    if "B" in phases:
        phase_B(); B.barrier()
    gen = None
    if "D" in phases:
        phase_D1(); B.barrier()
        gen = rec_gen()
    if "C" in phases:
        phase_C(gen); B.barrier()
    if gen is not None:
        for _ in gen:
            pass
        B.barrier()
        phase_D2(); B.barrier()
    Tst = {}
    tst = ExitStack()
    if "T2" in phases:
        Tst["wpre"] = B.sb([128, 8, 2, 512], BF16, tst, "wpre")
        Tst["wpre_b"] = Buf()
        Tst["wpre_ds"] = B.ds()

    def prefetch_T2():
        if "T2" in phases:
            for gi, base in enumerate((0, FF)):
                dma("pool", Tst["wpre"][:, :, gi, :], I["w_ffn_in"][:, base:base + 512].rearrange("(k p) n -> p k n", p=128), Tst["wpre_ds"], writes=[Tst["wpre_b"]])
    if "T1" in phases:
        phase_T1(prefetch_T2); B.barrier()
    else:
        prefetch_T2()
    if "T2" in phases:
        phase_T2(); B.barrier()
    tst.close()
    return nc, B


_CACHE = {}


def kernel(**inputs):
    consts = host_consts()
    in_maps = []
    for b in range(8):
        m = host_layout(inputs, b)
        m.update(consts)
        in_maps.append(m)
    if "nc" not in _CACHE:
        _CACHE["nc"] = build_program()[0]
    res = run_bass_kernel_spmd(_CACHE["nc"], in_maps, core_ids=list(range(8)))
    return np.stack([np.asarray(r["out"]) for r in res.results], axis=0).astype(np.float32)
```

```python
import math
from contextlib import ExitStack

import numpy as np

import concourse.bass as bass
import concourse.mybir as mybir
from concourse.bass_utils import run_bass_kernel_spmd

F32 = mybir.dt.float32
BF16 = mybir.dt.bfloat16
AF = mybir.ActivationFunctionType
ALU = mybir.AluOpType
AX = mybir.AxisListType

S = 4096
D = 1024
NTT = 32
NTB = 8
FF = 2816
NH = 22
NEG = -30000.0
LAM_INIT = 0.8 - 0.6 * math.exp(0.0)


class Buf:
    __slots__ = ("w", "r")

    def __init__(self):
        self.w = None
        self.r = {}


class Eng:
    def __init__(self, obj, sem, name):
        self.obj = obj
        self.sem = sem
        self.count = 0
        self.seen = {}
        self.name = name


class DS:
    def __init__(self, sem):
        self.sem = sem
        self.count = 0


class Builder:
    def __init__(self, nc, debug=False):
        self.nc = nc
        self.debug = debug
        self.es = ExitStack()
        self.E = {}
        for name, obj in (("pe", nc.tensor), ("act", nc.scalar), ("dve", nc.vector), ("pool", nc.gpsimd), ("sp", nc.sync)):
            self.E[name] = Eng(obj, self.es.enter_context(nc.semaphore("sem_" + name)), name)
        self.all_ds = []
        self.nname = 0

    def sb(self, shape, dt, stack=None, name=None):
        self.nname += 1
        return (stack or self.es).enter_context(self.nc.sbuf_tensor("%s_%d" % (name or "t", self.nname), list(shape), dt))

    def ps(self, shape, dt, stack=None, name=None):
        self.nname += 1
        return (stack or self.es).enter_context(self.nc.psum_tensor("%s_%d" % (name or "p", self.nname), list(shape), dt))

    def ds(self):
        self.nname += 1
        d = DS(self.es.enter_context(self.nc.semaphore("ds_%d" % self.nname)))
        self.all_ds.append(d)
        return d

    def _wait(self, E, ev):
        if ev is None:
            return
        sem, val = ev
        k = id(sem)
        if E.seen.get(k, 0) >= val:
            return
        E.obj.wait_ge(sem, val)
        E.seen[k] = val

    def _deps(self, E, reads, writes):
        own = E.sem
        pe = E.name == "pe"
        for b in reads:
            if b.w is not None and not (pe and b.w[0] is own):
                self._wait(E, b.w)
        for b in writes:
            if b.w is not None and not (pe and b.w[0] is own):
                self._wait(E, b.w)
            for ev in b.r.values():
                if not (pe and ev[0] is own):
                    self._wait(E, ev)

    def op(self, e, fn, reads=(), writes=()):
        E = self.E[e]
        self._deps(E, reads, writes)
        ins = fn(E.obj)
        E.count += 1
        ins.then_inc(E.sem, 1)
        ev = (E.sem, E.count)
        for b in reads:
            b.r[id(E.sem)] = ev
        for b in writes:
            b.w = ev
            b.r = {}

    def dma(self, q, out, in_, ds, reads=(), writes=()):
        E = self.E[q]
        self._deps(E, reads, writes)
        ins = E.obj.dma_start(out=out, in_=in_)
        ds.count += 16
        ins.then_inc(ds.sem, 16)
        ev = (ds.sem, ds.count)
        for b in reads:
            b.r[id(ds.sem)] = ev
        for b in writes:
            b.w = ev
            b.r = {}

    def barrier(self):
        for E in self.E.values():
            for Fo in self.E.values():
                if Fo.count > 0 and not (Fo is E and E.name == "pe"):
                    self._wait(E, (Fo.sem, Fo.count))
            for d in self.all_ds:
                if d.count > 0:
                    self._wait(E, (d.sem, d.count))


def _t5_bucket(rel):
    half, max_exact = 16, 8
    ret = np.where(rel > 0, half, 0)
    n = np.abs(rel)
    nf = np.maximum(n, 1).astype(np.float32)
    large = max_exact + (np.log(nf / np.float32(max_exact)) / np.float32(math.log(256 / max_exact)) * np.float32(half - max_exact)).astype(np.int32)
    large = np.minimum(large, half - 1)
    return ret + np.where(n < max_exact, n, large)


def host_consts():
    c = {}
    c["ident"] = np.eye(128, dtype=np.float32)
    c["antiid"] = np.eye(128, dtype=np.float32)[::-1].copy()
    ii = np.arange(1280)
    b = _t5_bucket(511 - ii)
    oh = np.zeros((32, 1280), np.float32)
    oh[b, ii] = 1.0
    c["onehot"] = oh
    p = np.arange(128)[:, None]
    j = np.arange(1152)[None, :]
    c["maskmb"] = np.where((p // 64) <= np.floor_divide(j - 384, 64), 0.0, NEG).astype(np.float32)
    r = np.arange(128)[:, None, None]
    it = np.arange(4)[None, :, None]
    col = np.arange(512)[None, None, :]
    c["maskd"] = ((col[:, 0, 0:128] // 32) >= (r[:, 0, :] // 32)).astype(np.float32)
    sel = np.zeros((128, 4, 128), np.float32)
    for rr in range(128):
        for il in range(4):
            sel[rr, il, 32 * il + rr % 32] = 1.0
    c["sel"] = sel
    return c


def host_layout(inp, b):
    f = np.float32
    m = {}
    m["x"] = np.ascontiguousarray(inp["x"][b])
    m["mem"] = np.ascontiguousarray(inp["mem"][b])
    for k in ("w_in", "glu_w", "w_mem_kv", "w_br_attn", "w_br_ssm", "w_br_xattn", "w_out", "w_ffn_in", "w_ffn_out"):
        m[k] = np.ascontiguousarray(inp[k][0])
    gb = np.stack([np.broadcast_to(inp["norm1_g"][0], (128, D)), np.broadcast_to(inp["mem_norm_g"][0], (128, D)),
                   np.broadcast_to(inp["norm2_g"][0], (128, D)), np.broadcast_to(inp["final_g"], (128, D))], axis=1)
    m["gains"] = np.ascontiguousarray(gb, dtype=f)
    m["subg"] = np.ascontiguousarray(np.broadcast_to(inp["da_subln_g"][0], (128, 128)), dtype=f)
    lqk = np.stack([inp["da_lq1"][0], inp["da_lk1"][0], inp["da_lq2"][0], inp["da_lk2"][0]], axis=0)
    m["lqk"] = np.ascontiguousarray(np.broadcast_to(lqk, (128, 4, 64)), dtype=f)
    m["relb"] = np.ascontiguousarray(inp["rel_bias"], dtype=f)
    m["relb15"] = np.ascontiguousarray(np.broadcast_to(inp["rel_bias"][15], (128, 8)), dtype=f)

    def st(a):
        return np.ascontiguousarray(a.reshape(32, 2, 64).transpose(1, 2, 0).reshape(128, 32), dtype=f)
    m["s_are"] = st(inp["ssm_a_re"][0])
    m["s_aim"] = st(inp["ssm_a_im"][0])
    m["s_ldt"] = st(np.broadcast_to(inp["ssm_log_dt"][0][:, None], (64, 64)))
    def stb(a):
        return np.ascontiguousarray(a.reshape(32, 2, 64, 16).transpose(1, 2, 0, 3).reshape(128, 32, 16), dtype=f)
    m["s_bre"] = stb(inp["ssm_b_re"][0])
    m["s_bim"] = stb(inp["ssm_b_im"][0])
    def stc(a):
        return np.ascontiguousarray(a.reshape(32, 2, 16, 64).transpose(1, 3, 0, 2).reshape(128, 32, 16), dtype=f)
    m["s_cre"] = stc(inp["ssm_c_re"][0])
    m["s_cim"] = stc(inp["ssm_c_im"][0])
    m["s_d"] = np.ascontiguousarray(inp["ssm_d"][0].reshape(8, 128).T, dtype=f)
    m["glu_b"] = np.ascontiguousarray(inp["glu_b"][0].reshape(8, 128).T, dtype=f)
    return m


INPUT_SHAPES = {
    "x": [S, D], "mem": [256, D], "w_in": [D, 8192], "glu_w": [D, D], "w_mem_kv": [D, 2048], "w_br_attn": [D, D],
    "w_br_ssm": [D, D], "w_br_xattn": [D, D], "w_out": [D, D], "w_ffn_in": [D, 2 * FF], "w_ffn_out": [FF, D],
    "gains": [128, 4, D], "subg": [128, 128], "lqk": [128, 4, 64], "relb": [32, 8], "relb15": [128, 8],
    "s_are": [128, 32], "s_aim": [128, 32], "s_ldt": [128, 32], "s_bre": [128, 32, 16], "s_bim": [128, 32, 16],
    "s_cre": [128, 32, 16], "s_cim": [128, 32, 16], "s_d": [128, 8], "glu_b": [128, 8],
    "ident": [128, 128], "antiid": [128, 128], "onehot": [32, 1280], "maskmb": [128, 1152], "maskd": [128, 128],
    "sel": [128, 4, 128],
}


ALL_PHASES = ("AP", "B", "C", "D", "T1", "T2")


def build_program(debug=False, upto="all", phases=ALL_PHASES):
    nc = bass.Bass("TRN2", target_bir_lowering=False)
    B = Builder(nc, debug)
    I = {k: nc.dram_tensor(k, shp, F32, kind="ExternalInput").ap() for k, shp in INPUT_SHAPES.items()}
    out_d = nc.dram_tensor("out", [S, D], F32, kind="ExternalOutput").ap()
    def scratch(name, shape, dt, producer=None):
        if producer is not None and producer not in phases:
            kind = "ExternalInput"
        else:
            kind = "ExternalOutput" if debug else "Internal"
        return nc.dram_tensor(name, shape, dt, kind=kind)

    QTs = scratch("QTs", [D, S], BF16, "AP")
    KTs = scratch("KTs", [D, S], BF16, "AP")
    Vs = scratch("Vs", [S, D], BF16, "AP")
    UTs = scratch("UTs", [D, S], BF16, "AP")
    XQs = scratch("XQs", [D, S], BF16, "AP")
    GTs = scratch("GTs", [3 * D, S], BF16, "AP")
    YAs = scratch("YAs", [D, S], BF16, "B")
    YXs = scratch("YXs", [D, S], BF16, "C")
    ZTs = scratch("ZTs", [D, S], BF16, "D")
    YIs = scratch("YIs", [D, S], BF16)
    CXs = scratch("CXs", [32, 128, 1024], BF16)
    X1s = scratch("X1s", [S, D], F32, "T1")
    Gd = scratch("Gd", [8, 1280], F32)
    db = {k: Buf() for k in ("QTs", "KTs", "Vs", "UTs", "XQs", "GTs", "YAs", "YXs", "ZTs", "YIs", "CXs", "X1s", "Gd", "out")}

    op, dma = B.op, B.dma
    es = B.es

    def fv(apobj, dims):
        return bass.AP(apobj.tensor, apobj.offset, [list(apobj.ap[0])] + [list(d) for d in dims])

    ident_f = B.sb([128, 128], F32); ident_b = B.sb([128, 128], BF16)
    gains = B.sb([128, 4, D], F32)
    eps_t = B.sb([128, 1], F32)
    eps2_t = B.sb([128, 1], F32)
    cbuf = Buf()
    cds = B.ds()
    dma("sp", ident_f[:], I["ident"], cds, writes=[cbuf])
    dma("sp", gains[:], I["gains"], cds, writes=[cbuf])
    op("dve", lambda e: e.tensor_copy(out=ident_b[:], in_=ident_f[:]), reads=[cbuf], writes=[cbuf])
    op("dve", lambda e: e.memset(eps_t[:], 1e-6), writes=[cbuf])
    op("dve", lambda e: e.memset(eps2_t[:], 1e-6 / 0.64), writes=[cbuf])
    mhalf = B.sb([128, 1], F32)
    op("dve", lambda e: e.memset(mhalf[:], -0.5), writes=[cbuf])
    mhalf4 = B.sb([128, 4], F32)
    op("dve", lambda e: e.memset(mhalf4[:], -0.5), writes=[cbuf])

    def rms_T(src, src_buf, gidx, dstT, dst_buf, sl, psT, psT_buf, evac="dve"):
        op("act", lambda e: e.activation(out=sl["hb"][:], in_=src, func=AF.Square, accum_out=sl["ss"][:]),
           reads=[src_buf], writes=[sl["hbb"], sl["ssb"]])
        op("pool", lambda e: e.tensor_scalar(out=sl["rs"][:], in0=sl["ss"][:], scalar1=1.0 / D, scalar2=1e-6, op0=ALU.mult, op1=ALU.add),
           reads=[sl["ssb"]], writes=[sl["rsb"]])
        op("pool", lambda e: e.tensor_tensor(out=sl["rr"][:], in0=sl["rs"][:], in1=mhalf[:], op=ALU.pow), reads=[sl["rsb"], cbuf], writes=[sl["rrb"]])
        op("dve", lambda e: e.scalar_tensor_tensor(out=sl["hb"][:], in0=src, scalar=sl["rr"][:], in1=gains[:, gidx, :],
                                                   op0=ALU.mult, op1=ALU.mult),
           reads=[src_buf, sl["rrb"], cbuf], writes=[sl["hbb"]])

        def tr(e):
            for k in range(8):
                ins = e.transpose(out=psT[:, k, :], in_=sl["hb"][:, k * 128:(k + 1) * 128], identity=ident_b[:])
            return ins
        op("pe", tr, reads=[sl["hbb"], cbuf], writes=[psT_buf])
        op(evac, lambda e: (e.tensor_copy(out=dstT, in_=psT[:]) if evac == "dve" else e.copy(out=dstT, in_=psT[:])),
           reads=[psT_buf], writes=[dst_buf])

    def rms_slot(stack):
        return {"hb": B.sb([128, D], BF16, stack), "ss": B.sb([128, 1], F32, stack), "rs": B.sb([128, 1], F32, stack),
                "rr": B.sb([128, 1], F32, stack), "hbb": Buf(), "ssb": Buf(), "rsb": Buf(), "rrb": Buf()}

    def phase_AP():
        with ExitStack() as st:
            hT = B.sb([128, 8, S], BF16, st, "hT")
            hTb = [Buf() for _ in range(NTT)]
            psT = B.ps([128, 8, 128], BF16, st)
            psTb = Buf()
            wsl = [(B.sb([128, 8, 512], BF16, st), Buf(), B.ds()) for _ in range(2)]
            for cb in range(2):
                dma("pool", wsl[cb][0][:], I["w_in"][:, cb * 512:(cb + 1) * 512].rearrange("(k p) n -> p k n", p=128), wsl[cb][2], writes=[wsl[cb][1]])
            with ExitStack() as st2:
                xs = [(B.sb([128, D], F32, st2), Buf(), B.ds()) for _ in range(2)]
                sls = [rms_slot(st2) for _ in range(2)]
                for tt in range(NTT):
                    xt, xb, xd = xs[tt % 2]
                    dma("sp", xt[:], I["x"][tt * 128:(tt + 1) * 128, :], xd, writes=[xb])
                    rms_T(xt[:], xb, 0, hT[:, :, tt * 128:(tt + 1) * 128], hTb[tt], sls[tt % 2], psT, psTb,
                          evac=("dve" if tt % 2 == 0 else "act"))
                B.barrier()
            if upto == "A":
                return
            with ExitStack() as st2:
                pbank = [(B.ps([128, 512], F32, st2), Buf()) for _ in range(4)]
                stg = [(B.sb([128, S], BF16, st2), Buf(), B.ds()) for _ in range(2)]
                vst = [(B.sb([128, 512], BF16, st2), Buf(), B.ds()) for _ in range(3)]
                hT_all = hTb
                npb = 0
                nst = 0
                nv = 0
                for cb in range(16):
                    wt, wb, wd = wsl[cb % 2]
                    if cb >= 2:
                        dma("pool", wt[:], I["w_in"][:, cb * 512:(cb + 1) * 512].rearrange("(k p) n -> p k n", p=128), wd, writes=[wb])
                    if 4 <= cb < 6:
                        for tt in range(NTT):
                            pt, pb = pbank[npb % 4]; npb += 1

                            def mm(e, pt=pt, tt=tt, wt=wt):
                                for k in range(8):
                                    ins = e.matmul(pt[:], lhsT=hT[:, k, tt * 128:(tt + 1) * 128], rhs=wt[:, k, :], start=(k == 0), stop=(k == 7))
                                return ins
                            op("pe", mm, reads=[wb, hT_all[tt]], writes=[pb])
                            vt, vb, vd = vst[nv % 3]; nv += 1
                            eng = "dve" if tt % 2 == 0 else "act"
                            op(eng, lambda e, vt=vt, pt=pt, eng=eng: (e.tensor_copy(out=vt[:], in_=pt[:]) if eng == "dve" else e.copy(out=vt[:], in_=pt[:])),
                               reads=[pb], writes=[vb])
                            dma("sp", Vs.ap()[tt * 128:(tt + 1) * 128, (cb - 4) * 512:(cb - 3) * 512], vt[:], vd, reads=[vb], writes=[db["Vs"]])
                        continue
                    for ct in range(4):
                        gcol = cb * 512 + ct * 128
                        sg, sgb, sgd = stg[nst % 2]; nst += 1
                        for tb in range(NTB):
                            pt, pb = pbank[npb % 4]; npb += 1

                            def mm(e, pt=pt, tb=tb, wt=wt, ct=ct):
                                for k in range(8):
                                    ins = e.matmul(pt[:], lhsT=wt[:, k, ct * 128:(ct + 1) * 128], rhs=hT[:, k, tb * 512:(tb + 1) * 512], start=(k == 0), stop=(k == 7))
                                return ins
                            op("pe", mm, reads=[wb] + hT_all[tb * 4:tb * 4 + 4], writes=[pb])
                            dst = sg[:, tb * 512:(tb + 1) * 512]
                            if gcol < 1024:
                                op("act", lambda e, dst=dst, pt=pt: e.mul(out=dst, in_=pt[:], mul=0.125), reads=[pb], writes=[sgb])
                            elif gcol < 2048:
                                op("dve", lambda e, dst=dst, pt=pt: e.tensor_copy(out=dst, in_=pt[:]), reads=[pb], writes=[sgb])
                            elif gcol < 4096:
                                eng = "dve" if tb % 2 == 0 else "act"
                                op(eng, lambda e, dst=dst, pt=pt, eng=eng: (e.tensor_copy(out=dst, in_=pt[:]) if eng == "dve" else e.copy(out=dst, in_=pt[:])),
                                   reads=[pb], writes=[sgb])
                            elif gcol < 5120:
                                op("act", lambda e, dst=dst, pt=pt: e.mul(out=dst, in_=pt[:], mul=0.0625), reads=[pb], writes=[sgb])
                            else:
                                op("act", lambda e, dst=dst, pt=pt: e.activation(out=dst, in_=pt[:], func=AF.Sigmoid), reads=[pb], writes=[sgb])
                        if gcol < 1024:
                            dd, dbuf, r0 = QTs, db["QTs"], gcol
                        elif gcol < 2048:
                            dd, dbuf, r0 = KTs, db["KTs"], gcol - 1024
                        elif gcol < 4096:
                            dd, dbuf, r0 = UTs, db["UTs"], gcol - 3072
                        elif gcol < 5120:
                            dd, dbuf, r0 = XQs, db["XQs"], gcol - 4096
                        else:
                            dd, dbuf, r0 = GTs, db["GTs"], gcol - 5120
                        dma("sp", dd.ap()[r0:r0 + 128, :], sg[:], sgd, reads=[sgb], writes=[dbuf])
                B.barrier()

    def phase_B(gen=None):
        with ExitStack() as st:
            lqk = B.sb([128, 4, 64], F32, st)
            lpr = B.sb([128, 2, 64], F32, st)
            lsum = B.sb([128, 2], F32, st)
            lexp = B.sb([128, 2], F32, st)
            neglam = B.sb([128, 1], F32, st)
            relb15 = B.sb([128, 8], F32, st)
            subg = B.sb([128, 128], F32, st)
            lb = Buf(); lds = B.ds()
            dma("sp", lqk[:], I["lqk"], lds, writes=[lb])
            dma("sp", relb15[:], I["relb15"], lds, writes=[lb])
            dma("sp", subg[:], I["subg"], lds, writes=[lb])
            lqv = lqk[:].rearrange("p (a b) d -> p a b d", b=2)
            op("dve", lambda e: e.tensor_tensor(out=lpr[:], in0=lqv[:, :, 0, :], in1=lqv[:, :, 1, :], op=ALU.mult), reads=[lb], writes=[lb])
            op("dve", lambda e: e.reduce_sum(out=lsum[:], in_=lpr[:], axis=AX.X), reads=[lb], writes=[lb])
            op("act", lambda e: e.activation(out=lexp[:], in_=lsum[:], func=AF.Exp), reads=[lb], writes=[lb])
            op("dve", lambda e: e.tensor_tensor(out=neglam[:], in0=lexp[:, 1:2], in1=lexp[:, 0:1], op=ALU.subtract), reads=[lb], writes=[lb])
            op("dve", lambda e: e.tensor_scalar(out=neglam[:], in0=neglam[:], scalar1=-LAM_INIT, scalar2=None, op0=ALU.add), reads=[lb], writes=[lb])

            MB = B.sb([128, 8, 1152], BF16, st, "MB")
            mbb = Buf()
            with ExitStack() as st2:
                relb = B.sb([32, 8], F32, st2)
                onehot = B.sb([32, 1280], F32, st2)
                antiid = B.sb([128, 128], F32, st2)
                maskmb = B.sb([128, 1152], F32, st2)
                gsb = B.sb([8, 1280], F32, st2)
                hk = [(B.sb([128, 1152], F32, st2), Buf(), B.ds()) for _ in range(2)]
                tb_ = Buf(); tds = B.ds()
                dma("sp", relb[:], I["relb"], tds, writes=[tb_])
                dma("sp", onehot[:], I["onehot"], tds, writes=[tb_])
                dma("sp", antiid[:], I["antiid"], tds, writes=[tb_])
                dma("sp", maskmb[:], I["maskmb"], tds, writes=[tb_])
                pg = [(B.ps([128, 512], F32, st2), Buf()) for _ in range(3)]
                for n in range(3):
                    n0, n1 = n * 512, min(1280, (n + 1) * 512)
                    op("pe", lambda e, n=n, n0=n0, n1=n1: e.matmul(pg[n][0][0:8, 0:n1 - n0], lhsT=relb[:], rhs=onehot[:, n0:n1], start=True, stop=True),
                       reads=[tb_], writes=[pg[n][1]])
                    op("dve", lambda e, n=n, n0=n0, n1=n1: e.tensor_copy(out=gsb[:, n0:n1], in_=pg[n][0][0:8, 0:n1 - n0]), reads=[pg[n][1]], writes=[tb_])
                gdd = B.ds()
                dma("sp", Gd.ap(), gsb[:], gdd, reads=[tb_], writes=[db["Gd"]])
                for h in range(8):
                    ht, hb_, hd = hk[h % 2]
                    dma("sp", ht[:], bass.AP(Gd, h * 1280, [[1, 128], [1, 1152]]), hd, reads=[db["Gd"]], writes=[hb_])
                    for n in range(3):
                        n0, n1 = n * 512, min(1152, (n + 1) * 512)
                        op("pe", lambda e, n=n, n0=n0, n1=n1, ht=ht: e.matmul(pg[n][0][:, 0:n1 - n0], lhsT=antiid[:], rhs=ht[:, n0:n1], start=True, stop=True),
                           reads=[tb_, hb_], writes=[pg[n][1]])
                        op("dve", lambda e, n=n, n0=n0, n1=n1, h=h: e.scalar_tensor_tensor(out=MB[:, h, n0:n1], in0=pg[n][0][:, 0:n1 - n0], scalar=relb15[:, h:h + 1],
                                                                                             in1=maskmb[:, n0:n1], op0=ALU.subtract, op1=ALU.add),
                           reads=[pg[n][1], tb_, lb], writes=[mbb])
                B.barrier()

            sets = []
            for s_ in range(2):
                sets.append({"QT": B.sb([128, S], BF16, st), "KT": B.sb([128, S], BF16, st), "V": B.sb([128, NTT, 129], BF16, st),
                             "b": Buf(), "ds": B.ds()})
            for s_ in sets:
                op("dve", lambda e, s_=s_: e.memset(s_["V"][:, :, 128:129], 1.0), writes=[s_["b"]])
            sc = [(B.ps([128, 2, 512], F32, st), Buf()) for _ in range(2)]
            ob = [(B.ps([128, 512], F32, st), Buf()) for _ in range(3)]
            psT = B.ps([128, 8, 128], BF16, st)
            psTb = Buf()
            NPT = 4
            PT = [(B.sb([128, 2, 512], BF16, st), Buf()) for _ in range(NPT)]
            oc = [(B.sb([128, 3, 387], F32, st), Buf()) for _ in range(2)]
            fin = [{"rr": B.sb([128, 8], F32, st), "t1": B.sb([128, 4, 128], F32, st), "ot": B.sb([128, 4, 128], F32, st),
                    "ss": B.sb([128, 4], F32, st), "rs": B.sb([128, 4], F32, st), "r2": B.sb([128, 4], F32, st),
                    "yb": B.sb([128, 4, 128], BF16, st), "b": Buf()} for _ in range(3)]
            ystg = [(B.sb([128, 512], BF16, st), Buf(), B.ds()) for _ in range(3)]
            state = {"nfin": 0, "nys": 0, "noc": 0}
            pending = []

            def tick():
                for p_ in pending:
                    p_[0] -= 1
                while pending and pending[0][0] <= 0:
                    pending.pop(0)[1]()

            def oreg(r):
                return r // 3, (r % 3) * 129

            blocks = []
            for h in range(8):
                for j in range(NTB):
                    nkt = 4 * (j + 1)
                    for kt in range(nkt):
                        blocks.append((h, j, kt, nkt))

            def load_head(h):
                hs = sets[h % 2]
                dma("sp", hs["QT"][:], QTs.ap()[h * 128:(h + 1) * 128, :], hs["ds"], reads=[db["QTs"]], writes=[hs["b"]])
                dma("sp", hs["KT"][:], KTs.ap()[h * 128:(h + 1) * 128, :], hs["ds"], reads=[db["KTs"]], writes=[hs["b"]])
                dma("sp", hs["V"][:, :, 0:128], Vs.ap()[:, h * 128:(h + 1) * 128].rearrange("(t p) e -> p t e", p=128), hs["ds"],
                    reads=[db["Vs"]], writes=[hs["b"]])

            def emit_scores(n):
                h, j, kt, nkt = blocks[n]
                hs = sets[h % 2]
                m = max(0, kt - 4 * j)
                qlo = 128 * m
                near = kt >= 4 * j - 2
                off = 512 * j - 128 * kt + 384
                pt, pb = sc[n % 2]

                def smm(e):
                    for c in range(2):
                        ins = e.matmul(pt[:, c, qlo:512], lhsT=hs["KT"][64 * c:64 * c + 64, kt * 128:(kt + 1) * 128],
                                       rhs=hs["QT"][64 * c:64 * c + 64, j * 512 + qlo:(j + 1) * 512], start=True, stop=not near)
                    if near:
                        for c in range(2):
                            ins = e.matmul(pt[:, c, qlo:512], lhsT=ident_b[:], rhs=MB[:, h, off + qlo:off + 512], start=False, stop=True)
                    return ins
                op("pe", smm, reads=[hs["b"], mbb, cbuf], writes=[pb])
                ptile, ptb = PT[n % NPT]
                if near:
                    op("act", lambda e: e.activation(out=ptile[:, :, qlo:512], in_=pt[:, :, qlo:512], func=AF.Exp), reads=[pb], writes=[ptb])
                else:
                    op("act", lambda e: e.activation(out=ptile[:], in_=pt[:], func=AF.Exp), reads=[pb], writes=[ptb])

            def emit_av(n):
                h, j, kt, nkt = blocks[n]
                hs = sets[h % 2]
                m = max(0, kt - 4 * j)
                ptile, ptb = PT[n % NPT]

                def avmm(e):
                    for c in range(2):
                        for qs in range(m, 4):
                            bank, co = oreg(c * 4 + qs)
                            first = (kt == 0) and ((c * 4 + qs) % 3 == 0)
                            ins = e.matmul(ob[bank][0][:, co:co + 129], lhsT=ptile[:, c, qs * 128:(qs + 1) * 128], rhs=hs["V"][:, kt, :],
                                           start=first, stop=(kt == 4 * j + qs), skip_group_check=True)
                    return ins
                op("pe", avmm, reads=[ptb, hs["b"]], writes=[ob[0][1], ob[1][1], ob[2][1]])
                if kt == nkt - 1:
                    finalize(h, j)

            def finalize(h, j):
                o_, ocb = oc[state["noc"] % 2]; state["noc"] += 1
                for bk in range(3):
                    w_ = 387 if bk < 2 else 258
                    op("dve", lambda e, bk=bk, w_=w_: e.tensor_copy(out=o_[:, bk, 0:w_], in_=ob[bk][0][:, 0:w_]), reads=[ob[bk][1]], writes=[ocb])
                ys, ysb, ysd = ystg[state["nys"] % 3]; state["nys"] += 1
                f = fin[state["nfin"] % 3]; state["nfin"] += 1
                reg = o_[:].rearrange("p a b -> p (a b)")[:, 0:1032].rearrange("p (r c) -> p r c", c=129)
                fb = f["b"]
                op("dve", lambda e: e.reciprocal(out=f["rr"][:], in_=reg[:, :, 128]), reads=[ocb], writes=[fb])
                op("dve", lambda e: e.tensor_scalar(out=f["rr"][:, 4:8], in0=f["rr"][:, 4:8], scalar1=neglam[:, 0:1], scalar2=None, op0=ALU.mult), reads=[fb, lb], writes=[fb])
                op("dve", lambda e: e.tensor_tensor(out=f["t1"][:], in0=reg[:, 4:8, 0:128], in1=fv(f["rr"][:, 4:5], [[1, 4], [0, 128]]), op=ALU.mult), reads=[ocb, fb], writes=[fb])
                op("dve", lambda e: e.tensor_tensor(out=f["ot"][:], in0=reg[:, 0:4, 0:128], in1=fv(f["rr"][:, 0:1], [[1, 4], [0, 128]]), op=ALU.mult), reads=[ocb, fb], writes=[fb])
                op("dve", lambda e: e.tensor_tensor(out=f["ot"][:], in0=f["ot"][:], in1=f["t1"][:], op=ALU.add), reads=[fb], writes=[fb])
                op("dve", lambda e: e.tensor_tensor(out=f["t1"][:], in0=f["ot"][:], in1=f["ot"][:], op=ALU.mult), reads=[fb], writes=[fb])
                op("dve", lambda e: e.reduce_sum(out=f["ss"][:], in_=f["t1"][:], axis=AX.X), reads=[fb], writes=[fb])
                op("pool", lambda e: e.tensor_scalar(out=f["rs"][:], in0=f["ss"][:], scalar1=1.0 / (128 * 0.64), scalar2=1e-6 / 0.64, op0=ALU.mult, op1=ALU.add),
                   reads=[fb], writes=[fb])
                op("pool", lambda e: e.tensor_tensor(out=f["r2"][:], in0=f["rs"][:], in1=mhalf4[:], op=ALU.pow), reads=[fb, cbuf], writes=[fb])
                op("dve", lambda e: e.tensor_tensor(out=f["t1"][:], in0=f["ot"][:], in1=fv(f["r2"][:, 0:1], [[1, 4], [0, 128]]), op=ALU.mult), reads=[fb], writes=[fb])
                op("dve", lambda e: e.tensor_tensor(out=f["yb"][:], in0=f["t1"][:], in1=fv(subg[:, 0:1], [[0, 4], [1, 128]]), op=ALU.mult), reads=[fb, lb], writes=[fb])

                def later():
                    def tr(e):
                        for qs in range(4):
                            ins = e.transpose(out=psT[:, qs, :], in_=f["yb"][:, qs, :], identity=ident_b[:])
                        return ins
                    op("pe", tr, reads=[fb, cbuf], writes=[psTb])
                    op("dve", lambda e: e.tensor_copy(out=ys[:].rearrange("p (a b) -> p a b", a=4), in_=psT[:, 0:4, :]), reads=[psTb], writes=[ysb])
                    dma("sp", YAs.ap()[h * 128:(h + 1) * 128, j * 512:(j + 1) * 512], ys[:], ysd, reads=[ysb], writes=[db["YAs"]])
                pending.append([10, later])

            NBLK = len(blocks)
            load_head(0)
            for n in range(NBLK + 2):
                if gen is not None and n % 4 == 0:
                    next(gen, None)
                if n < NBLK:
                    h, j, kt, nkt = blocks[n]
                    if j == 0 and kt == 2 and h + 1 < 8:
                        load_head(h + 1)
                    emit_scores(n)
                if n >= 2:
                    emit_av(n - 2)
                tick()
            while pending:
                pending.pop(0)[1]()
            B.barrier()

    def phase_C(gen=None):
        def pull(n):
            if gen is not None:
                for _ in range(n):
                    next(gen, None)

        with ExitStack() as st:
            memnT = B.sb([128, 8, 256], BF16, st)
            mnb = Buf()
            KxT = B.sb([128, 8, 256], BF16, st)
            Vx = B.sb([128, 2, 4, 257], BF16, st)
            kvb = Buf()
            psT = B.ps([128, 8, 128], BF16, st)
            psTb = Buf()
            with ExitStack() as st2:
                pk = [(B.ps([128, 512], F32, st2), Buf()) for _ in range(2)]
                wkv = B.sb([128, 8, 2048], BF16, st2)
                wkb = Buf(); wkd = B.ds()
                for n in range(4):
                    dma("pool", wkv[:, :, n * 512:(n + 1) * 512], I["w_mem_kv"][:, n * 512:(n + 1) * 512].rearrange("(k p) n -> p k n", p=128), wkd, writes=[wkb])
                ms = [(B.sb([128, D], F32, st2), Buf(), B.ds()) for _ in range(2)]
                sls = [rms_slot(st2) for _ in range(2)]
                for mt in range(2):
                    xt, xb, xd = ms[mt]
                    dma("sp", xt[:], I["mem"][mt * 128:(mt + 1) * 128, :], xd, writes=[xb])
                    rms_T(xt[:], xb, 1, memnT[:, :, mt * 128:(mt + 1) * 128], mnb, sls[mt], psT, psTb)
                op("dve", lambda e: e.memset(Vx[:, :, :, 256:257], 1.0), writes=[kvb])
                for ct in range(8):
                    pt, pb = pk[ct % 2]

                    def mm(e, pt=pt, ct=ct):
                        for k in range(8):
                            ins = e.matmul(pt[:, 0:256], lhsT=wkv[:, k, ct * 128:(ct + 1) * 128], rhs=memnT[:, k, :], start=(k == 0), stop=(k == 7))
                        return ins
                    op("pe", mm, reads=[wkb, mnb], writes=[pb])
                    op("dve", lambda e, pt=pt, ct=ct: e.tensor_copy(out=KxT[:, ct, :], in_=pt[:, 0:256]), reads=[pb], writes=[kvb])
                    pull(3)
                n = 0
                for mt in range(2):
                    for half in range(2):
                        pt, pb = pk[n % 2]; n += 1

                        def mm(e, pt=pt, mt=mt, half=half):
                            for k in range(8):
                                ins = e.matmul(pt[:], lhsT=memnT[:, k, mt * 128:(mt + 1) * 128], rhs=wkv[:, k, 1024 + half * 512:1024 + (half + 1) * 512], start=(k == 0), stop=(k == 7))
                            return ins
                        op("pe", mm, reads=[wkb, mnb], writes=[pb])
                        op("dve", lambda e, pt=pt, mt=mt, half=half: e.tensor_copy(out=Vx[:, mt, 2 * half:2 * half + 2, 0:256], in_=pt[:].rearrange("p (a b) -> p a b", a=2)),
                           reads=[pb], writes=[kvb])
                B.barrier()
            xqs = [(B.sb([128, 2, S], BF16, st), Buf(), B.ds()) for _ in range(2)]
            psS = [(B.ps([128, 512], F32, st), Buf()) for _ in range(2)]
            psO = (B.ps([128, 4, 512], F32, st), Buf())
            PX = [[(B.sb([128, 512], BF16, st), Buf()) for _ in range(2)] for _ in range(2)]
            fx = [{"rr": B.sb([128, 4], F32, st), "yb": B.sb([128, 4, 256], BF16, st), "b": Buf()} for _ in range(2)]
            ystg = [(B.sb([128, 2, 512], BF16, st), Buf(), B.ds()) for _ in range(2)]

            def load_xq(hx):
                xq, xqb, xqd = xqs[hx % 2]
                dma("sp", xq[:], XQs.ap()[hx * 256:(hx + 1) * 256, :].rearrange("(a p) t -> p a t", p=128), xqd, reads=[db["XQs"]], writes=[xqb])

            its = [(hx, tb) for hx in range(4) for tb in range(NTB)]

            def emit_S(n):
                hx, tb = its[n]
                xq, xqb, xqd = xqs[hx % 2]
                slot = n % 2
                for mt in range(2):
                    pt, pb = psS[mt]

                    def smm(e, pt=pt, mt=mt):
                        for dt in range(2):
                            ins = e.matmul(pt[:], lhsT=KxT[:, 2 * hx + dt, mt * 128:(mt + 1) * 128], rhs=xq[:, dt, tb * 512:(tb + 1) * 512], start=(dt == 0), stop=(dt == 1))
                        return ins
                    op("pe", smm, reads=[kvb, xqb], writes=[pb])
                    px, pxb = PX[mt][slot]
                    op("act", lambda e, px=px, pt=pt: e.activation(out=px[:], in_=pt[:], func=AF.Exp), reads=[pb], writes=[pxb])

            def emit_O(n):
                hx, tb = its[n]
                slot = n % 2
                po, pob = psO
                f = fx[n % 2]

                def omm(e):
                    for qs in range(4):
                        for mt in range(2):
                            ins = e.matmul(po[:, qs, 0:257], lhsT=PX[mt][slot][0][:, qs * 128:(qs + 1) * 128], rhs=Vx[:, mt, hx, :], start=(mt == 0), stop=(mt == 1))
                    return ins
                op("pe", omm, reads=[PX[0][slot][1], PX[1][slot][1], kvb], writes=[pob])
                op("dve", lambda e: e.reciprocal(out=f["rr"][:], in_=po[:, :, 256]), reads=[pob], writes=[f["b"]])
                op("dve", lambda e: e.tensor_tensor(out=f["yb"][:], in0=po[:, :, 0:256], in1=fv(f["rr"][:, 0:1], [[1, 4], [0, 256]]), op=ALU.mult), reads=[pob, f["b"]], writes=[f["b"]])

            def emit_T(n):
                hx, tb = its[n]
                f = fx[n % 2]
                ys, ysb, ysd = ystg[n % 2]

                def tr(e):
                    for qs in range(4):
                        for dt in range(2):
                            ins = e.transpose(out=psT[:, dt * 4 + qs, :], in_=f["yb"][:, qs, dt * 128:(dt + 1) * 128], identity=ident_b[:])
                    return ins
                op("pe", tr, reads=[f["b"], cbuf], writes=[psTb])
                op("act", lambda e: e.copy(out=ys[:].rearrange("p a (q t) -> p (a q) t", q=4), in_=psT[:]), reads=[psTb], writes=[ysb])
                dma("sp", YXs.ap()[hx * 256:(hx + 1) * 256, tb * 512:(tb + 1) * 512].rearrange("(a p) t -> p a t", p=128), ys[:], ysd, reads=[ysb], writes=[db["YXs"]])

            load_xq(0)
            for n in range(len(its)):
                hx, tb = its[n]
                if tb == 0 and hx + 1 < 4:
                    load_xq(hx + 1)
                emit_S(n)
                if n >= 1:
                    emit_T(n - 1)
                emit_O(n)
                pull(8)
            emit_T(len(its) - 1)
            B.barrier()

    Dst = {}

    def phase_D1():
        st = ExitStack()
        Dst["st"] = st
        if True:
            WW = B.sb([128, 2, 32, 256], F32, st, "WW")
            wwb = Buf()
            AA1 = B.sb([128, 2, 32], F32, st)
            AA2 = B.sb([128, 2, 32], F32, st)
            sd = B.sb([128, 8], F32, st)
            tb_ = Buf(); tds = B.ds()
            r1_ = B.sb([128, 2, 32], F32, st); r2_ = B.sb([128, 2, 32], F32, st)
            Dst.update(WW=WW, wwb=wwb, AA1=AA1, AA2=AA2, tb_=tb_, r1=r1_, r2=r2_)
            dma("sp", sd[:], I["s_d"], tds, writes=[tb_])
            with ExitStack() as st1:
                are = B.sb([128, 32], F32, st1); aim = B.sb([128, 32], F32, st1); ldt = B.sb([128, 32], F32, st1)
                bre = B.sb([128, 32, 16], F32, st1); bim = B.sb([128, 32, 16], F32, st1)
                cre = B.sb([128, 32, 16], F32, st1); cim = B.sb([128, 32, 16], F32, st1)
                for t_, k_ in ((are, "s_are"), (aim, "s_aim"), (ldt, "s_ldt"), (bre, "s_bre"), (bim, "s_bim"), (cre, "s_cre"), (cim, "s_cim")):
                    dma("sp", t_[:], I[k_], tds, writes=[tb_])
                sel_f = B.sb([128, 4, 128], F32, st1); sel_b = B.sb([128, 4, 128], BF16, st1)
                maskd = B.sb([128, 128], F32, st1)
                dma("sp", sel_f[:], I["sel"], tds, writes=[tb_])
                dma("sp", maskd[:], I["maskd"], tds, writes=[tb_])
                T = lambda shape: B.sb(shape, F32, st1)
                dtt = T([128, 32]); adr = T([128, 32]); th = T([128, 32]); cc = T([128, 32]); ss_ = T([128, 32])
                t1 = T([128, 32]); t2 = T([128, 32]); hpi = T([128, 1])
                ER = T([128, 32, 17]); EI = T([128, 32, 17]); MAGP = T([128, 32, 17]); MAGN = T([128, 32, 17])
                PPr = T([128, 32, 17]); PPi = T([128, 32, 17]); PNr = T([128, 32, 17]); PNi = T([128, 32, 17])
                bbr = T([128, 32, 16]); bbi = T([128, 32, 16])
                tmpA = T([128, 32, 16]); tmpB = T([128, 32, 16])
                tb2 = tb_

                def D_(fn):
                    op("dve", fn, reads=[tb2], writes=[tb2])

                def A_(fn):
                    op("act", fn, reads=[tb2], writes=[tb2])
                D_(lambda e: e.tensor_copy(out=sel_b[:], in_=sel_f[:]))
                maskd_b = B.sb([128, 128], BF16, st1)
                D_(lambda e: e.tensor_copy(out=maskd_b[:], in_=maskd[:]))
                D_(lambda e: e.memset(hpi[:], math.pi / 2))
                A_(lambda e: e.activation(out=dtt[:], in_=ldt[:], func=AF.Exp))
                D_(lambda e: e.tensor_tensor(out=adr[:], in0=are[:], in1=dtt[:], op=ALU.mult))
                D_(lambda e: e.tensor_tensor(out=th[:], in0=aim[:], in1=dtt[:], op=ALU.mult))
                A_(lambda e: e.activation(out=ss_[:], in_=th[:], func=AF.Sin, scale=1.0 / 32))
                A_(lambda e: e.activation(out=cc[:], in_=th[:], func=AF.Sin, scale=1.0 / 32, bias=hpi[:]))
                for _ in range(5):
                    D_(lambda e: e.tensor_tensor(out=t1[:], in0=cc[:], in1=cc[:], op=ALU.mult))
                    D_(lambda e: e.tensor_tensor(out=t2[:], in0=ss_[:], in1=ss_[:], op=ALU.mult))
                    D_(lambda e: e.scalar_tensor_tensor(out=ss_[:], in0=cc[:], scalar=2.0, in1=ss_[:], op0=ALU.mult, op1=ALU.mult))
                    D_(lambda e: e.tensor_tensor(out=cc[:], in0=t1[:], in1=t2[:], op=ALU.subtract))
                D_(lambda e: e.memset(ER[:, :, 0:1], 1.0))
                D_(lambda e: e.memset(EI[:, :, 0:1], 0.0))
                D_(lambda e: e.tensor_copy(out=ER[:, :, 1], in_=cc[:]))
                D_(lambda e: e.tensor_copy(out=EI[:, :, 1], in_=ss_[:]))
                tmpE = [T([128, 32, 8]) for _ in range(4)]
                k = 1
                while k < 16:
                    a_r, a_i = ER[:, :, 1:k + 1], EI[:, :, 1:k + 1]
                    b_r = fv(ER[:, :, k:k + 1], [[17, 32], [0, k]])
                    b_i = fv(EI[:, :, k:k + 1], [[17, 32], [0, k]])
                    q = [t_[:, :, 0:k] for t_ in tmpE]
                    D_(lambda e, a_r=a_r, b_r=b_r, q=q: e.tensor_tensor(out=q[0], in0=a_r, in1=b_r, op=ALU.mult))
                    D_(lambda e, a_i=a_i, b_i=b_i, q=q: e.tensor_tensor(out=q[1], in0=a_i, in1=b_i, op=ALU.mult))
                    D_(lambda e, a_r=a_r, b_i=b_i, q=q: e.tensor_tensor(out=q[2], in0=a_r, in1=b_i, op=ALU.mult))
                    D_(lambda e, a_i=a_i, b_r=b_r, q=q: e.tensor_tensor(out=q[3], in0=a_i, in1=b_r, op=ALU.mult))
                    D_(lambda e, k=k, q=q: e.tensor_tensor(out=ER[:, :, k + 1:2 * k + 1], in0=q[0], in1=q[1], op=ALU.subtract))
                    D_(lambda e, k=k, q=q: e.tensor_tensor(out=EI[:, :, k + 1:2 * k + 1], in0=q[2], in1=q[3], op=ALU.add))
                    k *= 2
                for tau in range(17):
                    A_(lambda e, tau=tau: e.activation(out=MAGP[:, :, tau], in_=adr[:], func=AF.Exp, scale=float(tau)))
                    A_(lambda e, tau=tau: e.activation(out=MAGN[:, :, tau], in_=adr[:], func=AF.Exp, scale=-float(tau)))
                D_(lambda e: e.tensor_tensor(out=PPr[:], in0=MAGP[:], in1=ER[:], op=ALU.mult))
                D_(lambda e: e.tensor_tensor(out=PPi[:], in0=MAGP[:], in1=EI[:], op=ALU.mult))
                D_(lambda e: e.tensor_tensor(out=PNr[:], in0=MAGN[:], in1=ER[:], op=ALU.mult))
                D_(lambda e: e.scalar_tensor_tensor(out=PNi[:], in0=MAGN[:], scalar=-1.0, in1=EI[:], op0=ALU.mult, op1=ALU.mult))
                xr = T([128, 32]); nr = T([128, 32]); ni = T([128, 32]); den = T([128, 32]); cfr = T([128, 32]); cfi = T([128, 32])
                D_(lambda e: e.tensor_scalar(out=xr[:], in0=PPr[:, :, 1], scalar1=-1.0, scalar2=None, op0=ALU.add))
                D_(lambda e: e.tensor_tensor(out=t1[:], in0=xr[:], in1=are[:], op=ALU.mult))
                D_(lambda e: e.tensor_tensor(out=t2[:], in0=PPi[:, :, 1], in1=aim[:], op=ALU.mult))
                D_(lambda e: e.tensor_tensor(out=nr[:], in0=t1[:], in1=t2[:], op=ALU.add))
                D_(lambda e: e.tensor_tensor(out=t1[:], in0=PPi[:, :, 1], in1=are[:], op=ALU.mult))
                D_(lambda e: e.tensor_tensor(out=t2[:], in0=xr[:], in1=aim[:], op=ALU.mult))
                D_(lambda e: e.tensor_tensor(out=ni[:], in0=t1[:], in1=t2[:], op=ALU.subtract))
                D_(lambda e: e.tensor_tensor(out=t1[:], in0=are[:], in1=are[:], op=ALU.mult))
                D_(lambda e: e.tensor_tensor(out=t2[:], in0=aim[:], in1=aim[:], op=ALU.mult))
                D_(lambda e: e.tensor_tensor(out=den[:], in0=t1[:], in1=t2[:], op=ALU.add))
                D_(lambda e: e.reciprocal(out=den[:], in_=den[:]))
                D_(lambda e: e.tensor_tensor(out=cfr[:], in0=nr[:], in1=den[:], op=ALU.mult))
                D_(lambda e: e.tensor_tensor(out=cfi[:], in0=ni[:], in1=den[:], op=ALU.mult))
                cfr_b = fv(cfr[:], [[1, 32], [0, 16]]); cfi_b = fv(cfi[:], [[1, 32], [0, 16]])
                D_(lambda e: e.tensor_tensor(out=tmpA[:], in0=bre[:], in1=cfr_b, op=ALU.mult))
                D_(lambda e: e.tensor_tensor(out=tmpB[:], in0=bim[:], in1=cfi_b, op=ALU.mult))
                D_(lambda e: e.tensor_tensor(out=bbr[:], in0=tmpA[:], in1=tmpB[:], op=ALU.subtract))
                D_(lambda e: e.tensor_tensor(out=tmpA[:], in0=bim[:], in1=cfr_b, op=ALU.mult))
                D_(lambda e: e.tensor_tensor(out=tmpB[:], in0=bre[:], in1=cfi_b, op=ALU.mult))
                D_(lambda e: e.tensor_tensor(out=bbi[:], in0=tmpA[:], in1=tmpB[:], op=ALU.add))
                D_(lambda e: e.tensor_copy(out=AA1[:, 0, :], in_=PPr[:, :, 16]))
                D_(lambda e: e.tensor_copy(out=AA1[:, 1, :], in_=PPr[:, :, 16]))
                D_(lambda e: e.tensor_scalar(out=AA2[:, 0, :], in0=PPi[:, :, 16], scalar1=-1.0, scalar2=None, op0=ALU.mult))
                D_(lambda e: e.tensor_copy(out=AA2[:, 1, :], in_=PPi[:, :, 16]))

                if upto == "D0":
                    B.barrier()
                    return
                X = [T([128, 16, 16]) for _ in range(4)]
                xb_ = Buf()
                Bx = [{"Bexp": B.sb([128, 2, 512], BF16, st1), "Bfull": B.sb([128, 2, 512], BF16, st1), "BblkT": B.sb([128, 8, 128], BF16, st1),
                       "bxb": Buf(), "bfb": Buf(), "btb": Buf()} for _ in range(2)]
                CexpB = [B.sb([128, 2, 512], BF16, st1) for _ in range(4)]
                cxb = [Buf() for _ in range(4)]
                cxd = [B.ds() for _ in range(4)]
                DblkB = [B.sb([128, 4, 512], BF16, st1) for _ in range(4)]
                dkb = [Buf() for _ in range(4)]
                Vp = [B.sb([128, 4, 256], BF16, st1) for _ in range(4)]
                vpb = [Buf() for _ in range(4)]
                uTn = (B.sb([128, S], BF16, st1), Buf(), B.ds())
                uTs = [(B.sb([128, 16, 256], BF16, st1), Buf()) for _ in range(2)]
                Yqs = [(B.sb([128, 16, 256], BF16, st1), Buf(), B.ds()) for _ in range(1)]
                psT = B.ps([128, 8, 128], BF16, st1); psTb = Buf()
                psD = [(B.ps([128, 512], F32, st1), Buf()) for _ in range(2)]
                psV = [(B.ps([128, 2, 256], F32, st1), Buf()) for _ in range(2)]
                psW = (B.ps([128, 2, 256], F32, st1), Buf())
                psY = [(B.ps([128, 512], F32, st1), Buf()) for _ in range(2)]
                for bx in Bx:
                    op("dve", lambda e, bx=bx: e.memset(bx["Bexp"][:], 0.0), writes=[bx["bxb"]])
                    op("dve", lambda e, bx=bx: e.memset(bx["Bfull"][:], 0.0), writes=[bx["bfb"]])
                for j in range(4):
                    op("dve", lambda e, j=j: e.memset(CexpB[j][:], 0.0), writes=[cxb[j]])

                def cmul(pr, pi_, qr, qi, outs_r, outs_i, neg_i, rbufs, wbufs):
                    op("dve", lambda e: e.tensor_tensor(out=X[0][:], in0=pr, in1=qr, op=ALU.mult), reads=rbufs, writes=[xb_])
                    op("dve", lambda e: e.tensor_tensor(out=X[1][:], in0=pi_, in1=qi, op=ALU.mult), reads=rbufs, writes=[xb_])
                    op("dve", lambda e: e.tensor_tensor(out=X[2][:], in0=pr, in1=qi, op=ALU.mult), reads=rbufs, writes=[xb_])
                    op("dve", lambda e: e.tensor_tensor(out=X[3][:], in0=pi_, in1=qr, op=ALU.mult), reads=rbufs, writes=[xb_])
                    for lo, hi, o in outs_r:
                        op("dve", lambda e, lo=lo, hi=hi, o=o: e.tensor_tensor(out=o, in0=X[0][lo:hi], in1=X[1][lo:hi], op=ALU.subtract), reads=[xb_], writes=wbufs)
                    for lo, hi, o in outs_i:
                        if neg_i:
                            op("dve", lambda e, lo=lo, hi=hi, o=o: e.scalar_tensor_tensor(out=o, in0=X[2][lo:hi], scalar=-1.0, in1=X[3][lo:hi], op0=ALU.mult, op1=ALU.subtract),
                               reads=[xb_], writes=wbufs)
                        else:
                            op("dve", lambda e, lo=lo, hi=hi, o=o: e.tensor_tensor(out=o, in0=X[2][lo:hi], in1=X[3][lo:hi], op=ALU.add), reads=[xb_], writes=wbufs)

                def blocked(tile, c):
                    v = tile[:].rearrange("p c (i g k) -> p c i g k", g=2, k=16)
                    return [(0, 64, v[0:64, c, :, 0, :]), (64, 128, v[64:128, c, :, 1, :])]

                def prep(m):
                    j = m % 4
                    bx = Bx[m % 2]
                    ppr = fv(PPr[:, m, 1:2], [[1, 16], [0, 16]]); ppi = fv(PPi[:, m, 1:2], [[1, 16], [0, 16]])
                    pnr = fv(PNr[:, m, 1:2], [[1, 16], [0, 16]]); pni = fv(PNi[:, m, 1:2], [[1, 16], [0, 16]])
                    prr = fv(PPr[:, m, 15:16], [[-1, 16], [0, 16]]); pri = fv(PPi[:, m, 15:16], [[-1, 16], [0, 16]])
                    c_r = fv(cre[:, m, 0:1], [[0, 16], [1, 16]]); c_i = fv(cim[:, m, 0:1], [[0, 16], [1, 16]])
                    b_r = fv(bbr[:, m, 0:1], [[0, 16], [1, 16]]); b_i = fv(bbi[:, m, 0:1], [[0, 16], [1, 16]])
                    cmul(pnr, pni, b_r, b_i, blocked(bx["Bexp"], 0), blocked(bx["Bexp"], 1), False, [tb2], [bx["bxb"]])
                    cmul(ppr, ppi, c_r, c_i, blocked(CexpB[j], 0), blocked(CexpB[j], 1), True, [tb2], [cxb[j]])
                    dma("sp", CXs.ap()[m], CexpB[j][:].rearrange("p c n -> p (c n)"), cxd[j], reads=[cxb[j]], writes=[db["CXs"]])

                def work(m):
                    q4, j = divmod(m, 4)
                    bx = Bx[m % 2]
                    uT, utb = uTs[q4 % 2]

                    def trB(e):
                        for c in range(2):
                            for it in range(4):
                                ins = e.transpose(out=psT[:, c * 4 + it, :], in_=bx["Bexp"][:, c, it * 128:(it + 1) * 128], identity=ident_b[:])
                        return ins
                    op("pe", trB, reads=[bx["bxb"], cbuf], writes=[psTb])
                    op("act", lambda e: e.copy(out=bx["BblkT"][:], in_=psT[:]), reads=[psTb], writes=[bx["btb"]])
                    for hb2 in range(2):
                        pv, pvb = psV[hb2]

                        def vmm(e, pv=pv, hb2=hb2):
                            for a_ in range(2):
                                it = hb2 * 2 + a_
                                for il in range(4):
                                    ins = e.matmul(pv[:, a_, :], lhsT=sel_b[32 * j:32 * j + 32, il, :], rhs=uT[32 * j:32 * j + 32, 4 * it + il, :],
                                                   start=(il == 0), stop=(il == 3), tile_position=(32 * j, 0))
                            return ins
                        op("pe", vmm, reads=[utb, tb2], writes=[pvb])
                        op("act", lambda e, pv=pv, hb2=hb2: e.copy(out=Vp[j][:, 2 * hb2:2 * hb2 + 2, :], in_=pv[:]), reads=[pvb], writes=[vpb[j]])
                    for it in range(4):
                        pd, pdb = psD[it % 2]

                        def dmm(e, pd=pd, it=it):
                            for c in range(2):
                                ins = e.matmul(pd[:], lhsT=bx["Bexp"][:, c, it * 128:(it + 1) * 128], rhs=CexpB[j][:, c, :], start=(c == 0), stop=(c == 1))
                            return ins
                        op("pe", dmm, reads=[bx["bxb"], cxb[j]], writes=[pdb])
                        op("act", lambda e, pd=pd, it=it: e.copy(out=DblkB[j][:, it, :], in_=pd[:]), reads=[pdb], writes=[dkb[j]])
                        op("pool", lambda e, it=it: e.tensor_tensor(out=DblkB[j][:, it, it * 128:(it + 1) * 128], in0=DblkB[j][:, it, it * 128:(it + 1) * 128], in1=maskd_b[:],
                                                                     op=ALU.mult), reads=[tb2], writes=[dkb[j]])
                    pw, pwb = psW

                    def wmm(e):
                        for c in range(2):
                            for it in range(4):
                                ins = e.matmul(pw[:, c, :], lhsT=bx["BblkT"][:, c * 4 + it, :], rhs=Vp[j][:, it, :], start=(it == 0), stop=(it == 3))
                        return ins
                    op("pe", wmm, reads=[bx["btb"], vpb[j]], writes=[pwb])
                    op("act", lambda e: e.activation(out=WW[:, 0, m, :], in_=pw[:, 0, :], func=AF.Copy, scale=AA1[:, 0, m:m + 1]), reads=[pwb, tb_], writes=[wwb])
                    op("act", lambda e: e.activation(out=WW[:, 1, m, :], in_=pw[:, 1, :], func=AF.Copy, scale=AA1[:, 0, m:m + 1]), reads=[pwb, tb_], writes=[wwb])
                    op("dve", lambda e: e.scalar_tensor_tensor(out=WW[:, 0, m, :], in0=pw[:, 1, :], scalar=AA2[:, 0, m:m + 1], in1=WW[:, 0, m, :], op0=ALU.mult, op1=ALU.add),
                       reads=[pwb, tb_, wwb], writes=[wwb])
                    op("dve", lambda e: e.scalar_tensor_tensor(out=WW[:, 1, m, :], in0=pw[:, 0, :], scalar=AA2[:, 1, m:m + 1], in1=WW[:, 1, m, :], op0=ALU.mult, op1=ALU.add),
                       reads=[pwb, tb_, wwb], writes=[wwb])

                def yintra(q4):
                    uT, utb = uTs[q4 % 2]
                    Yq, yqb, yqd = Yqs[0]
                    for i in range(16):
                        py = psY[i % 2][0][:, 0:256]
                        pyb = psY[i % 2][1]

                        def ymm(e, py=py, i=i):
                            for it in range(i // 4 + 1):
                                for j in range(4):
                                    ins = e.matmul(py[32 * j:32 * j + 32, :], lhsT=DblkB[j][:, it, i * 32:(i + 1) * 32], rhs=Vp[j][:, it, :],
                                                   start=(it == 0), stop=(it == i // 4), tile_position=(0, 32 * j))
                            return ins
                        op("pe", ymm, reads=dkb + vpb, writes=[pyb])
                        op("dve", lambda e, py=py, i=i: e.scalar_tensor_tensor(out=Yq[:, i, :], in0=uT[:, i, :], scalar=sd[:, q4:q4 + 1], in1=py,
                                                                                op0=ALU.mult, op1=ALU.add), reads=[pyb, utb, tb_], writes=[yqb])
                    dma("sp", YIs.ap()[q4 * 128:(q4 + 1) * 128, :], Yq[:].rearrange("p i c -> p (i c)"), yqd, reads=[yqb], writes=[db["YIs"]])

                def load_u(q4):
                    un, unb, und = uTn
                    uT, utb = uTs[q4 % 2]
                    dma("sp", un[:], UTs.ap()[q4 * 128:(q4 + 1) * 128, :], und, reads=[db["UTs"]], writes=[unb])
                    unv = un[:].rearrange("p (c i) -> p i c", i=16)
                    for ih in range(2):
                        op("act", lambda e, ih=ih: e.copy(out=uT[:, ih * 8:(ih + 1) * 8, :], in_=unv[:, ih * 8:(ih + 1) * 8, :]), reads=[unb], writes=[utb])

                load_u(0)
                prep(0)
                for m in range(32):
                    q4, j = divmod(m, 4)
                    if j == 0 and q4 + 1 < 8:
                        load_u(q4 + 1)
                    if m + 1 < 32:
                        prep(m + 1)
                    work(m)
                    if j == 3:
                        yintra(q4)
                B.barrier()

    def rec_gen():
        WW, AA1, AA2, tb_ = Dst["WW"], Dst["AA1"], Dst["AA2"], Dst["tb_"]
        r1, r2 = Dst["r1"], Dst["r2"]
        halves = [(0, 16, Buf(), Buf()), (16, 32, Buf(), Buf())]
        for c in range(1, 256):
            for (p0, p1, wb2, rb) in halves:
                prev = WW[:, :, p0:p1, c - 1]
                cur = WW[:, :, p0:p1, c]
                prev_sw = fv(WW[:, 1:2, p0:p0 + 1, c - 1:c], [[-32 * 256, 2], [256, 16]])
                op("dve", lambda e, prev=prev, p0=p0, p1=p1: e.tensor_tensor(out=r1[:, :, p0:p1], in0=prev, in1=AA1[:, :, p0:p1], op=ALU.mult), reads=[wb2, tb_], writes=[rb])
                op("dve", lambda e, prev_sw=prev_sw, p0=p0, p1=p1: e.tensor_tensor(out=r2[:, :, p0:p1], in0=prev_sw, in1=AA2[:, :, p0:p1], op=ALU.mult), reads=[wb2, tb_], writes=[rb])
            for (p0, p1, wb2, rb) in halves:
                op("dve", lambda e, p0=p0, p1=p1: e.tensor_tensor(out=r1[:, :, p0:p1], in0=r1[:, :, p0:p1], in1=r2[:, :, p0:p1], op=ALU.add), reads=[rb], writes=[rb])
            for (p0, p1, wb2, rb) in halves:
                cur = WW[:, :, p0:p1, c]
                op("dve", lambda e, cur=cur, p0=p0, p1=p1: e.tensor_tensor(out=cur, in0=cur, in1=r1[:, :, p0:p1], op=ALU.add), reads=[rb, wb2], writes=[wb2])
            yield

    def phase_D2():
        WW, wwb = Dst["WW"], Dst["wwb"]
        if True:
            with ExitStack() as st1:
                Hb = B.sb([128, 2, 32, 256], BF16, st1); hbb = Buf()
                op("dve", lambda e: e.memset(Hb[:, :, :, 0:1], 0.0), writes=[hbb])
                for c in range(2):
                    op("dve" if c == 0 else "act", lambda e, c=c: (e.tensor_copy(out=Hb[:, c, :, 1:256], in_=WW[:, c, :, 0:255]) if c == 0
                                                                    else e.copy(out=Hb[:, c, :, 1:256], in_=WW[:, c, :, 0:255])), reads=[wwb], writes=[hbb])
                Cx = [(B.sb([128, 2, 512], BF16, st1), Buf(), B.ds()) for _ in range(8)]
                Yin = [(B.sb([128, 16, 256], BF16, st1), Buf(), B.ds()) for _ in range(2)]
                Yf = [(B.sb([128, 16, 256], F32, st1), Buf()) for _ in range(2)]
                zT = [(B.sb([128, S], BF16, st1), Buf(), B.ds()) for _ in range(2)]
                psY2 = [(B.ps([128, 512], F32, st1), Buf()) for _ in range(4)]
                npy = 0
                def load_q(q4):
                    yi, yib, yid = Yin[q4 % 2]
                    dma("sp", yi[:].rearrange("p i c -> p (i c)"), YIs.ap()[q4 * 128:(q4 + 1) * 128, :], yid, reads=[db["YIs"]], writes=[yib])
                    for j in range(4):
                        ct, ctb, ctd = Cx[(q4 % 2) * 4 + j]
                        dma("sp", ct[:].rearrange("p c n -> p (c n)"), CXs.ap()[q4 * 4 + j], ctd, reads=[db["CXs"]], writes=[ctb])

                load_q(0)
                for q4 in range(8):
                    yi, yib, yid = Yin[q4 % 2]
                    yf, yfb = Yf[q4 % 2]
                    z_, zb, zd = zT[q4 % 2]
                    if q4 + 1 < 8:
                        load_q(q4 + 1)
                    cxs = [(Cx[(q4 % 2) * 4 + j][0], Cx[(q4 % 2) * 4 + j][1]) for j in range(4)]
                    for i in range(16):
                        py, pyb = psY2[npy % 4]; npy += 1

                        def ymm(e, py=py, i=i, cxs=cxs, q4=q4):
                            for c in range(2):
                                for j in range(4):
                                    ins = e.matmul(py[32 * j:32 * j + 32, 0:256], lhsT=cxs[j][0][:, c, i * 32:(i + 1) * 32], rhs=Hb[:, c, q4 * 4 + j, :],
                                                   start=(c == 0), stop=(c == 1), tile_position=(0, 32 * j))
                            return ins
                        op("pe", ymm, reads=[hbb] + [c_[1] for c_ in cxs], writes=[pyb])
                        op("dve", lambda e, py=py, i=i, yi=yi, yf=yf: e.tensor_tensor(out=yf[:, i, :], in0=py[:, 0:256], in1=yi[:, i, :], op=ALU.add),
                           reads=[pyb, yib], writes=[yfb])
                    op("act", lambda e, z_=z_, yf=yf: e.activation(out=z_[:].rearrange("p (c i) -> p c i", i=16), in_=yf[:].rearrange("p i c -> p c i"), func=AF.Gelu_apprx_tanh),
                       reads=[yfb], writes=[zb])
                    dma("pool", ZTs.ap()[q4 * 128:(q4 + 1) * 128, :], z_[:], zd, reads=[zb], writes=[db["ZTs"]])
                B.barrier()
        Dst["st"].close()
    def phase_T1(after_weights=None):
        with ExitStack() as st:
            W = {}
            WB = {}
            for nm in ("glu_w", "w_br_attn", "w_br_ssm", "w_br_xattn", "w_out"):
                W[nm] = B.sb([128, 8, D], BF16, st, nm)
                WB[nm] = Buf()
                wd = B.ds()
                for n in range(2):
                    dma("pool", W[nm][:, :, n * 512:(n + 1) * 512], I[nm][:, n * 512:(n + 1) * 512].rearrange("(k p) n -> p k n", p=128), wd, writes=[WB[nm]])
            glub = B.sb([128, 8], F32, st)
            wb_ = Buf()
            dma("sp", glub[:], I["glu_b"], B.ds(), writes=[wb_])
            if after_weights is not None:
                after_weights()
            zt = (B.sb([128, 8, 512], BF16, st), Buf(), B.ds())
            ya = (B.sb([128, 8, 512], BF16, st), Buf(), B.ds())
            yx = (B.sb([128, 8, 512], BF16, st), Buf(), B.ds())
            gt = (B.sb([128, 24, 512], BF16, st), Buf(), B.ds())
            yssm = (B.sb([128, 8, 512], BF16, st), Buf())
            mixT = (B.sb([128, 8, 512], BF16, st), Buf())
            sig = [(B.sb([128, 512], F32, st), Buf()) for _ in range(2)]
            mm_ = [(B.sb([128, 3, 512], F32, st), Buf()) for _ in range(1)]
            xs = [(B.sb([128, D], F32, st), Buf(), B.ds()) for _ in range(2)]
            x1 = [(B.sb([128, D], F32, st), Buf(), B.ds()) for _ in range(2)]
            pG = [(B.ps([128, 512], F32, st), Buf()) for _ in range(2)]
            pB = [(B.ps([128, 512], F32, st), Buf()) for _ in range(3)]
            pO = [(B.ps([128, 512], F32, st), Buf()) for _ in range(2)]
            ng = 0; nx = 0; no = 0
            def load_z(tb):
                tsl = slice(tb * 512, (tb + 1) * 512)
                dma("act", zt[0][:], ZTs.ap()[:, tsl].rearrange("(k p) t -> p k t", p=128), zt[2], reads=[db["ZTs"]], writes=[zt[1]])

            def load_rest(tb):
                tsl = slice(tb * 512, (tb + 1) * 512)
                dma("act", ya[0][:], YAs.ap()[:, tsl].rearrange("(k p) t -> p k t", p=128), ya[2], reads=[db["YAs"]], writes=[ya[1]])
                dma("act", yx[0][:], YXs.ap()[:, tsl].rearrange("(k p) t -> p k t", p=128), yx[2], reads=[db["YXs"]], writes=[yx[1]])
                dma("act", gt[0][:], GTs.ap()[:, tsl].rearrange("(k p) t -> p k t", p=128), gt[2], reads=[db["GTs"]], writes=[gt[1]])

            load_z(0)
            load_rest(0)
            for tb in range(NTB):
                for ct in range(8):
                    pg, pgb = pG[ng % 2]
                    sg, sgb = sig[ng % 2]; ng += 1

                    def gmm(e, pg=pg, ct=ct):
                        for k in range(8):
                            ins = e.matmul(pg[:], lhsT=W["glu_w"][:, k, ct * 128:(ct + 1) * 128], rhs=zt[0][:, k, :], start=(k == 0), stop=(k == 7))
                        return ins
                    op("pe", gmm, reads=[WB["glu_w"], zt[1]], writes=[pgb])
                    op("act", lambda e, sg=sg, pg=pg, ct=ct: e.activation(out=sg[:], in_=pg[:], func=AF.Sigmoid, bias=glub[:, ct:ct + 1], scale=1.0),
                       reads=[pgb, wb_], writes=[sgb])
                    op("dve", lambda e, sg=sg, ct=ct: e.tensor_tensor(out=yssm[0][:, ct, :], in0=zt[0][:, ct, :], in1=sg[:], op=ALU.mult),
                       reads=[sgb, zt[1]], writes=[yssm[1]])
                if tb + 1 < NTB:
                    load_z(tb + 1)
                for ct in range(8):
                    srcs = ((W["w_br_attn"], ya[0], ya[1], WB["w_br_attn"]), (W["w_br_ssm"], yssm[0], yssm[1], WB["w_br_ssm"]),
                            (W["w_br_xattn"], yx[0], yx[1], WB["w_br_xattn"]))
                    m3, m3b = mm_[0]
                    for bi, (w_, y_, yb_, wbf) in enumerate(srcs):
                        pb_, pbb = pB[bi]

                        def bmm(e, pb_=pb_, w_=w_, y_=y_, ct=ct):
                            for k in range(8):
                                ins = e.matmul(pb_[:], lhsT=w_[:, k, ct * 128:(ct + 1) * 128], rhs=y_[:, k, :], start=(k == 0), stop=(k == 7))
                            return ins
                        op("pe", bmm, reads=[wbf, yb_], writes=[pbb])
                        op("dve", lambda e, m3=m3, pb_=pb_, bi=bi, ct=ct: e.tensor_tensor(out=m3[:, bi, :], in0=pb_[:], in1=gt[0][:, bi * 8 + ct, :], op=ALU.mult),
                           reads=[pbb, gt[1]], writes=[m3b])
                    op("dve", lambda e, m3=m3: e.tensor_tensor(out=m3[:, 0, :], in0=m3[:, 0, :], in1=m3[:, 1, :], op=ALU.add), reads=[m3b], writes=[m3b])
                    op("dve", lambda e, m3=m3, ct=ct: e.tensor_tensor(out=mixT[0][:, ct, :], in0=m3[:, 0, :], in1=m3[:, 2, :], op=ALU.add), reads=[m3b], writes=[mixT[1]])
                if tb + 1 < NTB:
                    load_rest(tb + 1)
                for ts in range(4):
                    xt, xb, xd = xs[nx % 2]
                    x1t, x1b, x1d = x1[nx % 2]; nx += 1
                    r0 = tb * 512 + ts * 128
                    dma("sp", xt[:], I["x"][r0:r0 + 128, :], xd, writes=[xb])
                    for half in range(2):
                        po, pob = pO[no % 2]; no += 1

                        def omm(e, po=po, ts=ts, half=half):
                            for k in range(8):
                                ins = e.matmul(po[:], lhsT=mixT[0][:, k, ts * 128:(ts + 1) * 128], rhs=W["w_out"][:, k, half * 512:(half + 1) * 512], start=(k == 0), stop=(k == 7))
                            return ins
                        op("pe", omm, reads=[WB["w_out"], mixT[1]], writes=[pob])
                        op("dve", lambda e, po=po, half=half, xt=xt, x1t=x1t: e.tensor_tensor(out=x1t[:, half * 512:(half + 1) * 512], in0=po[:], in1=xt[:, half * 512:(half + 1) * 512], op=ALU.add),
                           reads=[pob, xb], writes=[x1b])
                    dma("pool", X1s.ap()[r0:r0 + 128, :], x1t[:], x1d, reads=[x1b], writes=[db["X1s"]])
            B.barrier()

    def phase_T2():
        TB = 256
        with ExitStack() as st:
            REST = FF - 512
            wfi = B.sb([128, 8, 2, REST], BF16, st, "wfi")
            wpre = Tst["wpre"]
            wfo = B.sb([128, NH, D], BF16, st, "wfo")
            wgb = [Tst["wpre_b"]] + [Buf() for _ in range(5)]
            for k in range(1, 6):
                wd = B.ds()
                for gi, base in enumerate((0, FF)):
                    c0 = base + 512 * k
                    c1 = min(base + 512 * (k + 1), base + FF)
                    dma("pool", wfi[:, :, gi, c0 - base - 512:c1 - base - 512], I["w_ffn_in"][:, c0:c1].rearrange("(k p) n -> p k n", p=128), wd, writes=[wgb[k]])

            def wg(k, ht):
                return wpre[:, k, 0, ht * 128:(ht + 1) * 128] if ht < 4 else wfi[:, k, 0, (ht - 4) * 128:(ht - 3) * 128]

            def wu(k, ht):
                return wpre[:, k, 1, ht * 128:(ht + 1) * 128] if ht < 4 else wfi[:, k, 1, (ht - 4) * 128:(ht - 3) * 128]
            wb_ = Buf(); wd = B.ds()
            for n in range(2):
                dma("pool", wfo[:, :, n * 512:(n + 1) * 512], I["w_ffn_out"][:, n * 512:(n + 1) * 512].rearrange("(k p) n -> p k n", p=128), wd, writes=[wb_])
            psT = B.ps([128, 8, 128], BF16, st); psTb = Buf()
            pG = [(B.ps([128, 512], F32, st), Buf()) for _ in range(2)]
            pU = [(B.ps([128, 512], F32, st), Buf()) for _ in range(2)]
            pO = [(B.ps([128, 512], F32, st), Buf()) for _ in range(2)]
            x1t = [(B.sb([128, D], F32, st), Buf(), B.ds()) for _ in range(4)]
            sls = [rms_slot(st) for _ in range(2)]
            h2T = [(B.sb([128, 8, TB], BF16, st), Buf()) for _ in range(2)]
            aT = [(B.sb([128, NH, TB], BF16, st), Buf()) for _ in range(1)]
            sg = [(B.sb([128, TB], F32, st), Buf()) for _ in range(2)]
            x2 = [(B.sb([128, D], F32, st), Buf()) for _ in range(2)]
            fs = [{"ss": B.sb([128, 1], F32, st), "rs": B.sb([128, 1], F32, st), "rr": B.sb([128, 1], F32, st), "b": Buf()} for _ in range(2)]
            ot = [(B.sb([128, D], F32, st), Buf(), B.ds()) for _ in range(1)]
            cnt = {"nx": 0, "ng": 0, "no": 0, "nf": 0}
            nsub = TB // 128
            NTB2 = S // TB

            def norm_in(tb):
                hT_, hTb_ = h2T[tb % 2]
                xts = []
                for ts in range(nsub):
                    xt, xb, xd = x1t[cnt["nx"] % 4]
                    sl = sls[cnt["nx"] % 2]; cnt["nx"] += 1
                    r0 = tb * TB + ts * 128
                    dma("sp", xt[:], X1s.ap()[r0:r0 + 128, :], xd, reads=[db["X1s"]], writes=[xb])
                    rms_T(xt[:], xb, 2, hT_[:, :, ts * 128:(ts + 1) * 128], hTb_, sl, psT, psTb, evac=("dve" if ts % 2 == 0 else "act"))
                    xts.append((xt, xb, r0))
                return xts

            def ffn_in(tb):
                hT_, hTb_ = h2T[tb % 2]
                a_, ab_ = aT[0]
                for ht in range(NH):
                    pg, pgb = pG[cnt["ng"] % 2]
                    pu, pub = pU[cnt["ng"] % 2]
                    s_, sb_ = sg[cnt["ng"] % 2]; cnt["ng"] += 1

                    def gm(e, pg=pg, ht=ht):
                        for k in range(8):
                            ins = e.matmul(pg[:, 0:TB], lhsT=wg(k, ht), rhs=hT_[:, k, :], start=(k == 0), stop=(k == 7))
                        return ins

                    def um(e, pu=pu, ht=ht):
                        for k in range(8):
                            ins = e.matmul(pu[:, 0:TB], lhsT=wu(k, ht), rhs=hT_[:, k, :], start=(k == 0), stop=(k == 7))
                        return ins
                    op("pe", gm, reads=[wgb[ht // 4], hTb_], writes=[pgb])
                    op("pe", um, reads=[wgb[ht // 4], hTb_], writes=[pub])
                    op("act", lambda e, s_=s_, pg=pg: e.activation(out=s_[:], in_=pg[:, 0:TB], func=AF.Silu), reads=[pgb], writes=[sb_])
                    op("dve", lambda e, s_=s_, pu=pu, ht=ht: e.tensor_tensor(out=a_[:, ht, :], in0=pu[:, 0:TB], in1=s_[:], op=ALU.mult), reads=[pub, sb_], writes=[ab_])

            def ffn_out(tb, xts):
                a_, ab_ = aT[0]
                for ts in range(nsub):
                    xt, xb, r0 = xts[ts]
                    x2t, x2b = x2[cnt["nf"] % 2]
                    f = fs[cnt["nf"] % 2]
                    o_, ob_, od_ = ot[0]; cnt["nf"] += 1
                    for half in range(2):
                        po, pob = pO[cnt["no"] % 2]; cnt["no"] += 1

                        def om(e, po=po, ts=ts, half=half):
                            for k in range(NH):
                                ins = e.matmul(po[:], lhsT=a_[:, k, ts * 128:(ts + 1) * 128], rhs=wfo[:, k, half * 512:(half + 1) * 512], start=(k == 0), stop=(k == NH - 1))
                            return ins
                        op("pe", om, reads=[wb_, ab_], writes=[pob])
                        op("dve", lambda e, po=po, half=half, xt=xt, x2t=x2t: e.tensor_tensor(out=x2t[:, half * 512:(half + 1) * 512], in0=po[:], in1=xt[:, half * 512:(half + 1) * 512], op=ALU.add),
                           reads=[pob, xb], writes=[x2b])
                    op("act", lambda e, f=f, x2t=x2t, o_=o_: e.activation(out=o_[:], in_=x2t[:], func=AF.Square, accum_out=f["ss"][:]), reads=[x2b], writes=[f["b"], ob_])
                    op("pool", lambda e, f=f: e.tensor_scalar(out=f["rs"][:], in0=f["ss"][:], scalar1=1.0 / D, scalar2=1e-6, op0=ALU.mult, op1=ALU.add), reads=[f["b"]], writes=[f["b"]])
                    op("pool", lambda e, f=f: e.tensor_tensor(out=f["rr"][:], in0=f["rs"][:], in1=mhalf[:], op=ALU.pow), reads=[f["b"], cbuf], writes=[f["b"]])
                    op("dve", lambda e, f=f, x2t=x2t, o_=o_: e.scalar_tensor_tensor(out=o_[:], in0=x2t[:], scalar=f["rr"][:], in1=gains[:, 3, :], op0=ALU.mult, op1=ALU.mult),
                       reads=[x2b, f["b"], cbuf], writes=[ob_])
                    dma("sp", out_d[r0:r0 + 128, :], o_[:], od_, reads=[ob_], writes=[db["out"]])

            xts_cur = norm_in(0)
            for tb in range(NTB2):
                ffn_in(tb)
                xts_next = norm_in(tb + 1) if tb + 1 < NTB2 else None
                ffn_out(tb, xts_cur)
                xts_cur = xts_next
            B.barrier()

    if "AP" in phases:
        phase_AP(); B.barrier()
    gen = None
    if "D" in phases:
        phase_D1(); B.barrier()
        gen = rec_gen()
    if "B" in phases:
        phase_B(gen); B.barrier()
    if "C" in phases:
        phase_C(gen); B.barrier()
    if gen is not None:
        for _ in gen:
            pass
        B.barrier()
        phase_D2(); B.barrier()
    Tst = {}
    tst = ExitStack()
    if "T2" in phases:
        Tst["wpre"] = B.sb([128, 8, 2, 512], BF16, tst, "wpre")
        Tst["wpre_b"] = Buf()
        Tst["wpre_ds"] = B.ds()

    def prefetch_T2():
        if "T2" in phases:
            for gi, base in enumerate((0, FF)):
                dma("pool", Tst["wpre"][:, :, gi, :], I["w_ffn_in"][:, base:base + 512].rearrange("(k p) n -> p k n", p=128), Tst["wpre_ds"], writes=[Tst["wpre_b"]])
    if "T1" in phases:
        phase_T1(prefetch_T2); B.barrier()
    else:
        prefetch_T2()
    if "T2" in phases:
        phase_T2(); B.barrier()
    tst.close()
    return nc, B


_CACHE = {}


def kernel(**inputs):
    consts = host_consts()
    in_maps = []
    for b in range(8):
        m = host_layout(inputs, b)
        m.update(consts)
        in_maps.append(m)
    if "nc" not in _CACHE:
        _CACHE["nc"] = build_program()[0]
    res = run_bass_kernel_spmd(_CACHE["nc"], in_maps, core_ids=list(range(8)))
    return np.stack([np.asarray(r["out"]) for r in res.results], axis=0).astype(np.float32)
```
